# Optimizing a Trainium2 kernel written in Bass

```python
import jax
import jax.numpy as jnp
from jax import lax
import numpy as np

D_MODEL = 2048
BATCH = 4
SEQ = 4096
DEPTH = 2

HEAD_DIM = 64
N_MIX_HEADS = D_MODEL // HEAD_DIM
A_HEADS = 3 * N_MIX_HEADS // 8
B_GROUPS = N_MIX_HEADS // 4
C_HEADS = N_MIX_HEADS - A_HEADS - B_GROUPS
A_WIDTH = A_HEADS * HEAD_DIM
B_WIDTH = B_GROUPS * HEAD_DIM
C_WIDTH = C_HEADS * HEAD_DIM
D_MIX = A_WIDTH + B_WIDTH + C_WIDTH

DILATED_PATTERNS = ((128, 1), (512, 4), (2048, 16))
BLOCK = 128
REL_BUCKETS = 32
REL_MAX_DISTANCE = 2048

CHUNK = 128

W_LORA = 64
A_LORA = 64
G_LORA = 256
C_LORA = W_LORA + A_LORA + G_LORA
C_IN_COLS = 3 * C_WIDTH + C_LORA
RWKV_GN_EPS = 64e-5

MEM_TOKENS = 256
MEM_HEADS = 4
MEM_HEAD_DIM = 128
MEM_INNER = MEM_HEADS * MEM_HEAD_DIM

D_FF = 5632
CONV_WIDTH = 3

IN_COLS = 3 * A_WIDTH + 2 * B_WIDTH + C_IN_COLS
NORM_EPS = 1e-6
NEG_INF = -1e30

kernel_name = 'hybrid_dilated_sgu_rwkv7_block'


def rmsnorm(x, gain):
    xf = x.astype(jnp.float32)
    y = xf * lax.rsqrt(jnp.mean(xf * xf, axis=-1, keepdims=True) + NORM_EPS)
    return (y * gain.astype(jnp.float32)).astype(x.dtype)


def layernorm(x, gain):
    xf = x.astype(jnp.float32)
    xc = xf - jnp.mean(xf, axis=-1, keepdims=True)
    y = xc * lax.rsqrt(jnp.mean(xc * xc, axis=-1, keepdims=True) + NORM_EPS)
    return (y * gain.astype(jnp.float32)).astype(x.dtype)


def t5_causal_bucket(dist):
    max_exact = REL_BUCKETS // 2
    d = np.maximum(dist, 0)
    scaled = np.log(np.maximum(d, 1) / max_exact) / np.log(REL_MAX_DISTANCE / max_exact)
    large = np.minimum(max_exact + (scaled * (REL_BUCKETS - max_exact)).astype(np.int32), REL_BUCKETS - 1)
    return np.where(d < max_exact, d, large).astype(np.int32)


def dilated_pattern(q, k, v, rel_bias_table, window, dilation):
    b, s, h, dh = q.shape
    steps = window // dilation
    sub_len = s // dilation
    n_blk = -(-sub_len // BLOCK)
    pad = n_blk * BLOCK - sub_len

    def strided(z):
        z = z.reshape(b, sub_len, dilation, h, dh).transpose(0, 2, 3, 1, 4)
        z = jnp.pad(z, ((0, 0), (0, 0), (0, 0), (0, pad), (0, 0)))
        return z.reshape(b, dilation, h, n_blk, BLOCK, dh)

    def with_prev(z):
        prev = jnp.pad(z, ((0, 0), (0, 0), (0, 0), (1, 0), (0, 0), (0, 0)))[:, :, :, :-1]
        return jnp.concatenate([prev, z], axis=4)

    qb = strided(q)
    kw = with_prev(strided(k))
    vw = with_prev(strided(v))

    qi = np.arange(BLOCK)[:, None]
    kj = np.arange(2 * BLOCK)[None, :]
    delta = qi + BLOCK - kj
    band = (delta >= 0) & (delta <= steps)
    valid = np.where((np.arange(n_blk) == 0)[:, None, None], band & (kj >= BLOCK), band)
    bucket = t5_causal_bucket(np.clip(delta, 0, steps) * dilation)
    bias = jnp.transpose(rel_bias_table.astype(jnp.float32)[bucket], (2, 0, 1))

    scores = jnp.einsum('brhnqd,brhnkd->brhnqk', qb, kw).astype(jnp.float32) * (dh ** -0.5)
    scores = jnp.where(valid, scores + bias[None, None, :, None], NEG_INF)
    m = jnp.max(scores, axis=-1, keepdims=True)
    p = jnp.exp(scores - m)
    den = jnp.sum(p, axis=-1, keepdims=True)
    o = jnp.einsum('brhnqk,brhnkd->brhnqd', p, vw.astype(jnp.float32)) / den
    lse = (m + jnp.log(den))[..., 0]

    def unstrided(z):
        z = z.reshape(b, dilation, h, n_blk * BLOCK, *z.shape[5:])[:, :, :, :sub_len]
        z = jnp.moveaxis(z, 3, 1)
        return z.reshape(b, s, h, *z.shape[4:])

    return unstrided(o), unstrided(lse)


def dilated_attention(q, k, v, rel_bias_table):
    outs, lses = [], []
    for window, dilation in DILATED_PATTERNS:
        o, lse = dilated_pattern(q, k, v, rel_bias_table, window, dilation)
        outs.append(o)
        lses.append(lse)
    weights = jax.nn.softmax(jnp.stack(lses), axis=0)
    return jnp.sum(weights[..., None] * jnp.stack(outs), axis=0)


def spatial_gating(zb, norm_gain, w_s, b_s):
    b, s, _ = zb.shape
    u, g = jnp.split(jax.nn.gelu(zb), 2, axis=-1)
    g = layernorm(g, norm_gain).reshape(b, s // CHUNK, CHUNK, B_GROUPS, HEAD_DIM)
    w_causal = jnp.where(np.tril(np.ones((CHUNK, CHUNK), dtype=bool)), w_s, 0)
    mixed = jnp.einsum('gij,bcjgd->bcigd', w_causal, g) + b_s.T[None, None, :, :, None]
    return u * mixed.reshape(b, s, B_WIDTH)


def rwkv7_scan(r, decay, k, v, kk, a):
    b, s, h, n = r.shape

    def step(state, inp):
        r_t, w_t, k_t, v_t, kk_t, a_t = inp
        sa = jnp.einsum('bhvk,bhk->bhv', state, -kk_t)
        state = (state * w_t[:, :, None, :] + sa[..., None] * (kk_t * a_t)[:, :, None, :]
                 + v_t[..., None] * k_t[:, :, None, :])
        return state, jnp.einsum('bhvk,bhk->bhv', state, r_t)

    xs = tuple(jnp.moveaxis(z, 1, 0) for z in (r, decay, k, v, kk, a))
    _, ys = lax.scan(step, jnp.zeros((b, h, n, n), jnp.float32), xs)
    return jnp.moveaxis(ys, 0, 1)


def rwkv7_time_mix(zc, mu, w0, w_up, a0, a_up, g_up, k_k, k_a, r_k, ln_gain, ln_bias):
    b, s, _ = zc.shape
    zc = zc.astype(jnp.float32)
    prev = jnp.pad(zc, ((0, 0), (1, 0), (0, 0)))[:, :-1]
    zc = zc + (prev - zc) * mu
    cuts = [C_WIDTH, 2 * C_WIDTH, 3 * C_WIDTH, 3 * C_WIDTH + W_LORA, 3 * C_WIDTH + W_LORA + A_LORA]
    r, k, v, w_lo, a_lo, g_lo = jnp.split(zc, cuts, axis=-1)
    w_log = -jax.nn.softplus(-(w0 + jnp.tanh(w_lo) @ w_up)) - 0.5
    decay = jnp.exp(-jnp.exp(w_log))
    a = jax.nn.sigmoid(a0 + a_lo @ a_up)
    g = jax.nn.sigmoid(g_lo) @ g_up
    kk = k * k_k
    k = k * (1.0 + (a - 1.0) * k_a)

    def heads(z):
        return z.reshape(b, s, C_HEADS, HEAD_DIM)

    r, k, v, kk, a, decay = heads(r), heads(k), heads(v), heads(kk), heads(a), heads(decay)
    kk = kk / jnp.maximum(jnp.linalg.norm(kk, axis=-1, keepdims=True), 1e-12)
    y = rwkv7_scan(r, decay, k, v, kk, a)
    mean = jnp.mean(y, axis=-1, keepdims=True)
    yc = y - mean
    y = yc * lax.rsqrt(jnp.mean(yc * yc, axis=-1, keepdims=True) + RWKV_GN_EPS)
    y = y * ln_gain.reshape(C_HEADS, HEAD_DIM) + ln_bias.reshape(C_HEADS, HEAD_DIM)
    y = y + jnp.sum(r * k * r_k, axis=-1, keepdims=True) * v
    return y.reshape(b, s, C_WIDTH) * g


def hybrid_mixer(h, rel_bias_table, w_in, w_out, attn_out_gain, sgu_norm_gain, sgu_w, sgu_b,
                 sgu_out_gain, rwkv_mu, rwkv_w0, rwkv_w_up, rwkv_a0, rwkv_a_up, rwkv_g_up,
                 rwkv_k_k, rwkv_k_a, rwkv_r_k, rwkv_ln_gain, rwkv_ln_bias):
    b, s, _ = h.shape
    z = h @ w_in
    za, zb, zc = jnp.split(z, [3 * A_WIDTH, 3 * A_WIDTH + 2 * B_WIDTH], axis=-1)
    qkv = za.reshape(b, s, 3, A_HEADS, HEAD_DIM)
    oa = dilated_attention(qkv[:, :, 0], qkv[:, :, 1], qkv[:, :, 2], rel_bias_table)
    oa = rmsnorm(oa.reshape(b, s, A_WIDTH), attn_out_gain).astype(h.dtype)
    ob = rmsnorm(spatial_gating(zb, sgu_norm_gain, sgu_w, sgu_b), sgu_out_gain).astype(h.dtype)
    oc = rwkv7_time_mix(zc, rwkv_mu, rwkv_w0, rwkv_w_up, rwkv_a0, rwkv_a_up, rwkv_g_up,
                        rwkv_k_k, rwkv_k_a, rwkv_r_k, rwkv_ln_gain, rwkv_ln_bias).astype(h.dtype)
    return jnp.concatenate([oa, ob, oc], axis=-1) @ w_out


def memory_cross_attention(h, mem_n, wq, wkv, wo):
    b, s, _ = h.shape
    q = (h @ wq).reshape(b, s, MEM_HEADS, MEM_HEAD_DIM)
    kv = (mem_n @ wkv).reshape(b, mem_n.shape[1], 2, MEM_HEADS, MEM_HEAD_DIM)
    scores = jnp.einsum('bshd,bmhd->bhsm', q, kv[:, :, 0]).astype(jnp.float32) * (MEM_HEAD_DIM ** -0.5)
    p = jax.nn.softmax(scores, axis=-1).astype(h.dtype)
    o = jnp.einsum('bhsm,bmhd->bshd', p, kv[:, :, 1]).reshape(b, s, MEM_INNER)
    return o @ wo


def conv_ffn(h, w_up, conv_w, conv_b, w_down):
    s = h.shape[1]
    up = h @ w_up
    up_pad = jnp.pad(up, ((0, 0), (CONV_WIDTH - 1, 0), (0, 0)))
    conv = conv_b + sum(up_pad[:, j:j + s] * conv_w[j] for j in range(CONV_WIDTH))
    gate, val = jnp.split(conv, 2, axis=-1)
    return (jax.nn.gelu(gate, approximate=True) * val) @ w_down


def setup_inputs(seed: int = 0) -> dict:
    key = jax.random.key(seed)
    keys = jax.random.split(key, 40)
    counter = [0]

    def nxt():
        kk = keys[counter[0]]
        counter[0] += 1
        return kk

    def nrm(shape, scale=1.0):
        return scale * jax.random.normal(nxt(), shape, jnp.float32)

    def gain(shape):
        return 1.0 + nrm(shape, 0.1)

    L = DEPTH
    return {
        'x': nrm((BATCH, SEQ, D_MODEL)),
        'mem': nrm((BATCH, MEM_TOKENS, D_MODEL)),
        'rel_bias_table': nrm((REL_BUCKETS, A_HEADS), 0.5),
        'sandwich_gains': gain((L, 6, D_MODEL)),
        'mem_src_gain': gain((L, D_MODEL)),
        'w_in': nrm((L, D_MODEL, IN_COLS), D_MODEL ** -0.5),
        'w_out': nrm((L, D_MIX, D_MODEL), D_MIX ** -0.5),
        'attn_out_gain': gain((L, A_WIDTH)),
        'sgu_norm_gain': gain((L, B_WIDTH)),
        'sgu_w': nrm((L, B_GROUPS, CHUNK, CHUNK), CHUNK ** -0.5),
        'sgu_b': gain((L, B_GROUPS, CHUNK)),
        'sgu_out_gain': gain((L, B_WIDTH)),
        'rwkv_mu': jax.random.uniform(nxt(), (L, C_IN_COLS), jnp.float32),
        'rwkv_w0': nrm((L, C_WIDTH), 0.5),
        'rwkv_w_up': nrm((L, W_LORA, C_WIDTH), W_LORA ** -0.5),
        'rwkv_a0': nrm((L, C_WIDTH), 0.1),
        'rwkv_a_up': nrm((L, A_LORA, C_WIDTH), A_LORA ** -0.5),
        'rwkv_g_up': nrm((L, G_LORA, C_WIDTH), G_LORA ** -0.5),
        'rwkv_k_k': gain((L, C_WIDTH)),
        'rwkv_k_a': gain((L, C_WIDTH)),
        'rwkv_r_k': nrm((L, C_HEADS, HEAD_DIM), 0.1),
        'rwkv_ln_gain': gain((L, C_WIDTH)),
        'rwkv_ln_bias': nrm((L, C_WIDTH), 0.01),
        'mem_wq': nrm((L, D_MODEL, MEM_INNER), D_MODEL ** -0.5),
        'mem_wkv': nrm((L, D_MODEL, 2 * MEM_INNER), D_MODEL ** -0.5),
        'mem_wo': nrm((L, MEM_INNER, D_MODEL), MEM_INNER ** -0.5),
        'ffn_w_up': nrm((L, D_MODEL, 2 * D_FF), D_MODEL ** -0.5),
        'ffn_conv_w': nrm((L, CONV_WIDTH, 2 * D_FF), CONV_WIDTH ** -0.5),
        'ffn_conv_b': nrm((L, 2 * D_FF), 0.01),
        'ffn_w_down': nrm((L, D_FF, D_MODEL), D_FF ** -0.5),
    }


def reference(x, mem, rel_bias_table, sandwich_gains, mem_src_gain, w_in, w_out, attn_out_gain,
              sgu_norm_gain, sgu_w, sgu_b, sgu_out_gain, rwkv_mu, rwkv_w0, rwkv_w_up, rwkv_a0,
              rwkv_a_up, rwkv_g_up, rwkv_k_k, rwkv_k_a, rwkv_r_k, rwkv_ln_gain, rwkv_ln_bias,
              mem_wq, mem_wkv, mem_wo, ffn_w_up, ffn_conv_w, ffn_conv_b, ffn_w_down):
    for l in range(DEPTH):
        g = sandwich_gains[l]
        h = rmsnorm(x, g[0])
        y = hybrid_mixer(h, rel_bias_table, w_in[l], w_out[l], attn_out_gain[l], sgu_norm_gain[l],
                         sgu_w[l], sgu_b[l], sgu_out_gain[l], rwkv_mu[l], rwkv_w0[l], rwkv_w_up[l],
                         rwkv_a0[l], rwkv_a_up[l], rwkv_g_up[l], rwkv_k_k[l], rwkv_k_a[l], rwkv_r_k[l],
                         rwkv_ln_gain[l], rwkv_ln_bias[l])
        x = x + rmsnorm(y, g[1])
        h = rmsnorm(x, g[2])
        y = memory_cross_attention(h, rmsnorm(mem, mem_src_gain[l]), mem_wq[l], mem_wkv[l], mem_wo[l])
        x = x + rmsnorm(y, g[3])
        h = rmsnorm(x, g[4])
        y = conv_ffn(h, ffn_w_up[l], ffn_conv_w[l], ffn_conv_b[l], ffn_w_down[l])
        x = x + rmsnorm(y, g[5])
    return x
```

```python
import contextlib

import numpy as np
import concourse.bass as bass
import concourse.mybir as mybir
from concourse.bass_utils import run_bass_kernel_spmd

F32 = mybir.dt.float32
BF16 = mybir.dt.bfloat16
AF = mybir.ActivationFunctionType
ALU = mybir.AluOpType
AX = mybir.AxisListType

SEM_ROT = 30000


class Buf:
    __slots__ = ("name", "last_w", "readers", "dsem", "dval", "psum")

    def __init__(self, name, psum=False):
        self.name = name
        self.psum = psum
        self.last_w = None
        self.readers = []
        self.dsem = None
        self.dval = 0


class Sched:
    ENGS = ("pe", "act", "dve", "pool", "sp")

    def __init__(self, nc, sem_pool):
        self.nc = nc
        self.sem_pool = [(s, 0) for s in sem_pool]
        self.ops = {e: [] for e in self.ENGS}
        self.cnt = {e: 0 for e in self.ENGS}
        self.sem = {e: self.sem_pool.pop()[0] for e in self.ENGS}
        self.eng_spare = [self.sem_pool.pop()[0] for _ in range(10)]
        self.last_tok = {e: None for e in self.ENGS}
        self.waited = {e: {} for e in self.ENGS}
        self.nops = 0
        self.dma_bufs = {}

    def _new_sem(self):
        return self.sem_pool.pop()[0]

    def recycle_dma_sems(self):
        for b in self.dma_bufs.values():
            if b.dsem is not None:
                if b.dval < SEM_ROT:
                    self.sem_pool.insert(0, (b.dsem, b.dval))
                b.dsem = None
        self.dma_bufs = {}

    def op(self, eng, fn, reads=(), writes=(), dma=False, dinc=16):
        writes = list(writes) + [b for b in reads if b.psum]
        deps = {}
        for b in reads:
            if b.last_w is not None:
                s, v, e2 = b.last_w
                if not (eng == "pe" and e2 == "pe"):
                    deps[s] = max(deps.get(s, 0), v)
        for b in writes:
            if b.last_w is not None:
                s, v, e2 = b.last_w
                if not (eng == "pe" and e2 == "pe"):
                    deps[s] = max(deps.get(s, 0), v)
            for (s, v, e2) in b.readers:
                if not (eng == "pe" and e2 == "pe"):
                    deps[s] = max(deps.get(s, 0), v)
        waits = []
        wd = self.waited[eng]
        for s, v in deps.items():
            if wd.get(id(s), 0) < v:
                wd[id(s)] = v
                waits.append((s, v))
        if dma:
            b = writes[0]
            if b.dsem is None or b.dval >= SEM_ROT:
                b.dsem, b.dval = self.sem_pool.pop()
            b.dval += dinc
            tok = (b.dsem, b.dval, "dma")
            csem, inc = b.dsem, dinc
        else:
            if self.cnt[eng] >= SEM_ROT:
                self.sem[eng] = self.eng_spare.pop()
                self.cnt[eng] = 0
            self.cnt[eng] += 1
            tok = (self.sem[eng], self.cnt[eng], eng)
            self.last_tok[eng] = tok
            csem, inc = self.sem[eng], 1

        def run(e, fn=fn, waits=waits, csem=csem, inc=inc):
            for s, v in waits:
                e.wait_ge(s, v)
            fn(e).then_inc(csem, inc)

        self.ops[eng].append(run)
        self.nops += 1
        for b in reads:
            b.readers = [t for t in b.readers if t[0] is not tok[0]] + [tok]
        for b in writes:
            b.last_w = tok
            b.readers = []
            if dma:
                self.dma_bufs[id(b)] = b
        return tok

    def barrier(self):
        allw = [self.last_tok[e] for e in self.ENGS if self.last_tok[e] is not None]
        allw += [(b.dsem, b.dval, "dma") for b in self.dma_bufs.values() if b.dsem is not None]
        for eng in self.ENGS:
            waits = []
            wd = self.waited[eng]
            for s, v, e2 in allw:
                if e2 == eng and eng == "pe":
                    continue
                if wd.get(id(s), 0) < v:
                    wd[id(s)] = v
                    waits.append((s, v))

            def run(e, waits=waits):
                for s, v in waits:
                    e.wait_ge(s, v)
            self.ops[eng].append(run)

    def final_wait(self, eng, bufs):
        waits = []
        for b in bufs:
            if b.last_w is not None:
                waits.append((b.last_w[0], b.last_w[1]))

        def run(e, waits=waits):
            for s, v in waits:
                e.wait_ge(s, v)
        self.ops[eng].append(run)

    def emit(self):
        nc = self.nc
        with nc.Block() as block:
            @block.tensor
            def _(e):
                for f in self.ops["pe"]:
                    f(e)

            @block.scalar
            def _(e):
                for f in self.ops["act"]:
                    f(e)

            @block.vector
            def _(e):
                for f in self.ops["dve"]:
                    f(e)

            @block.gpsimd
            def _(e):
                for f in self.ops["pool"]:
                    f(e)

            @block.sync
            def _(e):
                for f in self.ops["sp"]:
                    f(e)


class Arena:
    BASE = 16512
    LIMIT = 229000

    def __init__(self, nc):
        self.nc = nc
        self.off = self.BASE
        self.n = 0
        self.peak = 0

    def alloc(self, name, shape, dt):
        esz = 2 if dt == BF16 else 4
        nbytes = esz * int(np.prod(shape[1:]))
        nbytes = (nbytes + 63) // 64 * 64
        assert self.off + nbytes <= self.LIMIT, (name, self.off, nbytes)
        self.n += 1
        t = self.nc.alloc_sbuf_tensor_at(f"{name}_{self.n}", list(shape), dt, offset=self.off)
        self.off += nbytes
        self.peak = max(self.peak, self.off)
        return t

    def mark(self):
        return self.off

    def release(self, m):
        self.off = m

class PS:
    def __init__(self, t, name):
        self.t = t
        self.b = Buf(name, psum=True)


import contextlib
import numpy as np

T = 4096
DM = 2048
NT = T // 128
NF_A = 768
NF_C = 1152
NTM = 1536
EPS_M = 1e-6


def declare_io_M(nc, sfx="", with_x=True):
    io = {}
    def din(name, shape):
        return nc.dram_tensor(name + sfx, list(shape), F32, kind="ExternalInput").ap()
    if with_x:
        io["x"] = din("x", [T, DM])
        io["x_rows"] = lambda t, xx=io["x"]: xx[t * 128:(t + 1) * 128, :]
    io["x_b"] = Buf("x_in")
    io["x_bf"] = lambda t, b=io["x_b"]: b
    io["g0T"] = din("g0T", [128, 16])
    io["w_fm"] = din("w_fm", [DM, NF_A + NF_C])
    io["w_tm"] = din("w_tm", [DM, NTM])
    io["biasT"] = din("biasT", [128, 36, 128])
    io["sgu_wT"] = din("sgu_wT", [128, 4, 128])
    io["sgu_bT"] = din("sgu_bT", [128, 4])
    io["sgu_ng"] = din("sgu_ng", [512])
    io["cp"] = din("cp", [128, 25])
    io["cf"] = din("cf", [3, 384])
    io["w_up"] = din("w_up", [64, 384])
    io["a_up"] = din("a_up", [64, 384])
    io["g_up"] = din("g_up", [256, 384])
    return io


def emit_M(S, nc, io, scr, out, psbig):
    ps = [PS(BankView(psbig[:, i * 512:(i + 1) * 512]), f"ps{i}") for i in range(8)]
    phase1(S, nc, io, scr, ps, S.ident)
    phase2(S, nc, io, scr, ps, out)
    S.op("sp", lambda e: e.dma_start(out=out["ssq"][0], in_=out["ssqA"][:]), reads=[out["ssqA_b"]], writes=[out["ssq_b"]], dma=True)
    phase3(S, nc, io, scr, ps, out)
    S.op("sp", lambda e: e.dma_start(out=out["ssq"][1], in_=out["ssqB"][:]), reads=[out["ssqB_b"]], writes=[out["ssq_b"]], dma=True)
    phase4(S, nc, io, scr, psbig, out)


def phase1(S, nc, io, scr, ps, ident):
    A = S.arena
    m0 = A.mark()
    if True:
        sb = A.alloc
        hT = sb("hT", [128, 16, T], BF16)
        hT_b = Buf("hT")
        g0T = sb("g0T_sb", [128, 16], F32)
        g0T_b = Buf("g0T")
        S.op("sp", lambda e: e.dma_start(out=g0T[:], in_=io["g0T"]), writes=[g0T_b], dma=True)
        m1 = A.mark()
        if True:
            sb2 = A.alloc
            xt = [sb2(f"xt{i}", [128, DM], F32) for i in range(2)]
            xt_b = [Buf(f"xt{i}") for i in range(2)]
            xs = [sb2(f"xs{i}", [128, DM], BF16) for i in range(2)]
            xs_b = [Buf(f"xs{i}") for i in range(2)]
            junk = sb2("junk", [128, DM], BF16)
            junk_b = Buf("junk")
            st = [sb2(f"st{i}", [128, 4], F32) for i in range(2)]
            st_b = [Buf(f"st{i}") for i in range(2)]
            for t in range(NT):
                i = t % 2
                S.op("sp", lambda e, t=t, i=i: e.dma_start(out=xt[i][:], in_=io["x_rows"](t)),
                     reads=[io["x_bf"](t)], writes=[xt_b[i]], dma=True)
                S.op("act", lambda e, i=i: e.activation(out=junk[:], in_=xt[i][:], func=AF.Square,
                                                        accum_out=st[i][:, 0:1]),
                     reads=[xt_b[i]], writes=[junk_b, st_b[i]])
                S.op("act", lambda e, i=i: e.activation(out=st[i][:, 1:2], in_=st[i][:, 0:1], func=AF.Sqrt,
                                                        scale=1.0 / DM, bias=EPS_M),
                     reads=[st_b[i]], writes=[st_b[i]])
                S.op("dve", lambda e, i=i: e.reciprocal(out=st[i][:, 2:3], in_=st[i][:, 1:2]),
                     reads=[st_b[i]], writes=[st_b[i]])
                S.op("dve", lambda e, i=i: e.tensor_scalar(out=xs[i][:], in0=xt[i][:], scalar1=st[i][:, 2:3],
                                                           scalar2=None, op0=ALU.mult),
                     reads=[xt_b[i], st_b[i]], writes=[xs_b[i]])
                pa, pb = ps[2 * i], ps[2 * i + 1]
                for c in range(16):
                    pp = pa if c < 8 else pb
                    S.op("pe", lambda e, c=c, pp=pp, i=i: e.transpose(
                        out=pp.t.ap().bitcast(BF16)[:, (c % 8) * 128:(c % 8 + 1) * 128],
                        in_=xs[i][:, c * 128:(c + 1) * 128], identity=ident[:]),
                        reads=[xs_b[i]], writes=[pp.b])
                for half, pp in enumerate((pa, pb)):
                    S.op("dve" if half == 0 else "pool" if False else "dve", lambda e, half=half, pp=pp, t=t: e.tensor_tensor(
                        out=hT[:, half * 8:(half + 1) * 8, t * 128:(t + 1) * 128],
                        in0=pp.t.ap().bitcast(BF16).rearrange("p (c n) -> p c n", c=8),
                        in1=g0T[:, half * 8:(half + 1) * 8].unsqueeze(2).to_broadcast([128, 8, 128]),
                        op=ALU.mult),
                        reads=[pp.b, g0T_b], writes=[hT_b])
        S.barrier()
        A.release(m1)
        if True:
            sb2 = A.alloc
            wj = [sb2(f"wj{i}", [128, 16, 128], BF16) for i in range(2)]
            wj_b = [Buf(f"wj{i}") for i in range(2)]
            zst16 = [sb2(f"zst16_{i}", [128, T], BF16) for i in range(2)]
            zst32 = [sb2(f"zst32_{i}", [128, T + 1], F32) for i in range(1)]
            zst16_b = [Buf(f"zst16_{i}") for i in range(2)]
            zst32_b = [Buf(f"zst32_{i}") for i in range(1)]
            wtm = [sb2(f"wtm{i}", [128, 16, 512], BF16) for i in range(1)]
            wtm_b = [Buf(f"wtm{i}") for i in range(1)]
            tst = [sb2(f"tst{i}", [128, 512], F32) for i in range(3)]
            tst_b = [Buf(f"tst{i}") for i in range(3)]
            S.op("pool", lambda e: e.memset(zst32[0][:, 0:1], 0.0), writes=[zst32_b[0]])
            zrow = sb2("zrow", [1, NTM], F32)
            zrow_b = Buf("zrow")
            S.op("pool", lambda e: e.memset(zrow[:], 0.0), writes=[zrow_b])
            S.op("sp", lambda e: e.dma_start(out=scr["ztm"][0:1, :], in_=zrow[:]), reads=[zrow_b], writes=[scr["ztm_b"]], dma=True)
            w_fm_v = io["w_fm"].rearrange("(kc p) n -> p kc n", p=128)
            w_tm_v = io["w_tm"].rearrange("(kc p) n -> p kc n", p=128)
            nchunks = (NF_A + NF_C) // 128
            pi = 0
            for j in range(nchunks):
                i = j % 2
                S.op("pool", lambda e, j=j, i=i: e.dma_start(out=wj[i][:], in_=w_fm_v[:, :, j * 128:(j + 1) * 128]),
                     writes=[wj_b[i]], dma=True)
                is_a = j < NF_A // 128
                if is_a:
                    zt, ztb = zst16[j % 2], zst16_b[j % 2]
                    zo = 0
                else:
                    zt, ztb = zst32[0], zst32_b[0]
                    zo = 1
                for tg in range(T // 512):
                    pp = ps[pi % 8]
                    pi += 1
                    for kc in range(16):
                        S.op("pe", lambda e, kc=kc, pp=pp, i=i, tg=tg: e.matmul(
                            pp.t[:, :], lhsT=wj[i][:, kc, :], rhs=hT[:, kc, tg * 512:(tg + 1) * 512],
                            start=(kc == 0), stop=(kc == 15)),
                            reads=[wj_b[i], hT_b], writes=[pp.b])
                    if tg % 2 == 0:
                        S.op("act", lambda e, pp=pp, zt=zt, tg=tg, zo=zo: e.copy(out=zt[:, zo + tg * 512:zo + (tg + 1) * 512], in_=pp.t[:, :]),
                             reads=[pp.b], writes=[ztb])
                    else:
                        S.op("dve", lambda e, pp=pp, zt=zt, tg=tg, zo=zo: e.tensor_copy(out=zt[:, zo + tg * 512:zo + (tg + 1) * 512], in_=pp.t[:, :]),
                             reads=[pp.b], writes=[ztb])
                if is_a:
                    S.op("sp", lambda e, j=j, zt=zt: e.dma_start(out=scr["zTa"][j * 128:(j + 1) * 128, :], in_=zt[:]),
                         reads=[ztb], writes=[scr["zTa_b"]], dma=True)
                else:
                    jj = j - NF_A // 128
                    S.op("sp", lambda e, jj=jj, zt=zt: e.dma_start(out=scr["zTc"][jj * 128:(jj + 1) * 128, :], in_=zt[:]),
                         reads=[ztb], writes=[scr["zTc_b"]], dma=True)
            k = 0
            for blk in range(NTM // 512):
                S.op("pool", lambda e, blk=blk: e.dma_start(out=wtm[0][:], in_=w_tm_v[:, :, blk * 512:(blk + 1) * 512]),
                     writes=[wtm_b[0]], dma=True)
                for t in range(NT):
                    pp = ps[pi % 8]
                    pi += 1
                    for kc in range(16):
                        S.op("pe", lambda e, kc=kc, pp=pp, t=t: e.matmul(
                            pp.t[:, :], lhsT=hT[:, kc, t * 128:(t + 1) * 128], rhs=wtm[0][:, kc, :],
                            start=(kc == 0), stop=(kc == 15)),
                            reads=[wtm_b[0], hT_b], writes=[pp.b])
                    ti = k % 3
                    k += 1
                    if t % 2 == 0:
                        S.op("act", lambda e, pp=pp, ti=ti: e.copy(out=tst[ti][:], in_=pp.t[:, :]),
                             reads=[pp.b], writes=[tst_b[ti]])
                    else:
                        S.op("dve", lambda e, pp=pp, ti=ti: e.tensor_copy(out=tst[ti][:], in_=pp.t[:, :]),
                             reads=[pp.b], writes=[tst_b[ti]])
                    S.op("sp", lambda e, ti=ti, t=t, blk=blk: e.dma_start(
                        out=scr["ztm"][1 + t * 128:1 + (t + 1) * 128, blk * 512:(blk + 1) * 512], in_=tst[ti][:]),
                        reads=[tst_b[ti]], writes=[scr["ztm_b"]], dma=True)
        S.barrier()
        A.release(m0)


class BankView:
    def __init__(self, ap):
        self._ap = ap

    def ap(self):
        return self._ap

    def __getitem__(self, k):
        return self._ap[k]


DILS = (1, 4, 16)


def phase2(S, nc, io, scr, ps, out):
    A = S.arena
    m0 = A.mark()
    sb = A.alloc
    biasT = sb("biasT", [128, 6 * 3 * 2, 128], F32)
    biasT_b = Buf("biasT")
    S.op("sp", lambda e: e.dma_start(out=biasT[:], in_=io["biasT"]), writes=[biasT_b], dma=True)
    qT = [sb(f"qT{i}", [128, T], BF16) for i in range(2)]
    kT = [sb(f"kT{i}", [128, T], BF16) for i in range(2)]
    qT_b = [Buf(f"qT{i}") for i in range(2)]
    kT_b = [Buf(f"kT{i}") for i in range(2)]
    Vd = [[sb(f"Vd{hh}_{p}", [128, 32, 65], BF16) for p in range(3)] for hh in range(2)]
    Vd_b = [[Buf(f"Vd{hh}_{p}") for p in range(3)] for hh in range(2)]
    for hh in range(2):
        for p in range(3):
            S.op("pool", lambda e, hh=hh, p=p: e.memset(Vd[hh][p][:, :, 64:65], 1.0), writes=[Vd_b[hh][p]])
    NTB = 4
    NPS = 4
    tt = [sb(f"att_t{i}", [128, 256], F32) for i in range(NTB)]
    tt_b = [Buf(f"att_t{i}") for i in range(NTB)]
    PT = [sb(f"att_PT{i}", [128, 256], BF16) for i in range(NTB)]
    PT_b = [Buf(f"att_PT{i}") for i in range(NTB)]
    Oacc = [sb(f"Oacc{p}", [128, 32, 65], F32) for p in range(3)]
    Oacc_b = [Buf(f"Oacc{p}") for p in range(3)]
    Old = [sb(f"Old{p}", [128, 32, 65], F32) for p in range(3)]
    Old_b = [Buf(f"Old{p}") for p in range(3)]
    rec = sb("att_rec", [128, 32], F32)
    rec_b = Buf("att_rec")
    oah = sb("oah", [128, 32, 64], F32)
    oah_b = Buf("oah")
    sq = sb("att_sq", [128, 32, 64], F32)
    sq_b = Buf("att_sq")
    ssq1 = sb("att_ssq1", [128, 32], F32)
    ssq1_b = Buf("att_ssq1")
    ssqA = out["ssqA"]
    ssqA_b = out["ssqA_b"]
    S.op("pool", lambda e: e.memset(ssqA[:], 0.0), writes=[ssqA_b])
    psS = ps[0:4]
    psO = ps[4:6]
    blkc = 0
    for pair in range(3):
        i = pair % 2
        S.op("sp", lambda e, pair=pair, i=i: e.dma_start(out=qT[i][:], in_=scr["zTa"][pair * 128:(pair + 1) * 128, :]),
             reads=[scr["zTa_b"]], writes=[qT_b[i]], dma=True)
        S.op("sp", lambda e, pair=pair, i=i: e.dma_start(out=kT[i][:], in_=scr["zTa"][384 + pair * 128:384 + (pair + 1) * 128, :]),
             reads=[scr["zTa_b"]], writes=[kT_b[i]], dma=True)
        for hh in range(2):
            h = pair * 2 + hh
            p0 = 64 * hh
            for p, d in enumerate(DILS):
                S.op("pool", lambda e, hh=hh, p=p, d=d, h=h: e.dma_start(
                    out=Vd[hh][p][:, :, 0:64].rearrange("j (r n) c -> j r n c", r=d),
                    in_=scr["ztm"][1:T + 1, h * 64:(h + 1) * 64].rearrange("(n j r) c -> j r n c", j=128, r=d)),
                    reads=[scr["ztm_b"]], writes=[Vd_b[hh][p]], dma=True)
            items = []
            for p, d in enumerate(DILS):
                nblk = 32 // d
                for r in range(d):
                    for n in range(nblk):
                        items.append((p, d, r, n, r * nblk + n))

            def emit_scores(k, item):
                p, d, r, n, blk = item
                pS = psS[k % NPS]
                ti = k % NTB
                qs = n * 128 * d + r
                qsl = slice(qs, qs + 127 * d + 1, d) if d > 1 else slice(qs, qs + 128)
                bidx = (h * 3 + p) * 2
                if n > 0:
                    ks = (n - 1) * 128 * d + r
                    ksl = slice(ks, ks + 127 * d + 1, d) if d > 1 else slice(ks, ks + 128)
                    S.op("pe", lambda e, pS=pS, ksl=ksl, qsl=qsl, i=i, p0=p0: e.matmul(
                        pS.t[:, 0:128], lhsT=kT[i][p0:p0 + 64, ksl], rhs=qT[i][p0:p0 + 64, qsl], start=True, stop=True),
                        reads=[kT_b[i], qT_b[i]], writes=[pS.b])
                S.op("pe", lambda e, pS=pS, qsl=qsl, i=i, p0=p0: e.matmul(
                    pS.t[:, 128:256], lhsT=kT[i][p0:p0 + 64, qsl], rhs=qT[i][p0:p0 + 64, qsl], start=True, stop=True),
                    reads=[kT_b[i], qT_b[i]], writes=[pS.b])
                c0 = 0 if n > 0 else 128
                S.op("dve", lambda e, pS=pS, ti=ti, c0=c0, bidx=bidx: e.scalar_tensor_tensor(
                    out=tt[ti][:, c0:256], in0=pS.t[:, c0:256], scalar=0.125,
                    in1=biasT[:, bidx:bidx + 2, :].rearrange("p a q -> p (a q)")[:, c0:256],
                    op0=ALU.mult, op1=ALU.add),
                    reads=[pS.b, biasT_b], writes=[tt_b[ti]])
                S.op("act", lambda e, ti=ti, c0=c0: e.activation(out=PT[ti][:, c0:256], in_=tt[ti][:, c0:256], func=AF.Exp),
                     reads=[tt_b[ti]], writes=[PT_b[ti]])

            def emit_pv(k, item):
                p, d, r, n, blk = item
                ti = k % NTB
                slot = blk % 4
                pO = psO[(blk // 4) % 2]
                if n > 0:
                    S.op("pe", lambda e, pO=pO, slot=slot, ti=ti, p=p, blk=blk, hh=hh: e.matmul(
                        pO.t[:, slot * 65:(slot + 1) * 65], lhsT=PT[ti][:, 0:128], rhs=Vd[hh][p][:, blk - 1, :],
                        start=True, stop=False),
                        reads=[PT_b[ti], Vd_b[hh][p]], writes=[pO.b])
                S.op("pe", lambda e, pO=pO, slot=slot, ti=ti, p=p, blk=blk, n=n, hh=hh: e.matmul(
                    pO.t[:, slot * 65:(slot + 1) * 65], lhsT=PT[ti][:, 128:256], rhs=Vd[hh][p][:, blk, :],
                    start=(n == 0), stop=True),
                    reads=[PT_b[ti], Vd_b[hh][p]], writes=[pO.b])
                if slot == 3:
                    b0 = blk - 3
                    if (blk // 4) % 2 == 0:
                        S.op("act", lambda e, pO=pO, p=p, b0=b0: e.copy(
                            out=Oacc[p][:, b0:b0 + 4, :].rearrange("p a c -> p (a c)"), in_=pO.t[:, 0:260]),
                            reads=[pO.b], writes=[Oacc_b[p]])
                    else:
                        S.op("dve", lambda e, pO=pO, p=p, b0=b0: e.tensor_copy(
                            out=Oacc[p][:, b0:b0 + 4, :].rearrange("p a c -> p (a c)"), in_=pO.t[:, 0:260]),
                            reads=[pO.b], writes=[Oacc_b[p]])
                if blk == 31:
                    S.op("sp", lambda e, p=p, d=d: e.dma_start(
                        out=scr["oa_scr"][p].rearrange("(n j r) c -> j r n c", j=128, r=d),
                        in_=Oacc[p][:].rearrange("j (r n) c -> j r n c", r=d)),
                        reads=[Oacc_b[p]], writes=[scr["oa_scr_b"][p]], dma=True)

            LAG = 3
            for step in range(len(items) + LAG):
                if step < len(items):
                    emit_scores(blkc + step, items[step])
                if step >= LAG:
                    emit_pv(blkc + step - LAG, items[step - LAG])
            blkc += len(items)
            for p in range(3):
                S.op("sp", lambda e, p=p: e.dma_start(
                    out=Old[p][:], in_=scr["oa_scr"][p].rearrange("(tb j) c -> j tb c", j=128)),
                    reads=[scr["oa_scr_b"][p]], writes=[Old_b[p]], dma=True)
            S.op("dve", lambda e: e.tensor_tensor(out=Old[0][:], in0=Old[0][:], in1=Old[1][:], op=ALU.add),
                 reads=[Old_b[0], Old_b[1]], writes=[Old_b[0]])
            S.op("dve", lambda e: e.tensor_tensor(out=Old[0][:], in0=Old[0][:], in1=Old[2][:], op=ALU.add),
                 reads=[Old_b[0], Old_b[2]], writes=[Old_b[0]])
            S.op("dve", lambda e: e.reciprocal(out=rec[:], in_=Old[0][:, :, 64]),
                 reads=[Old_b[0]], writes=[rec_b])
            S.op("dve", lambda e: e.tensor_tensor(out=oah[:], in0=Old[0][:, :, 0:64],
                                                  in1=rec[:].unsqueeze(2).to_broadcast([128, 32, 64]), op=ALU.mult),
                 reads=[Old_b[0], rec_b], writes=[oah_b])
            S.op("sp", lambda e, h=h: e.dma_start(
                out=out["o"][:, h * 64:(h + 1) * 64].rearrange("(tb j) c -> j tb c", j=128), in_=oah[:]),
                reads=[oah_b], writes=[out["o_b"]], dma=True)
            S.op("pool", lambda e: e.tensor_tensor(out=sq[:], in0=oah[:], in1=oah[:], op=ALU.mult),
                 reads=[oah_b], writes=[sq_b])
            S.op("dve", lambda e: e.tensor_reduce(out=ssq1[:], in_=sq[:], axis=AX.X, op=ALU.add),
                 reads=[sq_b], writes=[ssq1_b])
            S.op("dve", lambda e: e.tensor_tensor(out=ssqA[:], in0=ssqA[:], in1=ssq1[:], op=ALU.add),
                 reads=[ssq1_b, ssqA_b], writes=[ssqA_b])
    S.barrier()
    A.release(m0)


def phase3(S, nc, io, scr, ps, out):
    A = S.arena
    m0 = A.mark()
    sb = A.alloc
    wT32 = sb("sgu_wT32", [128, 4, 128], F32)
    wT = sb("sgu_wT", [128, 4, 128], BF16)
    wT_b = Buf("sgu_wT")
    mask = sb("sgu_mask", [128, 128], F32)
    mask_b = Buf("sgu_mask")
    S.op("sp", lambda e: e.dma_start(out=wT32[:], in_=io["sgu_wT"]), writes=[wT_b], dma=True)
    S.op("pool", lambda e: e.memset(mask[:], 1.0), writes=[mask_b])
    S.op("pool", lambda e: e.affine_select(out=mask[:], in_=mask[:], pattern=[[1, 128]], base=0,
                                           channel_multiplier=-1, compare_op=ALU.is_ge, fill=0.0),
         reads=[mask_b], writes=[mask_b])
    S.op("dve", lambda e: e.tensor_tensor(out=wT[:], in0=wT32[:], in1=mask[:].unsqueeze(1).to_broadcast([128, 4, 128]),
                                          op=ALU.mult), reads=[wT_b, mask_b], writes=[wT_b])
    bs = sb("sgu_b", [128, 4], F32)
    bs_b = Buf("sgu_b")
    S.op("sp", lambda e: e.dma_start(out=bs[:], in_=io["sgu_bT"]), writes=[bs_b], dma=True)
    gain = sb("sgu_gain", [128, 512], F32)
    gain_b = Buf("sgu_gain")
    S.op("sp", lambda e: e.dma_start(out=gain[:], in_=io["sgu_ng"].partition_broadcast(128)), writes=[gain_b], dma=True)
    NB = 2
    G = 4
    g_t = [sb(f"sgu_g{i}", [128, G, 512], F32) for i in range(NB)]
    g_b = [Buf(f"sgu_g{i}") for i in range(NB)]
    u_t = [sb(f"sgu_u{i}", [128, G, 256], F32) for i in range(NB)]
    u_b = [Buf(f"sgu_u{i}") for i in range(NB)]
    sq = sb("sgu_sq", [128, G, 512], F32)
    sq_b = Buf("sgu_sq")
    gn = [sb(f"sgu_gn{i}", [128, G, 256], BF16) for i in range(NB)]
    gn_b = [Buf(f"sgu_gn{i}") for i in range(NB)]
    stt = [sb(f"sgu_st{i}", [128, 8, G], F32) for i in range(NB)]
    stt_b = [Buf(f"sgu_st{i}") for i in range(NB)]
    ob = [sb(f"sgu_ob{i}", [128, G, 256], F32) for i in range(NB)]
    ob_b = [Buf(f"sgu_ob{i}") for i in range(NB)]
    ssqB, ssqB_b = out["ssqB"], out["ssqB_b"]
    pidx = 0
    for tg in range(32 // G):
        i = tg % NB
        rows = slice(tg * G * 128, (tg + 1) * G * 128)
        zrows = slice(1 + tg * G * 128, 1 + (tg + 1) * G * 128)
        S.op("sp", lambda e, i=i, zrows=zrows: e.dma_start(
            out=g_t[i][:], in_=scr["ztm"][zrows, 640:1152].rearrange("(tb j) c -> j tb c", j=128)),
            reads=[scr["ztm_b"]], writes=[g_b[i]], dma=True)
        S.op("sp", lambda e, i=i, zrows=zrows: e.dma_start(
            out=u_t[i][:], in_=scr["ztm"][zrows, 384:640].rearrange("(tb j) c -> j tb c", j=128)),
            reads=[scr["ztm_b"]], writes=[u_b[i]], dma=True)
        S.op("act", lambda e, i=i: e.activation(out=g_t[i][:], in_=g_t[i][:], func=AF.Gelu_apprx_tanh),
             reads=[g_b[i]], writes=[g_b[i]])
        S.op("act", lambda e, i=i: e.activation(out=u_t[i][:], in_=u_t[i][:], func=AF.Gelu_apprx_tanh),
             reads=[u_b[i]], writes=[u_b[i]])
        st = stt[i]
        S.op("dve", lambda e, i=i, st=st: e.tensor_reduce(out=st[:, 0, :], in_=g_t[i][:], axis=AX.X, op=ALU.add),
             reads=[g_b[i]], writes=[stt_b[i]])
        S.op("pool", lambda e, i=i: e.tensor_tensor(out=sq[:], in0=g_t[i][:], in1=g_t[i][:], op=ALU.mult),
             reads=[g_b[i]], writes=[sq_b])
        S.op("dve", lambda e, st=st: e.tensor_reduce(out=st[:, 1, :], in_=sq[:], axis=AX.X, op=ALU.add),
             reads=[sq_b], writes=[stt_b[i]])
        S.op("dve", lambda e, st=st: e.tensor_scalar(out=st[:, 2, :], in0=st[:, 0, :], scalar1=1.0 / 512, scalar2=None, op0=ALU.mult),
             reads=[stt_b[i]], writes=[stt_b[i]])
        S.op("dve", lambda e, st=st: e.tensor_tensor(out=st[:, 3, :], in0=st[:, 2, :], in1=st[:, 2, :], op=ALU.mult),
             reads=[stt_b[i]], writes=[stt_b[i]])
        S.op("dve", lambda e, st=st: e.scalar_tensor_tensor(out=st[:, 4, :], in0=st[:, 1, :], scalar=1.0 / 512, in1=st[:, 3, :],
                                                            op0=ALU.mult, op1=ALU.subtract),
             reads=[stt_b[i]], writes=[stt_b[i]])
        S.op("act", lambda e, st=st: e.activation(out=st[:, 5, :], in_=st[:, 4, :], func=AF.Sqrt, bias=EPS_M, scale=1.0),
             reads=[stt_b[i]], writes=[stt_b[i]])
        S.op("dve", lambda e, st=st: e.reciprocal(out=st[:, 6, :], in_=st[:, 5, :]),
             reads=[stt_b[i]], writes=[stt_b[i]])
        S.op("dve", lambda e, i=i, st=st: e.tensor_tensor(
            out=g_t[i][:, :, 0:256], in0=g_t[i][:, :, 0:256], in1=st[:, 2, :].unsqueeze(2).to_broadcast([128, G, 256]), op=ALU.subtract),
            reads=[g_b[i], stt_b[i]], writes=[g_b[i]])
        S.op("dve", lambda e, i=i, st=st: e.tensor_tensor(
            out=g_t[i][:, :, 0:256], in0=g_t[i][:, :, 0:256], in1=st[:, 6, :].unsqueeze(2).to_broadcast([128, G, 256]), op=ALU.mult),
            reads=[g_b[i], stt_b[i]], writes=[g_b[i]])
        S.op("pool", lambda e, i=i: e.tensor_tensor(
            out=gn[i][:], in0=g_t[i][:, :, 0:256], in1=gain[:, 0:256].unsqueeze(1).to_broadcast([128, G, 256]), op=ALU.mult),
            reads=[g_b[i], gain_b], writes=[gn_b[i]])
        for gi in range(4):
            pp = ps[pidx % 4]
            pidx += 1
            S.op("pe", lambda e, pp=pp, gi=gi, i=i: e.matmul(
                pp.t[:, 0:G * 64], lhsT=wT[:, gi, :], rhs=gn[i][:, :, gi * 64:(gi + 1) * 64], start=True, stop=True),
                reads=[wT_b, gn_b[i]], writes=[pp.b])
            S.op("dve", lambda e, pp=pp, gi=gi, i=i: e.scalar_tensor_tensor(
                out=ob[i][:, :, gi * 64:(gi + 1) * 64], in0=pp.t[:, 0:G * 64].rearrange("p (a c) -> p a c", a=G),
                scalar=bs[:, gi:gi + 1], in1=u_t[i][:, :, gi * 64:(gi + 1) * 64], op0=ALU.add, op1=ALU.mult),
                reads=[pp.b, bs_b, u_b[i]], writes=[ob_b[i]])
        S.op("sp", lambda e, i=i, rows=rows: e.dma_start(
            out=out["o"][rows, 384:640].rearrange("(tb j) c -> j tb c", j=128), in_=ob[i][:]),
            reads=[ob_b[i]], writes=[out["o_b"]], dma=True)
        S.op("pool", lambda e, i=i: e.tensor_tensor(out=sq[:, :, 0:256], in0=ob[i][:], in1=ob[i][:], op=ALU.mult),
             reads=[ob_b[i]], writes=[sq_b])
        S.op("dve", lambda e, tg=tg: e.tensor_reduce(out=ssqB[:, tg * G:(tg + 1) * G], in_=sq[:, :, 0:256], axis=AX.X, op=ALU.add),
             reads=[sq_b], writes=[ssqB_b])
    S.barrier()
    A.release(m0)


CDEC = 0.6065306597126334
GN_EPS = 64e-5
INV_DT = BF16
SEG = 1024
NCH = SEG // 128


P4_STOP = 99


class _StopPhase(Exception):
    pass


def phase4(S, nc, io, scr, psbig, out):
    try:
        _phase4(S, nc, io, scr, psbig, out)
    except _StopPhase:
        pass
    S.barrier()


def _phase4(S, nc, io, scr, psbig, out):
    A = S.arena
    m0 = A.mark()
    sb = A.alloc
    ident, identf, ident_b = S.ident, S.identf, S.ident_b
    zTc, ztm = scr["zTc"], scr["ztm"]

    bankbufs = {}

    def bankbuf(key):
        if key not in bankbufs:
            bankbufs[key] = Buf(f"psbank{key}", psum=True)
        return bankbufs[key]

    class R:
        def __init__(self, lo, hi, name, key):
            self.ap = psbig[:, lo:hi]
            self.b = bankbuf(key)
    class R2:
        def __init__(self, b0, lo, hi, name):
            self.b = bankbuf(b0)
            self.ap3 = psbig[:, b0 * 512:(b0 + 2) * 512].rearrange("p (h q) -> p h q", h=2)[:, :, lo:hi]
            self.h = [psbig[:, (b0 + hh) * 512 + lo:(b0 + hh) * 512 + hi] for hh in range(2)]
    psA = R2(0, 0, 512, "psA")
    psN_s = [R2(2, 0, 128, "psN0"), R2(4, 0, 128, "psN1")]
    psD_s = [R2(2, 128, 256, "psD0"), R2(4, 128, 256, "psD1")]
    psR_s = [R2(2, 256, 512, "psR0"), R2(4, 256, 512, "psR1")]
    psDC_s = [R2(2, 128, 384, "psDC0"), R2(4, 128, 384, "psDC1")]
    psRHS = R2(0, 0, 64, "psRHS")
    psSA = R2(0, 64, 128, "psSA")
    psY = R2(0, 128, 192, "psY")
    psDS = R2(0, 192, 256, "psDS")
    psP = [R(3072, 4096, "psP0", 6), R(3072 + 256, 3072 + 512, "psP1", 6)]

    cp = sb("cp", [128, 25], F32); cp_b = Buf("cp")
    S.op("sp", lambda e: e.dma_start(out=cp[:], in_=io["cp"]), writes=[cp_b], dma=True)
    cf = sb("cf", [128, 3, 384], F32); cf_b = Buf("cf")
    S.op("sp", lambda e: e.dma_start(out=cf[:], in_=io["cf"].partition_broadcast(128)), writes=[cf_b], dma=True)
    omka = sb("omka", [128, 3], F32); omka_b = Buf("omka")
    S.op("dve", lambda e: e.tensor_scalar(out=omka[:], in0=cp[:, 15:18], scalar1=-1.0, scalar2=1.0, op0=ALU.mult, op1=ALU.add),
         reads=[cp_b], writes=[omka_b])
    w_up = sb("w_up", [64, 384], BF16); a_up = sb("a_up", [64, 384], BF16); g_up = sb("g_up", [128, 2, 384], BF16)
    wts_b = Buf("rwkv_wts")
    S.op("pool", lambda e: e.dma_start(out=w_up[:], in_=io["w_up"]), writes=[wts_b], dma=True)
    S.op("pool", lambda e: e.dma_start(out=a_up[:], in_=io["a_up"]), writes=[wts_b], dma=True)
    S.op("pool", lambda e: e.dma_start(out=g_up[:], in_=io["g_up"].rearrange("(kc p) n -> p kc n", p=128)), writes=[wts_b], dma=True)
    bones = sb("bones", [128, 128], F32); bind = sb("bind", [128, 2], F32); cst_b = Buf("rwkv_cst")
    S.op("pool", lambda e: e.memset(bones[:], 0.0), writes=[cst_b])
    S.op("pool", lambda e: e.memset(bones[0:64, 0:64], 1.0), writes=[cst_b])
    S.op("pool", lambda e: e.memset(bones[64:128, 64:128], 1.0), writes=[cst_b])
    S.op("pool", lambda e: e.memset(bind[:], 0.0), writes=[cst_b])
    S.op("pool", lambda e: e.memset(bind[0:64, 0:1], 1.0), writes=[cst_b])
    S.op("pool", lambda e: e.memset(bind[64:128, 1:2], 1.0), writes=[cst_b])
    rmask = sb("rmask", [128, SEG], F32)
    S.op("pool", lambda e: e.memset(rmask[:], 1.0), writes=[cst_b])
    S.op("pool", lambda e: e.memset(rmask[:].rearrange("p (c j) -> p c j", j=128)[:, :, 0:1], 0.0), writes=[cst_b])
    mU = sb("mU", [128, 4, 128], F32)
    mL = sb("mL", [128, 128], F32)
    S.op("pool", lambda e: e.memset(mU[:], 1.0), writes=[cst_b])
    S.op("pool", lambda e: e.memset(mL[:], 1.0), writes=[cst_b])
    for q in range(4):
        base = -1 if q % 2 == 0 else 0
        S.op("pool", lambda e, q=q, base=base: e.affine_select(out=mU[:, q, :], in_=mU[:, q, :], pattern=[[1, 128]], base=base,
                                                               channel_multiplier=-1, compare_op=ALU.is_ge, fill=0.0),
             reads=[cst_b], writes=[cst_b])
    S.op("pool", lambda e: e.affine_select(out=mL[:], in_=mL[:], pattern=[[-1, 128]], base=-1,
                                           channel_multiplier=1, compare_op=ALU.is_ge, fill=0.0),
         reads=[cst_b], writes=[cst_b])
    identI = sb("identI", [128, 128], INV_DT)
    S.op("pool", lambda e: e.tensor_copy(out=identI[:], in_=identf[:]), reads=[ident_b], writes=[cst_b])

    TW = sb("TW", [64, T], BF16); AL = sb("AL", [64, T], BF16); SGG = sb("SGG", [128, 2, T], BF16)
    lora_b = Buf("lora")
    m1 = A.mark()
    la = [sb(f"lo_a{i}", [128, SEG], F32) for i in range(2)]
    lb = [sb(f"lo_b{i}", [128, SEG], F32) for i in range(2)]
    la_b = [Buf(f"lo_a{i}") for i in range(2)]
    lb_b = [Buf(f"lo_b{i}") for i in range(2)]
    k = 0
    for (row0, np_, mucol, kind) in ((768, 64, 21, "w"), (832, 64, 22, "a"), (896, 128, 23, "g0"), (1024, 128, 24, "g1")):
        for sg in range(T // SEG):
            i = k % 2
            k += 1
            t0 = sg * SEG
            S.op("sp", lambda e, i=i, row0=row0, np_=np_, t0=t0: e.dma_start(out=la[i][0:np_, :], in_=zTc[row0:row0 + np_, t0 + 1:t0 + 1 + SEG]),
                 reads=[scr["zTc_b"]], writes=[la_b[i]], dma=True)
            S.op("sp", lambda e, i=i, row0=row0, np_=np_, t0=t0: e.dma_start(out=lb[i][0:np_, :], in_=zTc[row0:row0 + np_, t0:t0 + SEG]),
                 reads=[scr["zTc_b"]], writes=[lb_b[i]], dma=True)
            S.op("dve", lambda e, i=i, np_=np_: e.tensor_tensor(out=lb[i][0:np_, :], in0=lb[i][0:np_, :], in1=la[i][0:np_, :], op=ALU.subtract),
                 reads=[la_b[i], lb_b[i]], writes=[lb_b[i]])
            S.op("dve", lambda e, i=i, np_=np_, mucol=mucol: e.scalar_tensor_tensor(
                out=la[i][0:np_, :], in0=lb[i][0:np_, :], scalar=cp[0:np_, mucol:mucol + 1], in1=la[i][0:np_, :], op0=ALU.mult, op1=ALU.add),
                reads=[la_b[i], lb_b[i], cp_b], writes=[la_b[i]])
            if kind == "w":
                S.op("act", lambda e, i=i, t0=t0: e.activation(out=TW[:, t0:t0 + SEG], in_=la[i][0:64, :], func=AF.Tanh),
                     reads=[la_b[i]], writes=[lora_b])
            elif kind == "a":
                S.op("act", lambda e, i=i, t0=t0: e.copy(out=AL[:, t0:t0 + SEG], in_=la[i][0:64, :]),
                     reads=[la_b[i]], writes=[lora_b])
            else:
                kc = 0 if kind == "g0" else 1
                S.op("act", lambda e, i=i, t0=t0, kc=kc: e.activation(out=SGG[:, kc, t0:t0 + SEG], in_=la[i][:, :], func=AF.Sigmoid),
                     reads=[la_b[i]], writes=[lora_b])
    S.barrier()
    A.release(m1)
    if P4_STOP <= 1:
        raise _StopPhase()

    def t32(name, n=1):
        return [sb(f"{name}{i}", [128, SEG], F32) for i in range(n)], [Buf(f"{name}{i}") for i in range(n)]
    Rr, Rr_b = t32("Rr"); Rp, Rp_b = t32("Rp"); Kk, Kk_b = t32("Kk"); Kp, Kp_b = t32("Kp")
    SGt, SG_b = t32("SGt"); CUM, CUM_b = t32("CUM"); WINC, WINC_b = t32("WINC", 2); WINV, WINV_b = t32("WINV"); WEXC, WEXC_b = t32("WEXC")
    AAt, AA_b = t32("AAt"); KKt, KK_b = t32("KKt"); TMPt, TMP_b = t32("TMPt"); K2t, K2_b = t32("K2t"); RKt, RK_b = t32("RKt")
    QA = [sb(f"QA{i}", [128, NCH, 2, 128], BF16) for i in range(2)]; QA_b = [Buf(f"QA{i}") for i in range(2)]
    KB = [sb(f"KB{i}", [128, NCH, 2, 128], BF16) for i in range(2)]; KB_b = [Buf(f"KB{i}") for i in range(2)]
    Btok = [sb(f"Btok{i}", [128, NCH, 128], BF16) for i in range(2)]; Btok_b = [Buf(f"Btok{i}") for i in range(2)]
    Ktok = [sb(f"Ktok{i}", [128, NCH, 128], BF16) for i in range(2)]; Ktok_b = [Buf(f"Ktok{i}") for i in range(2)]
    V32 = [sb(f"V32_{i}", [128, NCH, 128], F32) for i in range(2)]; V32_b = [Buf(f"V32_{i}") for i in range(2)]
    Vp = sb("Vp", [128, NCH, 128], F32); Vp_b = Buf("Vp")
    V16 = [sb(f"V16_{i}", [128, NCH, 128], BF16) for i in range(2)]; V16_b = [Buf(f"V16_{i}") for i in range(2)]
    Gt = [sb(f"Gt{i}", [128, NCH, 128], F32) for i in range(2)]; Gt_b = [Buf(f"Gt{i}") for i in range(2)]
    BS = [sb(f"BS{i}", [128, NCH, 2], F32) for i in range(2)]; BS_b = [Buf(f"BS{i}") for i in range(2)]
    Yall = [sb(f"Yall{i}", [128, NCH, 128], F32) for i in range(2)]; Yall_b = [Buf(f"Yall{i}") for i in range(2)]
    Ysq = sb("Ysq", [128, NCH, 128], F32); Ysq_b = Buf("Ysq")
    gst = sb("gst", [128, 8, NCH * 2], F32); gst_b = Buf("gst")
    DCU = [[sb(f"DCU{s}_{i}", [128, 2, 384], INV_DT) for i in range(2)] for s in range(2)]
    CU = [[DCU[s][i][:, :, 128:384] for i in range(2)] for s in range(2)]
    CU_b = [[Buf(f"CU{s}_{i}") for i in range(2)] for s in range(2)]
    Dm = [[DCU[s][i][:, :, 0:128] for i in range(2)] for s in range(2)]
    Dm_b = [[Buf(f"Dm{s}_{i}") for i in range(2)] for s in range(2)]
    SC = [sb(f"SC{s}", [128, 2, 384], BF16) for s in range(2)]; SC_b = [Buf(f"SC{s}") for s in range(2)]
    Ufin = [sb(f"Ufin{s}", [128, 2, 128], BF16) for s in range(2)]; Ufin_b = [Buf(f"Ufin{s}") for s in range(2)]
    RH = [sb(f"RH{s}", [128, 2, 64], BF16) for s in range(2)]; RH_b = [Buf(f"RH{s}") for s in range(2)]
    SA = [sb(f"SA{s}", [128, 2, 64], BF16) for s in range(2)]; SA_b = [Buf(f"SA{s}") for s in range(2)]
    S32 = sb("S32", [128, 64], F32); S16 = sb("S16", [128, 64], BF16); ST = sb("STtmp", [128, 64], F32)
    S32_b = Buf("S32"); S16_b = Buf("S16"); ST_b = Buf("ST")

    units = [(pr, sg) for pr in range(3) for sg in range(T // SEG)]
    pending = []
    real_op = S.op

    def rec_op(*a_, **k_):
        pending.append((a_, k_))

    def drain(n):
        for _ in range(min(n, len(pending))):
            a_, k_ = pending.pop(0)
            real_op(*a_, **k_)

    def prep(pr, sg, bi):
        t0 = sg * SEG
        r0 = pr * 128
        S.op("sp", lambda e, r0=r0, t0=t0: e.dma_start(out=Rr[0][:], in_=zTc[r0:r0 + 128, t0 + 1:t0 + 1 + SEG]), reads=[scr["zTc_b"]], writes=[Rr_b[0]], dma=True)
        S.op("sp", lambda e, r0=r0, t0=t0: e.dma_start(out=Rp[0][:], in_=zTc[r0:r0 + 128, t0:t0 + SEG]), reads=[scr["zTc_b"]], writes=[Rp_b[0]], dma=True)
        S.op("sp", lambda e, r0=r0, t0=t0: e.dma_start(out=Kk[0][:], in_=zTc[384 + r0:384 + r0 + 128, t0 + 1:t0 + 1 + SEG]), reads=[scr["zTc_b"]], writes=[Kk_b[0]], dma=True)
        S.op("sp", lambda e, r0=r0, t0=t0: e.dma_start(out=Kp[0][:], in_=zTc[384 + r0:384 + r0 + 128, t0:t0 + SEG]), reads=[scr["zTc_b"]], writes=[Kp_b[0]], dma=True)
        vrows = slice(1 + t0, 1 + t0 + SEG)
        vrows_p = slice(t0, t0 + SEG)
        vc = slice(1152 + r0, 1152 + r0 + 128)
        S.op("sp", lambda e, bi=bi, vrows=vrows, vc=vc: e.dma_start(out=V32[bi][:], in_=ztm[vrows, vc].rearrange("(tb j) c -> j tb c", j=128)),
             reads=[scr["ztm_b"]], writes=[V32_b[bi]], dma=True)
        S.op("sp", lambda e, vrows_p=vrows_p, vc=vc: e.dma_start(out=Vp[:], in_=ztm[vrows_p, vc].rearrange("(tb j) c -> j tb c", j=128)),
             reads=[scr["ztm_b"]], writes=[Vp_b], dma=True)
        S.op("dve", lambda e: e.tensor_tensor(out=Rp[0][:], in0=Rp[0][:], in1=Rr[0][:], op=ALU.subtract), reads=[Rr_b[0], Rp_b[0]], writes=[Rp_b[0]])
        S.op("dve", lambda e, pr=pr: e.scalar_tensor_tensor(out=Rr[0][:], in0=Rp[0][:], scalar=cp[:, 0 + pr:1 + pr], in1=Rr[0][:], op0=ALU.mult, op1=ALU.add),
             reads=[Rr_b[0], Rp_b[0], cp_b], writes=[Rr_b[0]])
        S.op("dve", lambda e: e.tensor_tensor(out=Kp[0][:], in0=Kp[0][:], in1=Kk[0][:], op=ALU.subtract), reads=[Kk_b[0], Kp_b[0]], writes=[Kp_b[0]])
        S.op("dve", lambda e, pr=pr: e.scalar_tensor_tensor(out=Kk[0][:], in0=Kp[0][:], scalar=cp[:, 3 + pr:4 + pr], in1=Kk[0][:], op0=ALU.mult, op1=ALU.add),
             reads=[Kk_b[0], Kp_b[0], cp_b], writes=[Kk_b[0]])
        S.op("pool", lambda e, bi=bi: e.tensor_tensor(out=Vp[:], in0=Vp[:], in1=V32[bi][:], op=ALU.subtract), reads=[Vp_b, V32_b[bi]], writes=[Vp_b])
        S.op("pool", lambda e, pr=pr: e.tensor_tensor(out=Vp[:], in0=Vp[:], in1=cf[:, 0, pr * 128:(pr + 1) * 128].unsqueeze(1).to_broadcast([128, NCH, 128]), op=ALU.mult),
             reads=[Vp_b, cf_b], writes=[Vp_b])
        S.op("pool", lambda e, bi=bi: e.tensor_tensor(out=V32[bi][:], in0=V32[bi][:], in1=Vp[:], op=ALU.add), reads=[Vp_b, V32_b[bi]], writes=[V32_b[bi]])
        S.op("act", lambda e, bi=bi: e.copy(out=V16[bi][:], in_=V32[bi][:]), reads=[V32_b[bi]], writes=[V16_b[bi]])
        for hf in range(2):
            S.op("pe", lambda e, hf=hf, pr=pr, t0=t0: e.matmul(psP[0].ap[:, hf * 512:(hf + 1) * 512], lhsT=w_up[:, pr * 128:(pr + 1) * 128],
                                                           rhs=TW[:, t0 + hf * 512:t0 + (hf + 1) * 512], start=True, stop=True),
                 reads=[wts_b, lora_b], writes=[psP[0].b])
        S.op("act", lambda e, pr=pr: e.activation(out=SGt[0][:], in_=psP[0].ap[:, :], func=AF.Sigmoid, bias=cp[:, 6 + pr:7 + pr], scale=1.0),
             reads=[psP[0].b, cp_b], writes=[SG_b[0]])
        S.op("dve", lambda e: e.tensor_tensor_scan(out=CUM[0][:], data0=rmask[:], data1=SGt[0][:], initial=0.0, op0=ALU.mult, op1=ALU.add),
             reads=[SG_b[0], cst_b], writes=[CUM_b[0]])
        wi = WINC[bi]
        S.op("act", lambda e, wi=wi: e.activation(out=wi[:], in_=CUM[0][:], func=AF.Exp, scale=-CDEC), reads=[CUM_b[0]], writes=[WINC_b[bi]])
        S.op("act", lambda e: e.activation(out=WINV[0][:], in_=CUM[0][:], func=AF.Exp, scale=CDEC), reads=[CUM_b[0]], writes=[WINV_b[0]])
        S.op("dve", lambda e: e.tensor_tensor(out=SGt[0][:], in0=CUM[0][:], in1=SGt[0][:], op=ALU.subtract), reads=[CUM_b[0], SG_b[0]], writes=[SG_b[0]])
        S.op("act", lambda e: e.activation(out=WEXC[0][:], in_=SGt[0][:], func=AF.Exp, scale=-CDEC), reads=[SG_b[0]], writes=[WEXC_b[0]])
        for hf in range(2):
            S.op("pe", lambda e, hf=hf, pr=pr, t0=t0: e.matmul(psP[0].ap[:, hf * 512:(hf + 1) * 512], lhsT=a_up[:, pr * 128:(pr + 1) * 128],
                                                           rhs=AL[:, t0 + hf * 512:t0 + (hf + 1) * 512], start=True, stop=True),
                 reads=[wts_b, lora_b], writes=[psP[0].b])
        S.op("act", lambda e, pr=pr: e.activation(out=AAt[0][:], in_=psP[0].ap[:, :], func=AF.Sigmoid, bias=cp[:, 9 + pr:10 + pr], scale=1.0),
             reads=[psP[0].b, cp_b], writes=[AA_b[0]])
        S.op("dve", lambda e, pr=pr: e.tensor_scalar(out=KKt[0][:], in0=Kk[0][:], scalar1=cp[:, 12 + pr:13 + pr], scalar2=None, op0=ALU.mult),
             reads=[Kk_b[0], cp_b], writes=[KK_b[0]])
        S.op("pool", lambda e: e.tensor_tensor(out=TMPt[0][:], in0=KKt[0][:], in1=KKt[0][:], op=ALU.mult), reads=[KK_b[0]], writes=[TMP_b[0]])
        for hf in range(2):
            S.op("pe", lambda e, hf=hf: e.matmul(psP[0].ap[:, hf * 512:(hf + 1) * 512], lhsT=bones[:], rhs=TMPt[0][:, hf * 512:(hf + 1) * 512], start=True, stop=True),
                 reads=[cst_b, TMP_b[0]], writes=[psP[0].b])
        S.op("act", lambda e: e.activation(out=TMPt[0][:], in_=psP[0].ap[:, :], func=AF.Sqrt), reads=[psP[0].b], writes=[TMP_b[0]])
        S.op("dve", lambda e: e.tensor_scalar(out=TMPt[0][:], in0=TMPt[0][:], scalar1=1e-12, scalar2=None, op0=ALU.max), reads=[TMP_b[0]], writes=[TMP_b[0]])
        S.op("dve", lambda e: e.reciprocal(out=TMPt[0][:], in_=TMPt[0][:]), reads=[TMP_b[0]], writes=[TMP_b[0]])
        S.op("dve", lambda e: e.tensor_tensor(out=KKt[0][:], in0=KKt[0][:], in1=TMPt[0][:], op=ALU.mult), reads=[KK_b[0], TMP_b[0]], writes=[KK_b[0]])
        S.op("dve", lambda e, pr=pr: e.tensor_scalar(out=TMPt[0][:], in0=AAt[0][:], scalar1=cp[:, 15 + pr:16 + pr], scalar2=omka[:, pr:pr + 1], op0=ALU.mult, op1=ALU.add),
             reads=[AA_b[0], cp_b, omka_b, TMP_b[0]], writes=[TMP_b[0]])
        S.op("dve", lambda e: e.tensor_tensor(out=K2t[0][:], in0=Kk[0][:], in1=TMPt[0][:], op=ALU.mult), reads=[Kk_b[0], TMP_b[0]], writes=[K2_b[0]])
        qa, kb = QA[bi], KB[bi]
        S.op("dve", lambda e, qa=qa: e.scalar_tensor_tensor(out=qa[:, :, 0, :], in0=KKt[0][:].rearrange("p (c j) -> p c j", j=128), scalar=-1.0,
                                                             in1=WEXC[0][:].rearrange("p (c j) -> p c j", j=128), op0=ALU.mult, op1=ALU.mult),
             reads=[KK_b[0], WEXC_b[0]], writes=[QA_b[bi]])
        S.op("pool", lambda e, qa=qa, wi=wi: e.tensor_tensor(out=qa[:, :, 1, :], in0=Rr[0][:].rearrange("p (c j) -> p c j", j=128),
                                                             in1=wi[:].rearrange("p (c j) -> p c j", j=128), op=ALU.mult),
             reads=[Rr_b[0], WINC_b[bi]], writes=[QA_b[bi]])
        S.op("dve", lambda e: e.tensor_tensor(out=TMPt[0][:], in0=KKt[0][:], in1=AAt[0][:], op=ALU.mult), reads=[KK_b[0], AA_b[0], TMP_b[0]], writes=[TMP_b[0]])
        S.op("dve", lambda e, kb=kb: e.tensor_tensor(out=kb[:, :, 0, :], in0=TMPt[0][:].rearrange("p (c j) -> p c j", j=128),
                                                     in1=WINV[0][:].rearrange("p (c j) -> p c j", j=128), op=ALU.mult),
             reads=[TMP_b[0], WINV_b[0]], writes=[KB_b[bi]])
        S.op("pool", lambda e, kb=kb: e.tensor_tensor(out=kb[:, :, 1, :], in0=K2t[0][:].rearrange("p (c j) -> p c j", j=128),
                                                      in1=WINV[0][:].rearrange("p (c j) -> p c j", j=128), op=ALU.mult),
             reads=[K2_b[0], WINV_b[0]], writes=[KB_b[bi]])
        S.op("dve", lambda e, pr=pr: e.scalar_tensor_tensor(out=RKt[0][:], in0=Rr[0][:], scalar=cp[:, 18 + pr:19 + pr], in1=K2t[0][:], op0=ALU.mult, op1=ALU.mult),
             reads=[Rr_b[0], K2_b[0], cp_b], writes=[RK_b[0]])
        for c in range(NCH):
            S.op("pe", lambda e, c=c: e.matmul(psP[1].ap[:, c * 2:c * 2 + 2], lhsT=RKt[0][:, c * 128:(c + 1) * 128], rhs=bind[:], start=True, stop=True),
                 reads=[RK_b[0], cst_b], writes=[psP[1].b])
        S.op("dve", lambda e, bi=bi: e.tensor_copy(out=BS[bi][:].rearrange("p c h -> p (c h)"), in_=psP[1].ap[:, 0:NCH * 2]), reads=[psP[1].b], writes=[BS_b[bi]])
        for c in range(NCH):
            for kc in range(2):
                S.op("pe", lambda e, c=c, kc=kc, pr=pr, t0=t0: e.matmul(psP[0].ap[:, c * 128:(c + 1) * 128], lhsT=SGG[:, kc, t0 + c * 128:t0 + (c + 1) * 128],
                                                                     rhs=g_up[:, kc, pr * 128:(pr + 1) * 128], start=(kc == 0), stop=(kc == 1)),
                     reads=[lora_b, wts_b], writes=[psP[0].b])
        S.op("act", lambda e, bi=bi: e.copy(out=Gt[bi][:].rearrange("p c n -> p (c n)"), in_=psP[0].ap[:, :]), reads=[psP[0].b], writes=[Gt_b[bi]])
        psTb = psP[0].ap.bitcast(BF16)
        for c in range(NCH):
            S.op("pe", lambda e, c=c, kb=kb: e.transpose(out=psTb[:, c * 128:(c + 1) * 128], in_=kb[:, c, 0, :], identity=ident[:]),
                 reads=[KB_b[bi], ident_b], writes=[psP[0].b])
            S.op("pe", lambda e, c=c, kb=kb: e.transpose(out=psTb[:, 1024 + c * 128:1024 + (c + 1) * 128], in_=kb[:, c, 1, :], identity=ident[:]),
                 reads=[KB_b[bi], ident_b], writes=[psP[0].b])
        S.op("dve", lambda e, bi=bi: e.tensor_copy(out=Btok[bi][:].rearrange("p c n -> p (c n)"), in_=psTb[:, 0:1024]), reads=[psP[0].b], writes=[Btok_b[bi]])
        S.op("act", lambda e, bi=bi: e.copy(out=Ktok[bi][:].rearrange("p c n -> p (c n)"), in_=psTb[:, 1024:2048]), reads=[psP[0].b], writes=[Ktok_b[bi]])


    def chunks_and_final(pr, sg, bi, defer_final):
        t0 = sg * SEG
        r0 = pr * 128
        wi = WINC[bi]
        qa, kb = QA[bi], KB[bi]
        if sg == 0:
            S.op("dve", lambda e: e.memset(S32[:], 0.0), writes=[S32_b])
            S.op("dve", lambda e: e.memset(S16[:], 0.0), writes=[S16_b])
        def unit_scores(c, s_):
            psN, psD, psR = psN_s[s_], psD_s[s_], psR_s[s_]
            for hh in range(2):
                p0 = 64 * hh
                rhsQA = qa[p0:p0 + 64, c, :, :].rearrange("p a j -> p (a j)")
                S.op("pe", lambda e, hh=hh, p0=p0, rhsQA=rhsQA, kb=kb, c=c: e.matmul(psA.h[hh][:, 0:256], lhsT=kb[p0:p0 + 64, c, 0, :], rhs=rhsQA, start=True, stop=True),
                     reads=[KB_b[bi], QA_b[bi]], writes=[psA.b])
                S.op("pe", lambda e, hh=hh, p0=p0, rhsQA=rhsQA, kb=kb, c=c: e.matmul(psA.h[hh][:, 256:512], lhsT=kb[p0:p0 + 64, c, 1, :], rhs=rhsQA, start=True, stop=True),
                     reads=[KB_b[bi], QA_b[bi]], writes=[psA.b])
                S.op("pe", lambda e, hh=hh, p0=p0, qa=qa, kb=kb, c=c: e.matmul(psN.h[hh][:, :], lhsT=qa[p0:p0 + 64, c, 0, :], rhs=kb[p0:p0 + 64, c, 0, :], start=True, stop=True),
                     reads=[KB_b[bi], QA_b[bi]], writes=[psN.b])
            psA3 = psA.ap3
            cu0 = CU[s_][0]
            S.op("dve", lambda e, cu0=cu0, psA3=psA3: e.tensor_tensor(out=cu0[:, :, 0:128], in0=psA3[:, :, 0:128], in1=mU[:, 0:1, :].to_broadcast([128, 2, 128]), op=ALU.mult),
                 reads=[psA.b, cst_b], writes=[CU_b[s_][0]])
            S.op("pool", lambda e, cu0=cu0: e.tensor_copy(out=cu0[:, :, 128:256], in_=identI[:].unsqueeze(1).to_broadcast([128, 2, 128])),
                 reads=[cst_b], writes=[CU_b[s_][0]])
            S.op("dve", lambda e, s_=s_, psA3=psA3: e.tensor_tensor(out=SC[s_][:], in0=psA3[:, :, 128:512],
                                                                   in1=mU[:, 1:4, :].rearrange("p a j -> p (a j)").unsqueeze(1).to_broadcast([128, 2, 384]), op=ALU.mult),
                 reads=[psA.b, cst_b], writes=[SC_b[s_]])
            S.op("dve", lambda e, s_=s_: e.tensor_tensor(out=Dm[s_][0], in0=psN.ap3,
                                                         in1=mL[:].unsqueeze(1).to_broadcast([128, 2, 128]), op=ALU.mult),
                 reads=[psN.b, cst_b], writes=[Dm_b[s_][0]])
        def unit_round(c, s_, rd):
            psN, psD, psR = psN_s[s_], psD_s[s_], psR_s[s_]
            cur, nxt = rd % 2, (rd + 1) % 2
            cuc, cun = CU[s_][cur], CU[s_][nxt]
            dc, dn = Dm[s_][cur], Dm[s_][nxt]
            last = (rd == 6)
            psR3 = psR.ap3
            for hh in range(2):
                if not last:
                    S.op("pe", lambda e, hh=hh, dc=dc, cuc=cuc: e.matmul(psR.h[hh][:, :], lhsT=dc[:, hh, :], rhs=cuc[:, hh, :], start=True, stop=True),
                         reads=[Dm_b[s_][cur], CU_b[s_][cur]], writes=[psR.b])
                    if rd < 5 or True:
                        S.op("pe", lambda e, hh=hh, dc=dc, cuc=cuc: e.matmul(psD.h[hh][:, :], lhsT=cuc[:, hh, 0:128], rhs=dc[:, hh, :], start=True, stop=True),
                             reads=[Dm_b[s_][cur], CU_b[s_][cur]], writes=[psD.b])
                else:
                    S.op("pe", lambda e, hh=hh, dc=dc, cuc=cuc: e.matmul(psR.h[hh][:, 128:256], lhsT=dc[:, hh, :], rhs=cuc[:, hh, 128:256], start=True, stop=True),
                         reads=[Dm_b[s_][cur], CU_b[s_][cur]], writes=[psR.b])
            if not last:
                S.op("act", lambda e, s_=s_, nxt=nxt: e.copy(out=DCU[s_][nxt][:, :, 0:256], in_=psDC_s[s_].ap3), reads=[psR.b], writes=[CU_b[s_][nxt], Dm_b[s_][nxt]])
                S.op("dve", lambda e, cun=cun, cuc=cuc, psR3=psR3: e.tensor_tensor(out=cun[:, :, 128:256], in0=psR3[:, :, 128:256], in1=cuc[:, :, 128:256], op=ALU.add),
                     reads=[psR.b, CU_b[s_][cur]], writes=[CU_b[s_][nxt]])
            else:
                S.op("dve", lambda e, s_=s_, cuc=cuc, psR3=psR3: e.tensor_tensor(out=Ufin[s_][:], in0=psR3[:, :, 128:256], in1=cuc[:, :, 128:256], op=ALU.add),
                     reads=[psR.b, CU_b[s_][cur]], writes=[Ufin_b[s_]])
        def unit_state(c, s_):
            for hh in range(2):
                p0 = 64 * hh
                S.op("pe", lambda e, hh=hh, p0=p0, qa=qa, c=c: e.matmul(psRHS.h[hh][:, :], lhsT=qa[p0:p0 + 64, c, 0, :], rhs=S16[p0:p0 + 64, :], start=True, stop=False),
                     reads=[QA_b[bi], S16_b], writes=[psRHS.b])
                S.op("pe", lambda e, hh=hh, p0=p0, s_=s_, c=c, bi=bi: e.matmul(psRHS.h[hh][:, :], lhsT=SC[s_][:, hh, 128:256], rhs=V16[bi][:, c, p0:p0 + 64], start=False, stop=True),
                     reads=[SC_b[s_], V16_b[bi]], writes=[psRHS.b])
            S.op("act", lambda e, s_=s_: e.copy(out=RH[s_][:], in_=psRHS.ap3), reads=[psRHS.b], writes=[RH_b[s_]])
            for hh in range(2):
                S.op("pe", lambda e, hh=hh, s_=s_: e.matmul(psSA.h[hh][:, :], lhsT=Ufin[s_][:, hh, :], rhs=RH[s_][:, hh, :], start=True, stop=True),
                     reads=[Ufin_b[s_], RH_b[s_]], writes=[psSA.b])
            S.op("dve", lambda e, s_=s_: e.tensor_copy(out=SA[s_][:], in_=psSA.ap3), reads=[psSA.b], writes=[SA_b[s_]])
            for hh in range(2):
                p0 = 64 * hh
                S.op("pe", lambda e, hh=hh, p0=p0, qa=qa, c=c: e.matmul(psY.h[hh][:, :], lhsT=qa[p0:p0 + 64, c, 1, :], rhs=S16[p0:p0 + 64, :], start=True, stop=False),
                     reads=[QA_b[bi], S16_b], writes=[psY.b])
                S.op("pe", lambda e, hh=hh, s_=s_: e.matmul(psY.h[hh][:, :], lhsT=SC[s_][:, hh, 0:128], rhs=SA[s_][:, hh, :], start=False, stop=False),
                     reads=[SC_b[s_], SA_b[s_]], writes=[psY.b])
                S.op("pe", lambda e, hh=hh, p0=p0, s_=s_, c=c, bi=bi: e.matmul(psY.h[hh][:, :], lhsT=SC[s_][:, hh, 256:384], rhs=V16[bi][:, c, p0:p0 + 64], start=False, stop=True),
                     reads=[SC_b[s_], V16_b[bi]], writes=[psY.b])
            S.op("act", lambda e, bi=bi, c=c: e.copy(out=Yall[bi][:, c, :].rearrange("p (h v) -> p h v", h=2), in_=psY.ap3), reads=[psY.b], writes=[Yall_b[bi]])
            for hh in range(2):
                p0 = 64 * hh
                S.op("pe", lambda e, hh=hh, p0=p0, s_=s_, c=c, bi=bi: e.matmul(psDS.h[hh][p0:p0 + 64, :], lhsT=Btok[bi][:, c, p0:p0 + 64], rhs=SA[s_][:, hh, :], start=True, stop=False),
                     reads=[Btok_b[bi], SA_b[s_]], writes=[psDS.b])
                S.op("pe", lambda e, hh=hh, p0=p0, c=c, bi=bi: e.matmul(psDS.h[hh][p0:p0 + 64, :], lhsT=Ktok[bi][:, c, p0:p0 + 64], rhs=V16[bi][:, c, p0:p0 + 64], start=False, stop=True),
                     reads=[Ktok_b[bi], V16_b[bi]], writes=[psDS.b])
            for hh in range(2):
                p0 = 64 * hh
                S.op("dve", lambda e, hh=hh, p0=p0: e.tensor_tensor(out=ST[p0:p0 + 64, :], in0=psDS.h[hh][p0:p0 + 64, :], in1=S32[p0:p0 + 64, :], op=ALU.add), reads=[psDS.b, S32_b], writes=[ST_b])
            wl = wi[:, c * 128 + 127:c * 128 + 128]
            S.op("dve", lambda e, wl=wl: e.tensor_scalar(out=S32[:], in0=ST[:], scalar1=wl, scalar2=None, op0=ALU.mult), reads=[ST_b, WINC_b[bi]], writes=[S32_b])
            S.op("act", lambda e, wl=wl: e.activation(out=S16[:], in_=ST[:], func=AF.Copy, scale=wl), reads=[ST_b, WINC_b[bi]], writes=[S16_b])

        for c0 in range(0, NCH, 2):
            unit_scores(c0, 0)
            unit_scores(c0 + 1, 1)
            for rd in range(7):
                unit_round(c0, 0, rd)
                unit_round(c0 + 1, 1, rd)
                drain(4)
            unit_state(c0, 0)
            unit_state(c0 + 1, 1)
            drain(4)
        drain(len(pending))

        def final():
            Y = Yall[bi]
            Y3 = Y[:].rearrange("p c (h v) -> p (c h) v", h=2)
            NG = NCH * 2
            S.op("dve", lambda e, Y3=Y3: e.tensor_reduce(out=gst[:, 0, :], in_=Y3, axis=AX.X, op=ALU.add), reads=[Yall_b[bi]], writes=[gst_b])
            S.op("pool", lambda e, Y=Y: e.tensor_tensor(out=Ysq[:], in0=Y[:], in1=Y[:], op=ALU.mult), reads=[Yall_b[bi]], writes=[Ysq_b])
            S.op("dve", lambda e: e.tensor_reduce(out=gst[:, 1, :], in_=Ysq[:].rearrange("p c (h v) -> p (c h) v", h=2), axis=AX.X, op=ALU.add), reads=[Ysq_b], writes=[gst_b])
            S.op("dve", lambda e: e.tensor_scalar(out=gst[:, 2, :], in0=gst[:, 0, :], scalar1=1.0 / 64, scalar2=None, op0=ALU.mult), reads=[gst_b], writes=[gst_b])
            S.op("dve", lambda e: e.tensor_tensor(out=gst[:, 3, :], in0=gst[:, 2, :], in1=gst[:, 2, :], op=ALU.mult), reads=[gst_b], writes=[gst_b])
            S.op("dve", lambda e: e.scalar_tensor_tensor(out=gst[:, 4, :], in0=gst[:, 1, :], scalar=1.0 / 64, in1=gst[:, 3, :], op0=ALU.mult, op1=ALU.subtract), reads=[gst_b], writes=[gst_b])
            S.op("act", lambda e: e.activation(out=gst[:, 5, :], in_=gst[:, 4, :], func=AF.Sqrt, bias=GN_EPS, scale=1.0), reads=[gst_b], writes=[gst_b])
            S.op("dve", lambda e: e.reciprocal(out=gst[:, 6, :], in_=gst[:, 5, :]), reads=[gst_b], writes=[gst_b])
            S.op("dve", lambda e, Y3=Y3: e.tensor_tensor(out=Y3, in0=Y3, in1=gst[:, 2, :].unsqueeze(2).to_broadcast([128, NG, 64]), op=ALU.subtract), reads=[Yall_b[bi], gst_b], writes=[Yall_b[bi]])
            S.op("dve", lambda e, Y3=Y3: e.tensor_tensor(out=Y3, in0=Y3, in1=gst[:, 6, :].unsqueeze(2).to_broadcast([128, NG, 64]), op=ALU.mult), reads=[Yall_b[bi], gst_b], writes=[Yall_b[bi]])
            S.op("pool", lambda e, Y=Y, pr=pr: e.tensor_tensor(out=Y[:], in0=Y[:], in1=cf[:, 1, pr * 128:(pr + 1) * 128].unsqueeze(1).to_broadcast([128, NCH, 128]), op=ALU.mult),
                 reads=[Yall_b[bi], cf_b], writes=[Yall_b[bi]])
            S.op("pool", lambda e, Y=Y, pr=pr: e.tensor_tensor(out=Y[:], in0=Y[:], in1=cf[:, 2, pr * 128:(pr + 1) * 128].unsqueeze(1).to_broadcast([128, NCH, 128]), op=ALU.add),
                 reads=[Yall_b[bi], cf_b], writes=[Yall_b[bi]])
            S.op("dve", lambda e, bi=bi: e.tensor_tensor(out=Ysq[:].rearrange("p c (h v) -> p (c h) v", h=2), in0=V32[bi][:].rearrange("p c (h v) -> p (c h) v", h=2),
                                                         in1=BS[bi][:].rearrange("p c h -> p (c h)").unsqueeze(2).to_broadcast([128, NG, 64]), op=ALU.mult),
                 reads=[V32_b[bi], BS_b[bi], Ysq_b], writes=[Ysq_b])
            S.op("dve", lambda e, Y=Y: e.tensor_tensor(out=Y[:], in0=Y[:], in1=Ysq[:], op=ALU.add), reads=[Yall_b[bi], Ysq_b], writes=[Yall_b[bi]])
            S.op("dve", lambda e, Y=Y, bi=bi: e.tensor_tensor(out=Y[:], in0=Y[:], in1=Gt[bi][:], op=ALU.mult), reads=[Yall_b[bi], Gt_b[bi]], writes=[Yall_b[bi]])
            S.op("sp", lambda e, Y=Y, t0=t0, pr=pr: e.dma_start(out=out["o"][t0:t0 + SEG, 640 + pr * 128:640 + (pr + 1) * 128].rearrange("(tb j) c -> j tb c", j=128), in_=Y[:]),
                 reads=[Yall_b[bi]], writes=[out["o_b"]], dma=True)

        if defer_final:
            S.op = rec_op
            try:
                final()
            finally:
                S.op = real_op
        else:
            final()

    prep(units[0][0], units[0][1], 0)
    for ui, (pr, sg) in enumerate(units):
        bi = ui % 2
        if ui + 1 < len(units):
            S.op = rec_op
            try:
                prep(units[ui + 1][0], units[ui + 1][1], (ui + 1) % 2)
            finally:
                S.op = real_op
        chunks_and_final(pr, sg, bi, defer_final=(ui + 1 < len(units)))
    drain(len(pending))
    S.barrier()
    A.release(m0)

import numpy as np

D = 2048
NTR = 17
TR = NTR * 128
DFF = 5632
NJ = DFF // 128
EPS = 1e-6
TGROUPS = [(0, 512), (512, 512), (1024, 512), (1536, 512), (2048, 128)]


def declare_io_R(nc, sfx, with_xh):
    def din(name, shape, dt=F32):
        return nc.dram_tensor("R_" + name + sfx, list(shape), dt, kind="ExternalInput").ap()
    io = dict(
        w_out=din("w_out", [D, D]), gvT=din("gvT", [128, 16]), gT=din("gT", [128, 2, 16]), gF=din("gF", [3, D]),
        mem=din("mem", [256, D]), msgT=din("msgT", [128, 16]),
        wq=din("wq", [D, 512]), wkv=din("wkv", [D, 1024]), wo=din("wo", [512, D]),
        w_up=din("w_up", [D, 2 * DFF]), cw=din("cw", [128, 2 * NJ, 3]), cb=din("cb", [128, 2 * NJ]), w_down=din("w_down", [DFF, D]),
    )
    if with_xh:
        io["xh"] = din("xh", [TR, D])
    return io


def emit_R(S, nc, io, psbig, cfg):
    x1s, x1s_b, x2s, x2s_b = cfg["x1s"], cfg["x1s_b"], cfg["x2s"], cfg["x2s_b"]
    acts, acts_b, y3s, y3s_b = cfg["acts"], cfg["acts_b"], cfg["y3s"], cfg["y3s_b"]
    xo, xo_b = cfg["xo"], cfg["xo_b"]
    G_o, G_o_b, G_ssq, G_ssq_b = cfg["G_o"], cfg["G_o_b"], cfg["G_ssq"], cfg["G_ssq_b"]
    flag2, flag_b = cfg["flag2"], cfg["flag_b"]
    if True:
        A = S.arena
        mR = A.mark()
        sb = A.alloc
        ps = [PS(psbig[:, i * 512:(i + 1) * 512], f"ps{i}") for i in range(8)]
        psb16 = psbig.bitcast(BF16)
        ident, ident_b, ones16 = S.ident, S.ident_b, S.ones16
        flag = flag2[:, 0:1]
        gvT = sb("gvT", [128, 16], F32); gT = sb("gT", [128, 2, 16], F32); msgT = sb("msgT", [128, 16], F32)
        par_b = Buf("par")
        S.op("sp", lambda e: e.dma_start(out=gvT[:], in_=io["gvT"]), writes=[par_b], dma=True)
        S.op("sp", lambda e: e.dma_start(out=gT[:], in_=io["gT"]), writes=[par_b], dma=True)
        S.op("sp", lambda e: e.dma_start(out=msgT[:], in_=io["msgT"]), writes=[par_b], dma=True)
        GS = sb("GS", [128, 4, 32], F32); GS_b = Buf("GS")
        S.op("sp", lambda e: e.dma_start(out=GS[:], in_=G_ssq.rearrange("k j tb -> j k tb")), reads=[G_ssq_b], writes=[GS_b], dma=True)
        gF = sb("gF", [128, D], F32); gF_b = Buf("gF")

        junk = sb("junk", [128, D], BF16); junk_b = Buf("junk")
        stt = [sb(f"stt{i}", [128, 8], F32) for i in range(2)]; stt_b = [Buf(f"stt{i}") for i in range(2)]
        xs16 = [sb(f"xs16_{i}", [128, D], BF16) for i in range(2)]; xs16_b = [Buf(f"xs16_{i}") for i in range(2)]
        cnt = {"nt": 0, "rs": 0}

        def rstd_from_sbuf(src, src_b, st, st_b, col):
            S.op("act", lambda e: e.activation(out=junk[:], in_=src[:], func=AF.Square, accum_out=st[:, col:col + 1]),
                 reads=[src_b], writes=[junk_b, st_b])
            S.op("act", lambda e: e.activation(out=st[:, col:col + 1], in_=st[:, col:col + 1], func=AF.Sqrt, scale=1.0 / D, bias=EPS),
                 reads=[st_b], writes=[st_b])
            S.op("dve", lambda e: e.reciprocal(out=st[:, col:col + 1], in_=st[:, col:col + 1]), reads=[st_b], writes=[st_b])

        def norm_T(src, src_b, scale_ap, scale_b, gainT_ap, dst3, dst_b, pbanks, use=None):
            if use is None:
                i = cnt["nt"] % 2
                cnt["nt"] += 1
            else:
                i = use
            if scale_ap is not None:
                S.op("dve", lambda e: e.tensor_scalar(out=xs16[i][:], in0=src[:], scalar1=scale_ap, scalar2=None, op0=ALU.mult),
                     reads=[src_b, scale_b], writes=[xs16_b[i]])
            pa, pb = ps[pbanks[0]], ps[pbanks[1]]
            for c in range(16):
                pp, bk = (pa, pbanks[0]) if c < 8 else (pb, pbanks[1])
                S.op("pe", lambda e, c=c, bk=bk: e.transpose(out=psb16[:, bk * 1024 + (c % 8) * 128:bk * 1024 + (c % 8 + 1) * 128],
                                                             in_=xs16[i][:, c * 128:(c + 1) * 128], identity=ident[:]),
                     reads=[xs16_b[i], ident_b], writes=[pp.b])
            for half, (pp, bk) in enumerate(((pa, pbanks[0]), (pb, pbanks[1]))):
                S.op("dve", lambda e, half=half, bk=bk: e.tensor_tensor(
                    out=dst3[:, half * 8:(half + 1) * 8, :],
                    in0=psb16[:, bk * 1024:bk * 1024 + 1024].rearrange("p (c n) -> p c n", c=8),
                    in1=gainT_ap[:, half * 8:(half + 1) * 8].unsqueeze(2).to_broadcast([128, 8, 128]), op=ALU.mult),
                    reads=[pp.b, par_b], writes=[dst_b])
            return xs16[i], xs16_b[i]

        def resid_update(xt, xt_b, ybanks, st, st_b, ytmp=None, ytmp_b=None):
            b0 = ybanks[0]
            yb = [ps[b].b for b in ybanks]
            for q, b in enumerate(ybanks):
                S.op("act", lambda e, q=q, b=b: e.activation(out=junk[:, q * 512:(q + 1) * 512], in_=ps[b].t[:, :], func=AF.Square,
                                                             accum_out=st[:, q:q + 1]), reads=[ps[b].b], writes=[junk_b, st_b])
            S.op("dve", lambda e: e.tensor_reduce(out=st[:, 4:5], in_=st[:, 0:4], axis=AX.X, op=ALU.add), reads=[st_b], writes=[st_b])
            S.op("act", lambda e: e.activation(out=st[:, 5:6], in_=st[:, 4:5], func=AF.Sqrt, scale=1.0 / D, bias=EPS), reads=[st_b], writes=[st_b])
            S.op("dve", lambda e: e.reciprocal(out=st[:, 6:7], in_=st[:, 5:6]), reads=[st_b], writes=[st_b])
            if ytmp is None:
                ytmp, ytmp_b = xs32, xs32_b
            S.op("dve", lambda e: e.scalar_tensor_tensor(out=ytmp[:], in0=psbig[:, b0 * 512:(b0 + 4) * 512], scalar=st[:, 6:7], in1=gF[:],
                                                         op0=ALU.mult, op1=ALU.mult), reads=yb + [st_b, gF_b], writes=[ytmp_b])
            S.op("dve", lambda e: e.tensor_tensor(out=xt[:], in0=xt[:], in1=ytmp[:], op=ALU.add), reads=[xt_b, ytmp_b], writes=[xt_b])

        xs32 = sb("xs32", [128, D], F32); xs32_b = Buf("xs32")

        mA = A.mark()
        S.op("sp", lambda e: e.dma_start(out=gF[:], in_=io["gF"][0].partition_broadcast(128)), writes=[gF_b], dma=True)
        hT = sb("hT", [128, 16, TR], BF16); hT_b = Buf("hT")
        mA2 = A.mark()
        w_out = sb("w_out_sb", [128, 16, D], BF16); w_out_b = Buf("w_out")
        for kc in range(0, 16, 4):
            S.op("pool", lambda e, kc=kc: e.dma_start(out=w_out[:, kc:kc + 4, :], in_=io["w_out"].rearrange("(kc p) n -> p kc n", p=128)[:, kc:kc + 4, :]),
                 writes=[w_out_b], dma=True)
        ot = [sb(f"ot{i}", [128, D], F32) for i in range(2)]; ot_b = [Buf(f"ot{i}") for i in range(2)]
        ot1 = [xs32, xs32]; ot1_b = [xs32_b, xs32_b]
        xt = [sb(f"xt{i}", [128, D], F32) for i in range(2)]; xt_b = [Buf(f"xt{i}") for i in range(2)]
        sq4 = [sb(f"sq4_{i}", [128, 4], F32) for i in range(2)]; sq4_b = [Buf(f"sq4_{i}") for i in range(2)]
        oT = [sb(f"oT{i}", [128, 16, 128], BF16) for i in range(2)]; oT_b = [Buf(f"oT{i}") for i in range(2)]
        jsel = {}
        def frontA(t):
            i = t % 2
            rows = slice(t * 128, (t + 1) * 128)
            r0 = max(t - 1, 0) * 128
            r1 = 1920 + t * 128
            for r in range(2):
                S.op("sp", lambda e, i=i, r=r, r0=r0: e.dma_start(out=ot[i][:, r * 1024:(r + 1) * 1024], in_=G_o[(r0 // 512) * 1024 + r * 512 + r0 % 512:(r0 // 512) * 1024 + r * 512 + r0 % 512 + 128, :]),
                     reads=[G_o_b[r0 // 512]], writes=[ot_b[i]], dma=True)
                S.op("sp", lambda e, i=i, r=r, r1=r1: e.dma_start(out=ot1[i][:, r * 1024:(r + 1) * 1024], in_=G_o[(r1 // 512) * 1024 + r * 512 + r1 % 512:(r1 // 512) * 1024 + r * 512 + r1 % 512 + 128, :]),
                     reads=[G_o_b[r1 // 512]], writes=[ot1_b[i]], dma=True)
            S.op("act", lambda e, i=i: e.activation(out=ot[i][:], in_=ot[i][:], func=AF.Copy, scale=flag2[:, 1:2]),
                 reads=[ot_b[i], flag_b], writes=[ot_b[i]])
            S.op("dve", lambda e, i=i: e.scalar_tensor_tensor(out=ot[i][:], in0=ot1[i][:], scalar=flag2[:, 0:1], in1=ot[i][:], op0=ALU.mult, op1=ALU.add),
                 reads=[ot_b[i], ot1_b[i], flag_b], writes=[ot_b[i]])
            xsrc, xsrc_b = cfg["x_rows"](t)
            S.op("sp", lambda e, i=i, xsrc=xsrc: e.dma_start(out=xt[i][:], in_=xsrc), reads=[xsrc_b], writes=[xt_b[i]], dma=True)
            tb0 = max(t - 1, 0)
            tb1 = 15 + t
            S.op("dve", lambda e, i=i, tb0=tb0: e.tensor_scalar(out=sq4[i][:], in0=GS[:, :, tb0], scalar1=flag2[:, 1:2], scalar2=None, op0=ALU.mult),
                 reads=[GS_b, flag_b], writes=[sq4_b[i]])
            S.op("dve", lambda e, i=i, tb1=tb1: e.scalar_tensor_tensor(out=sq4[i][:], in0=GS[:, :, tb1], scalar=flag2[:, 0:1], in1=sq4[i][:], op0=ALU.mult, op1=ALU.add),
                 reads=[GS_b, flag_b, sq4_b[i]], writes=[sq4_b[i]])
            st, stb = stt[i], stt_b[i]
            S.op("dve", lambda e, i=i, st=st: e.tensor_tensor(out=st[:, 0:2], in0=sq4[i][:, 0:2], in1=sq4[i][:, 2:4], op=ALU.add), reads=[sq4_b[i]], writes=[stb])
            S.op("act", lambda e, st=st: e.activation(out=st[:, 2:3], in_=st[:, 0:1], func=AF.Sqrt, scale=1.0 / 768, bias=EPS), reads=[stb], writes=[stb])
            S.op("act", lambda e, st=st: e.activation(out=st[:, 3:4], in_=st[:, 1:2], func=AF.Sqrt, scale=1.0 / 512, bias=EPS), reads=[stb], writes=[stb])
            S.op("dve", lambda e, st=st: e.reciprocal(out=st[:, 2:4], in_=st[:, 2:4]), reads=[stb], writes=[stb])
            j = cnt["nt"] % 2
            cnt["nt"] += 1
            jsel[t] = j
            xs = xs16[j]
            for c0 in (0, 1024):
                S.op("dve", lambda e, c0=c0, xs=xs, i=i, st=st: e.tensor_scalar(out=xs[:, c0:c0 + 384], in0=ot[i][:, c0:c0 + 384], scalar1=st[:, 2:3], scalar2=None, op0=ALU.mult),
                     reads=[ot_b[i], stb], writes=[xs16_b[j]])
                S.op("dve", lambda e, c0=c0, xs=xs, i=i, st=st: e.tensor_scalar(out=xs[:, c0 + 384:c0 + 640], in0=ot[i][:, c0 + 384:c0 + 640], scalar1=st[:, 3:4], scalar2=None, op0=ALU.mult),
                     reads=[ot_b[i], stb], writes=[xs16_b[j]])
                S.op("act", lambda e, c0=c0, xs=xs, i=i: e.copy(out=xs[:, c0 + 640:c0 + 1024], in_=ot[i][:, c0 + 640:c0 + 1024]),
                     reads=[ot_b[i]], writes=[xs16_b[j]])

        def frontT(t):
            i = t % 2
            norm_T(None, None, None, None, gvT, oT[i], oT_b[i], (4, 5), use=jsel[t])

        def backMM(t):
            i = t % 2
            for blk in range(4):
                for kc in range(16):
                    S.op("pe", lambda e, blk=blk, kc=kc, i=i: e.matmul(ps[blk].t[:, :], lhsT=oT[i][:, kc, :], rhs=w_out[:, kc, blk * 512:(blk + 1) * 512],
                                                                      start=(kc == 0), stop=(kc == 15)), reads=[oT_b[i], w_out_b], writes=[ps[blk].b])

        def backA(t):
            i = t % 2
            rows = slice(t * 128, (t + 1) * 128)
            st, stb = stt[i], stt_b[i]
            resid_update(xt[i], xt_b[i], (0, 1, 2, 3), st, stb, ytmp=ot[i], ytmp_b=ot_b[i])
            S.op("sp", lambda e, i=i, rows=rows: e.dma_start(out=x1s[rows, :], in_=xt[i][:]), reads=[xt_b[i]], writes=[x1s_b], dma=True)
            rstd_from_sbuf(xt[i], xt_b[i], st, stb, 7)
            norm_T(xt[i], xt_b[i], st[:, 7:8], stb, gT[:, 0, :], hT[:, :, t * 128:(t + 1) * 128], hT_b, (6, 7))
        frontA(0)
        frontT(0)
        if NTR > 1:
            frontA(1)
        for t in range(NTR):
            backMM(t)
            if t + 1 < NTR:
                frontT(t + 1)
            backA(t)
            if t + 2 < NTR:
                frontA(t + 2)
        S.barrier()
        A.release(mA2)
        if cfg.get("stop") == "A":
            A.release(mR)
            return

        S.op("sp", lambda e: e.dma_start(out=gF[:], in_=io["gF"][1].partition_broadcast(128)), writes=[gF_b], dma=True)
        wq = sb("wq_sb", [128, 16, 512], BF16); wkv = sb("wkv_sb", [128, 16, 1024], BF16); wo = sb("wo_sb", [128, 4, D], BF16)
        wB_b = Buf("wB")
        S.op("pool", lambda e: e.dma_start(out=wq[:], in_=io["wq"].rearrange("(kc p) n -> p kc n", p=128)), writes=[wB_b], dma=True)
        for kc in range(0, 16, 8):
            S.op("pool", lambda e, kc=kc: e.dma_start(out=wkv[:, kc:kc + 8, :], in_=io["wkv"].rearrange("(kc p) n -> p kc n", p=128)[:, kc:kc + 8, :]), writes=[wB_b], dma=True)
        S.op("pool", lambda e: e.dma_start(out=wo[:], in_=io["wo"].rearrange("(kc p) n -> p kc n", p=128)), writes=[wB_b], dma=True)
        memT = sb("memT", [128, 16, 256], BF16); memT_b = Buf("memT")
        x1t = [sb(f"x1t{i}", [128, D], F32) for i in range(2)]; x1t_b = [Buf(f"x1t{i}") for i in range(2)]
        mt, mt_b = x1t, x1t_b
        for mb in range(2):
            S.op("sp", lambda e, mb=mb: e.dma_start(out=mt[mb][:], in_=io["mem"][mb * 128:(mb + 1) * 128, :]), writes=[mt_b[mb]], dma=True)
            rstd_from_sbuf(mt[mb], mt_b[mb], stt[mb], stt_b[mb], 7)
            norm_T(mt[mb], mt_b[mb], stt[mb][:, 7:8], stt_b[mb], msgT, memT[:, :, mb * 128:(mb + 1) * 128], memT_b, (4, 5))
        kT = sb("kT", [128, 4, 256], BF16); kT_b = Buf("kT")
        Vm = sb("Vm", [128, 2, 512], BF16); Vm_b = Buf("Vm")
        for h in range(4):
            bk = h // 2
            for kc in range(16):
                S.op("pe", lambda e, h=h, kc=kc, bk=bk: e.matmul(ps[bk].t[:, (h % 2) * 256:(h % 2 + 1) * 256], lhsT=wkv[:, kc, h * 128:(h + 1) * 128], rhs=memT[:, kc, :],
                                                                 start=(kc == 0), stop=(kc == 15)), reads=[wB_b, memT_b], writes=[ps[bk].b])
        for bk in range(2):
            S.op("act", lambda e, bk=bk: e.copy(out=kT[:, 2 * bk:2 * bk + 2, :].rearrange("p a m -> p (a m)"), in_=ps[bk].t[:, :]), reads=[ps[bk].b], writes=[kT_b])
        for mb in range(2):
            for kc in range(16):
                S.op("pe", lambda e, mb=mb, kc=kc: e.matmul(ps[2 + mb].t[:, :], lhsT=memT[:, kc, mb * 128:(mb + 1) * 128], rhs=wkv[:, kc, 512:1024],
                                                            start=(kc == 0), stop=(kc == 15)), reads=[wB_b, memT_b], writes=[ps[2 + mb].b])
            S.op("dve", lambda e, mb=mb: e.tensor_copy(out=Vm[:, mb, :], in_=ps[2 + mb].t[:, :]), reads=[ps[2 + mb].b], writes=[Vm_b])
        qT = sb("qT", [128, 4, 512], BF16); qT_b = Buf("qT")
        PT = [sb(f"PT{i}", [128, 2, 512], BF16) for i in range(2)]; PT_b = [Buf(f"PT{i}") for i in range(2)]
        rec = sb("rec", [128, 512], F32); rec_b = Buf("rec")
        oT2 = sb("oT2", [128, 4, 512], BF16); oT2_b = Buf("oT2")
        hk = 0
        SCL = 128 ** -0.5
        for (g0, n) in TGROUPS:
            for h in range(4):
                for kc in range(16):
                    S.op("pe", lambda e, h=h, kc=kc, g0=g0, n=n: e.matmul(ps[h].t[:, 0:n], lhsT=wq[:, kc, h * 128:(h + 1) * 128], rhs=hT[:, kc, g0:g0 + n],
                                                                          start=(kc == 0), stop=(kc == 15)), reads=[wB_b, hT_b], writes=[ps[h].b])
                if h % 2 == 0:
                    S.op("act", lambda e, h=h, n=n: e.copy(out=qT[:, h, 0:n], in_=ps[h].t[:, 0:n]), reads=[ps[h].b], writes=[qT_b])
                else:
                    S.op("dve", lambda e, h=h, n=n: e.tensor_copy(out=qT[:, h, 0:n], in_=ps[h].t[:, 0:n]), reads=[ps[h].b], writes=[qT_b])
            for h in range(4):
                pi = hk % 2
                hk += 1
                for mb in range(2):
                    S.op("pe", lambda e, h=h, mb=mb, n=n: e.matmul(ps[4 + mb].t[:, 0:n], lhsT=kT[:, h, mb * 128:(mb + 1) * 128], rhs=qT[:, h, 0:n], start=True, stop=True),
                         reads=[kT_b, qT_b], writes=[ps[4 + mb].b])
                    S.op("act", lambda e, mb=mb, n=n, pi=pi: e.activation(out=PT[pi][:, mb, 0:n], in_=ps[4 + mb].t[:, 0:n], func=AF.Exp, scale=SCL),
                         reads=[ps[4 + mb].b], writes=[PT_b[pi]])
                for mb in range(2):
                    S.op("pe", lambda e, h=h, mb=mb, n=n, pi=pi: e.matmul(ps[6].t[:, 0:n], lhsT=Vm[:, mb, h * 128:(h + 1) * 128], rhs=PT[pi][:, mb, 0:n],
                                                                          start=(mb == 0), stop=(mb == 1)), reads=[Vm_b, PT_b[pi]], writes=[ps[6].b])
                for mb in range(2):
                    S.op("pe", lambda e, mb=mb, n=n, pi=pi: e.matmul(ps[7].t[:, 0:n], lhsT=ones16[:], rhs=PT[pi][:, mb, 0:n],
                                                                     start=(mb == 0), stop=(mb == 1)), reads=[ident_b, PT_b[pi]], writes=[ps[7].b])
                S.op("dve", lambda e, n=n: e.reciprocal(out=rec[:, 0:n], in_=ps[7].t[:, 0:n]), reads=[ps[7].b], writes=[rec_b])
                S.op("dve", lambda e, h=h, n=n: e.tensor_tensor(out=oT2[:, h, 0:n], in0=ps[6].t[:, 0:n], in1=rec[:, 0:n], op=ALU.mult),
                     reads=[ps[6].b, rec_b], writes=[oT2_b])
            for tt in range(n // 128):
                t = g0 // 128 + tt
                i = t % 2
                rows = slice(t * 128, (t + 1) * 128)
                S.op("sp", lambda e, i=i, rows=rows: e.dma_start(out=x1t[i][:], in_=x1s[rows, :]), reads=[x1s_b], writes=[x1t_b[i]], dma=True)
                for blk in range(4):
                    for h in range(4):
                        S.op("pe", lambda e, blk=blk, h=h, tt=tt: e.matmul(ps[blk].t[:, :], lhsT=oT2[:, h, tt * 128:(tt + 1) * 128], rhs=wo[:, h, blk * 512:(blk + 1) * 512],
                                                                         start=(h == 0), stop=(h == 3)), reads=[oT2_b, wB_b], writes=[ps[blk].b])
                st, stb = stt[i], stt_b[i]
                resid_update(x1t[i], x1t_b[i], (0, 1, 2, 3), st, stb)
                S.op("sp", lambda e, i=i, rows=rows: e.dma_start(out=x2s[rows, :], in_=x1t[i][:]), reads=[x1t_b[i]], writes=[x2s_b], dma=True)
                rstd_from_sbuf(x1t[i], x1t_b[i], st, stb, 7)
                if t == 0:
                    S.op("dve", lambda e, st=st: e.tensor_tensor(out=st[:, 7:8], in0=st[:, 7:8], in1=flag, op=ALU.mult), reads=[stb, flag_b], writes=[stb])
                norm_T(x1t[i], x1t_b[i], st[:, 7:8], stb, gT[:, 1, :], hT[:, :, t * 128:(t + 1) * 128], hT_b, (4, 5))
        S.barrier()
        A.release(mA2)
        if cfg.get("stop") == "B":
            A.release(mR)
            return

        cw = sb("cw", [128, 2 * NJ, 3], F32); cb = sb("cb", [128, 2 * NJ], F32); cwb_b = Buf("cwb")
        S.op("sp", lambda e: e.dma_start(out=cw[:], in_=io["cw"]), writes=[cwb_b], dma=True)
        S.op("sp", lambda e: e.dma_start(out=cb[:], in_=io["cb"]), writes=[cwb_b], dma=True)
        wu = [sb(f"wu{i}", [128, 16, 256], BF16) for i in range(3)]; wu_b = [Buf(f"wu{i}") for i in range(3)]
        UG = [sb(f"UG{i}", [128, 2, TR + 2], F32) for i in range(2)]; UG_b = [Buf(f"UG{i}") for i in range(2)]
        CV = [sb(f"CV{i}", [128, 2, TR], F32) for i in range(2)]; CV_b = [Buf(f"CV{i}") for i in range(2)]
        AC = [sb(f"AC{i}", [128, TR], BF16) for i in range(2)]; AC_b = [Buf(f"AC{i}") for i in range(2)]
        for i in range(2):
            S.op("pool", lambda e, i=i: e.memset(UG[i][:, :, 0:2], 0.0), writes=[UG_b[i]])
        w_up_v = io["w_up"].rearrange("(kc p) n -> p kc n", p=128)
        pk = 0
        def load_wu(j):
            w = j % 3
            S.op("pool", lambda e, w=w, j=j: e.dma_start(out=wu[w][:, :, 0:128], in_=w_up_v[:, :, j * 128:(j + 1) * 128]), writes=[wu_b[w]], dma=True)
            S.op("pool", lambda e, w=w, j=j: e.dma_start(out=wu[w][:, :, 128:256], in_=w_up_v[:, :, DFF + j * 128:DFF + (j + 1) * 128]), writes=[wu_b[w]], dma=True)
        load_wu(0)
        load_wu(1)
        for j in range(NJ):
            i = j % 2
            w = j % 3
            if j + 2 < NJ:
                load_wu(j + 2)
            for (g0, n) in TGROUPS:
                for gv in range(2):
                    pp = ps[pk % 8]
                    pk += 1
                    for kc in range(16):
                        S.op("pe", lambda e, pp=pp, kc=kc, gv=gv, w=w, g0=g0, n=n: e.matmul(pp.t[:, 0:n], lhsT=wu[w][:, kc, gv * 128:(gv + 1) * 128], rhs=hT[:, kc, g0:g0 + n],
                                                                                         start=(kc == 0), stop=(kc == 15)), reads=[wu_b[w], hT_b], writes=[pp.b])
                    if gv == 0:
                        S.op("act", lambda e, pp=pp, gv=gv, i=i, g0=g0, n=n: e.copy(out=UG[i][:, gv, 2 + g0:2 + g0 + n], in_=pp.t[:, 0:n]), reads=[pp.b], writes=[UG_b[i]])
                    else:
                        S.op("dve", lambda e, pp=pp, gv=gv, i=i, g0=g0, n=n: e.tensor_copy(out=UG[i][:, gv, 2 + g0:2 + g0 + n], in_=pp.t[:, 0:n]), reads=[pp.b], writes=[UG_b[i]])
            for gv in range(2):
                cidx = j if gv == 0 else NJ + j
                S.op("act", lambda e, gv=gv, i=i, cidx=cidx: e.activation(out=CV[i][:, gv, :], in_=UG[i][:, gv, 2:TR + 2], func=AF.Identity,
                                                                        scale=cw[:, cidx, 2:3], bias=cb[:, cidx:cidx + 1]), reads=[UG_b[i], cwb_b], writes=[CV_b[i]])
                S.op("dve", lambda e, gv=gv, i=i, cidx=cidx: e.scalar_tensor_tensor(out=CV[i][:, gv, :], in0=UG[i][:, gv, 1:TR + 1], scalar=cw[:, cidx, 1:2], in1=CV[i][:, gv, :],
                                                                                   op0=ALU.mult, op1=ALU.add), reads=[UG_b[i], cwb_b, CV_b[i]], writes=[CV_b[i]])
                S.op("dve", lambda e, gv=gv, i=i, cidx=cidx: e.scalar_tensor_tensor(out=CV[i][:, gv, :], in0=UG[i][:, gv, 0:TR], scalar=cw[:, cidx, 0:1], in1=CV[i][:, gv, :],
                                                                                   op0=ALU.mult, op1=ALU.add), reads=[UG_b[i], cwb_b, CV_b[i]], writes=[CV_b[i]])
            S.op("act", lambda e, i=i: e.activation(out=CV[i][:, 0, :], in_=CV[i][:, 0, :], func=AF.Gelu_apprx_tanh), reads=[CV_b[i]], writes=[CV_b[i]])
            S.op("dve", lambda e, i=i: e.tensor_tensor(out=AC[i][:], in0=CV[i][:, 0, :], in1=CV[i][:, 1, :], op=ALU.mult), reads=[CV_b[i]], writes=[AC_b[i]])
            S.op("sp", lambda e, i=i, j=j: e.dma_start(out=acts.rearrange("(tt p j) t -> p tt j t", p=128, j=NJ)[:, :, j, :],
                                                       in_=AC[i][:].rearrange("p (tt t) -> p tt t", t=128)), reads=[AC_b[i]], writes=[acts_b], dma=True)
        S.barrier()
        A.release(mA)
        if cfg.get("stop") == "C":
            A.release(mR)
            return

        S.op("sp", lambda e: e.dma_start(out=gF[:], in_=io["gF"][2].partition_broadcast(128)), writes=[gF_b], dma=True)
        wd = [sb(f"wd{i}", [128, NJ, 512], BF16) for i in range(2)]; wd_b = [Buf(f"wd{i}") for i in range(2)]
        at = [sb(f"at{i}", [128, NJ, 128], BF16) for i in range(3)]; at_b = [Buf(f"at{i}") for i in range(3)]
        yst = [sb(f"yst{i}", [128, 512], F32) for i in range(3)]; yst_b = [Buf(f"yst{i}") for i in range(3)]
        ssq3 = sb("ssq3", [128, 16, 4], F32); ssq3_b = Buf("ssq3")
        acts_v = acts.rearrange("(tt p j) t -> tt p j t", p=128, j=NJ)
        w_down_v = io["w_down"].rearrange("(j p) n -> p j n", p=128)
        seq = [(blk, t) for blk in range(4) for t in range(1, NTR)]

        def load_wd(blk):
            i = blk % 2
            for j0 in range(0, NJ, 11):
                S.op("pool", lambda e, i=i, blk=blk, j0=j0: e.dma_start(out=wd[i][:, j0:j0 + 11, :], in_=w_down_v[:, j0:j0 + 11, blk * 512:(blk + 1) * 512]), writes=[wd_b[i]], dma=True)

        def load_at(k):
            a = k % 3
            t = seq[k][1]
            S.op("sp", lambda e, a=a, t=t: e.dma_start(out=at[a][:], in_=acts_v[t]), reads=[acts_b], writes=[at_b[a]], dma=True)
        load_wd(0)
        load_wd(1)
        load_at(0)
        load_at(1)
        for k, (blk, t) in enumerate(seq):
            i = blk % 2
            a = k % 3
            pp = ps[k % 8]
            if t == 1 and blk >= 1 and blk + 1 < 4:
                load_wd(blk + 1)
            if k + 2 < len(seq):
                load_at(k + 2)
            for j in range(NJ):
                S.op("pe", lambda e, pp=pp, j=j, a=a, i=i: e.matmul(pp.t[:, :], lhsT=at[a][:, j, :], rhs=wd[i][:, j, :], start=(j == 0), stop=(j == NJ - 1)),
                     reads=[at_b[a], wd_b[i]], writes=[pp.b])
            S.op("act", lambda e, pp=pp, a=a: e.copy(out=yst[a][:], in_=pp.t[:, :]), reads=[pp.b], writes=[yst_b[a]])
            S.op("dve", lambda e, a=a, t=t, blk=blk: e.scalar_tensor_tensor(out=junk[:, 0:512], in0=yst[a][:], scalar=1.0, in1=yst[a][:], op0=ALU.mult, op1=ALU.mult,
                                                                       accum_out=ssq3[:, t - 1, blk:blk + 1]),
                 reads=[yst_b[a]], writes=[junk_b, ssq3_b])
            S.op("sp", lambda e, a=a, t=t, blk=blk: e.dma_start(out=y3s[(t - 1) * 128:t * 128, blk * 512:(blk + 1) * 512], in_=yst[a][:]),
                 reads=[yst_b[a]], writes=[y3s_b], dma=True)
        rs = sb("rs", [128, 16, 4], F32); rs_b = Buf("rs")
        S.op("dve", lambda e: e.tensor_reduce(out=rs[:, :, 0], in_=ssq3[:], axis=AX.X, op=ALU.add), reads=[ssq3_b], writes=[rs_b])
        S.op("act", lambda e: e.activation(out=rs[:, :, 1], in_=rs[:, :, 0], func=AF.Sqrt, scale=1.0 / D, bias=EPS), reads=[rs_b], writes=[rs_b])
        S.op("dve", lambda e: e.reciprocal(out=rs[:, :, 2], in_=rs[:, :, 1]), reads=[rs_b], writes=[rs_b])
        yt = [sb(f"yt{i}", [128, D], F32) for i in range(2)]; yt_b = [Buf(f"yt{i}") for i in range(2)]
        x2t = [sb(f"x2t{i}", [128, D], F32) for i in range(2)]; x2t_b = [Buf(f"x2t{i}") for i in range(2)]
        def final_loads(t):
            i = t % 2
            S.op("sp", lambda e, i=i, t=t: e.dma_start(out=yt[i][:], in_=y3s[(t - 1) * 128:t * 128, :]), reads=[y3s_b], writes=[yt_b[i]], dma=True)
            S.op("sp", lambda e, i=i, t=t: e.dma_start(out=x2t[i][:], in_=x2s[t * 128:(t + 1) * 128, :]), reads=[x2s_b], writes=[x2t_b[i]], dma=True)
        final_loads(1)
        for t in range(1, NTR):
            i = t % 2
            if t + 1 < NTR:
                final_loads(t + 1)
            S.op("dve", lambda e, i=i, t=t: e.scalar_tensor_tensor(out=yt[i][:], in0=yt[i][:], scalar=rs[:, t - 1, 2:3], in1=gF[:], op0=ALU.mult, op1=ALU.mult),
                 reads=[yt_b[i], rs_b, gF_b], writes=[yt_b[i]])
            S.op("dve", lambda e, i=i: e.tensor_tensor(out=x2t[i][:], in0=x2t[i][:], in1=yt[i][:], op=ALU.add), reads=[x2t_b[i], yt_b[i]], writes=[x2t_b[i]])
            S.op("sp", lambda e, i=i, t=t: e.dma_start(out=xo[(t - 1) * 128:t * 128, :], in_=x2t[i][:]), reads=[x2t_b[i]], writes=[xo_b], dma=True)
        S.barrier()
        A.release(mR)

import contextlib

PAIRS = [[0, 1], [2, 3], [4, 5], [6, 7]]


def build_fused(stages=4, debug=False, r1_stop=None):
    nc = bass.Bass("TRN2", target_bir_lowering=False)
    T, D = 4096, 2048
    ioM = [declare_io_M(nc, "_l0", with_x=True), declare_io_M(nc, "_l1", with_x=False)]
    ioR = [declare_io_R(nc, "_l0", with_xh=True), declare_io_R(nc, "_l1", with_xh=False)]
    flag_in = nc.dram_tensor("flag2", [128, 2], F32, kind="ExternalInput").ap()
    xo_ext = nc.dram_tensor("xo", [2048, D], F32, kind="ExternalOutput").ap()

    def dint(name, shape, dt=F32):
        return nc.dram_tensor(name, list(shape), dt, kind="Internal").ap()
    scr = {"zTa": dint("zTa", [NF_A, T], BF16), "zTc": dint("zTc", [NF_C, T + 1]), "ztm": dint("ztm", [T + 1, NTM])}
    for k in ("zTa", "zTc", "ztm"):
        scr[k + "_b"] = Buf(k)
    scr["oa_scr"] = [dint(f"oa_scr{p}", [T, 65]) for p in range(3)]
    scr["oa_scr_b"] = [Buf(f"oa_scr{p}") for p in range(3)]
    o_scr = dint("o_scr", [T, 1024]); o_b = Buf("o_scr")
    ssq_scr = dint("ssq_scr", [2, 128, 32]); ssq_b = Buf("ssq_scr")
    G_o = dint("G_o", [2 * T, 1024]); G_o_b = [Buf(f"G_o{k}") for k in range(8)]
    G_ssq = dint("G_ssq", [4, 128, 32]); G_ssq_b = Buf("G_ssq")
    xo_scr = dint("xo_scr", [2048, D]); xo_scr_b = Buf("xo_scr")
    G_x = dint("G_x", [T, D]); G_x_b = [Buf(f"G_x{k}") for k in range(8)]
    xo_scr2 = dint("xo_scr2", [2048, D]); xo_scr2_b = Buf("xo_scr2")
    rs = dict(x1s=dint("x1s", [TR, D]), x1s_b=Buf("x1s"), x2s=dint("x2s", [TR, D]), x2s_b=Buf("x2s"),
              acts=dint("acts", [NTR * 128 * NJ, 128], BF16), acts_b=Buf("acts"), y3s=dint("y3s", [2048, D]), y3s_b=Buf("y3s"))
    xo_ext_b = Buf("xo_ext")
    with contextlib.ExitStack() as es:
        sems = [es.enter_context(nc.semaphore(f"s{i}")) for i in range(98)]
        S = Sched(nc, sems)
        S.arena = Arena(nc)
        A = S.arena
        psbig = es.enter_context(nc.psum_tensor("psbig", [128, 4096], F32)).ap()
        ident = A.alloc("ident", [128, 128], BF16); identf = A.alloc("identf", [128, 128], F32); ones16 = A.alloc("ones16", [128, 128], BF16)
        ident_b = Buf("ident")
        S.op("pool", lambda e: e.memset(identf[:], 0.0), writes=[ident_b])
        S.op("pool", lambda e: e.affine_select(out=identf[:], in_=identf[:], pattern=[[-1, 128]], base=0,
                                               channel_multiplier=1, compare_op=ALU.not_equal, fill=1.0), reads=[ident_b], writes=[ident_b])
        S.op("pool", lambda e: e.tensor_copy(out=ident[:], in_=identf[:]), reads=[ident_b], writes=[ident_b])
        S.op("pool", lambda e: e.memset(ones16[:], 1.0), writes=[ident_b])
        S.ident, S.identf, S.ident_b, S.ones16 = ident, identf, ident_b, ones16
        flag2 = A.alloc("flag2", [128, 2], F32); flag_b = Buf("flag2")
        S.op("sp", lambda e: e.dma_start(out=flag2[:], in_=flag_in), writes=[flag_b], dma=True)
        outM = {"o": o_scr, "o_b": o_b, "ssq": ssq_scr, "ssq_b": ssq_b,
                "ssqA": A.alloc("ssqA", [128, 32], F32), "ssqA_b": Buf("ssqA"), "ssqB": A.alloc("ssqB", [128, 32], F32), "ssqB_b": Buf("ssqB")}
        S.barrier()
        for l in range(2):
            io = ioM[l]
            if l == 1:
                io["x_rows"] = lambda t: G_x[((t % 16) // 2) * 512 + (t // 16) * 256 + (t % 2) * 128:((t % 16) // 2) * 512 + (t // 16) * 256 + (t % 2) * 128 + 128, :]
                io["x_bf"] = lambda t: G_x_b[(t % 16) // 2]
            if 2 * l + 1 > stages:
                break
            emit_M(S, nc, io, scr, outM, psbig)
            S.recycle_dma_sems()
            for k in range(8):
                S.op("pool", lambda e, k=k: e.collective_compute("AllGather", ALU.bypass, replica_groups=PAIRS,
                                                                 ins=[o_scr[k * 512:(k + 1) * 512, :].opt()], outs=[G_o[k * 1024:(k + 1) * 1024, :].opt()]),
                     reads=[o_b], writes=[G_o_b[k]], dma=True, dinc=1)
            S.op("pool", lambda e: e.collective_compute("AllGather", ALU.bypass, replica_groups=PAIRS,
                                                        ins=[ssq_scr.rearrange("g j t -> (g j) t").opt()], outs=[G_ssq.rearrange("k j t -> (k j) t").opt()]),
                 reads=[ssq_b], writes=[G_ssq_b], dma=True, dinc=1)
            if debug and 2 * l + 1 == stages:
                dbg = nc.dram_tensor("dbg_Go", [2 * T, 1024], F32, kind="ExternalOutput").ap()
                dbg2 = nc.dram_tensor("dbg_Gssq", [4, 128, 32], F32, kind="ExternalOutput").ap()
                S.op("sp", lambda e: e.dma_start(out=dbg[:, :], in_=G_o[:, :]), reads=G_o_b, writes=[xo_ext_b], dma=True)
                S.op("sp", lambda e: e.dma_start(out=dbg2.rearrange("k j t -> (k j) t"), in_=G_ssq.rearrange("k j t -> (k j) t")), reads=[G_ssq_b], writes=[xo_ext_b], dma=True)
            if 2 * l + 2 > stages:
                break
            if l == 0:
                def x_rows(tt, io=ioR[0]):
                    return io["xh"][tt * 128:(tt + 1) * 128, :], Buf("xh_in")
                xo, xo_b = xo_scr, xo_scr_b
            else:
                def x_rows(tt):
                    if tt == 0:
                        return G_x[3712:3840, :], G_x_b[7]
                    return xo_scr[(tt - 1) * 128:tt * 128, :], xo_scr_b
                xo, xo_b = xo_scr2, xo_scr2_b
            cfg = dict(rs, stop=(r1_stop if l == 1 else None), G_o=G_o, G_o_b=G_o_b, G_ssq=G_ssq, G_ssq_b=G_ssq_b, x_rows=x_rows, xo=xo, xo_b=xo_b, flag2=flag2, flag_b=flag_b)
            emit_R(S, nc, ioR[l], psbig, cfg)
            S.recycle_dma_sems()
            if debug and 2 * l + 2 == stages and l == 0:
                dbg3 = nc.dram_tensor("dbg_xo", [2048, D], F32, kind="ExternalOutput").ap()
                S.op("sp", lambda e: e.dma_start(out=dbg3[:, :], in_=xo_scr[:, :]), reads=[xo_scr_b], writes=[xo_ext_b], dma=True)
            if l == 0:
                for k in range(8):
                    S.op("pool", lambda e, k=k: e.collective_compute("AllGather", ALU.bypass, replica_groups=PAIRS,
                                                                     ins=[xo_scr[k * 256:(k + 1) * 256, :].opt()], outs=[G_x[k * 512:(k + 1) * 512, :].opt()]),
                         reads=[xo_scr_b], writes=[G_x_b[k]], dma=True, dinc=1)
        if stages >= 4:
            for q in range(4):
                S.op("sp", lambda e, q=q: e.dma_start(out=xo_ext[q * 512:(q + 1) * 512, :], in_=xo_scr2[q * 512:(q + 1) * 512, :]),
                     reads=[xo_scr2_b], writes=[xo_ext_b], dma=True)
        S.final_wait("sp", [xo_ext_b])
        S.emit()
        print("fused ops", S.nops, "arena peak", A.peak, "sems left", len(S.sem_pool))
    return nc

import numpy as np
def t5_bucket(dist):
    max_exact = 16
    d = np.maximum(dist, 0)
    scaled = np.log(np.maximum(d, 1) / max_exact) / np.log(2048 / max_exact)
    large = np.minimum(max_exact + (scaled * (32 - max_exact)).astype(np.int32), 31)
    return np.where(d < max_exact, d, large).astype(np.int32)

def make_biasT(table, heads):
    qi = np.arange(128)[:, None]
    kj = np.arange(256)[None, :]
    delta = qi + 128 - kj
    valid = (delta >= 0) & (delta <= 128)
    out = np.zeros((128, len(heads), 3, 2, 128), np.float32)
    for p, d in enumerate((1, 4, 16)):
        bucket = t5_bucket(np.clip(delta, 0, 128) * d)
        for hi, h in enumerate(heads):
            b = np.where(valid, table[bucket, h], np.float32(-30000.0)).astype(np.float32)
            bt = b.T.reshape(2, 128, 128)
            out[:, hi, p, 0, :] = bt[0]
            out[:, hi, p, 1, :] = bt[1]
    return np.ascontiguousarray(out.reshape(128, len(heads) * 6, 128))

def cols_M(half):
    hA = np.arange(6) + 6 * half
    def hc(base, heads): return np.concatenate([base + h * 64 + np.arange(64) for h in heads])
    q = hc(0, hA); k = hc(768, hA); v = hc(1536, hA)
    gB = np.arange(4) + 4 * half
    u = hc(2304, gB); g = 2304 + 512 + np.concatenate([np.arange(256) + 256 * half, np.arange(256) + 256 * (1 - half)])
    c0 = 3328
    hC = np.arange(6) + 6 * half
    r = hc(c0, hC); kc = hc(c0 + 768, hC); vc = hc(c0 + 1536, hC)
    lo = c0 + 2304 + np.arange(384)
    fm = np.concatenate([q, k, r, kc, lo]); tm = np.concatenate([v, u, g, vc])
    return fm, tm


def inputs_M(d, l, b, half, x):
    fm, tm = cols_M(half)
    w_in = d["w_in"][l]
    g0 = d["sandwich_gains"][l][0]
    gB = np.arange(4) + 4 * half
    gperm = np.concatenate([np.arange(256) + 256 * half, np.arange(256) + 256 * (1 - half)])
    inm = {"x": np.ascontiguousarray(x), "g0T": np.ascontiguousarray(g0.reshape(16, 128).T),
           "w_fm": np.ascontiguousarray(w_in[:, fm]), "w_tm": np.ascontiguousarray(w_in[:, tm]),
           "biasT": make_biasT(d["rel_bias_table"], list(range(6 * half, 6 * half + 6))),
           "sgu_wT": np.ascontiguousarray(d["sgu_w"][l][gB].transpose(2, 0, 1)),
           "sgu_bT": np.ascontiguousarray(d["sgu_b"][l][gB].T),
           "sgu_ng": np.ascontiguousarray(d["sgu_norm_gain"][l][gperm]),
           }
    hC = np.arange(6) + 6 * half
    def hcols(base): return np.concatenate([base + hh * 64 + np.arange(64) for hh in hC])
    mu = d["rwkv_mu"][l]
    cp = np.zeros((128, 25), np.float32)
    def pairs(v384): return np.ascontiguousarray(v384.reshape(3, 128).T)
    cp[:, 0:3] = pairs(mu[hcols(0)]); cp[:, 3:6] = pairs(mu[hcols(768)])
    cp[:, 6:9] = pairs(d["rwkv_w0"][l][hcols(0)]); cp[:, 9:12] = pairs(d["rwkv_a0"][l][hcols(0)])
    cp[:, 12:15] = pairs(d["rwkv_k_k"][l][hcols(0)]); cp[:, 15:18] = pairs(d["rwkv_k_a"][l][hcols(0)])
    cp[:, 18:21] = pairs(d["rwkv_r_k"][l].reshape(-1)[hcols(0)])
    cp[0:64, 21] = mu[2304:2368]; cp[0:64, 22] = mu[2368:2432]
    cp[:, 23] = mu[2432:2560]; cp[:, 24] = mu[2560:2688]
    inm["cp"] = cp
    inm["cf"] = np.ascontiguousarray(np.stack([mu[hcols(1536)], d["rwkv_ln_gain"][l][hcols(0)], d["rwkv_ln_bias"][l][hcols(0)]]))
    inm["w_up"] = np.ascontiguousarray(d["rwkv_w_up"][l][:, hcols(0)])
    inm["a_up"] = np.ascontiguousarray(d["rwkv_a_up"][l][:, hcols(0)])
    inm["g_up"] = np.ascontiguousarray(d["rwkv_g_up"][l][:, hcols(0)])
    return inm


def wout_perm():
    A0 = np.arange(0, 384); A1 = np.arange(384, 768)
    B0 = 768 + np.arange(0, 256); B1 = 768 + np.arange(256, 512)
    C0 = 1280 + np.arange(0, 384); C1 = 1280 + np.arange(384, 768)
    return np.concatenate([A0, B0, C0, A1, B1, C1])


def fmT(v, n=16):
    return np.ascontiguousarray(v.reshape(n, 128).T)


def inputs_R(d, l, b, half, x_b, o0, o1, ssq0, ssq1):
    TR = 17 * 128
    lo = half * 2048 - 128
    def halo_rows(a):
        out = np.zeros((TR,) + a.shape[1:], a.dtype)
        if lo < 0:
            out[128:] = a[0:2048]
        else:
            out[:] = a[lo:lo + TR]
        return out
    xh = halo_rows(x_b)
    oc = np.concatenate([halo_rows(o0), halo_rows(o1)], axis=1)
    ssq = np.zeros((128, 17, 4), np.float32)
    for tt in range(17):
        tb = half * 16 - 1 + tt
        if tb < 0:
            continue
        ssq[:, tt, 0] = ssq0[0][:, tb]; ssq[:, tt, 1] = ssq0[1][:, tb]
        ssq[:, tt, 2] = ssq1[0][:, tb]; ssq[:, tt, 3] = ssq1[1][:, tb]
    g = d["sandwich_gains"][l]
    ag = d["attn_out_gain"][l]; sg = d["sgu_out_gain"][l]
    one = np.ones(384, np.float32)
    gv = np.concatenate([ag[0:384], sg[0:256], one, ag[384:768], sg[256:512], one])
    cwv = d["ffn_conv_w"][l]
    inm = dict(
        xh=xh, oc=np.ascontiguousarray(oc), ssq=ssq,
        w_out=np.ascontiguousarray(d["w_out"][l][wout_perm(), :]), gvT=fmT(gv),
        gT=np.ascontiguousarray(np.stack([fmT(g[2]), fmT(g[4])], axis=1)), gF=np.ascontiguousarray(np.stack([g[1], g[3], g[5]])),
        mem=np.ascontiguousarray(d["mem"][b]), msgT=fmT(d["mem_src_gain"][l]),
        wq=np.ascontiguousarray(d["mem_wq"][l]), wkv=np.ascontiguousarray(d["mem_wkv"][l]), wo=np.ascontiguousarray(d["mem_wo"][l]),
        w_up=np.ascontiguousarray(d["ffn_w_up"][l]),
        cw=np.ascontiguousarray(cwv.reshape(3, 88, 128).transpose(2, 1, 0)), cb=np.ascontiguousarray(d["ffn_conv_b"][l].reshape(88, 128).T),
        w_down=np.ascontiguousarray(d["ffn_w_down"][l]),
        flag=np.full((128, 1), float(half), np.float32),
    )
    return inm


def m_out_from_ref(oa_pre, ob_pre, oc, half):
    o = np.concatenate([oa_pre[:, 384 * half:384 * half + 384], ob_pre[:, 256 * half:256 * half + 256], oc[:, 384 * half:384 * half + 384]], axis=1)
    sA = (oa_pre[:, 384 * half:384 * half + 384] ** 2).sum(-1); sB = (ob_pre[:, 256 * half:256 * half + 256] ** 2).sum(-1)
    ssq = np.stack([sA.reshape(32, 128).T, sB.reshape(32, 128).T])
    return np.ascontiguousarray(o.astype(np.float32)), np.ascontiguousarray(ssq.astype(np.float32))


def inputs_R_fused(d, l, b, half, x_b=None):
    g = d["sandwich_gains"][l]
    ag = d["attn_out_gain"][l]; sg = d["sgu_out_gain"][l]
    one = np.ones(384, np.float32)
    gv = np.concatenate([ag[0:384], sg[0:256], one, ag[384:768], sg[256:512], one])
    cwv = d["ffn_conv_w"][l]
    inm = dict(
        w_out=np.ascontiguousarray(d["w_out"][l][wout_perm(), :]), gvT=fmT(gv),
        gT=np.ascontiguousarray(np.stack([fmT(g[2]), fmT(g[4])], axis=1)), gF=np.ascontiguousarray(np.stack([g[1], g[3], g[5]])),
        mem=np.ascontiguousarray(d["mem"][b]), msgT=fmT(d["mem_src_gain"][l]),
        wq=np.ascontiguousarray(d["mem_wq"][l]), wkv=np.ascontiguousarray(d["mem_wkv"][l]), wo=np.ascontiguousarray(d["mem_wo"][l]),
        w_up=np.ascontiguousarray(d["ffn_w_up"][l]),
        cw=np.ascontiguousarray(cwv.reshape(3, 88, 128).transpose(2, 1, 0)), cb=np.ascontiguousarray(d["ffn_conv_b"][l].reshape(88, 128).T),
        w_down=np.ascontiguousarray(d["ffn_w_down"][l]),
    )
    if x_b is not None:
        TR = 17 * 128
        lo = half * 2048 - 128
        xh = np.zeros((TR, x_b.shape[1]), np.float32)
        if lo < 0:
            xh[128:] = x_b[0:2048]
        else:
            xh[:] = x_b[lo:lo + TR]
        inm["xh"] = xh
    return inm


def inputs_fused(d, c):
    b, half = c // 2, c % 2
    x = np.asarray(d["x"], dtype=np.float32)
    out = {}
    for l in range(2):
        im = inputs_M(d, l, b, half, x[b])
        if l == 1:
            im.pop("x")
        for k, v in im.items():
            out[f"{k}_l{l}"] = v
        ir = inputs_R_fused(d, l, b, half, x[b] if l == 0 else None)
        for k, v in ir.items():
            out[f"R_{k}_l{l}"] = v
    fl = np.zeros((128, 2), np.float32)
    fl[:, 0] = float(half)
    fl[:, 1] = 1.0 - float(half)
    out["flag2"] = fl
    return out


_NC_CACHE = {}


def kernel(**inputs):
    d = {k: np.asarray(v) for k, v in inputs.items()}
    if "F" not in _NC_CACHE:
        _NC_CACHE["F"] = build_fused()
    nc = _NC_CACHE["F"]
    cores = list(range(8))
    in_maps = [inputs_fused(d, c) for c in cores]
    res = run_bass_kernel_spmd(nc, in_maps, core_ids=cores)
    x = np.stack([np.concatenate([np.asarray(res.results[2 * b]["xo"]), np.asarray(res.results[2 * b + 1]["xo"])], axis=0) for b in range(4)])
    return np.ascontiguousarray(x.astype(np.float32))
```

```python
import contextlib

import numpy as np
import concourse.bass as bass
import concourse.mybir as mybir
from concourse.bass_utils import run_bass_kernel_spmd

F32 = mybir.dt.float32
BF16 = mybir.dt.bfloat16
AF = mybir.ActivationFunctionType
ALU = mybir.AluOpType
AX = mybir.AxisListType

SEM_ROT = 30000


class Buf:
    __slots__ = ("name", "last_w", "readers", "dsem", "dval", "psum")

    def __init__(self, name, psum=False):
        self.name = name
        self.psum = psum
        self.last_w = None
        self.readers = []
        self.dsem = None
        self.dval = 0


class Sched:
    ENGS = ("pe", "act", "dve", "pool", "sp")

    def __init__(self, nc, sem_pool):
        self.nc = nc
        self.sem_pool = [(s, 0) for s in sem_pool]
        self.ops = {e: [] for e in self.ENGS}
        self.cnt = {e: 0 for e in self.ENGS}
        self.sem = {e: self.sem_pool.pop()[0] for e in self.ENGS}
        self.eng_spare = [self.sem_pool.pop()[0] for _ in range(10)]
        self.last_tok = {e: None for e in self.ENGS}
        self.waited = {e: {} for e in self.ENGS}
        self.nops = 0
        self.dma_bufs = {}

    def _new_sem(self):
        return self.sem_pool.pop()[0]

    def recycle_dma_sems(self):
        for b in self.dma_bufs.values():
            if b.dsem is not None:
                if b.dval < SEM_ROT:
                    self.sem_pool.insert(0, (b.dsem, b.dval))
                b.dsem = None
        self.dma_bufs = {}

    def op(self, eng, fn, reads=(), writes=(), dma=False, dinc=16):
        writes = list(writes) + [b for b in reads if b.psum]
        deps = {}
        for b in reads:
            if b.last_w is not None:
                s, v, e2 = b.last_w
                if not (eng == "pe" and e2 == "pe"):
                    deps[s] = max(deps.get(s, 0), v)
        for b in writes:
            if b.last_w is not None:
                s, v, e2 = b.last_w
                if not (eng == "pe" and e2 == "pe"):
                    deps[s] = max(deps.get(s, 0), v)
            for (s, v, e2) in b.readers:
                if not (eng == "pe" and e2 == "pe"):
                    deps[s] = max(deps.get(s, 0), v)
        waits = []
        wd = self.waited[eng]
        for s, v in deps.items():
            if wd.get(id(s), 0) < v:
                wd[id(s)] = v
                waits.append((s, v))
        if dma:
            b = writes[0]
            if b.dsem is None or b.dval >= SEM_ROT:
                b.dsem, b.dval = self.sem_pool.pop()
            b.dval += dinc
            tok = (b.dsem, b.dval, "dma")
            csem, inc = b.dsem, dinc
        else:
            if self.cnt[eng] >= SEM_ROT:
                self.sem[eng] = self.eng_spare.pop()
                self.cnt[eng] = 0
            self.cnt[eng] += 1
            tok = (self.sem[eng], self.cnt[eng], eng)
            self.last_tok[eng] = tok
            csem, inc = self.sem[eng], 1

        def run(e, fn=fn, waits=waits, csem=csem, inc=inc):
            for s, v in waits:
                e.wait_ge(s, v)
            fn(e).then_inc(csem, inc)

        self.ops[eng].append(run)
        self.nops += 1
        for b in reads:
            b.readers = [t for t in b.readers if t[0] is not tok[0]] + [tok]
        for b in writes:
            b.last_w = tok
            b.readers = []
            if dma:
                self.dma_bufs[id(b)] = b
        return tok

    def barrier(self):
        allw = [self.last_tok[e] for e in self.ENGS if self.last_tok[e] is not None]
        allw += [(b.dsem, b.dval, "dma") for b in self.dma_bufs.values() if b.dsem is not None]
        for eng in self.ENGS:
            waits = []
            wd = self.waited[eng]
            for s, v, e2 in allw:
                if e2 == eng and eng == "pe":
                    continue
                if wd.get(id(s), 0) < v:
                    wd[id(s)] = v
                    waits.append((s, v))

            def run(e, waits=waits):
                for s, v in waits:
                    e.wait_ge(s, v)
            self.ops[eng].append(run)

    def final_wait(self, eng, bufs):
        waits = []
        for b in bufs:
            if b.last_w is not None:
                waits.append((b.last_w[0], b.last_w[1]))

        def run(e, waits=waits):
            for s, v in waits:
                e.wait_ge(s, v)
        self.ops[eng].append(run)

    def emit(self):
        nc = self.nc
        with nc.Block() as block:
            @block.tensor
            def _(e):
                for f in self.ops["pe"]:
                    f(e)

            @block.scalar
            def _(e):
                for f in self.ops["act"]:
                    f(e)

            @block.vector
            def _(e):
                for f in self.ops["dve"]:
                    f(e)

            @block.gpsimd
            def _(e):
                for f in self.ops["pool"]:
                    f(e)

            @block.sync
            def _(e):
                for f in self.ops["sp"]:
                    f(e)


class Arena:
    BASE = 16512
    LIMIT = 229000

    def __init__(self, nc):
        self.nc = nc
        self.off = self.BASE
        self.n = 0
        self.peak = 0

    def alloc(self, name, shape, dt):
        esz = 2 if dt == BF16 else 4
        nbytes = esz * int(np.prod(shape[1:]))
        nbytes = (nbytes + 63) // 64 * 64
        assert self.off + nbytes <= self.LIMIT, (name, self.off, nbytes)
        self.n += 1
        t = self.nc.alloc_sbuf_tensor_at(f"{name}_{self.n}", list(shape), dt, offset=self.off)
        self.off += nbytes
        self.peak = max(self.peak, self.off)
        return t

    def mark(self):
        return self.off

    def release(self, m):
        self.off = m

class PS:
    def __init__(self, t, name):
        self.t = t
        self.b = Buf(name, psum=True)


import contextlib
import numpy as np

T = 4096
DM = 2048
NT = T // 128
NF_A = 768
NF_C = 1152
NTM = 1536
EPS_M = 1e-6


def declare_io_M(nc, sfx="", with_x=True):
    io = {}
    def din(name, shape):
        return nc.dram_tensor(name + sfx, list(shape), F32, kind="ExternalInput").ap()
    if with_x:
        io["x"] = din("x", [T, DM])
        io["x_rows"] = lambda t, xx=io["x"]: xx[t * 128:(t + 1) * 128, :]
    io["x_b"] = Buf("x_in")
    io["x_bf"] = lambda t, b=io["x_b"]: b
    io["g0T"] = din("g0T", [128, 16])
    io["w_fm"] = din("w_fm", [DM, NF_A + NF_C])
    io["w_tm"] = din("w_tm", [DM, NTM])
    io["biasT"] = din("biasT", [128, 36, 128])
    io["sgu_wT"] = din("sgu_wT", [128, 4, 128])
    io["sgu_bT"] = din("sgu_bT", [128, 4])
    io["sgu_ng"] = din("sgu_ng", [512])
    io["cp"] = din("cp", [128, 25])
    io["cf"] = din("cf", [3, 384])
    io["w_up"] = din("w_up", [64, 384])
    io["a_up"] = din("a_up", [64, 384])
    io["g_up"] = din("g_up", [256, 384])
    return io


def emit_M(S, nc, io, scr, out, psbig):
    ps = [PS(BankView(psbig[:, i * 512:(i + 1) * 512]), f"ps{i}") for i in range(8)]
    phase1(S, nc, io, scr, ps, S.ident)
    phase2(S, nc, io, scr, ps, out)
    S.op("sp", lambda e: e.dma_start(out=out["ssq"][0], in_=out["ssqA"][:]), reads=[out["ssqA_b"]], writes=[out["ssq_b"]], dma=True)
    phase3(S, nc, io, scr, ps, out)
    S.op("sp", lambda e: e.dma_start(out=out["ssq"][1], in_=out["ssqB"][:]), reads=[out["ssqB_b"]], writes=[out["ssq_b"]], dma=True)
    phase4(S, nc, io, scr, psbig, out)


def phase1(S, nc, io, scr, ps, ident):
    A = S.arena
    m0 = A.mark()
    if True:
        sb = A.alloc
        hT = sb("hT", [128, 16, T], BF16)
        hT_b = Buf("hT")
        g0T = sb("g0T_sb", [128, 16], F32)
        g0T_b = Buf("g0T")
        S.op("sp", lambda e: e.dma_start(out=g0T[:], in_=io["g0T"]), writes=[g0T_b], dma=True)
        m1 = A.mark()
        if True:
            sb2 = A.alloc
            xt = [sb2(f"xt{i}", [128, DM], F32) for i in range(2)]
            xt_b = [Buf(f"xt{i}") for i in range(2)]
            xs = [sb2(f"xs{i}", [128, DM], BF16) for i in range(2)]
            xs_b = [Buf(f"xs{i}") for i in range(2)]
            junk = sb2("junk", [128, DM], BF16)
            junk_b = Buf("junk")
            st = [sb2(f"st{i}", [128, 4], F32) for i in range(2)]
            st_b = [Buf(f"st{i}") for i in range(2)]
            for t in range(NT):
                i = t % 2
                S.op("sp", lambda e, t=t, i=i: e.dma_start(out=xt[i][:], in_=io["x_rows"](t)),
                     reads=[io["x_b"]], writes=[xt_b[i]], dma=True)
                S.op("act", lambda e, i=i: e.activation(out=junk[:], in_=xt[i][:], func=AF.Square,
                                                        accum_out=st[i][:, 0:1]),
                     reads=[xt_b[i]], writes=[junk_b, st_b[i]])
                S.op("act", lambda e, i=i: e.activation(out=st[i][:, 1:2], in_=st[i][:, 0:1], func=AF.Sqrt,
                                                        scale=1.0 / DM, bias=EPS_M),
                     reads=[st_b[i]], writes=[st_b[i]])
                S.op("dve", lambda e, i=i: e.reciprocal(out=st[i][:, 2:3], in_=st[i][:, 1:2]),
                     reads=[st_b[i]], writes=[st_b[i]])
                S.op("dve", lambda e, i=i: e.tensor_scalar(out=xs[i][:], in0=xt[i][:], scalar1=st[i][:, 2:3],
                                                           scalar2=None, op0=ALU.mult),
                     reads=[xt_b[i], st_b[i]], writes=[xs_b[i]])
                pa, pb = ps[2 * i], ps[2 * i + 1]
                for c in range(16):
                    pp = pa if c < 8 else pb
                    S.op("pe", lambda e, c=c, pp=pp, i=i: e.transpose(
                        out=pp.t.ap().bitcast(BF16)[:, (c % 8) * 128:(c % 8 + 1) * 128],
                        in_=xs[i][:, c * 128:(c + 1) * 128], identity=ident[:]),
                        reads=[xs_b[i]], writes=[pp.b])
                for half, pp in enumerate((pa, pb)):
                    S.op("dve" if half == 0 else "pool" if False else "dve", lambda e, half=half, pp=pp, t=t: e.tensor_tensor(
                        out=hT[:, half * 8:(half + 1) * 8, t * 128:(t + 1) * 128],
                        in0=pp.t.ap().bitcast(BF16).rearrange("p (c n) -> p c n", c=8),
                        in1=g0T[:, half * 8:(half + 1) * 8].unsqueeze(2).to_broadcast([128, 8, 128]),
                        op=ALU.mult),
                        reads=[pp.b, g0T_b], writes=[hT_b])
        S.barrier()
        A.release(m1)
        if True:
            sb2 = A.alloc
            wj = [sb2(f"wj{i}", [128, 16, 128], BF16) for i in range(2)]
            wj_b = [Buf(f"wj{i}") for i in range(2)]
            zst16 = [sb2(f"zst16_{i}", [128, T], BF16) for i in range(2)]
            zst32 = [sb2(f"zst32_{i}", [128, T + 1], F32) for i in range(1)]
            zst16_b = [Buf(f"zst16_{i}") for i in range(2)]
            zst32_b = [Buf(f"zst32_{i}") for i in range(1)]
            wtm = [sb2(f"wtm{i}", [128, 16, 512], BF16) for i in range(1)]
            wtm_b = [Buf(f"wtm{i}") for i in range(1)]
            tst = [sb2(f"tst{i}", [128, 512], F32) for i in range(3)]
            tst_b = [Buf(f"tst{i}") for i in range(3)]
            S.op("pool", lambda e: e.memset(zst32[0][:, 0:1], 0.0), writes=[zst32_b[0]])
            zrow = sb2("zrow", [1, NTM], F32)
            zrow_b = Buf("zrow")
            S.op("pool", lambda e: e.memset(zrow[:], 0.0), writes=[zrow_b])
            S.op("sp", lambda e: e.dma_start(out=scr["ztm"][0:1, :], in_=zrow[:]), reads=[zrow_b], writes=[scr["ztm_b"]], dma=True)
            w_fm_v = io["w_fm"].rearrange("(kc p) n -> p kc n", p=128)
            w_tm_v = io["w_tm"].rearrange("(kc p) n -> p kc n", p=128)
            nchunks = (NF_A + NF_C) // 128
            pi = 0
            for j in range(nchunks):
                i = j % 2
                S.op("pool", lambda e, j=j, i=i: e.dma_start(out=wj[i][:], in_=w_fm_v[:, :, j * 128:(j + 1) * 128]),
                     writes=[wj_b[i]], dma=True)
                is_a = j < NF_A // 128
                if is_a:
                    zt, ztb = zst16[j % 2], zst16_b[j % 2]
                    zo = 0
                else:
                    zt, ztb = zst32[0], zst32_b[0]
                    zo = 1
                for tg in range(T // 512):
                    pp = ps[pi % 8]
                    pi += 1
                    for kc in range(16):
                        S.op("pe", lambda e, kc=kc, pp=pp, i=i, tg=tg: e.matmul(
                            pp.t[:, :], lhsT=wj[i][:, kc, :], rhs=hT[:, kc, tg * 512:(tg + 1) * 512],
                            start=(kc == 0), stop=(kc == 15)),
                            reads=[wj_b[i], hT_b], writes=[pp.b])
                    if tg % 2 == 0:
                        S.op("act", lambda e, pp=pp, zt=zt, tg=tg, zo=zo: e.copy(out=zt[:, zo + tg * 512:zo + (tg + 1) * 512], in_=pp.t[:, :]),
                             reads=[pp.b], writes=[ztb])
                    else:
                        S.op("dve", lambda e, pp=pp, zt=zt, tg=tg, zo=zo: e.tensor_copy(out=zt[:, zo + tg * 512:zo + (tg + 1) * 512], in_=pp.t[:, :]),
                             reads=[pp.b], writes=[ztb])
                if is_a:
                    S.op("sp", lambda e, j=j, zt=zt: e.dma_start(out=scr["zTa"][j * 128:(j + 1) * 128, :], in_=zt[:]),
                         reads=[ztb], writes=[scr["zTa_b"]], dma=True)
                else:
                    jj = j - NF_A // 128
                    S.op("sp", lambda e, jj=jj, zt=zt: e.dma_start(out=scr["zTc"][jj * 128:(jj + 1) * 128, :], in_=zt[:]),
                         reads=[ztb], writes=[scr["zTc_b"]], dma=True)
            k = 0
            for blk in range(NTM // 512):
                S.op("pool", lambda e, blk=blk: e.dma_start(out=wtm[0][:], in_=w_tm_v[:, :, blk * 512:(blk + 1) * 512]),
                     writes=[wtm_b[0]], dma=True)
                for t in range(NT):
                    pp = ps[pi % 8]
                    pi += 1
                    for kc in range(16):
                        S.op("pe", lambda e, kc=kc, pp=pp, t=t: e.matmul(
                            pp.t[:, :], lhsT=hT[:, kc, t * 128:(t + 1) * 128], rhs=wtm[0][:, kc, :],
                            start=(kc == 0), stop=(kc == 15)),
                            reads=[wtm_b[0], hT_b], writes=[pp.b])
                    ti = k % 3
                    k += 1
                    if t % 2 == 0:
                        S.op("act", lambda e, pp=pp, ti=ti: e.copy(out=tst[ti][:], in_=pp.t[:, :]),
                             reads=[pp.b], writes=[tst_b[ti]])
                    else:
                        S.op("dve", lambda e, pp=pp, ti=ti: e.tensor_copy(out=tst[ti][:], in_=pp.t[:, :]),
                             reads=[pp.b], writes=[tst_b[ti]])
                    S.op("sp", lambda e, ti=ti, t=t, blk=blk: e.dma_start(
                        out=scr["ztm"][1 + t * 128:1 + (t + 1) * 128, blk * 512:(blk + 1) * 512], in_=tst[ti][:]),
                        reads=[tst_b[ti]], writes=[scr["ztm_b"]], dma=True)
        S.barrier()
        A.release(m0)


class BankView:
    def __init__(self, ap):
        self._ap = ap

    def ap(self):
        return self._ap

    def __getitem__(self, k):
        return self._ap[k]


DILS = (1, 4, 16)


def phase2(S, nc, io, scr, ps, out):
    A = S.arena
    m0 = A.mark()
    sb = A.alloc
    biasT = sb("biasT", [128, 6 * 3 * 2, 128], F32)
    biasT_b = Buf("biasT")
    S.op("sp", lambda e: e.dma_start(out=biasT[:], in_=io["biasT"]), writes=[biasT_b], dma=True)
    qT = [sb(f"qT{i}", [128, T], BF16) for i in range(2)]
    kT = [sb(f"kT{i}", [128, T], BF16) for i in range(2)]
    qT_b = [Buf(f"qT{i}") for i in range(2)]
    kT_b = [Buf(f"kT{i}") for i in range(2)]
    Vd = [[sb(f"Vd{hh}_{p}", [128, 32, 65], BF16) for p in range(3)] for hh in range(2)]
    Vd_b = [[Buf(f"Vd{hh}_{p}") for p in range(3)] for hh in range(2)]
    for hh in range(2):
        for p in range(3):
            S.op("pool", lambda e, hh=hh, p=p: e.memset(Vd[hh][p][:, :, 64:65], 1.0), writes=[Vd_b[hh][p]])
    NTB = 4
    NPS = 4
    tt = [sb(f"att_t{i}", [128, 256], F32) for i in range(NTB)]
    tt_b = [Buf(f"att_t{i}") for i in range(NTB)]
    PT = [sb(f"att_PT{i}", [128, 256], BF16) for i in range(NTB)]
    PT_b = [Buf(f"att_PT{i}") for i in range(NTB)]
    Oacc = [sb(f"Oacc{p}", [128, 32, 65], F32) for p in range(3)]
    Oacc_b = [Buf(f"Oacc{p}") for p in range(3)]
    Old = [sb(f"Old{p}", [128, 32, 65], F32) for p in range(3)]
    Old_b = [Buf(f"Old{p}") for p in range(3)]
    rec = sb("att_rec", [128, 32], F32)
    rec_b = Buf("att_rec")
    oah = sb("oah", [128, 32, 64], F32)
    oah_b = Buf("oah")
    sq = sb("att_sq", [128, 32, 64], F32)
    sq_b = Buf("att_sq")
    ssq1 = sb("att_ssq1", [128, 32], F32)
    ssq1_b = Buf("att_ssq1")
    ssqA = out["ssqA"]
    ssqA_b = out["ssqA_b"]
    S.op("pool", lambda e: e.memset(ssqA[:], 0.0), writes=[ssqA_b])
    psS = ps[0:4]
    psO = ps[4:6]
    blkc = 0
    for pair in range(3):
        i = pair % 2
        S.op("sp", lambda e, pair=pair, i=i: e.dma_start(out=qT[i][:], in_=scr["zTa"][pair * 128:(pair + 1) * 128, :]),
             reads=[scr["zTa_b"]], writes=[qT_b[i]], dma=True)
        S.op("sp", lambda e, pair=pair, i=i: e.dma_start(out=kT[i][:], in_=scr["zTa"][384 + pair * 128:384 + (pair + 1) * 128, :]),
             reads=[scr["zTa_b"]], writes=[kT_b[i]], dma=True)
        for hh in range(2):
            h = pair * 2 + hh
            p0 = 64 * hh
            for p, d in enumerate(DILS):
                S.op("pool", lambda e, hh=hh, p=p, d=d, h=h: e.dma_start(
                    out=Vd[hh][p][:, :, 0:64].rearrange("j (r n) c -> j r n c", r=d),
                    in_=scr["ztm"][1:T + 1, h * 64:(h + 1) * 64].rearrange("(n j r) c -> j r n c", j=128, r=d)),
                    reads=[scr["ztm_b"]], writes=[Vd_b[hh][p]], dma=True)
            items = []
            for p, d in enumerate(DILS):
                nblk = 32 // d
                for r in range(d):
                    for n in range(nblk):
                        items.append((p, d, r, n, r * nblk + n))

            def emit_scores(k, item):
                p, d, r, n, blk = item
                pS = psS[k % NPS]
                ti = k % NTB
                qs = n * 128 * d + r
                qsl = slice(qs, qs + 127 * d + 1, d) if d > 1 else slice(qs, qs + 128)
                bidx = (h * 3 + p) * 2
                if n > 0:
                    ks = (n - 1) * 128 * d + r
                    ksl = slice(ks, ks + 127 * d + 1, d) if d > 1 else slice(ks, ks + 128)
                    S.op("pe", lambda e, pS=pS, ksl=ksl, qsl=qsl, i=i, p0=p0: e.matmul(
                        pS.t[:, 0:128], lhsT=kT[i][p0:p0 + 64, ksl], rhs=qT[i][p0:p0 + 64, qsl], start=True, stop=True),
                        reads=[kT_b[i], qT_b[i]], writes=[pS.b])
                S.op("pe", lambda e, pS=pS, qsl=qsl, i=i, p0=p0: e.matmul(
                    pS.t[:, 128:256], lhsT=kT[i][p0:p0 + 64, qsl], rhs=qT[i][p0:p0 + 64, qsl], start=True, stop=True),
                    reads=[kT_b[i], qT_b[i]], writes=[pS.b])
                c0 = 0 if n > 0 else 128
                S.op("dve", lambda e, pS=pS, ti=ti, c0=c0, bidx=bidx: e.scalar_tensor_tensor(
                    out=tt[ti][:, c0:256], in0=pS.t[:, c0:256], scalar=0.125,
                    in1=biasT[:, bidx:bidx + 2, :].rearrange("p a q -> p (a q)")[:, c0:256],
                    op0=ALU.mult, op1=ALU.add),
                    reads=[pS.b, biasT_b], writes=[tt_b[ti]])
                S.op("act", lambda e, ti=ti, c0=c0: e.activation(out=PT[ti][:, c0:256], in_=tt[ti][:, c0:256], func=AF.Exp),
                     reads=[tt_b[ti]], writes=[PT_b[ti]])

            def emit_pv(k, item):
                p, d, r, n, blk = item
                ti = k % NTB
                slot = blk % 4
                pO = psO[(blk // 4) % 2]
                if n > 0:
                    S.op("pe", lambda e, pO=pO, slot=slot, ti=ti, p=p, blk=blk, hh=hh: e.matmul(
                        pO.t[:, slot * 65:(slot + 1) * 65], lhsT=PT[ti][:, 0:128], rhs=Vd[hh][p][:, blk - 1, :],
                        start=True, stop=False),
                        reads=[PT_b[ti], Vd_b[hh][p]], writes=[pO.b])
                S.op("pe", lambda e, pO=pO, slot=slot, ti=ti, p=p, blk=blk, n=n, hh=hh: e.matmul(
                    pO.t[:, slot * 65:(slot + 1) * 65], lhsT=PT[ti][:, 128:256], rhs=Vd[hh][p][:, blk, :],
                    start=(n == 0), stop=True),
                    reads=[PT_b[ti], Vd_b[hh][p]], writes=[pO.b])
                if slot == 3:
                    b0 = blk - 3
                    if (blk // 4) % 2 == 0:
                        S.op("act", lambda e, pO=pO, p=p, b0=b0: e.copy(
                            out=Oacc[p][:, b0:b0 + 4, :].rearrange("p a c -> p (a c)"), in_=pO.t[:, 0:260]),
                            reads=[pO.b], writes=[Oacc_b[p]])
                    else:
                        S.op("dve", lambda e, pO=pO, p=p, b0=b0: e.tensor_copy(
                            out=Oacc[p][:, b0:b0 + 4, :].rearrange("p a c -> p (a c)"), in_=pO.t[:, 0:260]),
                            reads=[pO.b], writes=[Oacc_b[p]])
                if blk == 31:
                    S.op("sp", lambda e, p=p, d=d: e.dma_start(
                        out=scr["oa_scr"][p].rearrange("(n j r) c -> j r n c", j=128, r=d),
                        in_=Oacc[p][:].rearrange("j (r n) c -> j r n c", r=d)),
                        reads=[Oacc_b[p]], writes=[scr["oa_scr_b"][p]], dma=True)

            LAG = 3
            for step in range(len(items) + LAG):
                if step < len(items):
                    emit_scores(blkc + step, items[step])
                if step >= LAG:
                    emit_pv(blkc + step - LAG, items[step - LAG])
            blkc += len(items)
            for p in range(3):
                S.op("sp", lambda e, p=p: e.dma_start(
                    out=Old[p][:], in_=scr["oa_scr"][p].rearrange("(tb j) c -> j tb c", j=128)),
                    reads=[scr["oa_scr_b"][p]], writes=[Old_b[p]], dma=True)
            S.op("dve", lambda e: e.tensor_tensor(out=Old[0][:], in0=Old[0][:], in1=Old[1][:], op=ALU.add),
                 reads=[Old_b[0], Old_b[1]], writes=[Old_b[0]])
            S.op("dve", lambda e: e.tensor_tensor(out=Old[0][:], in0=Old[0][:], in1=Old[2][:], op=ALU.add),
                 reads=[Old_b[0], Old_b[2]], writes=[Old_b[0]])
            S.op("dve", lambda e: e.reciprocal(out=rec[:], in_=Old[0][:, :, 64]),
                 reads=[Old_b[0]], writes=[rec_b])
            S.op("dve", lambda e: e.tensor_tensor(out=oah[:], in0=Old[0][:, :, 0:64],
                                                  in1=rec[:].unsqueeze(2).to_broadcast([128, 32, 64]), op=ALU.mult),
                 reads=[Old_b[0], rec_b], writes=[oah_b])
            S.op("sp", lambda e, h=h: e.dma_start(
                out=out["o"][:, h * 64:(h + 1) * 64].rearrange("(tb j) c -> j tb c", j=128), in_=oah[:]),
                reads=[oah_b], writes=[out["o_b"]], dma=True)
            S.op("dve", lambda e: e.tensor_tensor(out=sq[:], in0=oah[:], in1=oah[:], op=ALU.mult),
                 reads=[oah_b], writes=[sq_b])
            S.op("dve", lambda e: e.tensor_reduce(out=ssq1[:], in_=sq[:], axis=AX.X, op=ALU.add),
                 reads=[sq_b], writes=[ssq1_b])
            S.op("dve", lambda e: e.tensor_tensor(out=ssqA[:], in0=ssqA[:], in1=ssq1[:], op=ALU.add),
                 reads=[ssq1_b, ssqA_b], writes=[ssqA_b])
    S.barrier()
    A.release(m0)


def phase3(S, nc, io, scr, ps, out):
    A = S.arena
    m0 = A.mark()
    sb = A.alloc
    wT32 = sb("sgu_wT32", [128, 4, 128], F32)
    wT = sb("sgu_wT", [128, 4, 128], BF16)
    wT_b = Buf("sgu_wT")
    mask = sb("sgu_mask", [128, 128], F32)
    mask_b = Buf("sgu_mask")
    S.op("sp", lambda e: e.dma_start(out=wT32[:], in_=io["sgu_wT"]), writes=[wT_b], dma=True)
    S.op("pool", lambda e: e.memset(mask[:], 1.0), writes=[mask_b])
    S.op("pool", lambda e: e.affine_select(out=mask[:], in_=mask[:], pattern=[[1, 128]], base=0,
                                           channel_multiplier=-1, compare_op=ALU.is_ge, fill=0.0),
         reads=[mask_b], writes=[mask_b])
    S.op("dve", lambda e: e.tensor_tensor(out=wT[:], in0=wT32[:], in1=mask[:].unsqueeze(1).to_broadcast([128, 4, 128]),
                                          op=ALU.mult), reads=[wT_b, mask_b], writes=[wT_b])
    bs = sb("sgu_b", [128, 4], F32)
    bs_b = Buf("sgu_b")
    S.op("sp", lambda e: e.dma_start(out=bs[:], in_=io["sgu_bT"]), writes=[bs_b], dma=True)
    gain = sb("sgu_gain", [128, 512], F32)
    gain_b = Buf("sgu_gain")
    S.op("sp", lambda e: e.dma_start(out=gain[:], in_=io["sgu_ng"].partition_broadcast(128)), writes=[gain_b], dma=True)
    NB = 2
    G = 4
    g_t = [sb(f"sgu_g{i}", [128, G, 512], F32) for i in range(NB)]
    g_b = [Buf(f"sgu_g{i}") for i in range(NB)]
    u_t = [sb(f"sgu_u{i}", [128, G, 256], F32) for i in range(NB)]
    u_b = [Buf(f"sgu_u{i}") for i in range(NB)]
    sq = sb("sgu_sq", [128, G, 512], F32)
    sq_b = Buf("sgu_sq")
    gn = [sb(f"sgu_gn{i}", [128, G, 256], BF16) for i in range(NB)]
    gn_b = [Buf(f"sgu_gn{i}") for i in range(NB)]
    stt = [sb(f"sgu_st{i}", [128, 8, G], F32) for i in range(NB)]
    stt_b = [Buf(f"sgu_st{i}") for i in range(NB)]
    ob = [sb(f"sgu_ob{i}", [128, G, 256], F32) for i in range(NB)]
    ob_b = [Buf(f"sgu_ob{i}") for i in range(NB)]
    ssqB, ssqB_b = out["ssqB"], out["ssqB_b"]
    pidx = 0
    for tg in range(32 // G):
        i = tg % NB
        rows = slice(tg * G * 128, (tg + 1) * G * 128)
        zrows = slice(1 + tg * G * 128, 1 + (tg + 1) * G * 128)
        S.op("sp", lambda e, i=i, zrows=zrows: e.dma_start(
            out=g_t[i][:], in_=scr["ztm"][zrows, 640:1152].rearrange("(tb j) c -> j tb c", j=128)),
            reads=[scr["ztm_b"]], writes=[g_b[i]], dma=True)
        S.op("sp", lambda e, i=i, zrows=zrows: e.dma_start(
            out=u_t[i][:], in_=scr["ztm"][zrows, 384:640].rearrange("(tb j) c -> j tb c", j=128)),
            reads=[scr["ztm_b"]], writes=[u_b[i]], dma=True)
        S.op("act", lambda e, i=i: e.activation(out=g_t[i][:], in_=g_t[i][:], func=AF.Gelu_apprx_tanh),
             reads=[g_b[i]], writes=[g_b[i]])
        S.op("act", lambda e, i=i: e.activation(out=u_t[i][:], in_=u_t[i][:], func=AF.Gelu_apprx_tanh),
             reads=[u_b[i]], writes=[u_b[i]])
        st = stt[i]
        S.op("dve", lambda e, i=i, st=st: e.tensor_reduce(out=st[:, 0, :], in_=g_t[i][:], axis=AX.X, op=ALU.add),
             reads=[g_b[i]], writes=[stt_b[i]])
        S.op("dve", lambda e, i=i: e.tensor_tensor(out=sq[:], in0=g_t[i][:], in1=g_t[i][:], op=ALU.mult),
             reads=[g_b[i]], writes=[sq_b])
        S.op("dve", lambda e, st=st: e.tensor_reduce(out=st[:, 1, :], in_=sq[:], axis=AX.X, op=ALU.add),
             reads=[sq_b], writes=[stt_b[i]])
        S.op("dve", lambda e, st=st: e.tensor_scalar(out=st[:, 2, :], in0=st[:, 0, :], scalar1=1.0 / 512, scalar2=None, op0=ALU.mult),
             reads=[stt_b[i]], writes=[stt_b[i]])
        S.op("dve", lambda e, st=st: e.tensor_tensor(out=st[:, 3, :], in0=st[:, 2, :], in1=st[:, 2, :], op=ALU.mult),
             reads=[stt_b[i]], writes=[stt_b[i]])
        S.op("dve", lambda e, st=st: e.scalar_tensor_tensor(out=st[:, 4, :], in0=st[:, 1, :], scalar=1.0 / 512, in1=st[:, 3, :],
                                                            op0=ALU.mult, op1=ALU.subtract),
             reads=[stt_b[i]], writes=[stt_b[i]])
        S.op("act", lambda e, st=st: e.activation(out=st[:, 5, :], in_=st[:, 4, :], func=AF.Sqrt, bias=EPS_M, scale=1.0),
             reads=[stt_b[i]], writes=[stt_b[i]])
        S.op("dve", lambda e, st=st: e.reciprocal(out=st[:, 6, :], in_=st[:, 5, :]),
             reads=[stt_b[i]], writes=[stt_b[i]])
        S.op("dve", lambda e, i=i, st=st: e.tensor_tensor(
            out=g_t[i][:, :, 0:256], in0=g_t[i][:, :, 0:256], in1=st[:, 2, :].unsqueeze(2).to_broadcast([128, G, 256]), op=ALU.subtract),
            reads=[g_b[i], stt_b[i]], writes=[g_b[i]])
        S.op("dve", lambda e, i=i, st=st: e.tensor_tensor(
            out=g_t[i][:, :, 0:256], in0=g_t[i][:, :, 0:256], in1=st[:, 6, :].unsqueeze(2).to_broadcast([128, G, 256]), op=ALU.mult),
            reads=[g_b[i], stt_b[i]], writes=[g_b[i]])
        S.op("dve", lambda e, i=i: e.tensor_tensor(
            out=gn[i][:], in0=g_t[i][:, :, 0:256], in1=gain[:, 0:256].unsqueeze(1).to_broadcast([128, G, 256]), op=ALU.mult),
            reads=[g_b[i], gain_b], writes=[gn_b[i]])
        for gi in range(4):
            pp = ps[pidx % 4]
            pidx += 1
            S.op("pe", lambda e, pp=pp, gi=gi, i=i: e.matmul(
                pp.t[:, 0:G * 64], lhsT=wT[:, gi, :], rhs=gn[i][:, :, gi * 64:(gi + 1) * 64], start=True, stop=True),
                reads=[wT_b, gn_b[i]], writes=[pp.b])
            S.op("dve", lambda e, pp=pp, gi=gi, i=i: e.scalar_tensor_tensor(
                out=ob[i][:, :, gi * 64:(gi + 1) * 64], in0=pp.t[:, 0:G * 64].rearrange("p (a c) -> p a c", a=G),
                scalar=bs[:, gi:gi + 1], in1=u_t[i][:, :, gi * 64:(gi + 1) * 64], op0=ALU.add, op1=ALU.mult),
                reads=[pp.b, bs_b, u_b[i]], writes=[ob_b[i]])
        S.op("sp", lambda e, i=i, rows=rows: e.dma_start(
            out=out["o"][rows, 384:640].rearrange("(tb j) c -> j tb c", j=128), in_=ob[i][:]),
            reads=[ob_b[i]], writes=[out["o_b"]], dma=True)
        S.op("dve", lambda e, i=i: e.tensor_tensor(out=sq[:, :, 0:256], in0=ob[i][:], in1=ob[i][:], op=ALU.mult),
             reads=[ob_b[i]], writes=[sq_b])
        S.op("dve", lambda e, tg=tg: e.tensor_reduce(out=ssqB[:, tg * G:(tg + 1) * G], in_=sq[:, :, 0:256], axis=AX.X, op=ALU.add),
             reads=[sq_b], writes=[ssqB_b])
    S.barrier()
    A.release(m0)


CDEC = 0.6065306597126334
GN_EPS = 64e-5
INV_DT = BF16
SEG = 1024
NCH = SEG // 128


P4_STOP = 99


class _StopPhase(Exception):
    pass


def phase4(S, nc, io, scr, psbig, out):
    try:
        _phase4(S, nc, io, scr, psbig, out)
    except _StopPhase:
        pass
    S.barrier()


def _phase4(S, nc, io, scr, psbig, out):
    A = S.arena
    m0 = A.mark()
    sb = A.alloc
    ident, identf, ident_b = S.ident, S.identf, S.ident_b
    zTc, ztm = scr["zTc"], scr["ztm"]

    bankbufs = {}

    def bankbuf(key):
        if key not in bankbufs:
            bankbufs[key] = Buf(f"psbank{key}", psum=True)
        return bankbufs[key]

    class R:
        def __init__(self, lo, hi, name, key):
            self.ap = psbig[:, lo:hi]
            self.b = bankbuf(key)
    class R2:
        def __init__(self, b0, lo, hi, name):
            self.b = bankbuf(b0)
            self.ap3 = psbig[:, b0 * 512:(b0 + 2) * 512].rearrange("p (h q) -> p h q", h=2)[:, :, lo:hi]
            self.h = [psbig[:, (b0 + hh) * 512 + lo:(b0 + hh) * 512 + hi] for hh in range(2)]
    psA = R2(0, 0, 512, "psA")
    psN_s = [R2(2, 0, 128, "psN0"), R2(4, 0, 128, "psN1")]
    psD_s = [R2(2, 128, 256, "psD0"), R2(4, 128, 256, "psD1")]
    psR_s = [R2(2, 256, 512, "psR0"), R2(4, 256, 512, "psR1")]
    psDC_s = [R2(2, 128, 384, "psDC0"), R2(4, 128, 384, "psDC1")]
    psRHS = R2(0, 0, 64, "psRHS")
    psSA = R2(0, 64, 128, "psSA")
    psY = R2(0, 128, 192, "psY")
    psDS = R2(0, 192, 256, "psDS")
    psP = [R(3072, 4096, "psP0", 6), R(3072 + 256, 3072 + 512, "psP1", 6)]

    cp = sb("cp", [128, 25], F32); cp_b = Buf("cp")
    S.op("sp", lambda e: e.dma_start(out=cp[:], in_=io["cp"]), writes=[cp_b], dma=True)
    cf = sb("cf", [128, 3, 384], F32); cf_b = Buf("cf")
    S.op("sp", lambda e: e.dma_start(out=cf[:], in_=io["cf"].partition_broadcast(128)), writes=[cf_b], dma=True)
    omka = sb("omka", [128, 3], F32); omka_b = Buf("omka")
    S.op("dve", lambda e: e.tensor_scalar(out=omka[:], in0=cp[:, 15:18], scalar1=-1.0, scalar2=1.0, op0=ALU.mult, op1=ALU.add),
         reads=[cp_b], writes=[omka_b])
    w_up = sb("w_up", [64, 384], BF16); a_up = sb("a_up", [64, 384], BF16); g_up = sb("g_up", [128, 2, 384], BF16)
    wts_b = Buf("rwkv_wts")
    S.op("pool", lambda e: e.dma_start(out=w_up[:], in_=io["w_up"]), writes=[wts_b], dma=True)
    S.op("pool", lambda e: e.dma_start(out=a_up[:], in_=io["a_up"]), writes=[wts_b], dma=True)
    S.op("pool", lambda e: e.dma_start(out=g_up[:], in_=io["g_up"].rearrange("(kc p) n -> p kc n", p=128)), writes=[wts_b], dma=True)
    bones = sb("bones", [128, 128], F32); bind = sb("bind", [128, 2], F32); cst_b = Buf("rwkv_cst")
    S.op("pool", lambda e: e.memset(bones[:], 0.0), writes=[cst_b])
    S.op("pool", lambda e: e.memset(bones[0:64, 0:64], 1.0), writes=[cst_b])
    S.op("pool", lambda e: e.memset(bones[64:128, 64:128], 1.0), writes=[cst_b])
    S.op("pool", lambda e: e.memset(bind[:], 0.0), writes=[cst_b])
    S.op("pool", lambda e: e.memset(bind[0:64, 0:1], 1.0), writes=[cst_b])
    S.op("pool", lambda e: e.memset(bind[64:128, 1:2], 1.0), writes=[cst_b])
    rmask = sb("rmask", [128, SEG], F32)
    S.op("pool", lambda e: e.memset(rmask[:], 1.0), writes=[cst_b])
    S.op("pool", lambda e: e.memset(rmask[:].rearrange("p (c j) -> p c j", j=128)[:, :, 0:1], 0.0), writes=[cst_b])
    mU = sb("mU", [128, 4, 128], F32)
    mL = sb("mL", [128, 128], F32)
    S.op("pool", lambda e: e.memset(mU[:], 1.0), writes=[cst_b])
    S.op("pool", lambda e: e.memset(mL[:], 1.0), writes=[cst_b])
    for q in range(4):
        base = -1 if q % 2 == 0 else 0
        S.op("pool", lambda e, q=q, base=base: e.affine_select(out=mU[:, q, :], in_=mU[:, q, :], pattern=[[1, 128]], base=base,
                                                               channel_multiplier=-1, compare_op=ALU.is_ge, fill=0.0),
             reads=[cst_b], writes=[cst_b])
    S.op("pool", lambda e: e.affine_select(out=mL[:], in_=mL[:], pattern=[[-1, 128]], base=-1,
                                           channel_multiplier=1, compare_op=ALU.is_ge, fill=0.0),
         reads=[cst_b], writes=[cst_b])
    identI = sb("identI", [128, 128], INV_DT)
    S.op("pool", lambda e: e.tensor_copy(out=identI[:], in_=identf[:]), reads=[ident_b], writes=[cst_b])

    TW = sb("TW", [64, T], BF16); AL = sb("AL", [64, T], BF16); SGG = sb("SGG", [128, 2, T], BF16)
    lora_b = Buf("lora")
    m1 = A.mark()
    la = [sb(f"lo_a{i}", [128, SEG], F32) for i in range(2)]
    lb = [sb(f"lo_b{i}", [128, SEG], F32) for i in range(2)]
    la_b = [Buf(f"lo_a{i}") for i in range(2)]
    lb_b = [Buf(f"lo_b{i}") for i in range(2)]
    k = 0
    for (row0, np_, mucol, kind) in ((768, 64, 21, "w"), (832, 64, 22, "a"), (896, 128, 23, "g0"), (1024, 128, 24, "g1")):
        for sg in range(T // SEG):
            i = k % 2
            k += 1
            t0 = sg * SEG
            S.op("sp", lambda e, i=i, row0=row0, np_=np_, t0=t0: e.dma_start(out=la[i][0:np_, :], in_=zTc[row0:row0 + np_, t0 + 1:t0 + 1 + SEG]),
                 reads=[scr["zTc_b"]], writes=[la_b[i]], dma=True)
            S.op("sp", lambda e, i=i, row0=row0, np_=np_, t0=t0: e.dma_start(out=lb[i][0:np_, :], in_=zTc[row0:row0 + np_, t0:t0 + SEG]),
                 reads=[scr["zTc_b"]], writes=[lb_b[i]], dma=True)
            S.op("dve", lambda e, i=i, np_=np_: e.tensor_tensor(out=lb[i][0:np_, :], in0=lb[i][0:np_, :], in1=la[i][0:np_, :], op=ALU.subtract),
                 reads=[la_b[i], lb_b[i]], writes=[lb_b[i]])
            S.op("dve", lambda e, i=i, np_=np_, mucol=mucol: e.scalar_tensor_tensor(
                out=la[i][0:np_, :], in0=lb[i][0:np_, :], scalar=cp[0:np_, mucol:mucol + 1], in1=la[i][0:np_, :], op0=ALU.mult, op1=ALU.add),
                reads=[la_b[i], lb_b[i], cp_b], writes=[la_b[i]])
            if kind == "w":
                S.op("act", lambda e, i=i, t0=t0: e.activation(out=TW[:, t0:t0 + SEG], in_=la[i][0:64, :], func=AF.Tanh),
                     reads=[la_b[i]], writes=[lora_b])
            elif kind == "a":
                S.op("act", lambda e, i=i, t0=t0: e.copy(out=AL[:, t0:t0 + SEG], in_=la[i][0:64, :]),
                     reads=[la_b[i]], writes=[lora_b])
            else:
                kc = 0 if kind == "g0" else 1
                S.op("act", lambda e, i=i, t0=t0, kc=kc: e.activation(out=SGG[:, kc, t0:t0 + SEG], in_=la[i][:, :], func=AF.Sigmoid),
                     reads=[la_b[i]], writes=[lora_b])
    S.barrier()
    A.release(m1)
    if P4_STOP <= 1:
        raise _StopPhase()

    def t32(name, n=1):
        return [sb(f"{name}{i}", [128, SEG], F32) for i in range(n)], [Buf(f"{name}{i}") for i in range(n)]
    Rr, Rr_b = t32("Rr"); Rp, Rp_b = t32("Rp"); Kk, Kk_b = t32("Kk"); Kp, Kp_b = t32("Kp")
    SGt, SG_b = t32("SGt"); CUM, CUM_b = t32("CUM"); WINC, WINC_b = t32("WINC", 2); WINV, WINV_b = t32("WINV"); WEXC, WEXC_b = t32("WEXC")
    AAt, AA_b = t32("AAt"); KKt, KK_b = t32("KKt"); TMPt, TMP_b = t32("TMPt"); K2t, K2_b = t32("K2t"); RKt, RK_b = t32("RKt")
    QA = [sb(f"QA{i}", [128, NCH, 2, 128], BF16) for i in range(2)]; QA_b = [Buf(f"QA{i}") for i in range(2)]
    KB = [sb(f"KB{i}", [128, NCH, 2, 128], BF16) for i in range(2)]; KB_b = [Buf(f"KB{i}") for i in range(2)]
    Btok = [sb(f"Btok{i}", [128, NCH, 128], BF16) for i in range(2)]; Btok_b = [Buf(f"Btok{i}") for i in range(2)]
    Ktok = [sb(f"Ktok{i}", [128, NCH, 128], BF16) for i in range(2)]; Ktok_b = [Buf(f"Ktok{i}") for i in range(2)]
    V32 = [sb(f"V32_{i}", [128, NCH, 128], F32) for i in range(2)]; V32_b = [Buf(f"V32_{i}") for i in range(2)]
    Vp = sb("Vp", [128, NCH, 128], F32); Vp_b = Buf("Vp")
    V16 = [sb(f"V16_{i}", [128, NCH, 128], BF16) for i in range(2)]; V16_b = [Buf(f"V16_{i}") for i in range(2)]
    Gt = [sb(f"Gt{i}", [128, NCH, 128], F32) for i in range(2)]; Gt_b = [Buf(f"Gt{i}") for i in range(2)]
    BS = [sb(f"BS{i}", [128, NCH, 2], F32) for i in range(2)]; BS_b = [Buf(f"BS{i}") for i in range(2)]
    Yall = [sb(f"Yall{i}", [128, NCH, 128], F32) for i in range(2)]; Yall_b = [Buf(f"Yall{i}") for i in range(2)]
    Ysq = sb("Ysq", [128, NCH, 128], F32); Ysq_b = Buf("Ysq")
    gst = sb("gst", [128, 8, NCH * 2], F32); gst_b = Buf("gst")
    DCU = [[sb(f"DCU{s}_{i}", [128, 2, 384], INV_DT) for i in range(2)] for s in range(2)]
    CU = [[DCU[s][i][:, :, 128:384] for i in range(2)] for s in range(2)]
    CU_b = [[Buf(f"CU{s}_{i}") for i in range(2)] for s in range(2)]
    Dm = [[DCU[s][i][:, :, 0:128] for i in range(2)] for s in range(2)]
    Dm_b = [[Buf(f"Dm{s}_{i}") for i in range(2)] for s in range(2)]
    SC = [sb(f"SC{s}", [128, 2, 384], BF16) for s in range(2)]; SC_b = [Buf(f"SC{s}") for s in range(2)]
    Ufin = [sb(f"Ufin{s}", [128, 2, 128], BF16) for s in range(2)]; Ufin_b = [Buf(f"Ufin{s}") for s in range(2)]
    RH = [sb(f"RH{s}", [128, 2, 64], BF16) for s in range(2)]; RH_b = [Buf(f"RH{s}") for s in range(2)]
    SA = [sb(f"SA{s}", [128, 2, 64], BF16) for s in range(2)]; SA_b = [Buf(f"SA{s}") for s in range(2)]
    S32 = sb("S32", [128, 64], F32); S16 = sb("S16", [128, 64], BF16); ST = sb("STtmp", [128, 64], F32)
    S32_b = Buf("S32"); S16_b = Buf("S16"); ST_b = Buf("ST")

    units = [(pr, sg) for pr in range(3) for sg in range(T // SEG)]
    pending = []
    real_op = S.op

    def rec_op(*a_, **k_):
        pending.append((a_, k_))

    def drain(n):
        for _ in range(min(n, len(pending))):
            a_, k_ = pending.pop(0)
            real_op(*a_, **k_)

    def prep(pr, sg, bi):
        t0 = sg * SEG
        r0 = pr * 128
        S.op("sp", lambda e, r0=r0, t0=t0: e.dma_start(out=Rr[0][:], in_=zTc[r0:r0 + 128, t0 + 1:t0 + 1 + SEG]), reads=[scr["zTc_b"]], writes=[Rr_b[0]], dma=True)
        S.op("sp", lambda e, r0=r0, t0=t0: e.dma_start(out=Rp[0][:], in_=zTc[r0:r0 + 128, t0:t0 + SEG]), reads=[scr["zTc_b"]], writes=[Rp_b[0]], dma=True)
        S.op("sp", lambda e, r0=r0, t0=t0: e.dma_start(out=Kk[0][:], in_=zTc[384 + r0:384 + r0 + 128, t0 + 1:t0 + 1 + SEG]), reads=[scr["zTc_b"]], writes=[Kk_b[0]], dma=True)
        S.op("sp", lambda e, r0=r0, t0=t0: e.dma_start(out=Kp[0][:], in_=zTc[384 + r0:384 + r0 + 128, t0:t0 + SEG]), reads=[scr["zTc_b"]], writes=[Kp_b[0]], dma=True)
        vrows = slice(1 + t0, 1 + t0 + SEG)
        vrows_p = slice(t0, t0 + SEG)
        vc = slice(1152 + r0, 1152 + r0 + 128)
        S.op("sp", lambda e, bi=bi, vrows=vrows, vc=vc: e.dma_start(out=V32[bi][:], in_=ztm[vrows, vc].rearrange("(tb j) c -> j tb c", j=128)),
             reads=[scr["ztm_b"]], writes=[V32_b[bi]], dma=True)
        S.op("sp", lambda e, vrows_p=vrows_p, vc=vc: e.dma_start(out=Vp[:], in_=ztm[vrows_p, vc].rearrange("(tb j) c -> j tb c", j=128)),
             reads=[scr["ztm_b"]], writes=[Vp_b], dma=True)
        S.op("dve", lambda e: e.tensor_tensor(out=Rp[0][:], in0=Rp[0][:], in1=Rr[0][:], op=ALU.subtract), reads=[Rr_b[0], Rp_b[0]], writes=[Rp_b[0]])
        S.op("dve", lambda e, pr=pr: e.scalar_tensor_tensor(out=Rr[0][:], in0=Rp[0][:], scalar=cp[:, 0 + pr:1 + pr], in1=Rr[0][:], op0=ALU.mult, op1=ALU.add),
             reads=[Rr_b[0], Rp_b[0], cp_b], writes=[Rr_b[0]])
        S.op("dve", lambda e: e.tensor_tensor(out=Kp[0][:], in0=Kp[0][:], in1=Kk[0][:], op=ALU.subtract), reads=[Kk_b[0], Kp_b[0]], writes=[Kp_b[0]])
        S.op("dve", lambda e, pr=pr: e.scalar_tensor_tensor(out=Kk[0][:], in0=Kp[0][:], scalar=cp[:, 3 + pr:4 + pr], in1=Kk[0][:], op0=ALU.mult, op1=ALU.add),
             reads=[Kk_b[0], Kp_b[0], cp_b], writes=[Kk_b[0]])
        S.op("pool", lambda e, bi=bi: e.tensor_tensor(out=Vp[:], in0=Vp[:], in1=V32[bi][:], op=ALU.subtract), reads=[Vp_b, V32_b[bi]], writes=[Vp_b])
        S.op("pool", lambda e, pr=pr: e.tensor_tensor(out=Vp[:], in0=Vp[:], in1=cf[:, 0, pr * 128:(pr + 1) * 128].unsqueeze(1).to_broadcast([128, NCH, 128]), op=ALU.mult),
             reads=[Vp_b, cf_b], writes=[Vp_b])
        S.op("pool", lambda e, bi=bi: e.tensor_tensor(out=V32[bi][:], in0=V32[bi][:], in1=Vp[:], op=ALU.add), reads=[Vp_b, V32_b[bi]], writes=[V32_b[bi]])
        S.op("act", lambda e, bi=bi: e.copy(out=V16[bi][:], in_=V32[bi][:]), reads=[V32_b[bi]], writes=[V16_b[bi]])
        for hf in range(2):
            S.op("pe", lambda e, hf=hf, pr=pr, t0=t0: e.matmul(psP[0].ap[:, hf * 512:(hf + 1) * 512], lhsT=w_up[:, pr * 128:(pr + 1) * 128],
                                                           rhs=TW[:, t0 + hf * 512:t0 + (hf + 1) * 512], start=True, stop=True),
                 reads=[wts_b, lora_b], writes=[psP[0].b])
        S.op("act", lambda e, pr=pr: e.activation(out=SGt[0][:], in_=psP[0].ap[:, :], func=AF.Sigmoid, bias=cp[:, 6 + pr:7 + pr], scale=1.0),
             reads=[psP[0].b, cp_b], writes=[SG_b[0]])
        S.op("dve", lambda e: e.tensor_tensor_scan(out=CUM[0][:], data0=rmask[:], data1=SGt[0][:], initial=0.0, op0=ALU.mult, op1=ALU.add),
             reads=[SG_b[0], cst_b], writes=[CUM_b[0]])
        wi = WINC[bi]
        S.op("act", lambda e, wi=wi: e.activation(out=wi[:], in_=CUM[0][:], func=AF.Exp, scale=-CDEC), reads=[CUM_b[0]], writes=[WINC_b[bi]])
        S.op("act", lambda e: e.activation(out=WINV[0][:], in_=CUM[0][:], func=AF.Exp, scale=CDEC), reads=[CUM_b[0]], writes=[WINV_b[0]])
        S.op("dve", lambda e: e.tensor_tensor(out=SGt[0][:], in0=CUM[0][:], in1=SGt[0][:], op=ALU.subtract), reads=[CUM_b[0], SG_b[0]], writes=[SG_b[0]])
        S.op("act", lambda e: e.activation(out=WEXC[0][:], in_=SGt[0][:], func=AF.Exp, scale=-CDEC), reads=[SG_b[0]], writes=[WEXC_b[0]])
        for hf in range(2):
            S.op("pe", lambda e, hf=hf, pr=pr, t0=t0: e.matmul(psP[0].ap[:, hf * 512:(hf + 1) * 512], lhsT=a_up[:, pr * 128:(pr + 1) * 128],
                                                           rhs=AL[:, t0 + hf * 512:t0 + (hf + 1) * 512], start=True, stop=True),
                 reads=[wts_b, lora_b], writes=[psP[0].b])
        S.op("act", lambda e, pr=pr: e.activation(out=AAt[0][:], in_=psP[0].ap[:, :], func=AF.Sigmoid, bias=cp[:, 9 + pr:10 + pr], scale=1.0),
             reads=[psP[0].b, cp_b], writes=[AA_b[0]])
        S.op("dve", lambda e, pr=pr: e.tensor_scalar(out=KKt[0][:], in0=Kk[0][:], scalar1=cp[:, 12 + pr:13 + pr], scalar2=None, op0=ALU.mult),
             reads=[Kk_b[0], cp_b], writes=[KK_b[0]])
        S.op("pool", lambda e: e.tensor_tensor(out=TMPt[0][:], in0=KKt[0][:], in1=KKt[0][:], op=ALU.mult), reads=[KK_b[0]], writes=[TMP_b[0]])
        for hf in range(2):
            S.op("pe", lambda e, hf=hf: e.matmul(psP[0].ap[:, hf * 512:(hf + 1) * 512], lhsT=bones[:], rhs=TMPt[0][:, hf * 512:(hf + 1) * 512], start=True, stop=True),
                 reads=[cst_b, TMP_b[0]], writes=[psP[0].b])
        S.op("act", lambda e: e.activation(out=TMPt[0][:], in_=psP[0].ap[:, :], func=AF.Sqrt), reads=[psP[0].b], writes=[TMP_b[0]])
        S.op("dve", lambda e: e.tensor_scalar(out=TMPt[0][:], in0=TMPt[0][:], scalar1=1e-12, scalar2=None, op0=ALU.max), reads=[TMP_b[0]], writes=[TMP_b[0]])
        S.op("dve", lambda e: e.reciprocal(out=TMPt[0][:], in_=TMPt[0][:]), reads=[TMP_b[0]], writes=[TMP_b[0]])
        S.op("dve", lambda e: e.tensor_tensor(out=KKt[0][:], in0=KKt[0][:], in1=TMPt[0][:], op=ALU.mult), reads=[KK_b[0], TMP_b[0]], writes=[KK_b[0]])
        S.op("dve", lambda e, pr=pr: e.tensor_scalar(out=TMPt[0][:], in0=AAt[0][:], scalar1=cp[:, 15 + pr:16 + pr], scalar2=omka[:, pr:pr + 1], op0=ALU.mult, op1=ALU.add),
             reads=[AA_b[0], cp_b, omka_b, TMP_b[0]], writes=[TMP_b[0]])
        S.op("dve", lambda e: e.tensor_tensor(out=K2t[0][:], in0=Kk[0][:], in1=TMPt[0][:], op=ALU.mult), reads=[Kk_b[0], TMP_b[0]], writes=[K2_b[0]])
        qa, kb = QA[bi], KB[bi]
        S.op("dve", lambda e, qa=qa: e.scalar_tensor_tensor(out=qa[:, :, 0, :], in0=KKt[0][:].rearrange("p (c j) -> p c j", j=128), scalar=-1.0,
                                                             in1=WEXC[0][:].rearrange("p (c j) -> p c j", j=128), op0=ALU.mult, op1=ALU.mult),
             reads=[KK_b[0], WEXC_b[0]], writes=[QA_b[bi]])
        S.op("pool", lambda e, qa=qa, wi=wi: e.tensor_tensor(out=qa[:, :, 1, :], in0=Rr[0][:].rearrange("p (c j) -> p c j", j=128),
                                                             in1=wi[:].rearrange("p (c j) -> p c j", j=128), op=ALU.mult),
             reads=[Rr_b[0], WINC_b[bi]], writes=[QA_b[bi]])
        S.op("dve", lambda e: e.tensor_tensor(out=TMPt[0][:], in0=KKt[0][:], in1=AAt[0][:], op=ALU.mult), reads=[KK_b[0], AA_b[0], TMP_b[0]], writes=[TMP_b[0]])
        S.op("dve", lambda e, kb=kb: e.tensor_tensor(out=kb[:, :, 0, :], in0=TMPt[0][:].rearrange("p (c j) -> p c j", j=128),
                                                     in1=WINV[0][:].rearrange("p (c j) -> p c j", j=128), op=ALU.mult),
             reads=[TMP_b[0], WINV_b[0]], writes=[KB_b[bi]])
        S.op("pool", lambda e, kb=kb: e.tensor_tensor(out=kb[:, :, 1, :], in0=K2t[0][:].rearrange("p (c j) -> p c j", j=128),
                                                      in1=WINV[0][:].rearrange("p (c j) -> p c j", j=128), op=ALU.mult),
             reads=[K2_b[0], WINV_b[0]], writes=[KB_b[bi]])
        S.op("dve", lambda e, pr=pr: e.scalar_tensor_tensor(out=RKt[0][:], in0=Rr[0][:], scalar=cp[:, 18 + pr:19 + pr], in1=K2t[0][:], op0=ALU.mult, op1=ALU.mult),
             reads=[Rr_b[0], K2_b[0], cp_b], writes=[RK_b[0]])
        for c in range(NCH):
            S.op("pe", lambda e, c=c: e.matmul(psP[1].ap[:, c * 2:c * 2 + 2], lhsT=RKt[0][:, c * 128:(c + 1) * 128], rhs=bind[:], start=True, stop=True),
                 reads=[RK_b[0], cst_b], writes=[psP[1].b])
        S.op("dve", lambda e, bi=bi: e.tensor_copy(out=BS[bi][:].rearrange("p c h -> p (c h)"), in_=psP[1].ap[:, 0:NCH * 2]), reads=[psP[1].b], writes=[BS_b[bi]])
        for c in range(NCH):
            for kc in range(2):
                S.op("pe", lambda e, c=c, kc=kc, pr=pr, t0=t0: e.matmul(psP[0].ap[:, c * 128:(c + 1) * 128], lhsT=SGG[:, kc, t0 + c * 128:t0 + (c + 1) * 128],
                                                                     rhs=g_up[:, kc, pr * 128:(pr + 1) * 128], start=(kc == 0), stop=(kc == 1)),
                     reads=[lora_b, wts_b], writes=[psP[0].b])
        S.op("act", lambda e, bi=bi: e.copy(out=Gt[bi][:].rearrange("p c n -> p (c n)"), in_=psP[0].ap[:, :]), reads=[psP[0].b], writes=[Gt_b[bi]])
        psTb = psP[0].ap.bitcast(BF16)
        for c in range(NCH):
            S.op("pe", lambda e, c=c, kb=kb: e.transpose(out=psTb[:, c * 128:(c + 1) * 128], in_=kb[:, c, 0, :], identity=ident[:]),
                 reads=[KB_b[bi], ident_b], writes=[psP[0].b])
            S.op("pe", lambda e, c=c, kb=kb: e.transpose(out=psTb[:, 1024 + c * 128:1024 + (c + 1) * 128], in_=kb[:, c, 1, :], identity=ident[:]),
                 reads=[KB_b[bi], ident_b], writes=[psP[0].b])
        S.op("dve", lambda e, bi=bi: e.tensor_copy(out=Btok[bi][:].rearrange("p c n -> p (c n)"), in_=psTb[:, 0:1024]), reads=[psP[0].b], writes=[Btok_b[bi]])
        S.op("act", lambda e, bi=bi: e.copy(out=Ktok[bi][:].rearrange("p c n -> p (c n)"), in_=psTb[:, 1024:2048]), reads=[psP[0].b], writes=[Ktok_b[bi]])


    def chunks_and_final(pr, sg, bi, defer_final):
        t0 = sg * SEG
        r0 = pr * 128
        wi = WINC[bi]
        qa, kb = QA[bi], KB[bi]
        if sg == 0:
            S.op("dve", lambda e: e.memset(S32[:], 0.0), writes=[S32_b])
            S.op("dve", lambda e: e.memset(S16[:], 0.0), writes=[S16_b])
        def unit_scores(c, s_):
            psN, psD, psR = psN_s[s_], psD_s[s_], psR_s[s_]
            for hh in range(2):
                p0 = 64 * hh
                rhsQA = qa[p0:p0 + 64, c, :, :].rearrange("p a j -> p (a j)")
                S.op("pe", lambda e, hh=hh, p0=p0, rhsQA=rhsQA, kb=kb, c=c: e.matmul(psA.h[hh][:, 0:256], lhsT=kb[p0:p0 + 64, c, 0, :], rhs=rhsQA, start=True, stop=True),
                     reads=[KB_b[bi], QA_b[bi]], writes=[psA.b])
                S.op("pe", lambda e, hh=hh, p0=p0, rhsQA=rhsQA, kb=kb, c=c: e.matmul(psA.h[hh][:, 256:512], lhsT=kb[p0:p0 + 64, c, 1, :], rhs=rhsQA, start=True, stop=True),
                     reads=[KB_b[bi], QA_b[bi]], writes=[psA.b])
                S.op("pe", lambda e, hh=hh, p0=p0, qa=qa, kb=kb, c=c: e.matmul(psN.h[hh][:, :], lhsT=qa[p0:p0 + 64, c, 0, :], rhs=kb[p0:p0 + 64, c, 0, :], start=True, stop=True),
                     reads=[KB_b[bi], QA_b[bi]], writes=[psN.b])
            psA3 = psA.ap3
            cu0 = CU[s_][0]
            S.op("dve", lambda e, cu0=cu0, psA3=psA3: e.tensor_tensor(out=cu0[:, :, 0:128], in0=psA3[:, :, 0:128], in1=mU[:, 0:1, :].to_broadcast([128, 2, 128]), op=ALU.mult),
                 reads=[psA.b, cst_b], writes=[CU_b[s_][0]])
            S.op("pool", lambda e, cu0=cu0: e.tensor_copy(out=cu0[:, :, 128:256], in_=identI[:].unsqueeze(1).to_broadcast([128, 2, 128])),
                 reads=[cst_b], writes=[CU_b[s_][0]])
            S.op("dve", lambda e, s_=s_, psA3=psA3: e.tensor_tensor(out=SC[s_][:], in0=psA3[:, :, 128:512],
                                                                   in1=mU[:, 1:4, :].rearrange("p a j -> p (a j)").unsqueeze(1).to_broadcast([128, 2, 384]), op=ALU.mult),
                 reads=[psA.b, cst_b], writes=[SC_b[s_]])
            S.op("dve", lambda e, s_=s_: e.tensor_tensor(out=Dm[s_][0], in0=psN.ap3,
                                                         in1=mL[:].unsqueeze(1).to_broadcast([128, 2, 128]), op=ALU.mult),
                 reads=[psN.b, cst_b], writes=[Dm_b[s_][0]])
        def unit_round(c, s_, rd):
            psN, psD, psR = psN_s[s_], psD_s[s_], psR_s[s_]
            cur, nxt = rd % 2, (rd + 1) % 2
            cuc, cun = CU[s_][cur], CU[s_][nxt]
            dc, dn = Dm[s_][cur], Dm[s_][nxt]
            last = (rd == 6)
            psR3 = psR.ap3
            for hh in range(2):
                if not last:
                    S.op("pe", lambda e, hh=hh, dc=dc, cuc=cuc: e.matmul(psR.h[hh][:, :], lhsT=dc[:, hh, :], rhs=cuc[:, hh, :], start=True, stop=True),
                         reads=[Dm_b[s_][cur], CU_b[s_][cur]], writes=[psR.b])
                    if rd < 5 or True:
                        S.op("pe", lambda e, hh=hh, dc=dc, cuc=cuc: e.matmul(psD.h[hh][:, :], lhsT=cuc[:, hh, 0:128], rhs=dc[:, hh, :], start=True, stop=True),
                             reads=[Dm_b[s_][cur], CU_b[s_][cur]], writes=[psD.b])
                else:
                    S.op("pe", lambda e, hh=hh, dc=dc, cuc=cuc: e.matmul(psR.h[hh][:, 128:256], lhsT=dc[:, hh, :], rhs=cuc[:, hh, 128:256], start=True, stop=True),
                         reads=[Dm_b[s_][cur], CU_b[s_][cur]], writes=[psR.b])
            if not last:
                S.op("act", lambda e, s_=s_, nxt=nxt: e.copy(out=DCU[s_][nxt][:, :, 0:256], in_=psDC_s[s_].ap3), reads=[psR.b], writes=[CU_b[s_][nxt], Dm_b[s_][nxt]])
                S.op("dve", lambda e, cun=cun, cuc=cuc, psR3=psR3: e.tensor_tensor(out=cun[:, :, 128:256], in0=psR3[:, :, 128:256], in1=cuc[:, :, 128:256], op=ALU.add),
                     reads=[psR.b, CU_b[s_][cur]], writes=[CU_b[s_][nxt]])
            else:
                S.op("dve", lambda e, s_=s_, cuc=cuc, psR3=psR3: e.tensor_tensor(out=Ufin[s_][:], in0=psR3[:, :, 128:256], in1=cuc[:, :, 128:256], op=ALU.add),
                     reads=[psR.b, CU_b[s_][cur]], writes=[Ufin_b[s_]])
        def unit_state(c, s_):
            for hh in range(2):
                p0 = 64 * hh
                S.op("pe", lambda e, hh=hh, p0=p0, qa=qa, c=c: e.matmul(psRHS.h[hh][:, :], lhsT=qa[p0:p0 + 64, c, 0, :], rhs=S16[p0:p0 + 64, :], start=True, stop=False),
                     reads=[QA_b[bi], S16_b], writes=[psRHS.b])
                S.op("pe", lambda e, hh=hh, p0=p0, s_=s_, c=c, bi=bi: e.matmul(psRHS.h[hh][:, :], lhsT=SC[s_][:, hh, 128:256], rhs=V16[bi][:, c, p0:p0 + 64], start=False, stop=True),
                     reads=[SC_b[s_], V16_b[bi]], writes=[psRHS.b])
            S.op("act", lambda e, s_=s_: e.copy(out=RH[s_][:], in_=psRHS.ap3), reads=[psRHS.b], writes=[RH_b[s_]])
            for hh in range(2):
                S.op("pe", lambda e, hh=hh, s_=s_: e.matmul(psSA.h[hh][:, :], lhsT=Ufin[s_][:, hh, :], rhs=RH[s_][:, hh, :], start=True, stop=True),
                     reads=[Ufin_b[s_], RH_b[s_]], writes=[psSA.b])
            S.op("dve", lambda e, s_=s_: e.tensor_copy(out=SA[s_][:], in_=psSA.ap3), reads=[psSA.b], writes=[SA_b[s_]])
            for hh in range(2):
                p0 = 64 * hh
                S.op("pe", lambda e, hh=hh, p0=p0, qa=qa, c=c: e.matmul(psY.h[hh][:, :], lhsT=qa[p0:p0 + 64, c, 1, :], rhs=S16[p0:p0 + 64, :], start=True, stop=False),
                     reads=[QA_b[bi], S16_b], writes=[psY.b])
                S.op("pe", lambda e, hh=hh, s_=s_: e.matmul(psY.h[hh][:, :], lhsT=SC[s_][:, hh, 0:128], rhs=SA[s_][:, hh, :], start=False, stop=False),
                     reads=[SC_b[s_], SA_b[s_]], writes=[psY.b])
                S.op("pe", lambda e, hh=hh, p0=p0, s_=s_, c=c, bi=bi: e.matmul(psY.h[hh][:, :], lhsT=SC[s_][:, hh, 256:384], rhs=V16[bi][:, c, p0:p0 + 64], start=False, stop=True),
                     reads=[SC_b[s_], V16_b[bi]], writes=[psY.b])
            S.op("act", lambda e, bi=bi, c=c: e.copy(out=Yall[bi][:, c, :].rearrange("p (h v) -> p h v", h=2), in_=psY.ap3), reads=[psY.b], writes=[Yall_b[bi]])
            for hh in range(2):
                p0 = 64 * hh
                S.op("pe", lambda e, hh=hh, p0=p0, s_=s_, c=c, bi=bi: e.matmul(psDS.h[hh][p0:p0 + 64, :], lhsT=Btok[bi][:, c, p0:p0 + 64], rhs=SA[s_][:, hh, :], start=True, stop=False),
                     reads=[Btok_b[bi], SA_b[s_]], writes=[psDS.b])
                S.op("pe", lambda e, hh=hh, p0=p0, c=c, bi=bi: e.matmul(psDS.h[hh][p0:p0 + 64, :], lhsT=Ktok[bi][:, c, p0:p0 + 64], rhs=V16[bi][:, c, p0:p0 + 64], start=False, stop=True),
                     reads=[Ktok_b[bi], V16_b[bi]], writes=[psDS.b])
            for hh in range(2):
                p0 = 64 * hh
                S.op("dve", lambda e, hh=hh, p0=p0: e.tensor_tensor(out=ST[p0:p0 + 64, :], in0=psDS.h[hh][p0:p0 + 64, :], in1=S32[p0:p0 + 64, :], op=ALU.add), reads=[psDS.b, S32_b], writes=[ST_b])
            wl = wi[:, c * 128 + 127:c * 128 + 128]
            S.op("dve", lambda e, wl=wl: e.tensor_scalar(out=S32[:], in0=ST[:], scalar1=wl, scalar2=None, op0=ALU.mult), reads=[ST_b, WINC_b[bi]], writes=[S32_b])
            S.op("act", lambda e, wl=wl: e.activation(out=S16[:], in_=ST[:], func=AF.Copy, scale=wl), reads=[ST_b, WINC_b[bi]], writes=[S16_b])

        for c0 in range(0, NCH, 2):
            unit_scores(c0, 0)
            unit_scores(c0 + 1, 1)
            for rd in range(7):
                unit_round(c0, 0, rd)
                unit_round(c0 + 1, 1, rd)
                drain(4)
            unit_state(c0, 0)
            unit_state(c0 + 1, 1)
            drain(4)
        drain(len(pending))

        def final():
            Y = Yall[bi]
            Y3 = Y[:].rearrange("p c (h v) -> p (c h) v", h=2)
            NG = NCH * 2
            S.op("dve", lambda e, Y3=Y3: e.tensor_reduce(out=gst[:, 0, :], in_=Y3, axis=AX.X, op=ALU.add), reads=[Yall_b[bi]], writes=[gst_b])
            S.op("pool", lambda e, Y=Y: e.tensor_tensor(out=Ysq[:], in0=Y[:], in1=Y[:], op=ALU.mult), reads=[Yall_b[bi]], writes=[Ysq_b])
            S.op("dve", lambda e: e.tensor_reduce(out=gst[:, 1, :], in_=Ysq[:].rearrange("p c (h v) -> p (c h) v", h=2), axis=AX.X, op=ALU.add), reads=[Ysq_b], writes=[gst_b])
            S.op("dve", lambda e: e.tensor_scalar(out=gst[:, 2, :], in0=gst[:, 0, :], scalar1=1.0 / 64, scalar2=None, op0=ALU.mult), reads=[gst_b], writes=[gst_b])
            S.op("dve", lambda e: e.tensor_tensor(out=gst[:, 3, :], in0=gst[:, 2, :], in1=gst[:, 2, :], op=ALU.mult), reads=[gst_b], writes=[gst_b])
            S.op("dve", lambda e: e.scalar_tensor_tensor(out=gst[:, 4, :], in0=gst[:, 1, :], scalar=1.0 / 64, in1=gst[:, 3, :], op0=ALU.mult, op1=ALU.subtract), reads=[gst_b], writes=[gst_b])
            S.op("act", lambda e: e.activation(out=gst[:, 5, :], in_=gst[:, 4, :], func=AF.Sqrt, bias=GN_EPS, scale=1.0), reads=[gst_b], writes=[gst_b])
            S.op("dve", lambda e: e.reciprocal(out=gst[:, 6, :], in_=gst[:, 5, :]), reads=[gst_b], writes=[gst_b])
            S.op("dve", lambda e, Y3=Y3: e.tensor_tensor(out=Y3, in0=Y3, in1=gst[:, 2, :].unsqueeze(2).to_broadcast([128, NG, 64]), op=ALU.subtract), reads=[Yall_b[bi], gst_b], writes=[Yall_b[bi]])
            S.op("dve", lambda e, Y3=Y3: e.tensor_tensor(out=Y3, in0=Y3, in1=gst[:, 6, :].unsqueeze(2).to_broadcast([128, NG, 64]), op=ALU.mult), reads=[Yall_b[bi], gst_b], writes=[Yall_b[bi]])
            S.op("pool", lambda e, Y=Y, pr=pr: e.tensor_tensor(out=Y[:], in0=Y[:], in1=cf[:, 1, pr * 128:(pr + 1) * 128].unsqueeze(1).to_broadcast([128, NCH, 128]), op=ALU.mult),
                 reads=[Yall_b[bi], cf_b], writes=[Yall_b[bi]])
            S.op("pool", lambda e, Y=Y, pr=pr: e.tensor_tensor(out=Y[:], in0=Y[:], in1=cf[:, 2, pr * 128:(pr + 1) * 128].unsqueeze(1).to_broadcast([128, NCH, 128]), op=ALU.add),
                 reads=[Yall_b[bi], cf_b], writes=[Yall_b[bi]])
            S.op("dve", lambda e, bi=bi: e.tensor_tensor(out=Ysq[:].rearrange("p c (h v) -> p (c h) v", h=2), in0=V32[bi][:].rearrange("p c (h v) -> p (c h) v", h=2),
                                                         in1=BS[bi][:].rearrange("p c h -> p (c h)").unsqueeze(2).to_broadcast([128, NG, 64]), op=ALU.mult),
                 reads=[V32_b[bi], BS_b[bi], Ysq_b], writes=[Ysq_b])
            S.op("dve", lambda e, Y=Y: e.tensor_tensor(out=Y[:], in0=Y[:], in1=Ysq[:], op=ALU.add), reads=[Yall_b[bi], Ysq_b], writes=[Yall_b[bi]])
            S.op("dve", lambda e, Y=Y, bi=bi: e.tensor_tensor(out=Y[:], in0=Y[:], in1=Gt[bi][:], op=ALU.mult), reads=[Yall_b[bi], Gt_b[bi]], writes=[Yall_b[bi]])
            S.op("sp", lambda e, Y=Y, t0=t0, pr=pr: e.dma_start(out=out["o"][t0:t0 + SEG, 640 + pr * 128:640 + (pr + 1) * 128].rearrange("(tb j) c -> j tb c", j=128), in_=Y[:]),
                 reads=[Yall_b[bi]], writes=[out["o_b"]], dma=True)

        if defer_final:
            S.op = rec_op
            try:
                final()
            finally:
                S.op = real_op
        else:
            final()

    prep(units[0][0], units[0][1], 0)
    for ui, (pr, sg) in enumerate(units):
        bi = ui % 2
        if ui + 1 < len(units):
            S.op = rec_op
            try:
                prep(units[ui + 1][0], units[ui + 1][1], (ui + 1) % 2)
            finally:
                S.op = real_op
        chunks_and_final(pr, sg, bi, defer_final=(ui + 1 < len(units)))
    drain(len(pending))
    S.barrier()
    A.release(m0)

import numpy as np

D = 2048
NTR = 17
TR = NTR * 128
DFF = 5632
NJ = DFF // 128
EPS = 1e-6
TGROUPS = [(0, 512), (512, 512), (1024, 512), (1536, 512), (2048, 128)]


def declare_io_R(nc, sfx, with_xh):
    def din(name, shape, dt=F32):
        return nc.dram_tensor("R_" + name + sfx, list(shape), dt, kind="ExternalInput").ap()
    io = dict(
        w_out=din("w_out", [D, D]), gvT=din("gvT", [128, 16]), gT=din("gT", [128, 2, 16]), gF=din("gF", [3, D]),
        mem=din("mem", [256, D]), msgT=din("msgT", [128, 16]),
        wq=din("wq", [D, 512]), wkv=din("wkv", [D, 1024]), wo=din("wo", [512, D]),
        w_up=din("w_up", [D, 2 * DFF]), cw=din("cw", [128, 2 * NJ, 3]), cb=din("cb", [128, 2 * NJ]), w_down=din("w_down", [DFF, D]),
    )
    if with_xh:
        io["xh"] = din("xh", [TR, D])
    return io


def emit_R(S, nc, io, psbig, cfg):
    x1s, x1s_b, x2s, x2s_b = cfg["x1s"], cfg["x1s_b"], cfg["x2s"], cfg["x2s_b"]
    acts, acts_b, y3s, y3s_b = cfg["acts"], cfg["acts_b"], cfg["y3s"], cfg["y3s_b"]
    xo, xo_b = cfg["xo"], cfg["xo_b"]
    G_o, G_o_b, G_ssq, G_ssq_b = cfg["G_o"], cfg["G_o_b"], cfg["G_ssq"], cfg["G_ssq_b"]
    flag2, flag_b = cfg["flag2"], cfg["flag_b"]
    if True:
        A = S.arena
        mR = A.mark()
        sb = A.alloc
        ps = [PS(psbig[:, i * 512:(i + 1) * 512], f"ps{i}") for i in range(8)]
        psb16 = psbig.bitcast(BF16)
        ident, ident_b, ones16 = S.ident, S.ident_b, S.ones16
        flag = flag2[:, 0:1]
        gvT = sb("gvT", [128, 16], F32); gT = sb("gT", [128, 2, 16], F32); msgT = sb("msgT", [128, 16], F32)
        par_b = Buf("par")
        S.op("sp", lambda e: e.dma_start(out=gvT[:], in_=io["gvT"]), writes=[par_b], dma=True)
        S.op("sp", lambda e: e.dma_start(out=gT[:], in_=io["gT"]), writes=[par_b], dma=True)
        S.op("sp", lambda e: e.dma_start(out=msgT[:], in_=io["msgT"]), writes=[par_b], dma=True)
        GS = sb("GS", [128, 4, 32], F32); GS_b = Buf("GS")
        S.op("sp", lambda e: e.dma_start(out=GS[:], in_=G_ssq.rearrange("k j tb -> j k tb")), reads=[G_ssq_b], writes=[GS_b], dma=True)
        gF = sb("gF", [128, D], F32); gF_b = Buf("gF")

        junk = sb("junk", [128, D], BF16); junk_b = Buf("junk")
        stt = [sb(f"stt{i}", [128, 8], F32) for i in range(2)]; stt_b = [Buf(f"stt{i}") for i in range(2)]
        xs16 = [sb(f"xs16_{i}", [128, D], BF16) for i in range(2)]; xs16_b = [Buf(f"xs16_{i}") for i in range(2)]
        cnt = {"nt": 0, "rs": 0}

        def rstd_from_sbuf(src, src_b, st, st_b, col):
            S.op("act", lambda e: e.activation(out=junk[:], in_=src[:], func=AF.Square, accum_out=st[:, col:col + 1]),
                 reads=[src_b], writes=[junk_b, st_b])
            S.op("act", lambda e: e.activation(out=st[:, col:col + 1], in_=st[:, col:col + 1], func=AF.Sqrt, scale=1.0 / D, bias=EPS),
                 reads=[st_b], writes=[st_b])
            S.op("dve", lambda e: e.reciprocal(out=st[:, col:col + 1], in_=st[:, col:col + 1]), reads=[st_b], writes=[st_b])

        def norm_T(src, src_b, scale_ap, scale_b, gainT_ap, dst3, dst_b, pbanks, use=None):
            if use is None:
                i = cnt["nt"] % 2
                cnt["nt"] += 1
            else:
                i = use
            if scale_ap is not None:
                S.op("dve", lambda e: e.tensor_scalar(out=xs16[i][:], in0=src[:], scalar1=scale_ap, scalar2=None, op0=ALU.mult),
                     reads=[src_b, scale_b], writes=[xs16_b[i]])
            pa, pb = ps[pbanks[0]], ps[pbanks[1]]
            for c in range(16):
                pp, bk = (pa, pbanks[0]) if c < 8 else (pb, pbanks[1])
                S.op("pe", lambda e, c=c, bk=bk: e.transpose(out=psb16[:, bk * 1024 + (c % 8) * 128:bk * 1024 + (c % 8 + 1) * 128],
                                                             in_=xs16[i][:, c * 128:(c + 1) * 128], identity=ident[:]),
                     reads=[xs16_b[i], ident_b], writes=[pp.b])
            for half, (pp, bk) in enumerate(((pa, pbanks[0]), (pb, pbanks[1]))):
                S.op("dve", lambda e, half=half, bk=bk: e.tensor_tensor(
                    out=dst3[:, half * 8:(half + 1) * 8, :],
                    in0=psb16[:, bk * 1024:bk * 1024 + 1024].rearrange("p (c n) -> p c n", c=8),
                    in1=gainT_ap[:, half * 8:(half + 1) * 8].unsqueeze(2).to_broadcast([128, 8, 128]), op=ALU.mult),
                    reads=[pp.b, par_b], writes=[dst_b])
            return xs16[i], xs16_b[i]

        def resid_update(xt, xt_b, ybanks, st, st_b, ytmp=None, ytmp_b=None):
            b0 = ybanks[0]
            yb = [ps[b].b for b in ybanks]
            for q, b in enumerate(ybanks):
                S.op("act", lambda e, q=q, b=b: e.activation(out=junk[:, q * 512:(q + 1) * 512], in_=ps[b].t[:, :], func=AF.Square,
                                                             accum_out=st[:, q:q + 1]), reads=[ps[b].b], writes=[junk_b, st_b])
            S.op("dve", lambda e: e.tensor_reduce(out=st[:, 4:5], in_=st[:, 0:4], axis=AX.X, op=ALU.add), reads=[st_b], writes=[st_b])
            S.op("act", lambda e: e.activation(out=st[:, 5:6], in_=st[:, 4:5], func=AF.Sqrt, scale=1.0 / D, bias=EPS), reads=[st_b], writes=[st_b])
            S.op("dve", lambda e: e.reciprocal(out=st[:, 6:7], in_=st[:, 5:6]), reads=[st_b], writes=[st_b])
            if ytmp is None:
                ytmp, ytmp_b = xs32, xs32_b
            S.op("dve", lambda e: e.scalar_tensor_tensor(out=ytmp[:], in0=psbig[:, b0 * 512:(b0 + 4) * 512], scalar=st[:, 6:7], in1=gF[:],
                                                         op0=ALU.mult, op1=ALU.mult), reads=yb + [st_b, gF_b], writes=[ytmp_b])
            S.op("dve", lambda e: e.tensor_tensor(out=xt[:], in0=xt[:], in1=ytmp[:], op=ALU.add), reads=[xt_b, ytmp_b], writes=[xt_b])

        xs32 = sb("xs32", [128, D], F32); xs32_b = Buf("xs32")

        mA = A.mark()
        S.op("sp", lambda e: e.dma_start(out=gF[:], in_=io["gF"][0].partition_broadcast(128)), writes=[gF_b], dma=True)
        hT = sb("hT", [128, 16, TR], BF16); hT_b = Buf("hT")
        mA2 = A.mark()
        w_out = sb("w_out_sb", [128, 16, D], BF16); w_out_b = Buf("w_out")
        for kc in range(0, 16, 4):
            S.op("pool", lambda e, kc=kc: e.dma_start(out=w_out[:, kc:kc + 4, :], in_=io["w_out"].rearrange("(kc p) n -> p kc n", p=128)[:, kc:kc + 4, :]),
                 writes=[w_out_b], dma=True)
        ot = [sb(f"ot{i}", [128, D], F32) for i in range(2)]; ot_b = [Buf(f"ot{i}") for i in range(2)]
        ot1 = [xs32, xs32]; ot1_b = [xs32_b, xs32_b]
        xt = [sb(f"xt{i}", [128, D], F32) for i in range(2)]; xt_b = [Buf(f"xt{i}") for i in range(2)]
        sq4 = [sb(f"sq4_{i}", [128, 4], F32) for i in range(2)]; sq4_b = [Buf(f"sq4_{i}") for i in range(2)]
        oT = [sb(f"oT{i}", [128, 16, 128], BF16) for i in range(2)]; oT_b = [Buf(f"oT{i}") for i in range(2)]
        jsel = {}
        def frontA(t):
            i = t % 2
            rows = slice(t * 128, (t + 1) * 128)
            r0 = max(t - 1, 0) * 128
            r1 = 1920 + t * 128
            for r in range(2):
                S.op("sp", lambda e, i=i, r=r, r0=r0: e.dma_start(out=ot[i][:, r * 1024:(r + 1) * 1024], in_=G_o[(r0 // 512) * 1024 + r * 512 + r0 % 512:(r0 // 512) * 1024 + r * 512 + r0 % 512 + 128, :]),
                     reads=[G_o_b], writes=[ot_b[i]], dma=True)
                S.op("sp", lambda e, i=i, r=r, r1=r1: e.dma_start(out=ot1[i][:, r * 1024:(r + 1) * 1024], in_=G_o[(r1 // 512) * 1024 + r * 512 + r1 % 512:(r1 // 512) * 1024 + r * 512 + r1 % 512 + 128, :]),
                     reads=[G_o_b], writes=[ot1_b[i]], dma=True)
            S.op("act", lambda e, i=i: e.activation(out=ot[i][:], in_=ot[i][:], func=AF.Copy, scale=flag2[:, 1:2]),
                 reads=[ot_b[i], flag_b], writes=[ot_b[i]])
            S.op("dve", lambda e, i=i: e.scalar_tensor_tensor(out=ot[i][:], in0=ot1[i][:], scalar=flag2[:, 0:1], in1=ot[i][:], op0=ALU.mult, op1=ALU.add),
                 reads=[ot_b[i], ot1_b[i], flag_b], writes=[ot_b[i]])
            xsrc, xsrc_b = cfg["x_rows"](t)
            S.op("sp", lambda e, i=i, xsrc=xsrc: e.dma_start(out=xt[i][:], in_=xsrc), reads=[xsrc_b], writes=[xt_b[i]], dma=True)
            tb0 = max(t - 1, 0)
            tb1 = 15 + t
            S.op("dve", lambda e, i=i, tb0=tb0: e.tensor_scalar(out=sq4[i][:], in0=GS[:, :, tb0], scalar1=flag2[:, 1:2], scalar2=None, op0=ALU.mult),
                 reads=[GS_b, flag_b], writes=[sq4_b[i]])
            S.op("dve", lambda e, i=i, tb1=tb1: e.scalar_tensor_tensor(out=sq4[i][:], in0=GS[:, :, tb1], scalar=flag2[:, 0:1], in1=sq4[i][:], op0=ALU.mult, op1=ALU.add),
                 reads=[GS_b, flag_b, sq4_b[i]], writes=[sq4_b[i]])
            st, stb = stt[i], stt_b[i]
            S.op("dve", lambda e, i=i, st=st: e.tensor_tensor(out=st[:, 0:2], in0=sq4[i][:, 0:2], in1=sq4[i][:, 2:4], op=ALU.add), reads=[sq4_b[i]], writes=[stb])
            S.op("act", lambda e, st=st: e.activation(out=st[:, 2:3], in_=st[:, 0:1], func=AF.Sqrt, scale=1.0 / 768, bias=EPS), reads=[stb], writes=[stb])
            S.op("act", lambda e, st=st: e.activation(out=st[:, 3:4], in_=st[:, 1:2], func=AF.Sqrt, scale=1.0 / 512, bias=EPS), reads=[stb], writes=[stb])
            S.op("dve", lambda e, st=st: e.reciprocal(out=st[:, 2:4], in_=st[:, 2:4]), reads=[stb], writes=[stb])
            j = cnt["nt"] % 2
            cnt["nt"] += 1
            jsel[t] = j
            xs = xs16[j]
            for c0 in (0, 1024):
                S.op("dve", lambda e, c0=c0, xs=xs, i=i, st=st: e.tensor_scalar(out=xs[:, c0:c0 + 384], in0=ot[i][:, c0:c0 + 384], scalar1=st[:, 2:3], scalar2=None, op0=ALU.mult),
                     reads=[ot_b[i], stb], writes=[xs16_b[j]])
                S.op("dve", lambda e, c0=c0, xs=xs, i=i, st=st: e.tensor_scalar(out=xs[:, c0 + 384:c0 + 640], in0=ot[i][:, c0 + 384:c0 + 640], scalar1=st[:, 3:4], scalar2=None, op0=ALU.mult),
                     reads=[ot_b[i], stb], writes=[xs16_b[j]])
                S.op("act", lambda e, c0=c0, xs=xs, i=i: e.copy(out=xs[:, c0 + 640:c0 + 1024], in_=ot[i][:, c0 + 640:c0 + 1024]),
                     reads=[ot_b[i]], writes=[xs16_b[j]])

        def frontT(t):
            i = t % 2
            norm_T(None, None, None, None, gvT, oT[i], oT_b[i], (4, 5), use=jsel[t])

        def backMM(t):
            i = t % 2
            for blk in range(4):
                for kc in range(16):
                    S.op("pe", lambda e, blk=blk, kc=kc, i=i: e.matmul(ps[blk].t[:, :], lhsT=oT[i][:, kc, :], rhs=w_out[:, kc, blk * 512:(blk + 1) * 512],
                                                                      start=(kc == 0), stop=(kc == 15)), reads=[oT_b[i], w_out_b], writes=[ps[blk].b])

        def backA(t):
            i = t % 2
            rows = slice(t * 128, (t + 1) * 128)
            st, stb = stt[i], stt_b[i]
            resid_update(xt[i], xt_b[i], (0, 1, 2, 3), st, stb, ytmp=ot[i], ytmp_b=ot_b[i])
            S.op("sp", lambda e, i=i, rows=rows: e.dma_start(out=x1s[rows, :], in_=xt[i][:]), reads=[xt_b[i]], writes=[x1s_b], dma=True)
            rstd_from_sbuf(xt[i], xt_b[i], st, stb, 7)
            norm_T(xt[i], xt_b[i], st[:, 7:8], stb, gT[:, 0, :], hT[:, :, t * 128:(t + 1) * 128], hT_b, (6, 7))
        frontA(0)
        frontT(0)
        if NTR > 1:
            frontA(1)
        for t in range(NTR):
            backMM(t)
            if t + 1 < NTR:
                frontT(t + 1)
            backA(t)
            if t + 2 < NTR:
                frontA(t + 2)
        S.barrier()
        A.release(mA2)
        if cfg.get("stop") == "A":
            A.release(mR)
            return

        S.op("sp", lambda e: e.dma_start(out=gF[:], in_=io["gF"][1].partition_broadcast(128)), writes=[gF_b], dma=True)
        wq = sb("wq_sb", [128, 16, 512], BF16); wkv = sb("wkv_sb", [128, 16, 1024], BF16); wo = sb("wo_sb", [128, 4, D], BF16)
        wB_b = Buf("wB")
        S.op("pool", lambda e: e.dma_start(out=wq[:], in_=io["wq"].rearrange("(kc p) n -> p kc n", p=128)), writes=[wB_b], dma=True)
        for kc in range(0, 16, 8):
            S.op("pool", lambda e, kc=kc: e.dma_start(out=wkv[:, kc:kc + 8, :], in_=io["wkv"].rearrange("(kc p) n -> p kc n", p=128)[:, kc:kc + 8, :]), writes=[wB_b], dma=True)
        S.op("pool", lambda e: e.dma_start(out=wo[:], in_=io["wo"].rearrange("(kc p) n -> p kc n", p=128)), writes=[wB_b], dma=True)
        memT = sb("memT", [128, 16, 256], BF16); memT_b = Buf("memT")
        x1t = [sb(f"x1t{i}", [128, D], F32) for i in range(2)]; x1t_b = [Buf(f"x1t{i}") for i in range(2)]
        mt, mt_b = x1t, x1t_b
        for mb in range(2):
            S.op("sp", lambda e, mb=mb: e.dma_start(out=mt[mb][:], in_=io["mem"][mb * 128:(mb + 1) * 128, :]), writes=[mt_b[mb]], dma=True)
            rstd_from_sbuf(mt[mb], mt_b[mb], stt[mb], stt_b[mb], 7)
            norm_T(mt[mb], mt_b[mb], stt[mb][:, 7:8], stt_b[mb], msgT, memT[:, :, mb * 128:(mb + 1) * 128], memT_b, (4, 5))
        kT = sb("kT", [128, 4, 256], BF16); kT_b = Buf("kT")
        Vm = sb("Vm", [128, 2, 512], BF16); Vm_b = Buf("Vm")
        for h in range(4):
            bk = h // 2
            for kc in range(16):
                S.op("pe", lambda e, h=h, kc=kc, bk=bk: e.matmul(ps[bk].t[:, (h % 2) * 256:(h % 2 + 1) * 256], lhsT=wkv[:, kc, h * 128:(h + 1) * 128], rhs=memT[:, kc, :],
                                                                 start=(kc == 0), stop=(kc == 15)), reads=[wB_b, memT_b], writes=[ps[bk].b])
        for bk in range(2):
            S.op("act", lambda e, bk=bk: e.copy(out=kT[:, 2 * bk:2 * bk + 2, :].rearrange("p a m -> p (a m)"), in_=ps[bk].t[:, :]), reads=[ps[bk].b], writes=[kT_b])
        for mb in range(2):
            for kc in range(16):
                S.op("pe", lambda e, mb=mb, kc=kc: e.matmul(ps[2 + mb].t[:, :], lhsT=memT[:, kc, mb * 128:(mb + 1) * 128], rhs=wkv[:, kc, 512:1024],
                                                            start=(kc == 0), stop=(kc == 15)), reads=[wB_b, memT_b], writes=[ps[2 + mb].b])
            S.op("dve", lambda e, mb=mb: e.tensor_copy(out=Vm[:, mb, :], in_=ps[2 + mb].t[:, :]), reads=[ps[2 + mb].b], writes=[Vm_b])
        qT = sb("qT", [128, 4, 512], BF16); qT_b = Buf("qT")
        PT = [sb(f"PT{i}", [128, 2, 512], BF16) for i in range(2)]; PT_b = [Buf(f"PT{i}") for i in range(2)]
        rec = sb("rec", [128, 512], F32); rec_b = Buf("rec")
        oT2 = sb("oT2", [128, 4, 512], BF16); oT2_b = Buf("oT2")
        hk = 0
        SCL = 128 ** -0.5
        for (g0, n) in TGROUPS:
            for h in range(4):
                for kc in range(16):
                    S.op("pe", lambda e, h=h, kc=kc, g0=g0, n=n: e.matmul(ps[h].t[:, 0:n], lhsT=wq[:, kc, h * 128:(h + 1) * 128], rhs=hT[:, kc, g0:g0 + n],
                                                                          start=(kc == 0), stop=(kc == 15)), reads=[wB_b, hT_b], writes=[ps[h].b])
                if h % 2 == 0:
                    S.op("act", lambda e, h=h, n=n: e.copy(out=qT[:, h, 0:n], in_=ps[h].t[:, 0:n]), reads=[ps[h].b], writes=[qT_b])
                else:
                    S.op("dve", lambda e, h=h, n=n: e.tensor_copy(out=qT[:, h, 0:n], in_=ps[h].t[:, 0:n]), reads=[ps[h].b], writes=[qT_b])
            for h in range(4):
                pi = hk % 2
                hk += 1
                for mb in range(2):
                    S.op("pe", lambda e, h=h, mb=mb, n=n: e.matmul(ps[4 + mb].t[:, 0:n], lhsT=kT[:, h, mb * 128:(mb + 1) * 128], rhs=qT[:, h, 0:n], start=True, stop=True),
                         reads=[kT_b, qT_b], writes=[ps[4 + mb].b])
                    S.op("act", lambda e, mb=mb, n=n, pi=pi: e.activation(out=PT[pi][:, mb, 0:n], in_=ps[4 + mb].t[:, 0:n], func=AF.Exp, scale=SCL),
                         reads=[ps[4 + mb].b], writes=[PT_b[pi]])
                for mb in range(2):
                    S.op("pe", lambda e, h=h, mb=mb, n=n, pi=pi: e.matmul(ps[6].t[:, 0:n], lhsT=Vm[:, mb, h * 128:(h + 1) * 128], rhs=PT[pi][:, mb, 0:n],
                                                                          start=(mb == 0), stop=(mb == 1)), reads=[Vm_b, PT_b[pi]], writes=[ps[6].b])
                for mb in range(2):
                    S.op("pe", lambda e, mb=mb, n=n, pi=pi: e.matmul(ps[7].t[:, 0:n], lhsT=ones16[:], rhs=PT[pi][:, mb, 0:n],
                                                                     start=(mb == 0), stop=(mb == 1)), reads=[ident_b, PT_b[pi]], writes=[ps[7].b])
                S.op("dve", lambda e, n=n: e.reciprocal(out=rec[:, 0:n], in_=ps[7].t[:, 0:n]), reads=[ps[7].b], writes=[rec_b])
                S.op("dve", lambda e, h=h, n=n: e.tensor_tensor(out=oT2[:, h, 0:n], in0=ps[6].t[:, 0:n], in1=rec[:, 0:n], op=ALU.mult),
                     reads=[ps[6].b, rec_b], writes=[oT2_b])
            for tt in range(n // 128):
                t = g0 // 128 + tt
                i = t % 2
                rows = slice(t * 128, (t + 1) * 128)
                S.op("sp", lambda e, i=i, rows=rows: e.dma_start(out=x1t[i][:], in_=x1s[rows, :]), reads=[x1s_b], writes=[x1t_b[i]], dma=True)
                for blk in range(4):
                    for h in range(4):
                        S.op("pe", lambda e, blk=blk, h=h, tt=tt: e.matmul(ps[blk].t[:, :], lhsT=oT2[:, h, tt * 128:(tt + 1) * 128], rhs=wo[:, h, blk * 512:(blk + 1) * 512],
                                                                         start=(h == 0), stop=(h == 3)), reads=[oT2_b, wB_b], writes=[ps[blk].b])
                st, stb = stt[i], stt_b[i]
                resid_update(x1t[i], x1t_b[i], (0, 1, 2, 3), st, stb)
                S.op("sp", lambda e, i=i, rows=rows: e.dma_start(out=x2s[rows, :], in_=x1t[i][:]), reads=[x1t_b[i]], writes=[x2s_b], dma=True)
                rstd_from_sbuf(x1t[i], x1t_b[i], st, stb, 7)
                if t == 0:
                    S.op("dve", lambda e, st=st: e.tensor_tensor(out=st[:, 7:8], in0=st[:, 7:8], in1=flag, op=ALU.mult), reads=[stb, flag_b], writes=[stb])
                norm_T(x1t[i], x1t_b[i], st[:, 7:8], stb, gT[:, 1, :], hT[:, :, t * 128:(t + 1) * 128], hT_b, (4, 5))
        S.barrier()
        A.release(mA2)
        if cfg.get("stop") == "B":
            A.release(mR)
            return

        cw = sb("cw", [128, 2 * NJ, 3], F32); cb = sb("cb", [128, 2 * NJ], F32); cwb_b = Buf("cwb")
        S.op("sp", lambda e: e.dma_start(out=cw[:], in_=io["cw"]), writes=[cwb_b], dma=True)
        S.op("sp", lambda e: e.dma_start(out=cb[:], in_=io["cb"]), writes=[cwb_b], dma=True)
        wu = [sb(f"wu{i}", [128, 16, 256], BF16) for i in range(3)]; wu_b = [Buf(f"wu{i}") for i in range(3)]
        UG = [sb(f"UG{i}", [128, 2, TR + 2], F32) for i in range(2)]; UG_b = [Buf(f"UG{i}") for i in range(2)]
        CV = [sb(f"CV{i}", [128, 2, TR], F32) for i in range(2)]; CV_b = [Buf(f"CV{i}") for i in range(2)]
        AC = [sb(f"AC{i}", [128, TR], BF16) for i in range(2)]; AC_b = [Buf(f"AC{i}") for i in range(2)]
        for i in range(2):
            S.op("pool", lambda e, i=i: e.memset(UG[i][:, :, 0:2], 0.0), writes=[UG_b[i]])
        w_up_v = io["w_up"].rearrange("(kc p) n -> p kc n", p=128)
        pk = 0
        def load_wu(j):
            w = j % 3
            S.op("pool", lambda e, w=w, j=j: e.dma_start(out=wu[w][:, :, 0:128], in_=w_up_v[:, :, j * 128:(j + 1) * 128]), writes=[wu_b[w]], dma=True)
            S.op("pool", lambda e, w=w, j=j: e.dma_start(out=wu[w][:, :, 128:256], in_=w_up_v[:, :, DFF + j * 128:DFF + (j + 1) * 128]), writes=[wu_b[w]], dma=True)
        load_wu(0)
        load_wu(1)
        for j in range(NJ):
            i = j % 2
            w = j % 3
            if j + 2 < NJ:
                load_wu(j + 2)
            for (g0, n) in TGROUPS:
                for gv in range(2):
                    pp = ps[pk % 8]
                    pk += 1
                    for kc in range(16):
                        S.op("pe", lambda e, pp=pp, kc=kc, gv=gv, w=w, g0=g0, n=n: e.matmul(pp.t[:, 0:n], lhsT=wu[w][:, kc, gv * 128:(gv + 1) * 128], rhs=hT[:, kc, g0:g0 + n],
                                                                                         start=(kc == 0), stop=(kc == 15)), reads=[wu_b[w], hT_b], writes=[pp.b])
                    if gv == 0:
                        S.op("act", lambda e, pp=pp, gv=gv, i=i, g0=g0, n=n: e.copy(out=UG[i][:, gv, 2 + g0:2 + g0 + n], in_=pp.t[:, 0:n]), reads=[pp.b], writes=[UG_b[i]])
                    else:
                        S.op("dve", lambda e, pp=pp, gv=gv, i=i, g0=g0, n=n: e.tensor_copy(out=UG[i][:, gv, 2 + g0:2 + g0 + n], in_=pp.t[:, 0:n]), reads=[pp.b], writes=[UG_b[i]])
            for gv in range(2):
                cidx = j if gv == 0 else NJ + j
                S.op("act", lambda e, gv=gv, i=i, cidx=cidx: e.activation(out=CV[i][:, gv, :], in_=UG[i][:, gv, 2:TR + 2], func=AF.Identity,
                                                                        scale=cw[:, cidx, 2:3], bias=cb[:, cidx:cidx + 1]), reads=[UG_b[i], cwb_b], writes=[CV_b[i]])
                S.op("dve", lambda e, gv=gv, i=i, cidx=cidx: e.scalar_tensor_tensor(out=CV[i][:, gv, :], in0=UG[i][:, gv, 1:TR + 1], scalar=cw[:, cidx, 1:2], in1=CV[i][:, gv, :],
                                                                                   op0=ALU.mult, op1=ALU.add), reads=[UG_b[i], cwb_b, CV_b[i]], writes=[CV_b[i]])
                S.op("dve", lambda e, gv=gv, i=i, cidx=cidx: e.scalar_tensor_tensor(out=CV[i][:, gv, :], in0=UG[i][:, gv, 0:TR], scalar=cw[:, cidx, 0:1], in1=CV[i][:, gv, :],
                                                                                   op0=ALU.mult, op1=ALU.add), reads=[UG_b[i], cwb_b, CV_b[i]], writes=[CV_b[i]])
            S.op("act", lambda e, i=i: e.activation(out=CV[i][:, 0, :], in_=CV[i][:, 0, :], func=AF.Gelu_apprx_tanh), reads=[CV_b[i]], writes=[CV_b[i]])
            S.op("dve", lambda e, i=i: e.tensor_tensor(out=AC[i][:], in0=CV[i][:, 0, :], in1=CV[i][:, 1, :], op=ALU.mult), reads=[CV_b[i]], writes=[AC_b[i]])
            S.op("sp", lambda e, i=i, j=j: e.dma_start(out=acts.rearrange("(tt p j) t -> p tt j t", p=128, j=NJ)[:, :, j, :],
                                                       in_=AC[i][:].rearrange("p (tt t) -> p tt t", t=128)), reads=[AC_b[i]], writes=[acts_b], dma=True)
        S.barrier()
        A.release(mA)
        if cfg.get("stop") == "C":
            A.release(mR)
            return

        S.op("sp", lambda e: e.dma_start(out=gF[:], in_=io["gF"][2].partition_broadcast(128)), writes=[gF_b], dma=True)
        wd = [sb(f"wd{i}", [128, NJ, 512], BF16) for i in range(2)]; wd_b = [Buf(f"wd{i}") for i in range(2)]
        at = [sb(f"at{i}", [128, NJ, 128], BF16) for i in range(3)]; at_b = [Buf(f"at{i}") for i in range(3)]
        yst = [sb(f"yst{i}", [128, 512], F32) for i in range(3)]; yst_b = [Buf(f"yst{i}") for i in range(3)]
        ssq3 = sb("ssq3", [128, 16, 4], F32); ssq3_b = Buf("ssq3")
        acts_v = acts.rearrange("(tt p j) t -> tt p j t", p=128, j=NJ)
        w_down_v = io["w_down"].rearrange("(j p) n -> p j n", p=128)
        seq = [(blk, t) for blk in range(4) for t in range(1, NTR)]

        def load_wd(blk):
            i = blk % 2
            for j0 in range(0, NJ, 11):
                S.op("pool", lambda e, i=i, blk=blk, j0=j0: e.dma_start(out=wd[i][:, j0:j0 + 11, :], in_=w_down_v[:, j0:j0 + 11, blk * 512:(blk + 1) * 512]), writes=[wd_b[i]], dma=True)

        def load_at(k):
            a = k % 3
            t = seq[k][1]
            S.op("sp", lambda e, a=a, t=t: e.dma_start(out=at[a][:], in_=acts_v[t]), reads=[acts_b], writes=[at_b[a]], dma=True)
        load_wd(0)
        load_wd(1)
        load_at(0)
        load_at(1)
        for k, (blk, t) in enumerate(seq):
            i = blk % 2
            a = k % 3
            pp = ps[k % 8]
            if t == 1 and blk >= 1 and blk + 1 < 4:
                load_wd(blk + 1)
            if k + 2 < len(seq):
                load_at(k + 2)
            for j in range(NJ):
                S.op("pe", lambda e, pp=pp, j=j, a=a, i=i: e.matmul(pp.t[:, :], lhsT=at[a][:, j, :], rhs=wd[i][:, j, :], start=(j == 0), stop=(j == NJ - 1)),
                     reads=[at_b[a], wd_b[i]], writes=[pp.b])
            S.op("act", lambda e, pp=pp, a=a: e.copy(out=yst[a][:], in_=pp.t[:, :]), reads=[pp.b], writes=[yst_b[a]])
            S.op("dve", lambda e, a=a, t=t, blk=blk: e.scalar_tensor_tensor(out=junk[:, 0:512], in0=yst[a][:], scalar=1.0, in1=yst[a][:], op0=ALU.mult, op1=ALU.mult,
                                                                       accum_out=ssq3[:, t - 1, blk:blk + 1]),
                 reads=[yst_b[a]], writes=[junk_b, ssq3_b])
            S.op("sp", lambda e, a=a, t=t, blk=blk: e.dma_start(out=y3s[(t - 1) * 128:t * 128, blk * 512:(blk + 1) * 512], in_=yst[a][:]),
                 reads=[yst_b[a]], writes=[y3s_b], dma=True)
        rs = sb("rs", [128, 16, 4], F32); rs_b = Buf("rs")
        S.op("dve", lambda e: e.tensor_reduce(out=rs[:, :, 0], in_=ssq3[:], axis=AX.X, op=ALU.add), reads=[ssq3_b], writes=[rs_b])
        S.op("act", lambda e: e.activation(out=rs[:, :, 1], in_=rs[:, :, 0], func=AF.Sqrt, scale=1.0 / D, bias=EPS), reads=[rs_b], writes=[rs_b])
        S.op("dve", lambda e: e.reciprocal(out=rs[:, :, 2], in_=rs[:, :, 1]), reads=[rs_b], writes=[rs_b])
        yt = [sb(f"yt{i}", [128, D], F32) for i in range(2)]; yt_b = [Buf(f"yt{i}") for i in range(2)]
        x2t = [sb(f"x2t{i}", [128, D], F32) for i in range(2)]; x2t_b = [Buf(f"x2t{i}") for i in range(2)]
        def final_loads(t):
            i = t % 2
            S.op("sp", lambda e, i=i, t=t: e.dma_start(out=yt[i][:], in_=y3s[(t - 1) * 128:t * 128, :]), reads=[y3s_b], writes=[yt_b[i]], dma=True)
            S.op("sp", lambda e, i=i, t=t: e.dma_start(out=x2t[i][:], in_=x2s[t * 128:(t + 1) * 128, :]), reads=[x2s_b], writes=[x2t_b[i]], dma=True)
        final_loads(1)
        for t in range(1, NTR):
            i = t % 2
            if t + 1 < NTR:
                final_loads(t + 1)
            S.op("dve", lambda e, i=i, t=t: e.scalar_tensor_tensor(out=yt[i][:], in0=yt[i][:], scalar=rs[:, t - 1, 2:3], in1=gF[:], op0=ALU.mult, op1=ALU.mult),
                 reads=[yt_b[i], rs_b, gF_b], writes=[yt_b[i]])
            S.op("dve", lambda e, i=i: e.tensor_tensor(out=x2t[i][:], in0=x2t[i][:], in1=yt[i][:], op=ALU.add), reads=[x2t_b[i], yt_b[i]], writes=[x2t_b[i]])
            S.op("sp", lambda e, i=i, t=t: e.dma_start(out=xo[(t - 1) * 128:t * 128, :], in_=x2t[i][:]), reads=[x2t_b[i]], writes=[xo_b], dma=True)
        S.barrier()
        A.release(mR)

import contextlib

PAIRS = [[0, 1], [2, 3], [4, 5], [6, 7]]


def build_fused(stages=4, debug=False, r1_stop=None):
    nc = bass.Bass("TRN2", target_bir_lowering=False)
    T, D = 4096, 2048
    ioM = [declare_io_M(nc, "_l0", with_x=True), declare_io_M(nc, "_l1", with_x=False)]
    ioR = [declare_io_R(nc, "_l0", with_xh=True), declare_io_R(nc, "_l1", with_xh=False)]
    flag_in = nc.dram_tensor("flag2", [128, 2], F32, kind="ExternalInput").ap()
    xo_ext = nc.dram_tensor("xo", [2048, D], F32, kind="ExternalOutput").ap()

    def dint(name, shape, dt=F32):
        return nc.dram_tensor(name, list(shape), dt, kind="Internal").ap()
    scr = {"zTa": dint("zTa", [NF_A, T], BF16), "zTc": dint("zTc", [NF_C, T + 1]), "ztm": dint("ztm", [T + 1, NTM])}
    for k in ("zTa", "zTc", "ztm"):
        scr[k + "_b"] = Buf(k)
    scr["oa_scr"] = [dint(f"oa_scr{p}", [T, 65]) for p in range(3)]
    scr["oa_scr_b"] = [Buf(f"oa_scr{p}") for p in range(3)]
    o_scr = dint("o_scr", [T, 1024]); o_b = Buf("o_scr")
    ssq_scr = dint("ssq_scr", [2, 128, 32]); ssq_b = Buf("ssq_scr")
    G_o = dint("G_o", [2 * T, 1024]); G_o_b = Buf("G_o")
    G_ssq = dint("G_ssq", [4, 128, 32]); G_ssq_b = Buf("G_ssq")
    xo_scr = dint("xo_scr", [2048, D]); xo_scr_b = Buf("xo_scr")
    G_x = dint("G_x", [T, D]); G_x_b = Buf("G_x")
    xo_scr2 = dint("xo_scr2", [2048, D]); xo_scr2_b = Buf("xo_scr2")
    rs = dict(x1s=dint("x1s", [TR, D]), x1s_b=Buf("x1s"), x2s=dint("x2s", [TR, D]), x2s_b=Buf("x2s"),
              acts=dint("acts", [NTR * 128 * NJ, 128], BF16), acts_b=Buf("acts"), y3s=dint("y3s", [2048, D]), y3s_b=Buf("y3s"))
    xo_ext_b = Buf("xo_ext")
    with contextlib.ExitStack() as es:
        sems = [es.enter_context(nc.semaphore(f"s{i}")) for i in range(98)]
        S = Sched(nc, sems)
        S.arena = Arena(nc)
        A = S.arena
        psbig = es.enter_context(nc.psum_tensor("psbig", [128, 4096], F32)).ap()
        ident = A.alloc("ident", [128, 128], BF16); identf = A.alloc("identf", [128, 128], F32); ones16 = A.alloc("ones16", [128, 128], BF16)
        ident_b = Buf("ident")
        S.op("pool", lambda e: e.memset(identf[:], 0.0), writes=[ident_b])
        S.op("pool", lambda e: e.affine_select(out=identf[:], in_=identf[:], pattern=[[-1, 128]], base=0,
                                               channel_multiplier=1, compare_op=ALU.not_equal, fill=1.0), reads=[ident_b], writes=[ident_b])
        S.op("pool", lambda e: e.tensor_copy(out=ident[:], in_=identf[:]), reads=[ident_b], writes=[ident_b])
        S.op("pool", lambda e: e.memset(ones16[:], 1.0), writes=[ident_b])
        S.ident, S.identf, S.ident_b, S.ones16 = ident, identf, ident_b, ones16
        flag2 = A.alloc("flag2", [128, 2], F32); flag_b = Buf("flag2")
        S.op("sp", lambda e: e.dma_start(out=flag2[:], in_=flag_in), writes=[flag_b], dma=True)
        outM = {"o": o_scr, "o_b": o_b, "ssq": ssq_scr, "ssq_b": ssq_b,
                "ssqA": A.alloc("ssqA", [128, 32], F32), "ssqA_b": Buf("ssqA"), "ssqB": A.alloc("ssqB", [128, 32], F32), "ssqB_b": Buf("ssqB")}
        S.barrier()
        for l in range(2):
            io = ioM[l]
            if l == 1:
                io["x_rows"] = lambda t: G_x[((t % 16) // 2) * 512 + (t // 16) * 256 + (t % 2) * 128:((t % 16) // 2) * 512 + (t // 16) * 256 + (t % 2) * 128 + 128, :]
                io["x_b"] = G_x_b
            if 2 * l + 1 > stages:
                break
            emit_M(S, nc, io, scr, outM, psbig)
            S.recycle_dma_sems()
            for k in range(8):
                S.op("pool", lambda e, k=k: e.collective_compute("AllGather", ALU.bypass, replica_groups=PAIRS,
                                                                 ins=[o_scr[k * 512:(k + 1) * 512, :].opt()], outs=[G_o[k * 1024:(k + 1) * 1024, :].opt()]),
                     reads=[o_b], writes=[G_o_b], dma=True, dinc=1)
            S.op("pool", lambda e: e.collective_compute("AllGather", ALU.bypass, replica_groups=PAIRS,
                                                        ins=[ssq_scr.rearrange("g j t -> (g j) t").opt()], outs=[G_ssq.rearrange("k j t -> (k j) t").opt()]),
                 reads=[ssq_b], writes=[G_ssq_b], dma=True, dinc=1)
            if debug and 2 * l + 1 == stages:
                dbg = nc.dram_tensor("dbg_Go", [2 * T, 1024], F32, kind="ExternalOutput").ap()
                dbg2 = nc.dram_tensor("dbg_Gssq", [4, 128, 32], F32, kind="ExternalOutput").ap()
                S.op("sp", lambda e: e.dma_start(out=dbg[:, :], in_=G_o[:, :]), reads=[G_o_b], writes=[xo_ext_b], dma=True)
                S.op("sp", lambda e: e.dma_start(out=dbg2.rearrange("k j t -> (k j) t"), in_=G_ssq.rearrange("k j t -> (k j) t")), reads=[G_ssq_b], writes=[xo_ext_b], dma=True)
            if 2 * l + 2 > stages:
                break
            if l == 0:
                def x_rows(tt, io=ioR[0]):
                    return io["xh"][tt * 128:(tt + 1) * 128, :], Buf("xh_in")
                xo, xo_b = xo_scr, xo_scr_b
            else:
                def x_rows(tt):
                    if tt == 0:
                        return G_x[3712:3840, :], G_x_b
                    return xo_scr[(tt - 1) * 128:tt * 128, :], xo_scr_b
                xo, xo_b = xo_scr2, xo_scr2_b
            cfg = dict(rs, stop=(r1_stop if l == 1 else None), G_o=G_o, G_o_b=G_o_b, G_ssq=G_ssq, G_ssq_b=G_ssq_b, x_rows=x_rows, xo=xo, xo_b=xo_b, flag2=flag2, flag_b=flag_b)
            emit_R(S, nc, ioR[l], psbig, cfg)
            S.recycle_dma_sems()
            if debug and 2 * l + 2 == stages and l == 0:
                dbg3 = nc.dram_tensor("dbg_xo", [2048, D], F32, kind="ExternalOutput").ap()
                S.op("sp", lambda e: e.dma_start(out=dbg3[:, :], in_=xo_scr[:, :]), reads=[xo_scr_b], writes=[xo_ext_b], dma=True)
            if l == 0:
                for k in range(8):
                    S.op("pool", lambda e, k=k: e.collective_compute("AllGather", ALU.bypass, replica_groups=PAIRS,
                                                                     ins=[xo_scr[k * 256:(k + 1) * 256, :].opt()], outs=[G_x[k * 512:(k + 1) * 512, :].opt()]),
                         reads=[xo_scr_b], writes=[G_x_b], dma=True, dinc=1)
        if stages >= 4:
            for q in range(4):
                S.op("sp", lambda e, q=q: e.dma_start(out=xo_ext[q * 512:(q + 1) * 512, :], in_=xo_scr2[q * 512:(q + 1) * 512, :]),
                     reads=[xo_scr2_b], writes=[xo_ext_b], dma=True)
        S.final_wait("sp", [xo_ext_b])
        S.emit()
        print("fused ops", S.nops, "arena peak", A.peak, "sems left", len(S.sem_pool))
    return nc

import numpy as np
def t5_bucket(dist):
    max_exact = 16
    d = np.maximum(dist, 0)
    scaled = np.log(np.maximum(d, 1) / max_exact) / np.log(2048 / max_exact)
    large = np.minimum(max_exact + (scaled * (32 - max_exact)).astype(np.int32), 31)
    return np.where(d < max_exact, d, large).astype(np.int32)

def make_biasT(table, heads):
    qi = np.arange(128)[:, None]
    kj = np.arange(256)[None, :]
    delta = qi + 128 - kj
    valid = (delta >= 0) & (delta <= 128)
    out = np.zeros((128, len(heads), 3, 2, 128), np.float32)
    for p, d in enumerate((1, 4, 16)):
        bucket = t5_bucket(np.clip(delta, 0, 128) * d)
        for hi, h in enumerate(heads):
            b = np.where(valid, table[bucket, h], np.float32(-30000.0)).astype(np.float32)
            bt = b.T.reshape(2, 128, 128)
            out[:, hi, p, 0, :] = bt[0]
            out[:, hi, p, 1, :] = bt[1]
    return np.ascontiguousarray(out.reshape(128, len(heads) * 6, 128))

def cols_M(half):
    hA = np.arange(6) + 6 * half
    def hc(base, heads): return np.concatenate([base + h * 64 + np.arange(64) for h in heads])
    q = hc(0, hA); k = hc(768, hA); v = hc(1536, hA)
    gB = np.arange(4) + 4 * half
    u = hc(2304, gB); g = 2304 + 512 + np.concatenate([np.arange(256) + 256 * half, np.arange(256) + 256 * (1 - half)])
    c0 = 3328
    hC = np.arange(6) + 6 * half
    r = hc(c0, hC); kc = hc(c0 + 768, hC); vc = hc(c0 + 1536, hC)
    lo = c0 + 2304 + np.arange(384)
    fm = np.concatenate([q, k, r, kc, lo]); tm = np.concatenate([v, u, g, vc])
    return fm, tm


def inputs_M(d, l, b, half, x):
    fm, tm = cols_M(half)
    w_in = d["w_in"][l]
    g0 = d["sandwich_gains"][l][0]
    gB = np.arange(4) + 4 * half
    gperm = np.concatenate([np.arange(256) + 256 * half, np.arange(256) + 256 * (1 - half)])
    inm = {"x": np.ascontiguousarray(x), "g0T": np.ascontiguousarray(g0.reshape(16, 128).T),
           "w_fm": np.ascontiguousarray(w_in[:, fm]), "w_tm": np.ascontiguousarray(w_in[:, tm]),
           "biasT": make_biasT(d["rel_bias_table"], list(range(6 * half, 6 * half + 6))),
           "sgu_wT": np.ascontiguousarray(d["sgu_w"][l][gB].transpose(2, 0, 1)),
           "sgu_bT": np.ascontiguousarray(d["sgu_b"][l][gB].T),
           "sgu_ng": np.ascontiguousarray(d["sgu_norm_gain"][l][gperm]),
           }
    hC = np.arange(6) + 6 * half
    def hcols(base): return np.concatenate([base + hh * 64 + np.arange(64) for hh in hC])
    mu = d["rwkv_mu"][l]
    cp = np.zeros((128, 25), np.float32)
    def pairs(v384): return np.ascontiguousarray(v384.reshape(3, 128).T)
    cp[:, 0:3] = pairs(mu[hcols(0)]); cp[:, 3:6] = pairs(mu[hcols(768)])
    cp[:, 6:9] = pairs(d["rwkv_w0"][l][hcols(0)]); cp[:, 9:12] = pairs(d["rwkv_a0"][l][hcols(0)])
    cp[:, 12:15] = pairs(d["rwkv_k_k"][l][hcols(0)]); cp[:, 15:18] = pairs(d["rwkv_k_a"][l][hcols(0)])
    cp[:, 18:21] = pairs(d["rwkv_r_k"][l].reshape(-1)[hcols(0)])
    cp[0:64, 21] = mu[2304:2368]; cp[0:64, 22] = mu[2368:2432]
    cp[:, 23] = mu[2432:2560]; cp[:, 24] = mu[2560:2688]
    inm["cp"] = cp
    inm["cf"] = np.ascontiguousarray(np.stack([mu[hcols(1536)], d["rwkv_ln_gain"][l][hcols(0)], d["rwkv_ln_bias"][l][hcols(0)]]))
    inm["w_up"] = np.ascontiguousarray(d["rwkv_w_up"][l][:, hcols(0)])
    inm["a_up"] = np.ascontiguousarray(d["rwkv_a_up"][l][:, hcols(0)])
    inm["g_up"] = np.ascontiguousarray(d["rwkv_g_up"][l][:, hcols(0)])
    return inm


def wout_perm():
    A0 = np.arange(0, 384); A1 = np.arange(384, 768)
    B0 = 768 + np.arange(0, 256); B1 = 768 + np.arange(256, 512)
    C0 = 1280 + np.arange(0, 384); C1 = 1280 + np.arange(384, 768)
    return np.concatenate([A0, B0, C0, A1, B1, C1])


def fmT(v, n=16):
    return np.ascontiguousarray(v.reshape(n, 128).T)


def inputs_R(d, l, b, half, x_b, o0, o1, ssq0, ssq1):
    TR = 17 * 128
    lo = half * 2048 - 128
    def halo_rows(a):
        out = np.zeros((TR,) + a.shape[1:], a.dtype)
        if lo < 0:
            out[128:] = a[0:2048]
        else:
            out[:] = a[lo:lo + TR]
        return out
    xh = halo_rows(x_b)
    oc = np.concatenate([halo_rows(o0), halo_rows(o1)], axis=1)
    ssq = np.zeros((128, 17, 4), np.float32)
    for tt in range(17):
        tb = half * 16 - 1 + tt
        if tb < 0:
            continue
        ssq[:, tt, 0] = ssq0[0][:, tb]; ssq[:, tt, 1] = ssq0[1][:, tb]
        ssq[:, tt, 2] = ssq1[0][:, tb]; ssq[:, tt, 3] = ssq1[1][:, tb]
    g = d["sandwich_gains"][l]
    ag = d["attn_out_gain"][l]; sg = d["sgu_out_gain"][l]
    one = np.ones(384, np.float32)
    gv = np.concatenate([ag[0:384], sg[0:256], one, ag[384:768], sg[256:512], one])
    cwv = d["ffn_conv_w"][l]
    inm = dict(
        xh=xh, oc=np.ascontiguousarray(oc), ssq=ssq,
        w_out=np.ascontiguousarray(d["w_out"][l][wout_perm(), :]), gvT=fmT(gv),
        gT=np.ascontiguousarray(np.stack([fmT(g[2]), fmT(g[4])], axis=1)), gF=np.ascontiguousarray(np.stack([g[1], g[3], g[5]])),
        mem=np.ascontiguousarray(d["mem"][b]), msgT=fmT(d["mem_src_gain"][l]),
        wq=np.ascontiguousarray(d["mem_wq"][l]), wkv=np.ascontiguousarray(d["mem_wkv"][l]), wo=np.ascontiguousarray(d["mem_wo"][l]),
        w_up=np.ascontiguousarray(d["ffn_w_up"][l]),
        cw=np.ascontiguousarray(cwv.reshape(3, 88, 128).transpose(2, 1, 0)), cb=np.ascontiguousarray(d["ffn_conv_b"][l].reshape(88, 128).T),
        w_down=np.ascontiguousarray(d["ffn_w_down"][l]),
        flag=np.full((128, 1), float(half), np.float32),
    )
    return inm


def m_out_from_ref(oa_pre, ob_pre, oc, half):
    o = np.concatenate([oa_pre[:, 384 * half:384 * half + 384], ob_pre[:, 256 * half:256 * half + 256], oc[:, 384 * half:384 * half + 384]], axis=1)
    sA = (oa_pre[:, 384 * half:384 * half + 384] ** 2).sum(-1); sB = (ob_pre[:, 256 * half:256 * half + 256] ** 2).sum(-1)
    ssq = np.stack([sA.reshape(32, 128).T, sB.reshape(32, 128).T])
    return np.ascontiguousarray(o.astype(np.float32)), np.ascontiguousarray(ssq.astype(np.float32))


def inputs_R_fused(d, l, b, half, x_b=None):
    g = d["sandwich_gains"][l]
    ag = d["attn_out_gain"][l]; sg = d["sgu_out_gain"][l]
    one = np.ones(384, np.float32)
    gv = np.concatenate([ag[0:384], sg[0:256], one, ag[384:768], sg[256:512], one])
    cwv = d["ffn_conv_w"][l]
    inm = dict(
        w_out=np.ascontiguousarray(d["w_out"][l][wout_perm(), :]), gvT=fmT(gv),
        gT=np.ascontiguousarray(np.stack([fmT(g[2]), fmT(g[4])], axis=1)), gF=np.ascontiguousarray(np.stack([g[1], g[3], g[5]])),
        mem=np.ascontiguousarray(d["mem"][b]), msgT=fmT(d["mem_src_gain"][l]),
        wq=np.ascontiguousarray(d["mem_wq"][l]), wkv=np.ascontiguousarray(d["mem_wkv"][l]), wo=np.ascontiguousarray(d["mem_wo"][l]),
        w_up=np.ascontiguousarray(d["ffn_w_up"][l]),
        cw=np.ascontiguousarray(cwv.reshape(3, 88, 128).transpose(2, 1, 0)), cb=np.ascontiguousarray(d["ffn_conv_b"][l].reshape(88, 128).T),
        w_down=np.ascontiguousarray(d["ffn_w_down"][l]),
    )
    if x_b is not None:
        TR = 17 * 128
        lo = half * 2048 - 128
        xh = np.zeros((TR, x_b.shape[1]), np.float32)
        if lo < 0:
            xh[128:] = x_b[0:2048]
        else:
            xh[:] = x_b[lo:lo + TR]
        inm["xh"] = xh
    return inm


def inputs_fused(d, c):
    b, half = c // 2, c % 2
    x = np.asarray(d["x"], dtype=np.float32)
    out = {}
    for l in range(2):
        im = inputs_M(d, l, b, half, x[b])
        if l == 1:
            im.pop("x")
        for k, v in im.items():
            out[f"{k}_l{l}"] = v
        ir = inputs_R_fused(d, l, b, half, x[b] if l == 0 else None)
        for k, v in ir.items():
            out[f"R_{k}_l{l}"] = v
    fl = np.zeros((128, 2), np.float32)
    fl[:, 0] = float(half)
    fl[:, 1] = 1.0 - float(half)
    out["flag2"] = fl
    return out


_NC_CACHE = {}


def kernel(**inputs):
    d = {k: np.asarray(v) for k, v in inputs.items()}
    if "F" not in _NC_CACHE:
        _NC_CACHE["F"] = build_fused()
    nc = _NC_CACHE["F"]
    cores = list(range(8))
    in_maps = [inputs_fused(d, c) for c in cores]
    res = run_bass_kernel_spmd(nc, in_maps, core_ids=cores)
    x = np.stack([np.concatenate([np.asarray(res.results[2 * b]["xo"]), np.asarray(res.results[2 * b + 1]["xo"])], axis=0) for b in range(4)])
    return np.ascontiguousarray(x.astype(np.float32))
```

```python
import contextlib

import numpy as np
import concourse.bass as bass
import concourse.mybir as mybir
from concourse.bass_utils import run_bass_kernel_spmd

F32 = mybir.dt.float32
BF16 = mybir.dt.bfloat16
AF = mybir.ActivationFunctionType
ALU = mybir.AluOpType
AX = mybir.AxisListType

SEM_ROT = 30000


class Buf:
    __slots__ = ("name", "last_w", "readers", "dsem", "dval", "psum")

    def __init__(self, name, psum=False):
        self.name = name
        self.psum = psum
        self.last_w = None
        self.readers = []
        self.dsem = None
        self.dval = 0


class Sched:
    ENGS = ("pe", "act", "dve", "pool", "sp")

    def __init__(self, nc, sem_pool):
        self.nc = nc
        self.sem_pool = [(s, 0) for s in sem_pool]
        self.ops = {e: [] for e in self.ENGS}
        self.cnt = {e: 0 for e in self.ENGS}
        self.sem = {e: self.sem_pool.pop()[0] for e in self.ENGS}
        self.eng_spare = [self.sem_pool.pop()[0] for _ in range(10)]
        self.last_tok = {e: None for e in self.ENGS}
        self.waited = {e: {} for e in self.ENGS}
        self.nops = 0
        self.dma_bufs = {}

    def _new_sem(self):
        return self.sem_pool.pop()[0]

    def recycle_dma_sems(self):
        for b in self.dma_bufs.values():
            if b.dsem is not None:
                if b.dval < SEM_ROT:
                    self.sem_pool.insert(0, (b.dsem, b.dval))
                b.dsem = None
        self.dma_bufs = {}

    def op(self, eng, fn, reads=(), writes=(), dma=False, dinc=16):
        writes = list(writes) + [b for b in reads if b.psum]
        deps = {}
        for b in reads:
            if b.last_w is not None:
                s, v, e2 = b.last_w
                if not (eng == "pe" and e2 == "pe"):
                    deps[s] = max(deps.get(s, 0), v)
        for b in writes:
            if b.last_w is not None:
                s, v, e2 = b.last_w
                if not (eng == "pe" and e2 == "pe"):
                    deps[s] = max(deps.get(s, 0), v)
            for (s, v, e2) in b.readers:
                if not (eng == "pe" and e2 == "pe"):
                    deps[s] = max(deps.get(s, 0), v)
        waits = []
        wd = self.waited[eng]
        for s, v in deps.items():
            if wd.get(id(s), 0) < v:
                wd[id(s)] = v
                waits.append((s, v))
        if dma:
            b = writes[0]
            if b.dsem is None or b.dval >= SEM_ROT:
                b.dsem, b.dval = self.sem_pool.pop()
            b.dval += dinc
            tok = (b.dsem, b.dval, "dma")
            csem, inc = b.dsem, dinc
        else:
            if self.cnt[eng] >= SEM_ROT:
                self.sem[eng] = self.eng_spare.pop()
                self.cnt[eng] = 0
            self.cnt[eng] += 1
            tok = (self.sem[eng], self.cnt[eng], eng)
            self.last_tok[eng] = tok
            csem, inc = self.sem[eng], 1

        def run(e, fn=fn, waits=waits, csem=csem, inc=inc):
            for s, v in waits:
                e.wait_ge(s, v)
            fn(e).then_inc(csem, inc)

        self.ops[eng].append(run)
        self.nops += 1
        for b in reads:
            b.readers = [t for t in b.readers if t[0] is not tok[0]] + [tok]
        for b in writes:
            b.last_w = tok
            b.readers = []
            if dma:
                self.dma_bufs[id(b)] = b
        return tok

    def barrier(self):
        allw = [self.last_tok[e] for e in self.ENGS if self.last_tok[e] is not None]
        allw += [(b.dsem, b.dval, "dma") for b in self.dma_bufs.values() if b.dsem is not None]
        for eng in self.ENGS:
            waits = []
            wd = self.waited[eng]
            for s, v, e2 in allw:
                if e2 == eng and eng == "pe":
                    continue
                if wd.get(id(s), 0) < v:
                    wd[id(s)] = v
                    waits.append((s, v))

            def run(e, waits=waits):
                for s, v in waits:
                    e.wait_ge(s, v)
            self.ops[eng].append(run)

    def final_wait(self, eng, bufs):
        waits = []
        for b in bufs:
            if b.last_w is not None:
                waits.append((b.last_w[0], b.last_w[1]))

        def run(e, waits=waits):
            for s, v in waits:
                e.wait_ge(s, v)
        self.ops[eng].append(run)

    def emit(self):
        nc = self.nc
        with nc.Block() as block:
            @block.tensor
            def _(e):
                for f in self.ops["pe"]:
                    f(e)

            @block.scalar
            def _(e):
                for f in self.ops["act"]:
                    f(e)

            @block.vector
            def _(e):
                for f in self.ops["dve"]:
                    f(e)

            @block.gpsimd
            def _(e):
                for f in self.ops["pool"]:
                    f(e)

            @block.sync
            def _(e):
                for f in self.ops["sp"]:
                    f(e)


class Arena:
    BASE = 16512
    LIMIT = 229000

    def __init__(self, nc):
        self.nc = nc
        self.off = self.BASE
        self.n = 0
        self.peak = 0

    def alloc(self, name, shape, dt):
        esz = 2 if dt == BF16 else 4
        nbytes = esz * int(np.prod(shape[1:]))
        nbytes = (nbytes + 63) // 64 * 64
        assert self.off + nbytes <= self.LIMIT, (name, self.off, nbytes)
        self.n += 1
        t = self.nc.alloc_sbuf_tensor_at(f"{name}_{self.n}", list(shape), dt, offset=self.off)
        self.off += nbytes
        self.peak = max(self.peak, self.off)
        return t

    def mark(self):
        return self.off

    def release(self, m):
        self.off = m

class PS:
    def __init__(self, t, name):
        self.t = t
        self.b = Buf(name, psum=True)


import contextlib
import numpy as np

T = 4096
DM = 2048
NT = T // 128
NF_A = 768
NF_C = 1152
NTM = 1536
EPS_M = 1e-6


def declare_io_M(nc, sfx="", with_x=True):
    io = {}
    def din(name, shape):
        return nc.dram_tensor(name + sfx, list(shape), F32, kind="ExternalInput").ap()
    if with_x:
        io["x"] = din("x", [T, DM])
        io["x_rows"] = lambda t, xx=io["x"]: xx[t * 128:(t + 1) * 128, :]
    io["x_b"] = Buf("x_in")
    io["x_bf"] = lambda t, b=io["x_b"]: b
    io["g0T"] = din("g0T", [128, 16])
    io["w_fm"] = din("w_fm", [DM, NF_A + NF_C])
    io["w_tm"] = din("w_tm", [DM, NTM])
    io["biasT"] = din("biasT", [128, 36, 128])
    io["sgu_wT"] = din("sgu_wT", [128, 4, 128])
    io["sgu_bT"] = din("sgu_bT", [128, 4])
    io["sgu_ng"] = din("sgu_ng", [512])
    io["cp"] = din("cp", [128, 25])
    io["cf"] = din("cf", [3, 384])
    io["w_up"] = din("w_up", [64, 384])
    io["a_up"] = din("a_up", [64, 384])
    io["g_up"] = din("g_up", [256, 384])
    return io


def emit_M(S, nc, io, scr, out, psbig):
    ps = [PS(BankView(psbig[:, i * 512:(i + 1) * 512]), f"ps{i}") for i in range(8)]
    phase1(S, nc, io, scr, ps, S.ident)
    phase2(S, nc, io, scr, ps, out)
    S.op("sp", lambda e: e.dma_start(out=out["ssq"][0], in_=out["ssqA"][:]), reads=[out["ssqA_b"]], writes=[out["ssq_b"]], dma=True)
    phase3(S, nc, io, scr, ps, out)
    S.op("sp", lambda e: e.dma_start(out=out["ssq"][1], in_=out["ssqB"][:]), reads=[out["ssqB_b"]], writes=[out["ssq_b"]], dma=True)
    phase4(S, nc, io, scr, psbig, out)


def phase1(S, nc, io, scr, ps, ident):
    A = S.arena
    m0 = A.mark()
    if True:
        sb = A.alloc
        hT = sb("hT", [128, 16, T], BF16)
        hT_b = Buf("hT")
        g0T = sb("g0T_sb", [128, 16], F32)
        g0T_b = Buf("g0T")
        S.op("sp", lambda e: e.dma_start(out=g0T[:], in_=io["g0T"]), writes=[g0T_b], dma=True)
        m1 = A.mark()
        if True:
            sb2 = A.alloc
            xt = [sb2(f"xt{i}", [128, DM], F32) for i in range(2)]
            xt_b = [Buf(f"xt{i}") for i in range(2)]
            xs = [sb2(f"xs{i}", [128, DM], BF16) for i in range(2)]
            xs_b = [Buf(f"xs{i}") for i in range(2)]
            junk = sb2("junk", [128, DM], BF16)
            junk_b = Buf("junk")
            st = [sb2(f"st{i}", [128, 4], F32) for i in range(2)]
            st_b = [Buf(f"st{i}") for i in range(2)]
            for t in range(NT):
                i = t % 2
                S.op("sp", lambda e, t=t, i=i: e.dma_start(out=xt[i][:], in_=io["x_rows"](t)),
                     reads=[io["x_b"]], writes=[xt_b[i]], dma=True)
                S.op("act", lambda e, i=i: e.activation(out=junk[:], in_=xt[i][:], func=AF.Square,
                                                        accum_out=st[i][:, 0:1]),
                     reads=[xt_b[i]], writes=[junk_b, st_b[i]])
                S.op("act", lambda e, i=i: e.activation(out=st[i][:, 1:2], in_=st[i][:, 0:1], func=AF.Sqrt,
                                                        scale=1.0 / DM, bias=EPS_M),
                     reads=[st_b[i]], writes=[st_b[i]])
                S.op("dve", lambda e, i=i: e.reciprocal(out=st[i][:, 2:3], in_=st[i][:, 1:2]),
                     reads=[st_b[i]], writes=[st_b[i]])
                S.op("dve", lambda e, i=i: e.tensor_scalar(out=xs[i][:], in0=xt[i][:], scalar1=st[i][:, 2:3],
                                                           scalar2=None, op0=ALU.mult),
                     reads=[xt_b[i], st_b[i]], writes=[xs_b[i]])
                pa, pb = ps[2 * i], ps[2 * i + 1]
                for c in range(16):
                    pp = pa if c < 8 else pb
                    S.op("pe", lambda e, c=c, pp=pp, i=i: e.transpose(
                        out=pp.t.ap().bitcast(BF16)[:, (c % 8) * 128:(c % 8 + 1) * 128],
                        in_=xs[i][:, c * 128:(c + 1) * 128], identity=ident[:]),
                        reads=[xs_b[i]], writes=[pp.b])
                for half, pp in enumerate((pa, pb)):
                    S.op("dve" if half == 0 else "pool" if False else "dve", lambda e, half=half, pp=pp, t=t: e.tensor_tensor(
                        out=hT[:, half * 8:(half + 1) * 8, t * 128:(t + 1) * 128],
                        in0=pp.t.ap().bitcast(BF16).rearrange("p (c n) -> p c n", c=8),
                        in1=g0T[:, half * 8:(half + 1) * 8].unsqueeze(2).to_broadcast([128, 8, 128]),
                        op=ALU.mult),
                        reads=[pp.b, g0T_b], writes=[hT_b])
        S.barrier()
        A.release(m1)
        if True:
            sb2 = A.alloc
            wj = [sb2(f"wj{i}", [128, 16, 128], BF16) for i in range(2)]
            wj_b = [Buf(f"wj{i}") for i in range(2)]
            zst16 = [sb2(f"zst16_{i}", [128, T], BF16) for i in range(2)]
            zst32 = [sb2(f"zst32_{i}", [128, T + 1], F32) for i in range(1)]
            zst16_b = [Buf(f"zst16_{i}") for i in range(2)]
            zst32_b = [Buf(f"zst32_{i}") for i in range(1)]
            wtm = [sb2(f"wtm{i}", [128, 16, 512], BF16) for i in range(1)]
            wtm_b = [Buf(f"wtm{i}") for i in range(1)]
            tst = [sb2(f"tst{i}", [128, 512], F32) for i in range(3)]
            tst_b = [Buf(f"tst{i}") for i in range(3)]
            S.op("pool", lambda e: e.memset(zst32[0][:, 0:1], 0.0), writes=[zst32_b[0]])
            zrow = sb2("zrow", [1, NTM], F32)
            zrow_b = Buf("zrow")
            S.op("pool", lambda e: e.memset(zrow[:], 0.0), writes=[zrow_b])
            S.op("sp", lambda e: e.dma_start(out=scr["ztm"][0:1, :], in_=zrow[:]), reads=[zrow_b], writes=[scr["ztm_b"]], dma=True)
            w_fm_v = io["w_fm"].rearrange("(kc p) n -> p kc n", p=128)
            w_tm_v = io["w_tm"].rearrange("(kc p) n -> p kc n", p=128)
            nchunks = (NF_A + NF_C) // 128
            pi = 0
            for j in range(nchunks):
                i = j % 2
                S.op("pool", lambda e, j=j, i=i: e.dma_start(out=wj[i][:], in_=w_fm_v[:, :, j * 128:(j + 1) * 128]),
                     writes=[wj_b[i]], dma=True)
                is_a = j < NF_A // 128
                if is_a:
                    zt, ztb = zst16[j % 2], zst16_b[j % 2]
                    zo = 0
                else:
                    zt, ztb = zst32[0], zst32_b[0]
                    zo = 1
                for tg in range(T // 512):
                    pp = ps[pi % 8]
                    pi += 1
                    for kc in range(16):
                        S.op("pe", lambda e, kc=kc, pp=pp, i=i, tg=tg: e.matmul(
                            pp.t[:, :], lhsT=wj[i][:, kc, :], rhs=hT[:, kc, tg * 512:(tg + 1) * 512],
                            start=(kc == 0), stop=(kc == 15)),
                            reads=[wj_b[i], hT_b], writes=[pp.b])
                    if tg % 2 == 0:
                        S.op("act", lambda e, pp=pp, zt=zt, tg=tg, zo=zo: e.copy(out=zt[:, zo + tg * 512:zo + (tg + 1) * 512], in_=pp.t[:, :]),
                             reads=[pp.b], writes=[ztb])
                    else:
                        S.op("dve", lambda e, pp=pp, zt=zt, tg=tg, zo=zo: e.tensor_copy(out=zt[:, zo + tg * 512:zo + (tg + 1) * 512], in_=pp.t[:, :]),
                             reads=[pp.b], writes=[ztb])
                if is_a:
                    S.op("sp", lambda e, j=j, zt=zt: e.dma_start(out=scr["zTa"][j * 128:(j + 1) * 128, :], in_=zt[:]),
                         reads=[ztb], writes=[scr["zTa_b"]], dma=True)
                else:
                    jj = j - NF_A // 128
                    S.op("sp", lambda e, jj=jj, zt=zt: e.dma_start(out=scr["zTc"][jj * 128:(jj + 1) * 128, :], in_=zt[:]),
                         reads=[ztb], writes=[scr["zTc_b"]], dma=True)
            k = 0
            for blk in range(NTM // 512):
                S.op("pool", lambda e, blk=blk: e.dma_start(out=wtm[0][:], in_=w_tm_v[:, :, blk * 512:(blk + 1) * 512]),
                     writes=[wtm_b[0]], dma=True)
                for t in range(NT):
                    pp = ps[pi % 8]
                    pi += 1
                    for kc in range(16):
                        S.op("pe", lambda e, kc=kc, pp=pp, t=t: e.matmul(
                            pp.t[:, :], lhsT=hT[:, kc, t * 128:(t + 1) * 128], rhs=wtm[0][:, kc, :],
                            start=(kc == 0), stop=(kc == 15)),
                            reads=[wtm_b[0], hT_b], writes=[pp.b])
                    ti = k % 3
                    k += 1
                    if t % 2 == 0:
                        S.op("act", lambda e, pp=pp, ti=ti: e.copy(out=tst[ti][:], in_=pp.t[:, :]),
                             reads=[pp.b], writes=[tst_b[ti]])
                    else:
                        S.op("dve", lambda e, pp=pp, ti=ti: e.tensor_copy(out=tst[ti][:], in_=pp.t[:, :]),
                             reads=[pp.b], writes=[tst_b[ti]])
                    S.op("sp", lambda e, ti=ti, t=t, blk=blk: e.dma_start(
                        out=scr["ztm"][1 + t * 128:1 + (t + 1) * 128, blk * 512:(blk + 1) * 512], in_=tst[ti][:]),
                        reads=[tst_b[ti]], writes=[scr["ztm_b"]], dma=True)
        S.barrier()
        A.release(m0)


class BankView:
    def __init__(self, ap):
        self._ap = ap

    def ap(self):
        return self._ap

    def __getitem__(self, k):
        return self._ap[k]


DILS = (1, 4, 16)


def phase2(S, nc, io, scr, ps, out):
    A = S.arena
    m0 = A.mark()
    sb = A.alloc
    biasT = sb("biasT", [128, 6 * 3 * 2, 128], F32)
    biasT_b = Buf("biasT")
    S.op("sp", lambda e: e.dma_start(out=biasT[:], in_=io["biasT"]), writes=[biasT_b], dma=True)
    qT = [sb(f"qT{i}", [128, T], BF16) for i in range(2)]
    kT = [sb(f"kT{i}", [128, T], BF16) for i in range(2)]
    qT_b = [Buf(f"qT{i}") for i in range(2)]
    kT_b = [Buf(f"kT{i}") for i in range(2)]
    Vd = [[sb(f"Vd{hh}_{p}", [128, 32, 65], BF16) for p in range(3)] for hh in range(2)]
    Vd_b = [[Buf(f"Vd{hh}_{p}") for p in range(3)] for hh in range(2)]
    for hh in range(2):
        for p in range(3):
            S.op("pool", lambda e, hh=hh, p=p: e.memset(Vd[hh][p][:, :, 64:65], 1.0), writes=[Vd_b[hh][p]])
    NTB = 4
    NPS = 4
    tt = [sb(f"att_t{i}", [128, 256], F32) for i in range(NTB)]
    tt_b = [Buf(f"att_t{i}") for i in range(NTB)]
    PT = [sb(f"att_PT{i}", [128, 256], BF16) for i in range(NTB)]
    PT_b = [Buf(f"att_PT{i}") for i in range(NTB)]
    Oacc = [sb(f"Oacc{p}", [128, 32, 65], F32) for p in range(3)]
    Oacc_b = [Buf(f"Oacc{p}") for p in range(3)]
    Old = [sb(f"Old{p}", [128, 32, 65], F32) for p in range(3)]
    Old_b = [Buf(f"Old{p}") for p in range(3)]
    rec = sb("att_rec", [128, 32], F32)
    rec_b = Buf("att_rec")
    oah = sb("oah", [128, 32, 64], F32)
    oah_b = Buf("oah")
    sq = sb("att_sq", [128, 32, 64], F32)
    sq_b = Buf("att_sq")
    ssq1 = sb("att_ssq1", [128, 32], F32)
    ssq1_b = Buf("att_ssq1")
    ssqA = out["ssqA"]
    ssqA_b = out["ssqA_b"]
    S.op("pool", lambda e: e.memset(ssqA[:], 0.0), writes=[ssqA_b])
    psS = ps[0:4]
    psO = ps[4:6]
    blkc = 0
    for pair in range(3):
        i = pair % 2
        S.op("sp", lambda e, pair=pair, i=i: e.dma_start(out=qT[i][:], in_=scr["zTa"][pair * 128:(pair + 1) * 128, :]),
             reads=[scr["zTa_b"]], writes=[qT_b[i]], dma=True)
        S.op("sp", lambda e, pair=pair, i=i: e.dma_start(out=kT[i][:], in_=scr["zTa"][384 + pair * 128:384 + (pair + 1) * 128, :]),
             reads=[scr["zTa_b"]], writes=[kT_b[i]], dma=True)
        for hh in range(2):
            h = pair * 2 + hh
            p0 = 64 * hh
            for p, d in enumerate(DILS):
                S.op("pool", lambda e, hh=hh, p=p, d=d, h=h: e.dma_start(
                    out=Vd[hh][p][:, :, 0:64].rearrange("j (r n) c -> j r n c", r=d),
                    in_=scr["ztm"][1:T + 1, h * 64:(h + 1) * 64].rearrange("(n j r) c -> j r n c", j=128, r=d)),
                    reads=[scr["ztm_b"]], writes=[Vd_b[hh][p]], dma=True)
            items = []
            for p, d in enumerate(DILS):
                nblk = 32 // d
                for r in range(d):
                    for n in range(nblk):
                        items.append((p, d, r, n, r * nblk + n))

            def emit_scores(k, item):
                p, d, r, n, blk = item
                pS = psS[k % NPS]
                ti = k % NTB
                qs = n * 128 * d + r
                qsl = slice(qs, qs + 127 * d + 1, d) if d > 1 else slice(qs, qs + 128)
                bidx = (h * 3 + p) * 2
                if n > 0:
                    ks = (n - 1) * 128 * d + r
                    ksl = slice(ks, ks + 127 * d + 1, d) if d > 1 else slice(ks, ks + 128)
                    S.op("pe", lambda e, pS=pS, ksl=ksl, qsl=qsl, i=i, p0=p0: e.matmul(
                        pS.t[:, 0:128], lhsT=kT[i][p0:p0 + 64, ksl], rhs=qT[i][p0:p0 + 64, qsl], start=True, stop=True),
                        reads=[kT_b[i], qT_b[i]], writes=[pS.b])
                S.op("pe", lambda e, pS=pS, qsl=qsl, i=i, p0=p0: e.matmul(
                    pS.t[:, 128:256], lhsT=kT[i][p0:p0 + 64, qsl], rhs=qT[i][p0:p0 + 64, qsl], start=True, stop=True),
                    reads=[kT_b[i], qT_b[i]], writes=[pS.b])
                c0 = 0 if n > 0 else 128
                S.op("dve", lambda e, pS=pS, ti=ti, c0=c0, bidx=bidx: e.scalar_tensor_tensor(
                    out=tt[ti][:, c0:256], in0=pS.t[:, c0:256], scalar=0.125,
                    in1=biasT[:, bidx:bidx + 2, :].rearrange("p a q -> p (a q)")[:, c0:256],
                    op0=ALU.mult, op1=ALU.add),
                    reads=[pS.b, biasT_b], writes=[tt_b[ti]])
                S.op("act", lambda e, ti=ti, c0=c0: e.activation(out=PT[ti][:, c0:256], in_=tt[ti][:, c0:256], func=AF.Exp),
                     reads=[tt_b[ti]], writes=[PT_b[ti]])

            def emit_pv(k, item):
                p, d, r, n, blk = item
                ti = k % NTB
                slot = blk % 4
                pO = psO[(blk // 4) % 2]
                if n > 0:
                    S.op("pe", lambda e, pO=pO, slot=slot, ti=ti, p=p, blk=blk, hh=hh: e.matmul(
                        pO.t[:, slot * 65:(slot + 1) * 65], lhsT=PT[ti][:, 0:128], rhs=Vd[hh][p][:, blk - 1, :],
                        start=True, stop=False),
                        reads=[PT_b[ti], Vd_b[hh][p]], writes=[pO.b])
                S.op("pe", lambda e, pO=pO, slot=slot, ti=ti, p=p, blk=blk, n=n, hh=hh: e.matmul(
                    pO.t[:, slot * 65:(slot + 1) * 65], lhsT=PT[ti][:, 128:256], rhs=Vd[hh][p][:, blk, :],
                    start=(n == 0), stop=True),
                    reads=[PT_b[ti], Vd_b[hh][p]], writes=[pO.b])
                if slot == 3:
                    b0 = blk - 3
                    if (blk // 4) % 2 == 0:
                        S.op("act", lambda e, pO=pO, p=p, b0=b0: e.copy(
                            out=Oacc[p][:, b0:b0 + 4, :].rearrange("p a c -> p (a c)"), in_=pO.t[:, 0:260]),
                            reads=[pO.b], writes=[Oacc_b[p]])
                    else:
                        S.op("dve", lambda e, pO=pO, p=p, b0=b0: e.tensor_copy(
                            out=Oacc[p][:, b0:b0 + 4, :].rearrange("p a c -> p (a c)"), in_=pO.t[:, 0:260]),
                            reads=[pO.b], writes=[Oacc_b[p]])
                if blk == 31:
                    S.op("sp", lambda e, p=p, d=d: e.dma_start(
                        out=scr["oa_scr"][p].rearrange("(n j r) c -> j r n c", j=128, r=d),
                        in_=Oacc[p][:].rearrange("j (r n) c -> j r n c", r=d)),
                        reads=[Oacc_b[p]], writes=[scr["oa_scr_b"][p]], dma=True)

            LAG = 3
            for step in range(len(items) + LAG):
                if step < len(items):
                    emit_scores(blkc + step, items[step])
                if step >= LAG:
                    emit_pv(blkc + step - LAG, items[step - LAG])
            blkc += len(items)
            for p in range(3):
                S.op("sp", lambda e, p=p: e.dma_start(
                    out=Old[p][:], in_=scr["oa_scr"][p].rearrange("(tb j) c -> j tb c", j=128)),
                    reads=[scr["oa_scr_b"][p]], writes=[Old_b[p]], dma=True)
            S.op("dve", lambda e: e.tensor_tensor(out=Old[0][:], in0=Old[0][:], in1=Old[1][:], op=ALU.add),
                 reads=[Old_b[0], Old_b[1]], writes=[Old_b[0]])
            S.op("dve", lambda e: e.tensor_tensor(out=Old[0][:], in0=Old[0][:], in1=Old[2][:], op=ALU.add),
                 reads=[Old_b[0], Old_b[2]], writes=[Old_b[0]])
            S.op("dve", lambda e: e.reciprocal(out=rec[:], in_=Old[0][:, :, 64]),
                 reads=[Old_b[0]], writes=[rec_b])
            S.op("dve", lambda e: e.tensor_tensor(out=oah[:], in0=Old[0][:, :, 0:64],
                                                  in1=rec[:].unsqueeze(2).to_broadcast([128, 32, 64]), op=ALU.mult),
                 reads=[Old_b[0], rec_b], writes=[oah_b])
            S.op("sp", lambda e, h=h: e.dma_start(
                out=out["o"][:, h * 64:(h + 1) * 64].rearrange("(tb j) c -> j tb c", j=128), in_=oah[:]),
                reads=[oah_b], writes=[out["o_b"]], dma=True)
            S.op("pool", lambda e: e.tensor_tensor(out=sq[:], in0=oah[:], in1=oah[:], op=ALU.mult),
                 reads=[oah_b], writes=[sq_b])
            S.op("dve", lambda e: e.tensor_reduce(out=ssq1[:], in_=sq[:], axis=AX.X, op=ALU.add),
                 reads=[sq_b], writes=[ssq1_b])
            S.op("dve", lambda e: e.tensor_tensor(out=ssqA[:], in0=ssqA[:], in1=ssq1[:], op=ALU.add),
                 reads=[ssq1_b, ssqA_b], writes=[ssqA_b])
    S.barrier()
    A.release(m0)


def phase3(S, nc, io, scr, ps, out):
    A = S.arena
    m0 = A.mark()
    sb = A.alloc
    wT32 = sb("sgu_wT32", [128, 4, 128], F32)
    wT = sb("sgu_wT", [128, 4, 128], BF16)
    wT_b = Buf("sgu_wT")
    mask = sb("sgu_mask", [128, 128], F32)
    mask_b = Buf("sgu_mask")
    S.op("sp", lambda e: e.dma_start(out=wT32[:], in_=io["sgu_wT"]), writes=[wT_b], dma=True)
    S.op("pool", lambda e: e.memset(mask[:], 1.0), writes=[mask_b])
    S.op("pool", lambda e: e.affine_select(out=mask[:], in_=mask[:], pattern=[[1, 128]], base=0,
                                           channel_multiplier=-1, compare_op=ALU.is_ge, fill=0.0),
         reads=[mask_b], writes=[mask_b])
    S.op("dve", lambda e: e.tensor_tensor(out=wT[:], in0=wT32[:], in1=mask[:].unsqueeze(1).to_broadcast([128, 4, 128]),
                                          op=ALU.mult), reads=[wT_b, mask_b], writes=[wT_b])
    bs = sb("sgu_b", [128, 4], F32)
    bs_b = Buf("sgu_b")
    S.op("sp", lambda e: e.dma_start(out=bs[:], in_=io["sgu_bT"]), writes=[bs_b], dma=True)
    gain = sb("sgu_gain", [128, 512], F32)
    gain_b = Buf("sgu_gain")
    S.op("sp", lambda e: e.dma_start(out=gain[:], in_=io["sgu_ng"].partition_broadcast(128)), writes=[gain_b], dma=True)
    NB = 2
    G = 4
    g_t = [sb(f"sgu_g{i}", [128, G, 512], F32) for i in range(NB)]
    g_b = [Buf(f"sgu_g{i}") for i in range(NB)]
    u_t = [sb(f"sgu_u{i}", [128, G, 256], F32) for i in range(NB)]
    u_b = [Buf(f"sgu_u{i}") for i in range(NB)]
    sq = sb("sgu_sq", [128, G, 512], F32)
    sq_b = Buf("sgu_sq")
    gn = [sb(f"sgu_gn{i}", [128, G, 256], BF16) for i in range(NB)]
    gn_b = [Buf(f"sgu_gn{i}") for i in range(NB)]
    stt = [sb(f"sgu_st{i}", [128, 8, G], F32) for i in range(NB)]
    stt_b = [Buf(f"sgu_st{i}") for i in range(NB)]
    ob = [sb(f"sgu_ob{i}", [128, G, 256], F32) for i in range(NB)]
    ob_b = [Buf(f"sgu_ob{i}") for i in range(NB)]
    ssqB, ssqB_b = out["ssqB"], out["ssqB_b"]
    pidx = 0
    for tg in range(32 // G):
        i = tg % NB
        rows = slice(tg * G * 128, (tg + 1) * G * 128)
        zrows = slice(1 + tg * G * 128, 1 + (tg + 1) * G * 128)
        S.op("sp", lambda e, i=i, zrows=zrows: e.dma_start(
            out=g_t[i][:], in_=scr["ztm"][zrows, 640:1152].rearrange("(tb j) c -> j tb c", j=128)),
            reads=[scr["ztm_b"]], writes=[g_b[i]], dma=True)
        S.op("sp", lambda e, i=i, zrows=zrows: e.dma_start(
            out=u_t[i][:], in_=scr["ztm"][zrows, 384:640].rearrange("(tb j) c -> j tb c", j=128)),
            reads=[scr["ztm_b"]], writes=[u_b[i]], dma=True)
        S.op("act", lambda e, i=i: e.activation(out=g_t[i][:], in_=g_t[i][:], func=AF.Gelu_apprx_tanh),
             reads=[g_b[i]], writes=[g_b[i]])
        S.op("act", lambda e, i=i: e.activation(out=u_t[i][:], in_=u_t[i][:], func=AF.Gelu_apprx_tanh),
             reads=[u_b[i]], writes=[u_b[i]])
        st = stt[i]
        S.op("dve", lambda e, i=i, st=st: e.tensor_reduce(out=st[:, 0, :], in_=g_t[i][:], axis=AX.X, op=ALU.add),
             reads=[g_b[i]], writes=[stt_b[i]])
        S.op("pool", lambda e, i=i: e.tensor_tensor(out=sq[:], in0=g_t[i][:], in1=g_t[i][:], op=ALU.mult),
             reads=[g_b[i]], writes=[sq_b])
        S.op("dve", lambda e, st=st: e.tensor_reduce(out=st[:, 1, :], in_=sq[:], axis=AX.X, op=ALU.add),
             reads=[sq_b], writes=[stt_b[i]])
        S.op("dve", lambda e, st=st: e.tensor_scalar(out=st[:, 2, :], in0=st[:, 0, :], scalar1=1.0 / 512, scalar2=None, op0=ALU.mult),
             reads=[stt_b[i]], writes=[stt_b[i]])
        S.op("dve", lambda e, st=st: e.tensor_tensor(out=st[:, 3, :], in0=st[:, 2, :], in1=st[:, 2, :], op=ALU.mult),
             reads=[stt_b[i]], writes=[stt_b[i]])
        S.op("dve", lambda e, st=st: e.scalar_tensor_tensor(out=st[:, 4, :], in0=st[:, 1, :], scalar=1.0 / 512, in1=st[:, 3, :],
                                                            op0=ALU.mult, op1=ALU.subtract),
             reads=[stt_b[i]], writes=[stt_b[i]])
        S.op("act", lambda e, st=st: e.activation(out=st[:, 5, :], in_=st[:, 4, :], func=AF.Sqrt, bias=EPS_M, scale=1.0),
             reads=[stt_b[i]], writes=[stt_b[i]])
        S.op("dve", lambda e, st=st: e.reciprocal(out=st[:, 6, :], in_=st[:, 5, :]),
             reads=[stt_b[i]], writes=[stt_b[i]])
        S.op("dve", lambda e, i=i, st=st: e.tensor_tensor(
            out=g_t[i][:, :, 0:256], in0=g_t[i][:, :, 0:256], in1=st[:, 2, :].unsqueeze(2).to_broadcast([128, G, 256]), op=ALU.subtract),
            reads=[g_b[i], stt_b[i]], writes=[g_b[i]])
        S.op("dve", lambda e, i=i, st=st: e.tensor_tensor(
            out=g_t[i][:, :, 0:256], in0=g_t[i][:, :, 0:256], in1=st[:, 6, :].unsqueeze(2).to_broadcast([128, G, 256]), op=ALU.mult),
            reads=[g_b[i], stt_b[i]], writes=[g_b[i]])
        S.op("pool", lambda e, i=i: e.tensor_tensor(
            out=gn[i][:], in0=g_t[i][:, :, 0:256], in1=gain[:, 0:256].unsqueeze(1).to_broadcast([128, G, 256]), op=ALU.mult),
            reads=[g_b[i], gain_b], writes=[gn_b[i]])
        for gi in range(4):
            pp = ps[pidx % 4]
            pidx += 1
            S.op("pe", lambda e, pp=pp, gi=gi, i=i: e.matmul(
                pp.t[:, 0:G * 64], lhsT=wT[:, gi, :], rhs=gn[i][:, :, gi * 64:(gi + 1) * 64], start=True, stop=True),
                reads=[wT_b, gn_b[i]], writes=[pp.b])
            S.op("dve", lambda e, pp=pp, gi=gi, i=i: e.scalar_tensor_tensor(
                out=ob[i][:, :, gi * 64:(gi + 1) * 64], in0=pp.t[:, 0:G * 64].rearrange("p (a c) -> p a c", a=G),
                scalar=bs[:, gi:gi + 1], in1=u_t[i][:, :, gi * 64:(gi + 1) * 64], op0=ALU.add, op1=ALU.mult),
                reads=[pp.b, bs_b, u_b[i]], writes=[ob_b[i]])
        S.op("sp", lambda e, i=i, rows=rows: e.dma_start(
            out=out["o"][rows, 384:640].rearrange("(tb j) c -> j tb c", j=128), in_=ob[i][:]),
            reads=[ob_b[i]], writes=[out["o_b"]], dma=True)
        S.op("pool", lambda e, i=i: e.tensor_tensor(out=sq[:, :, 0:256], in0=ob[i][:], in1=ob[i][:], op=ALU.mult),
             reads=[ob_b[i]], writes=[sq_b])
        S.op("dve", lambda e, tg=tg: e.tensor_reduce(out=ssqB[:, tg * G:(tg + 1) * G], in_=sq[:, :, 0:256], axis=AX.X, op=ALU.add),
             reads=[sq_b], writes=[ssqB_b])
    S.barrier()
    A.release(m0)


CDEC = 0.6065306597126334
GN_EPS = 64e-5
INV_DT = BF16
SEG = 1024
NCH = SEG // 128


P4_STOP = 99


class _StopPhase(Exception):
    pass


def phase4(S, nc, io, scr, psbig, out):
    try:
        _phase4(S, nc, io, scr, psbig, out)
    except _StopPhase:
        pass
    S.barrier()


def _phase4(S, nc, io, scr, psbig, out):
    A = S.arena
    m0 = A.mark()
    sb = A.alloc
    ident, identf, ident_b = S.ident, S.identf, S.ident_b
    zTc, ztm = scr["zTc"], scr["ztm"]

    bankbufs = {}

    def bankbuf(key):
        if key not in bankbufs:
            bankbufs[key] = Buf(f"psbank{key}", psum=True)
        return bankbufs[key]

    class R:
        def __init__(self, lo, hi, name, key):
            self.ap = psbig[:, lo:hi]
            self.b = bankbuf(key)
    class R2:
        def __init__(self, b0, lo, hi, name):
            self.b = bankbuf(b0)
            self.ap3 = psbig[:, b0 * 512:(b0 + 2) * 512].rearrange("p (h q) -> p h q", h=2)[:, :, lo:hi]
            self.h = [psbig[:, (b0 + hh) * 512 + lo:(b0 + hh) * 512 + hi] for hh in range(2)]
    psA = R2(0, 0, 512, "psA")
    psN_s = [R2(2, 0, 128, "psN0"), R2(4, 0, 128, "psN1")]
    psD_s = [R2(2, 128, 256, "psD0"), R2(4, 128, 256, "psD1")]
    psR_s = [R2(2, 256, 512, "psR0"), R2(4, 256, 512, "psR1")]
    psDC_s = [R2(2, 128, 384, "psDC0"), R2(4, 128, 384, "psDC1")]
    psRHS = R2(0, 0, 64, "psRHS")
    psSA = R2(0, 64, 128, "psSA")
    psY = R2(0, 128, 192, "psY")
    psDS = R2(0, 192, 256, "psDS")
    psP = [R(3072, 4096, "psP0", 6), R(3072 + 256, 3072 + 512, "psP1", 6)]

    cp = sb("cp", [128, 25], F32); cp_b = Buf("cp")
    S.op("sp", lambda e: e.dma_start(out=cp[:], in_=io["cp"]), writes=[cp_b], dma=True)
    cf = sb("cf", [128, 3, 384], F32); cf_b = Buf("cf")
    S.op("sp", lambda e: e.dma_start(out=cf[:], in_=io["cf"].partition_broadcast(128)), writes=[cf_b], dma=True)
    omka = sb("omka", [128, 3], F32); omka_b = Buf("omka")
    S.op("dve", lambda e: e.tensor_scalar(out=omka[:], in0=cp[:, 15:18], scalar1=-1.0, scalar2=1.0, op0=ALU.mult, op1=ALU.add),
         reads=[cp_b], writes=[omka_b])
    w_up = sb("w_up", [64, 384], BF16); a_up = sb("a_up", [64, 384], BF16); g_up = sb("g_up", [128, 2, 384], BF16)
    wts_b = Buf("rwkv_wts")
    S.op("pool", lambda e: e.dma_start(out=w_up[:], in_=io["w_up"]), writes=[wts_b], dma=True)
    S.op("pool", lambda e: e.dma_start(out=a_up[:], in_=io["a_up"]), writes=[wts_b], dma=True)
    S.op("pool", lambda e: e.dma_start(out=g_up[:], in_=io["g_up"].rearrange("(kc p) n -> p kc n", p=128)), writes=[wts_b], dma=True)
    bones = sb("bones", [128, 128], F32); bind = sb("bind", [128, 2], F32); cst_b = Buf("rwkv_cst")
    S.op("pool", lambda e: e.memset(bones[:], 0.0), writes=[cst_b])
    S.op("pool", lambda e: e.memset(bones[0:64, 0:64], 1.0), writes=[cst_b])
    S.op("pool", lambda e: e.memset(bones[64:128, 64:128], 1.0), writes=[cst_b])
    S.op("pool", lambda e: e.memset(bind[:], 0.0), writes=[cst_b])
    S.op("pool", lambda e: e.memset(bind[0:64, 0:1], 1.0), writes=[cst_b])
    S.op("pool", lambda e: e.memset(bind[64:128, 1:2], 1.0), writes=[cst_b])
    rmask = sb("rmask", [128, SEG], F32)
    S.op("pool", lambda e: e.memset(rmask[:], 1.0), writes=[cst_b])
    S.op("pool", lambda e: e.memset(rmask[:].rearrange("p (c j) -> p c j", j=128)[:, :, 0:1], 0.0), writes=[cst_b])
    mU = sb("mU", [128, 4, 128], F32)
    mL = sb("mL", [128, 128], F32)
    S.op("pool", lambda e: e.memset(mU[:], 1.0), writes=[cst_b])
    S.op("pool", lambda e: e.memset(mL[:], 1.0), writes=[cst_b])
    for q in range(4):
        base = -1 if q % 2 == 0 else 0
        S.op("pool", lambda e, q=q, base=base: e.affine_select(out=mU[:, q, :], in_=mU[:, q, :], pattern=[[1, 128]], base=base,
                                                               channel_multiplier=-1, compare_op=ALU.is_ge, fill=0.0),
             reads=[cst_b], writes=[cst_b])
    S.op("pool", lambda e: e.affine_select(out=mL[:], in_=mL[:], pattern=[[-1, 128]], base=-1,
                                           channel_multiplier=1, compare_op=ALU.is_ge, fill=0.0),
         reads=[cst_b], writes=[cst_b])
    identI = sb("identI", [128, 128], INV_DT)
    S.op("pool", lambda e: e.tensor_copy(out=identI[:], in_=identf[:]), reads=[ident_b], writes=[cst_b])

    TW = sb("TW", [64, T], BF16); AL = sb("AL", [64, T], BF16); SGG = sb("SGG", [128, 2, T], BF16)
    lora_b = Buf("lora")
    m1 = A.mark()
    la = [sb(f"lo_a{i}", [128, SEG], F32) for i in range(2)]
    lb = [sb(f"lo_b{i}", [128, SEG], F32) for i in range(2)]
    la_b = [Buf(f"lo_a{i}") for i in range(2)]
    lb_b = [Buf(f"lo_b{i}") for i in range(2)]
    k = 0
    for (row0, np_, mucol, kind) in ((768, 64, 21, "w"), (832, 64, 22, "a"), (896, 128, 23, "g0"), (1024, 128, 24, "g1")):
        for sg in range(T // SEG):
            i = k % 2
            k += 1
            t0 = sg * SEG
            S.op("sp", lambda e, i=i, row0=row0, np_=np_, t0=t0: e.dma_start(out=la[i][0:np_, :], in_=zTc[row0:row0 + np_, t0 + 1:t0 + 1 + SEG]),
                 reads=[scr["zTc_b"]], writes=[la_b[i]], dma=True)
            S.op("sp", lambda e, i=i, row0=row0, np_=np_, t0=t0: e.dma_start(out=lb[i][0:np_, :], in_=zTc[row0:row0 + np_, t0:t0 + SEG]),
                 reads=[scr["zTc_b"]], writes=[lb_b[i]], dma=True)
            S.op("dve", lambda e, i=i, np_=np_: e.tensor_tensor(out=lb[i][0:np_, :], in0=lb[i][0:np_, :], in1=la[i][0:np_, :], op=ALU.subtract),
                 reads=[la_b[i], lb_b[i]], writes=[lb_b[i]])
            S.op("dve", lambda e, i=i, np_=np_, mucol=mucol: e.scalar_tensor_tensor(
                out=la[i][0:np_, :], in0=lb[i][0:np_, :], scalar=cp[0:np_, mucol:mucol + 1], in1=la[i][0:np_, :], op0=ALU.mult, op1=ALU.add),
                reads=[la_b[i], lb_b[i], cp_b], writes=[la_b[i]])
            if kind == "w":
                S.op("act", lambda e, i=i, t0=t0: e.activation(out=TW[:, t0:t0 + SEG], in_=la[i][0:64, :], func=AF.Tanh),
                     reads=[la_b[i]], writes=[lora_b])
            elif kind == "a":
                S.op("act", lambda e, i=i, t0=t0: e.copy(out=AL[:, t0:t0 + SEG], in_=la[i][0:64, :]),
                     reads=[la_b[i]], writes=[lora_b])
            else:
                kc = 0 if kind == "g0" else 1
                S.op("act", lambda e, i=i, t0=t0, kc=kc: e.activation(out=SGG[:, kc, t0:t0 + SEG], in_=la[i][:, :], func=AF.Sigmoid),
                     reads=[la_b[i]], writes=[lora_b])
    S.barrier()
    A.release(m1)
    if P4_STOP <= 1:
        raise _StopPhase()

    def t32(name, n=1):
        return [sb(f"{name}{i}", [128, SEG], F32) for i in range(n)], [Buf(f"{name}{i}") for i in range(n)]
    Rr, Rr_b = t32("Rr"); Rp, Rp_b = t32("Rp"); Kk, Kk_b = t32("Kk"); Kp, Kp_b = t32("Kp")
    SGt, SG_b = t32("SGt"); CUM, CUM_b = t32("CUM"); WINC, WINC_b = t32("WINC", 2); WINV, WINV_b = t32("WINV"); WEXC, WEXC_b = t32("WEXC")
    AAt, AA_b = t32("AAt"); KKt, KK_b = t32("KKt"); TMPt, TMP_b = t32("TMPt"); K2t, K2_b = t32("K2t"); RKt, RK_b = t32("RKt")
    QA = [sb(f"QA{i}", [128, NCH, 2, 128], BF16) for i in range(2)]; QA_b = [Buf(f"QA{i}") for i in range(2)]
    KB = [sb(f"KB{i}", [128, NCH, 2, 128], BF16) for i in range(2)]; KB_b = [Buf(f"KB{i}") for i in range(2)]
    Btok = [sb(f"Btok{i}", [128, NCH, 128], BF16) for i in range(2)]; Btok_b = [Buf(f"Btok{i}") for i in range(2)]
    Ktok = [sb(f"Ktok{i}", [128, NCH, 128], BF16) for i in range(2)]; Ktok_b = [Buf(f"Ktok{i}") for i in range(2)]
    V32 = [sb(f"V32_{i}", [128, NCH, 128], F32) for i in range(2)]; V32_b = [Buf(f"V32_{i}") for i in range(2)]
    Vp = sb("Vp", [128, NCH, 128], F32); Vp_b = Buf("Vp")
    V16 = [sb(f"V16_{i}", [128, NCH, 128], BF16) for i in range(2)]; V16_b = [Buf(f"V16_{i}") for i in range(2)]
    Gt = [sb(f"Gt{i}", [128, NCH, 128], F32) for i in range(2)]; Gt_b = [Buf(f"Gt{i}") for i in range(2)]
    BS = [sb(f"BS{i}", [128, NCH, 2], F32) for i in range(2)]; BS_b = [Buf(f"BS{i}") for i in range(2)]
    Yall = [sb(f"Yall{i}", [128, NCH, 128], F32) for i in range(2)]; Yall_b = [Buf(f"Yall{i}") for i in range(2)]
    Ysq = sb("Ysq", [128, NCH, 128], F32); Ysq_b = Buf("Ysq")
    gst = sb("gst", [128, 8, NCH * 2], F32); gst_b = Buf("gst")
    DCU = [[sb(f"DCU{s}_{i}", [128, 2, 384], INV_DT) for i in range(2)] for s in range(2)]
    CU = [[DCU[s][i][:, :, 128:384] for i in range(2)] for s in range(2)]
    CU_b = [[Buf(f"CU{s}_{i}") for i in range(2)] for s in range(2)]
    Dm = [[DCU[s][i][:, :, 0:128] for i in range(2)] for s in range(2)]
    Dm_b = [[Buf(f"Dm{s}_{i}") for i in range(2)] for s in range(2)]
    SC = [sb(f"SC{s}", [128, 2, 384], BF16) for s in range(2)]; SC_b = [Buf(f"SC{s}") for s in range(2)]
    Ufin = [sb(f"Ufin{s}", [128, 2, 128], BF16) for s in range(2)]; Ufin_b = [Buf(f"Ufin{s}") for s in range(2)]
    RH = [sb(f"RH{s}", [128, 2, 64], BF16) for s in range(2)]; RH_b = [Buf(f"RH{s}") for s in range(2)]
    SA = [sb(f"SA{s}", [128, 2, 64], BF16) for s in range(2)]; SA_b = [Buf(f"SA{s}") for s in range(2)]
    S32 = sb("S32", [128, 64], F32); S16 = sb("S16", [128, 64], BF16); ST = sb("STtmp", [128, 64], F32)
    S32_b = Buf("S32"); S16_b = Buf("S16"); ST_b = Buf("ST")

    units = [(pr, sg) for pr in range(3) for sg in range(T // SEG)]
    pending = []
    real_op = S.op

    def rec_op(*a_, **k_):
        pending.append((a_, k_))

    def drain(n):
        for _ in range(min(n, len(pending))):
            a_, k_ = pending.pop(0)
            real_op(*a_, **k_)

    def prep(pr, sg, bi):
        t0 = sg * SEG
        r0 = pr * 128
        S.op("sp", lambda e, r0=r0, t0=t0: e.dma_start(out=Rr[0][:], in_=zTc[r0:r0 + 128, t0 + 1:t0 + 1 + SEG]), reads=[scr["zTc_b"]], writes=[Rr_b[0]], dma=True)
        S.op("sp", lambda e, r0=r0, t0=t0: e.dma_start(out=Rp[0][:], in_=zTc[r0:r0 + 128, t0:t0 + SEG]), reads=[scr["zTc_b"]], writes=[Rp_b[0]], dma=True)
        S.op("sp", lambda e, r0=r0, t0=t0: e.dma_start(out=Kk[0][:], in_=zTc[384 + r0:384 + r0 + 128, t0 + 1:t0 + 1 + SEG]), reads=[scr["zTc_b"]], writes=[Kk_b[0]], dma=True)
        S.op("sp", lambda e, r0=r0, t0=t0: e.dma_start(out=Kp[0][:], in_=zTc[384 + r0:384 + r0 + 128, t0:t0 + SEG]), reads=[scr["zTc_b"]], writes=[Kp_b[0]], dma=True)
        vrows = slice(1 + t0, 1 + t0 + SEG)
        vrows_p = slice(t0, t0 + SEG)
        vc = slice(1152 + r0, 1152 + r0 + 128)
        S.op("sp", lambda e, bi=bi, vrows=vrows, vc=vc: e.dma_start(out=V32[bi][:], in_=ztm[vrows, vc].rearrange("(tb j) c -> j tb c", j=128)),
             reads=[scr["ztm_b"]], writes=[V32_b[bi]], dma=True)
        S.op("sp", lambda e, vrows_p=vrows_p, vc=vc: e.dma_start(out=Vp[:], in_=ztm[vrows_p, vc].rearrange("(tb j) c -> j tb c", j=128)),
             reads=[scr["ztm_b"]], writes=[Vp_b], dma=True)
        S.op("pool", lambda e: e.tensor_tensor(out=Rp[0][:], in0=Rp[0][:], in1=Rr[0][:], op=ALU.subtract), reads=[Rr_b[0], Rp_b[0]], writes=[Rp_b[0]])
        S.op("dve", lambda e, pr=pr: e.scalar_tensor_tensor(out=Rr[0][:], in0=Rp[0][:], scalar=cp[:, 0 + pr:1 + pr], in1=Rr[0][:], op0=ALU.mult, op1=ALU.add),
             reads=[Rr_b[0], Rp_b[0], cp_b], writes=[Rr_b[0]])
        S.op("pool", lambda e: e.tensor_tensor(out=Kp[0][:], in0=Kp[0][:], in1=Kk[0][:], op=ALU.subtract), reads=[Kk_b[0], Kp_b[0]], writes=[Kp_b[0]])
        S.op("dve", lambda e, pr=pr: e.scalar_tensor_tensor(out=Kk[0][:], in0=Kp[0][:], scalar=cp[:, 3 + pr:4 + pr], in1=Kk[0][:], op0=ALU.mult, op1=ALU.add),
             reads=[Kk_b[0], Kp_b[0], cp_b], writes=[Kk_b[0]])
        S.op("pool", lambda e, bi=bi: e.tensor_tensor(out=Vp[:], in0=Vp[:], in1=V32[bi][:], op=ALU.subtract), reads=[Vp_b, V32_b[bi]], writes=[Vp_b])
        S.op("pool", lambda e, pr=pr: e.tensor_tensor(out=Vp[:], in0=Vp[:], in1=cf[:, 0, pr * 128:(pr + 1) * 128].unsqueeze(1).to_broadcast([128, NCH, 128]), op=ALU.mult),
             reads=[Vp_b, cf_b], writes=[Vp_b])
        S.op("pool", lambda e, bi=bi: e.tensor_tensor(out=V32[bi][:], in0=V32[bi][:], in1=Vp[:], op=ALU.add), reads=[Vp_b, V32_b[bi]], writes=[V32_b[bi]])
        S.op("act", lambda e, bi=bi: e.copy(out=V16[bi][:], in_=V32[bi][:]), reads=[V32_b[bi]], writes=[V16_b[bi]])
        for hf in range(2):
            S.op("pe", lambda e, hf=hf, pr=pr, t0=t0: e.matmul(psP[0].ap[:, hf * 512:(hf + 1) * 512], lhsT=w_up[:, pr * 128:(pr + 1) * 128],
                                                           rhs=TW[:, t0 + hf * 512:t0 + (hf + 1) * 512], start=True, stop=True),
                 reads=[wts_b, lora_b], writes=[psP[0].b])
        S.op("act", lambda e, pr=pr: e.activation(out=SGt[0][:], in_=psP[0].ap[:, :], func=AF.Sigmoid, bias=cp[:, 6 + pr:7 + pr], scale=1.0),
             reads=[psP[0].b, cp_b], writes=[SG_b[0]])
        S.op("dve", lambda e: e.tensor_tensor_scan(out=CUM[0][:], data0=rmask[:], data1=SGt[0][:], initial=0.0, op0=ALU.mult, op1=ALU.add),
             reads=[SG_b[0], cst_b], writes=[CUM_b[0]])
        wi = WINC[bi]
        S.op("act", lambda e, wi=wi: e.activation(out=wi[:], in_=CUM[0][:], func=AF.Exp, scale=-CDEC), reads=[CUM_b[0]], writes=[WINC_b[bi]])
        S.op("act", lambda e: e.activation(out=WINV[0][:], in_=CUM[0][:], func=AF.Exp, scale=CDEC), reads=[CUM_b[0]], writes=[WINV_b[0]])
        S.op("pool", lambda e: e.tensor_tensor(out=SGt[0][:], in0=CUM[0][:], in1=SGt[0][:], op=ALU.subtract), reads=[CUM_b[0], SG_b[0]], writes=[SG_b[0]])
        S.op("act", lambda e: e.activation(out=WEXC[0][:], in_=SGt[0][:], func=AF.Exp, scale=-CDEC), reads=[SG_b[0]], writes=[WEXC_b[0]])
        for hf in range(2):
            S.op("pe", lambda e, hf=hf, pr=pr, t0=t0: e.matmul(psP[0].ap[:, hf * 512:(hf + 1) * 512], lhsT=a_up[:, pr * 128:(pr + 1) * 128],
                                                           rhs=AL[:, t0 + hf * 512:t0 + (hf + 1) * 512], start=True, stop=True),
                 reads=[wts_b, lora_b], writes=[psP[0].b])
        S.op("act", lambda e, pr=pr: e.activation(out=AAt[0][:], in_=psP[0].ap[:, :], func=AF.Sigmoid, bias=cp[:, 9 + pr:10 + pr], scale=1.0),
             reads=[psP[0].b, cp_b], writes=[AA_b[0]])
        S.op("dve", lambda e, pr=pr: e.tensor_scalar(out=KKt[0][:], in0=Kk[0][:], scalar1=cp[:, 12 + pr:13 + pr], scalar2=None, op0=ALU.mult),
             reads=[Kk_b[0], cp_b], writes=[KK_b[0]])
        S.op("pool", lambda e: e.tensor_tensor(out=TMPt[0][:], in0=KKt[0][:], in1=KKt[0][:], op=ALU.mult), reads=[KK_b[0]], writes=[TMP_b[0]])
        for hf in range(2):
            S.op("pe", lambda e, hf=hf: e.matmul(psP[0].ap[:, hf * 512:(hf + 1) * 512], lhsT=bones[:], rhs=TMPt[0][:, hf * 512:(hf + 1) * 512], start=True, stop=True),
                 reads=[cst_b, TMP_b[0]], writes=[psP[0].b])
        S.op("act", lambda e: e.activation(out=TMPt[0][:], in_=psP[0].ap[:, :], func=AF.Sqrt), reads=[psP[0].b], writes=[TMP_b[0]])
        S.op("dve", lambda e: e.tensor_scalar(out=TMPt[0][:], in0=TMPt[0][:], scalar1=1e-12, scalar2=None, op0=ALU.max), reads=[TMP_b[0]], writes=[TMP_b[0]])
        S.op("dve", lambda e: e.reciprocal(out=TMPt[0][:], in_=TMPt[0][:]), reads=[TMP_b[0]], writes=[TMP_b[0]])
        S.op("pool", lambda e: e.tensor_tensor(out=KKt[0][:], in0=KKt[0][:], in1=TMPt[0][:], op=ALU.mult), reads=[KK_b[0], TMP_b[0]], writes=[KK_b[0]])
        S.op("dve", lambda e, pr=pr: e.tensor_scalar(out=TMPt[0][:], in0=AAt[0][:], scalar1=cp[:, 15 + pr:16 + pr], scalar2=omka[:, pr:pr + 1], op0=ALU.mult, op1=ALU.add),
             reads=[AA_b[0], cp_b, omka_b, TMP_b[0]], writes=[TMP_b[0]])
        S.op("pool", lambda e: e.tensor_tensor(out=K2t[0][:], in0=Kk[0][:], in1=TMPt[0][:], op=ALU.mult), reads=[Kk_b[0], TMP_b[0]], writes=[K2_b[0]])
        qa, kb = QA[bi], KB[bi]
        S.op("dve", lambda e, qa=qa: e.scalar_tensor_tensor(out=qa[:, :, 0, :], in0=KKt[0][:].rearrange("p (c j) -> p c j", j=128), scalar=-1.0,
                                                             in1=WEXC[0][:].rearrange("p (c j) -> p c j", j=128), op0=ALU.mult, op1=ALU.mult),
             reads=[KK_b[0], WEXC_b[0]], writes=[QA_b[bi]])
        S.op("pool", lambda e, qa=qa, wi=wi: e.tensor_tensor(out=qa[:, :, 1, :], in0=Rr[0][:].rearrange("p (c j) -> p c j", j=128),
                                                             in1=wi[:].rearrange("p (c j) -> p c j", j=128), op=ALU.mult),
             reads=[Rr_b[0], WINC_b[bi]], writes=[QA_b[bi]])
        S.op("pool", lambda e: e.tensor_tensor(out=TMPt[0][:], in0=KKt[0][:], in1=AAt[0][:], op=ALU.mult), reads=[KK_b[0], AA_b[0], TMP_b[0]], writes=[TMP_b[0]])
        S.op("dve", lambda e, kb=kb: e.tensor_tensor(out=kb[:, :, 0, :], in0=TMPt[0][:].rearrange("p (c j) -> p c j", j=128),
                                                     in1=WINV[0][:].rearrange("p (c j) -> p c j", j=128), op=ALU.mult),
             reads=[TMP_b[0], WINV_b[0]], writes=[KB_b[bi]])
        S.op("pool", lambda e, kb=kb: e.tensor_tensor(out=kb[:, :, 1, :], in0=K2t[0][:].rearrange("p (c j) -> p c j", j=128),
                                                      in1=WINV[0][:].rearrange("p (c j) -> p c j", j=128), op=ALU.mult),
             reads=[K2_b[0], WINV_b[0]], writes=[KB_b[bi]])
        S.op("dve", lambda e, pr=pr: e.scalar_tensor_tensor(out=RKt[0][:], in0=Rr[0][:], scalar=cp[:, 18 + pr:19 + pr], in1=K2t[0][:], op0=ALU.mult, op1=ALU.mult),
             reads=[Rr_b[0], K2_b[0], cp_b], writes=[RK_b[0]])
        for c in range(NCH):
            S.op("pe", lambda e, c=c: e.matmul(psP[1].ap[:, c * 2:c * 2 + 2], lhsT=RKt[0][:, c * 128:(c + 1) * 128], rhs=bind[:], start=True, stop=True),
                 reads=[RK_b[0], cst_b], writes=[psP[1].b])
        S.op("dve", lambda e, bi=bi: e.tensor_copy(out=BS[bi][:].rearrange("p c h -> p (c h)"), in_=psP[1].ap[:, 0:NCH * 2]), reads=[psP[1].b], writes=[BS_b[bi]])
        for c in range(NCH):
            for kc in range(2):
                S.op("pe", lambda e, c=c, kc=kc, pr=pr, t0=t0: e.matmul(psP[0].ap[:, c * 128:(c + 1) * 128], lhsT=SGG[:, kc, t0 + c * 128:t0 + (c + 1) * 128],
                                                                     rhs=g_up[:, kc, pr * 128:(pr + 1) * 128], start=(kc == 0), stop=(kc == 1)),
                     reads=[lora_b, wts_b], writes=[psP[0].b])
        S.op("act", lambda e, bi=bi: e.copy(out=Gt[bi][:].rearrange("p c n -> p (c n)"), in_=psP[0].ap[:, :]), reads=[psP[0].b], writes=[Gt_b[bi]])
        psTb = psP[0].ap.bitcast(BF16)
        for c in range(NCH):
            S.op("pe", lambda e, c=c, kb=kb: e.transpose(out=psTb[:, c * 128:(c + 1) * 128], in_=kb[:, c, 0, :], identity=ident[:]),
                 reads=[KB_b[bi], ident_b], writes=[psP[0].b])
            S.op("pe", lambda e, c=c, kb=kb: e.transpose(out=psTb[:, 1024 + c * 128:1024 + (c + 1) * 128], in_=kb[:, c, 1, :], identity=ident[:]),
                 reads=[KB_b[bi], ident_b], writes=[psP[0].b])
        S.op("dve", lambda e, bi=bi: e.tensor_copy(out=Btok[bi][:].rearrange("p c n -> p (c n)"), in_=psTb[:, 0:1024]), reads=[psP[0].b], writes=[Btok_b[bi]])
        S.op("act", lambda e, bi=bi: e.copy(out=Ktok[bi][:].rearrange("p c n -> p (c n)"), in_=psTb[:, 1024:2048]), reads=[psP[0].b], writes=[Ktok_b[bi]])


    def chunks_and_final(pr, sg, bi, defer_final):
        t0 = sg * SEG
        r0 = pr * 128
        wi = WINC[bi]
        qa, kb = QA[bi], KB[bi]
        if sg == 0:
            S.op("dve", lambda e: e.memset(S32[:], 0.0), writes=[S32_b])
            S.op("dve", lambda e: e.memset(S16[:], 0.0), writes=[S16_b])
        def unit_scores(c, s_):
            psN, psD, psR = psN_s[s_], psD_s[s_], psR_s[s_]
            for hh in range(2):
                p0 = 64 * hh
                rhsQA = qa[p0:p0 + 64, c, :, :].rearrange("p a j -> p (a j)")
                S.op("pe", lambda e, hh=hh, p0=p0, rhsQA=rhsQA, kb=kb, c=c: e.matmul(psA.h[hh][:, 0:256], lhsT=kb[p0:p0 + 64, c, 0, :], rhs=rhsQA, start=True, stop=True),
                     reads=[KB_b[bi], QA_b[bi]], writes=[psA.b])
                S.op("pe", lambda e, hh=hh, p0=p0, rhsQA=rhsQA, kb=kb, c=c: e.matmul(psA.h[hh][:, 256:512], lhsT=kb[p0:p0 + 64, c, 1, :], rhs=rhsQA, start=True, stop=True),
                     reads=[KB_b[bi], QA_b[bi]], writes=[psA.b])
                S.op("pe", lambda e, hh=hh, p0=p0, qa=qa, kb=kb, c=c: e.matmul(psN.h[hh][:, :], lhsT=qa[p0:p0 + 64, c, 0, :], rhs=kb[p0:p0 + 64, c, 0, :], start=True, stop=True),
                     reads=[KB_b[bi], QA_b[bi]], writes=[psN.b])
            psA3 = psA.ap3
            cu0 = CU[s_][0]
            S.op("dve", lambda e, cu0=cu0, psA3=psA3: e.tensor_tensor(out=cu0[:, :, 0:128], in0=psA3[:, :, 0:128], in1=mU[:, 0:1, :].to_broadcast([128, 2, 128]), op=ALU.mult),
                 reads=[psA.b, cst_b], writes=[CU_b[s_][0]])
            S.op("pool", lambda e, cu0=cu0: e.tensor_copy(out=cu0[:, :, 128:256], in_=identI[:].unsqueeze(1).to_broadcast([128, 2, 128])),
                 reads=[cst_b], writes=[CU_b[s_][0]])
            S.op("dve", lambda e, s_=s_, psA3=psA3: e.tensor_tensor(out=SC[s_][:], in0=psA3[:, :, 128:512],
                                                                   in1=mU[:, 1:4, :].rearrange("p a j -> p (a j)").unsqueeze(1).to_broadcast([128, 2, 384]), op=ALU.mult),
                 reads=[psA.b, cst_b], writes=[SC_b[s_]])
            S.op("dve", lambda e, s_=s_: e.tensor_tensor(out=Dm[s_][0], in0=psN.ap3,
                                                         in1=mL[:].unsqueeze(1).to_broadcast([128, 2, 128]), op=ALU.mult),
                 reads=[psN.b, cst_b], writes=[Dm_b[s_][0]])
        def unit_round(c, s_, rd):
            psN, psD, psR = psN_s[s_], psD_s[s_], psR_s[s_]
            cur, nxt = rd % 2, (rd + 1) % 2
            cuc, cun = CU[s_][cur], CU[s_][nxt]
            dc, dn = Dm[s_][cur], Dm[s_][nxt]
            last = (rd == 6)
            psR3 = psR.ap3
            for hh in range(2):
                if not last:
                    S.op("pe", lambda e, hh=hh, dc=dc, cuc=cuc: e.matmul(psR.h[hh][:, :], lhsT=dc[:, hh, :], rhs=cuc[:, hh, :], start=True, stop=True),
                         reads=[Dm_b[s_][cur], CU_b[s_][cur]], writes=[psR.b])
                    if rd < 5 or True:
                        S.op("pe", lambda e, hh=hh, dc=dc, cuc=cuc: e.matmul(psD.h[hh][:, :], lhsT=cuc[:, hh, 0:128], rhs=dc[:, hh, :], start=True, stop=True),
                             reads=[Dm_b[s_][cur], CU_b[s_][cur]], writes=[psD.b])
                else:
                    S.op("pe", lambda e, hh=hh, dc=dc, cuc=cuc: e.matmul(psR.h[hh][:, 128:256], lhsT=dc[:, hh, :], rhs=cuc[:, hh, 128:256], start=True, stop=True),
                         reads=[Dm_b[s_][cur], CU_b[s_][cur]], writes=[psR.b])
            if not last:
                S.op("act", lambda e, s_=s_, nxt=nxt: e.copy(out=DCU[s_][nxt][:, :, 0:256], in_=psDC_s[s_].ap3), reads=[psR.b], writes=[CU_b[s_][nxt], Dm_b[s_][nxt]])
                S.op("dve", lambda e, cun=cun, cuc=cuc, psR3=psR3: e.tensor_tensor(out=cun[:, :, 128:256], in0=psR3[:, :, 128:256], in1=cuc[:, :, 128:256], op=ALU.add),
                     reads=[psR.b, CU_b[s_][cur]], writes=[CU_b[s_][nxt]])
            else:
                S.op("dve", lambda e, s_=s_, cuc=cuc, psR3=psR3: e.tensor_tensor(out=Ufin[s_][:], in0=psR3[:, :, 128:256], in1=cuc[:, :, 128:256], op=ALU.add),
                     reads=[psR.b, CU_b[s_][cur]], writes=[Ufin_b[s_]])
        def unit_state(c, s_):
            for hh in range(2):
                p0 = 64 * hh
                S.op("pe", lambda e, hh=hh, p0=p0, qa=qa, c=c: e.matmul(psRHS.h[hh][:, :], lhsT=qa[p0:p0 + 64, c, 0, :], rhs=S16[p0:p0 + 64, :], start=True, stop=False),
                     reads=[QA_b[bi], S16_b], writes=[psRHS.b])
                S.op("pe", lambda e, hh=hh, p0=p0, s_=s_, c=c, bi=bi: e.matmul(psRHS.h[hh][:, :], lhsT=SC[s_][:, hh, 128:256], rhs=V16[bi][:, c, p0:p0 + 64], start=False, stop=True),
                     reads=[SC_b[s_], V16_b[bi]], writes=[psRHS.b])
            S.op("act", lambda e, s_=s_: e.copy(out=RH[s_][:], in_=psRHS.ap3), reads=[psRHS.b], writes=[RH_b[s_]])
            for hh in range(2):
                S.op("pe", lambda e, hh=hh, s_=s_: e.matmul(psSA.h[hh][:, :], lhsT=Ufin[s_][:, hh, :], rhs=RH[s_][:, hh, :], start=True, stop=True),
                     reads=[Ufin_b[s_], RH_b[s_]], writes=[psSA.b])
            S.op("dve", lambda e, s_=s_: e.tensor_copy(out=SA[s_][:], in_=psSA.ap3), reads=[psSA.b], writes=[SA_b[s_]])
            for hh in range(2):
                p0 = 64 * hh
                S.op("pe", lambda e, hh=hh, p0=p0, qa=qa, c=c: e.matmul(psY.h[hh][:, :], lhsT=qa[p0:p0 + 64, c, 1, :], rhs=S16[p0:p0 + 64, :], start=True, stop=False),
                     reads=[QA_b[bi], S16_b], writes=[psY.b])
                S.op("pe", lambda e, hh=hh, s_=s_: e.matmul(psY.h[hh][:, :], lhsT=SC[s_][:, hh, 0:128], rhs=SA[s_][:, hh, :], start=False, stop=False),
                     reads=[SC_b[s_], SA_b[s_]], writes=[psY.b])
                S.op("pe", lambda e, hh=hh, p0=p0, s_=s_, c=c, bi=bi: e.matmul(psY.h[hh][:, :], lhsT=SC[s_][:, hh, 256:384], rhs=V16[bi][:, c, p0:p0 + 64], start=False, stop=True),
                     reads=[SC_b[s_], V16_b[bi]], writes=[psY.b])
            S.op("act", lambda e, bi=bi, c=c: e.copy(out=Yall[bi][:, c, :].rearrange("p (h v) -> p h v", h=2), in_=psY.ap3), reads=[psY.b], writes=[Yall_b[bi]])
            for hh in range(2):
                p0 = 64 * hh
                S.op("pe", lambda e, hh=hh, p0=p0, s_=s_, c=c, bi=bi: e.matmul(psDS.h[hh][p0:p0 + 64, :], lhsT=Btok[bi][:, c, p0:p0 + 64], rhs=SA[s_][:, hh, :], start=True, stop=False),
                     reads=[Btok_b[bi], SA_b[s_]], writes=[psDS.b])
                S.op("pe", lambda e, hh=hh, p0=p0, c=c, bi=bi: e.matmul(psDS.h[hh][p0:p0 + 64, :], lhsT=Ktok[bi][:, c, p0:p0 + 64], rhs=V16[bi][:, c, p0:p0 + 64], start=False, stop=True),
                     reads=[Ktok_b[bi], V16_b[bi]], writes=[psDS.b])
            for hh in range(2):
                p0 = 64 * hh
                S.op("dve", lambda e, hh=hh, p0=p0: e.tensor_tensor(out=ST[p0:p0 + 64, :], in0=psDS.h[hh][p0:p0 + 64, :], in1=S32[p0:p0 + 64, :], op=ALU.add), reads=[psDS.b, S32_b], writes=[ST_b])
            wl = wi[:, c * 128 + 127:c * 128 + 128]
            S.op("dve", lambda e, wl=wl: e.tensor_scalar(out=S32[:], in0=ST[:], scalar1=wl, scalar2=None, op0=ALU.mult), reads=[ST_b, WINC_b[bi]], writes=[S32_b])
            S.op("act", lambda e, wl=wl: e.activation(out=S16[:], in_=ST[:], func=AF.Copy, scale=wl), reads=[ST_b, WINC_b[bi]], writes=[S16_b])

        for c0 in range(0, NCH, 2):
            unit_scores(c0, 0)
            unit_scores(c0 + 1, 1)
            for rd in range(7):
                unit_round(c0, 0, rd)
                unit_round(c0 + 1, 1, rd)
                drain(3)
            unit_state(c0, 0)
            unit_state(c0 + 1, 1)
            drain(3)
        drain(len(pending))

        def final():
            Y = Yall[bi]
            Y3 = Y[:].rearrange("p c (h v) -> p (c h) v", h=2)
            NG = NCH * 2
            S.op("dve", lambda e, Y3=Y3: e.tensor_reduce(out=gst[:, 0, :], in_=Y3, axis=AX.X, op=ALU.add), reads=[Yall_b[bi]], writes=[gst_b])
            S.op("pool", lambda e, Y=Y: e.tensor_tensor(out=Ysq[:], in0=Y[:], in1=Y[:], op=ALU.mult), reads=[Yall_b[bi]], writes=[Ysq_b])
            S.op("dve", lambda e: e.tensor_reduce(out=gst[:, 1, :], in_=Ysq[:].rearrange("p c (h v) -> p (c h) v", h=2), axis=AX.X, op=ALU.add), reads=[Ysq_b], writes=[gst_b])
            S.op("dve", lambda e: e.tensor_scalar(out=gst[:, 2, :], in0=gst[:, 0, :], scalar1=1.0 / 64, scalar2=None, op0=ALU.mult), reads=[gst_b], writes=[gst_b])
            S.op("dve", lambda e: e.tensor_tensor(out=gst[:, 3, :], in0=gst[:, 2, :], in1=gst[:, 2, :], op=ALU.mult), reads=[gst_b], writes=[gst_b])
            S.op("dve", lambda e: e.scalar_tensor_tensor(out=gst[:, 4, :], in0=gst[:, 1, :], scalar=1.0 / 64, in1=gst[:, 3, :], op0=ALU.mult, op1=ALU.subtract), reads=[gst_b], writes=[gst_b])
            S.op("act", lambda e: e.activation(out=gst[:, 5, :], in_=gst[:, 4, :], func=AF.Sqrt, bias=GN_EPS, scale=1.0), reads=[gst_b], writes=[gst_b])
            S.op("dve", lambda e: e.reciprocal(out=gst[:, 6, :], in_=gst[:, 5, :]), reads=[gst_b], writes=[gst_b])
            S.op("dve", lambda e, Y3=Y3: e.tensor_tensor(out=Y3, in0=Y3, in1=gst[:, 2, :].unsqueeze(2).to_broadcast([128, NG, 64]), op=ALU.subtract), reads=[Yall_b[bi], gst_b], writes=[Yall_b[bi]])
            S.op("dve", lambda e, Y3=Y3: e.tensor_tensor(out=Y3, in0=Y3, in1=gst[:, 6, :].unsqueeze(2).to_broadcast([128, NG, 64]), op=ALU.mult), reads=[Yall_b[bi], gst_b], writes=[Yall_b[bi]])
            S.op("pool", lambda e, Y=Y, pr=pr: e.tensor_tensor(out=Y[:], in0=Y[:], in1=cf[:, 1, pr * 128:(pr + 1) * 128].unsqueeze(1).to_broadcast([128, NCH, 128]), op=ALU.mult),
                 reads=[Yall_b[bi], cf_b], writes=[Yall_b[bi]])
            S.op("pool", lambda e, Y=Y, pr=pr: e.tensor_tensor(out=Y[:], in0=Y[:], in1=cf[:, 2, pr * 128:(pr + 1) * 128].unsqueeze(1).to_broadcast([128, NCH, 128]), op=ALU.add),
                 reads=[Yall_b[bi], cf_b], writes=[Yall_b[bi]])
            S.op("dve", lambda e, bi=bi: e.tensor_tensor(out=Ysq[:].rearrange("p c (h v) -> p (c h) v", h=2), in0=V32[bi][:].rearrange("p c (h v) -> p (c h) v", h=2),
                                                         in1=BS[bi][:].rearrange("p c h -> p (c h)").unsqueeze(2).to_broadcast([128, NG, 64]), op=ALU.mult),
                 reads=[V32_b[bi], BS_b[bi], Ysq_b], writes=[Ysq_b])
            S.op("dve", lambda e, Y=Y: e.tensor_tensor(out=Y[:], in0=Y[:], in1=Ysq[:], op=ALU.add), reads=[Yall_b[bi], Ysq_b], writes=[Yall_b[bi]])
            S.op("dve", lambda e, Y=Y, bi=bi: e.tensor_tensor(out=Y[:], in0=Y[:], in1=Gt[bi][:], op=ALU.mult), reads=[Yall_b[bi], Gt_b[bi]], writes=[Yall_b[bi]])
            S.op("sp", lambda e, Y=Y, t0=t0, pr=pr: e.dma_start(out=out["o"][t0:t0 + SEG, 640 + pr * 128:640 + (pr + 1) * 128].rearrange("(tb j) c -> j tb c", j=128), in_=Y[:]),
                 reads=[Yall_b[bi]], writes=[out["o_b"]], dma=True)

        if defer_final:
            S.op = rec_op
            try:
                final()
            finally:
                S.op = real_op
        else:
            final()

    prep(units[0][0], units[0][1], 0)
    for ui, (pr, sg) in enumerate(units):
        bi = ui % 2
        if ui + 1 < len(units):
            S.op = rec_op
            try:
                prep(units[ui + 1][0], units[ui + 1][1], (ui + 1) % 2)
            finally:
                S.op = real_op
        chunks_and_final(pr, sg, bi, defer_final=(ui + 1 < len(units)))
    drain(len(pending))
    S.barrier()
    A.release(m0)

import numpy as np

D = 2048
NTR = 17
TR = NTR * 128
DFF = 5632
NJ = DFF // 128
EPS = 1e-6
TGROUPS = [(0, 512), (512, 512), (1024, 512), (1536, 512), (2048, 128)]


def declare_io_R(nc, sfx, with_xh):
    def din(name, shape, dt=F32):
        return nc.dram_tensor("R_" + name + sfx, list(shape), dt, kind="ExternalInput").ap()
    io = dict(
        w_out=din("w_out", [D, D]), gvT=din("gvT", [128, 16]), gT=din("gT", [128, 2, 16]), gF=din("gF", [3, D]),
        mem=din("mem", [256, D]), msgT=din("msgT", [128, 16]),
        wq=din("wq", [D, 512]), wkv=din("wkv", [D, 1024]), wo=din("wo", [512, D]),
        w_up=din("w_up", [D, 2 * DFF]), cw=din("cw", [128, 2 * NJ, 3]), cb=din("cb", [128, 2 * NJ]), w_down=din("w_down", [DFF, D]),
    )
    if with_xh:
        io["xh"] = din("xh", [TR, D])
    return io


def emit_R(S, nc, io, psbig, cfg):
    x1s, x1s_b, x2s, x2s_b = cfg["x1s"], cfg["x1s_b"], cfg["x2s"], cfg["x2s_b"]
    acts, acts_b, y3s, y3s_b = cfg["acts"], cfg["acts_b"], cfg["y3s"], cfg["y3s_b"]
    xo, xo_b = cfg["xo"], cfg["xo_b"]
    G_o, G_o_b, G_ssq, G_ssq_b = cfg["G_o"], cfg["G_o_b"], cfg["G_ssq"], cfg["G_ssq_b"]
    flag2, flag_b = cfg["flag2"], cfg["flag_b"]
    if True:
        A = S.arena
        mR = A.mark()
        sb = A.alloc
        ps = [PS(psbig[:, i * 512:(i + 1) * 512], f"ps{i}") for i in range(8)]
        psb16 = psbig.bitcast(BF16)
        ident, ident_b, ones16 = S.ident, S.ident_b, S.ones16
        flag = flag2[:, 0:1]
        gvT = sb("gvT", [128, 16], F32); gT = sb("gT", [128, 2, 16], F32); msgT = sb("msgT", [128, 16], F32)
        par_b = Buf("par")
        S.op("sp", lambda e: e.dma_start(out=gvT[:], in_=io["gvT"]), writes=[par_b], dma=True)
        S.op("sp", lambda e: e.dma_start(out=gT[:], in_=io["gT"]), writes=[par_b], dma=True)
        S.op("sp", lambda e: e.dma_start(out=msgT[:], in_=io["msgT"]), writes=[par_b], dma=True)
        GS = sb("GS", [128, 4, 32], F32); GS_b = Buf("GS")
        S.op("sp", lambda e: e.dma_start(out=GS[:], in_=G_ssq.rearrange("k j tb -> j k tb")), reads=[G_ssq_b], writes=[GS_b], dma=True)
        gF = sb("gF", [128, D], F32); gF_b = Buf("gF")

        junk = sb("junk", [128, D], BF16); junk_b = Buf("junk")
        stt = [sb(f"stt{i}", [128, 8], F32) for i in range(2)]; stt_b = [Buf(f"stt{i}") for i in range(2)]
        xs16 = [sb(f"xs16_{i}", [128, D], BF16) for i in range(2)]; xs16_b = [Buf(f"xs16_{i}") for i in range(2)]
        cnt = {"nt": 0, "rs": 0}

        def rstd_from_sbuf(src, src_b, st, st_b, col):
            S.op("act", lambda e: e.activation(out=junk[:], in_=src[:], func=AF.Square, accum_out=st[:, col:col + 1]),
                 reads=[src_b], writes=[junk_b, st_b])
            S.op("act", lambda e: e.activation(out=st[:, col:col + 1], in_=st[:, col:col + 1], func=AF.Sqrt, scale=1.0 / D, bias=EPS),
                 reads=[st_b], writes=[st_b])
            S.op("dve", lambda e: e.reciprocal(out=st[:, col:col + 1], in_=st[:, col:col + 1]), reads=[st_b], writes=[st_b])

        def norm_T(src, src_b, scale_ap, scale_b, gainT_ap, dst3, dst_b, pbanks, use=None):
            if use is None:
                i = cnt["nt"] % 2
                cnt["nt"] += 1
            else:
                i = use
            if scale_ap is not None:
                S.op("dve", lambda e: e.tensor_scalar(out=xs16[i][:], in0=src[:], scalar1=scale_ap, scalar2=None, op0=ALU.mult),
                     reads=[src_b, scale_b], writes=[xs16_b[i]])
            pa, pb = ps[pbanks[0]], ps[pbanks[1]]
            for c in range(16):
                pp, bk = (pa, pbanks[0]) if c < 8 else (pb, pbanks[1])
                S.op("pe", lambda e, c=c, bk=bk: e.transpose(out=psb16[:, bk * 1024 + (c % 8) * 128:bk * 1024 + (c % 8 + 1) * 128],
                                                             in_=xs16[i][:, c * 128:(c + 1) * 128], identity=ident[:]),
                     reads=[xs16_b[i], ident_b], writes=[pp.b])
            for half, (pp, bk) in enumerate(((pa, pbanks[0]), (pb, pbanks[1]))):
                S.op("dve", lambda e, half=half, bk=bk: e.tensor_tensor(
                    out=dst3[:, half * 8:(half + 1) * 8, :],
                    in0=psb16[:, bk * 1024:bk * 1024 + 1024].rearrange("p (c n) -> p c n", c=8),
                    in1=gainT_ap[:, half * 8:(half + 1) * 8].unsqueeze(2).to_broadcast([128, 8, 128]), op=ALU.mult),
                    reads=[pp.b, par_b], writes=[dst_b])
            return xs16[i], xs16_b[i]

        def resid_update(xt, xt_b, ybanks, st, st_b, ytmp=None, ytmp_b=None):
            b0 = ybanks[0]
            yb = [ps[b].b for b in ybanks]
            for q, b in enumerate(ybanks):
                S.op("act", lambda e, q=q, b=b: e.activation(out=junk[:, q * 512:(q + 1) * 512], in_=ps[b].t[:, :], func=AF.Square,
                                                             accum_out=st[:, q:q + 1]), reads=[ps[b].b], writes=[junk_b, st_b])
            S.op("dve", lambda e: e.tensor_reduce(out=st[:, 4:5], in_=st[:, 0:4], axis=AX.X, op=ALU.add), reads=[st_b], writes=[st_b])
            S.op("act", lambda e: e.activation(out=st[:, 5:6], in_=st[:, 4:5], func=AF.Sqrt, scale=1.0 / D, bias=EPS), reads=[st_b], writes=[st_b])
            S.op("dve", lambda e: e.reciprocal(out=st[:, 6:7], in_=st[:, 5:6]), reads=[st_b], writes=[st_b])
            if ytmp is None:
                ytmp, ytmp_b = xs32, xs32_b
            S.op("dve", lambda e: e.scalar_tensor_tensor(out=ytmp[:], in0=psbig[:, b0 * 512:(b0 + 4) * 512], scalar=st[:, 6:7], in1=gF[:],
                                                         op0=ALU.mult, op1=ALU.mult), reads=yb + [st_b, gF_b], writes=[ytmp_b])
            S.op("dve", lambda e: e.tensor_tensor(out=xt[:], in0=xt[:], in1=ytmp[:], op=ALU.add), reads=[xt_b, ytmp_b], writes=[xt_b])

        xs32 = sb("xs32", [128, D], F32); xs32_b = Buf("xs32")

        mA = A.mark()
        S.op("sp", lambda e: e.dma_start(out=gF[:], in_=io["gF"][0].partition_broadcast(128)), writes=[gF_b], dma=True)
        hT = sb("hT", [128, 16, TR], BF16); hT_b = Buf("hT")
        mA2 = A.mark()
        w_out = sb("w_out_sb", [128, 16, D], BF16); w_out_b = Buf("w_out")
        for kc in range(0, 16, 4):
            S.op("pool", lambda e, kc=kc: e.dma_start(out=w_out[:, kc:kc + 4, :], in_=io["w_out"].rearrange("(kc p) n -> p kc n", p=128)[:, kc:kc + 4, :]),
                 writes=[w_out_b], dma=True)
        ot = [sb(f"ot{i}", [128, D], F32) for i in range(2)]; ot_b = [Buf(f"ot{i}") for i in range(2)]
        ot1 = [xs32, xs32]; ot1_b = [xs32_b, xs32_b]
        xt = [sb(f"xt{i}", [128, D], F32) for i in range(2)]; xt_b = [Buf(f"xt{i}") for i in range(2)]
        sq4 = [sb(f"sq4_{i}", [128, 4], F32) for i in range(2)]; sq4_b = [Buf(f"sq4_{i}") for i in range(2)]
        oT = [sb(f"oT{i}", [128, 16, 128], BF16) for i in range(2)]; oT_b = [Buf(f"oT{i}") for i in range(2)]
        jsel = {}
        def frontA(t):
            i = t % 2
            rows = slice(t * 128, (t + 1) * 128)
            r0 = max(t - 1, 0) * 128
            r1 = 1920 + t * 128
            for r in range(2):
                S.op("sp", lambda e, i=i, r=r, r0=r0: e.dma_start(out=ot[i][:, r * 1024:(r + 1) * 1024], in_=G_o[(r0 // 512) * 1024 + r * 512 + r0 % 512:(r0 // 512) * 1024 + r * 512 + r0 % 512 + 128, :]),
                     reads=[G_o_b], writes=[ot_b[i]], dma=True)
                S.op("sp", lambda e, i=i, r=r, r1=r1: e.dma_start(out=ot1[i][:, r * 1024:(r + 1) * 1024], in_=G_o[(r1 // 512) * 1024 + r * 512 + r1 % 512:(r1 // 512) * 1024 + r * 512 + r1 % 512 + 128, :]),
                     reads=[G_o_b], writes=[ot1_b[i]], dma=True)
            S.op("act", lambda e, i=i: e.activation(out=ot[i][:], in_=ot[i][:], func=AF.Copy, scale=flag2[:, 1:2]),
                 reads=[ot_b[i], flag_b], writes=[ot_b[i]])
            S.op("dve", lambda e, i=i: e.scalar_tensor_tensor(out=ot[i][:], in0=ot1[i][:], scalar=flag2[:, 0:1], in1=ot[i][:], op0=ALU.mult, op1=ALU.add),
                 reads=[ot_b[i], ot1_b[i], flag_b], writes=[ot_b[i]])
            xsrc, xsrc_b = cfg["x_rows"](t)
            S.op("sp", lambda e, i=i, xsrc=xsrc: e.dma_start(out=xt[i][:], in_=xsrc), reads=[xsrc_b], writes=[xt_b[i]], dma=True)
            tb0 = max(t - 1, 0)
            tb1 = 15 + t
            S.op("dve", lambda e, i=i, tb0=tb0: e.tensor_scalar(out=sq4[i][:], in0=GS[:, :, tb0], scalar1=flag2[:, 1:2], scalar2=None, op0=ALU.mult),
                 reads=[GS_b, flag_b], writes=[sq4_b[i]])
            S.op("dve", lambda e, i=i, tb1=tb1: e.scalar_tensor_tensor(out=sq4[i][:], in0=GS[:, :, tb1], scalar=flag2[:, 0:1], in1=sq4[i][:], op0=ALU.mult, op1=ALU.add),
                 reads=[GS_b, flag_b, sq4_b[i]], writes=[sq4_b[i]])
            st, stb = stt[i], stt_b[i]
            S.op("dve", lambda e, i=i, st=st: e.tensor_tensor(out=st[:, 0:2], in0=sq4[i][:, 0:2], in1=sq4[i][:, 2:4], op=ALU.add), reads=[sq4_b[i]], writes=[stb])
            S.op("act", lambda e, st=st: e.activation(out=st[:, 2:3], in_=st[:, 0:1], func=AF.Sqrt, scale=1.0 / 768, bias=EPS), reads=[stb], writes=[stb])
            S.op("act", lambda e, st=st: e.activation(out=st[:, 3:4], in_=st[:, 1:2], func=AF.Sqrt, scale=1.0 / 512, bias=EPS), reads=[stb], writes=[stb])
            S.op("dve", lambda e, st=st: e.reciprocal(out=st[:, 2:4], in_=st[:, 2:4]), reads=[stb], writes=[stb])
            j = cnt["nt"] % 2
            cnt["nt"] += 1
            jsel[t] = j
            xs = xs16[j]
            for c0 in (0, 1024):
                S.op("dve", lambda e, c0=c0, xs=xs, i=i, st=st: e.tensor_scalar(out=xs[:, c0:c0 + 384], in0=ot[i][:, c0:c0 + 384], scalar1=st[:, 2:3], scalar2=None, op0=ALU.mult),
                     reads=[ot_b[i], stb], writes=[xs16_b[j]])
                S.op("dve", lambda e, c0=c0, xs=xs, i=i, st=st: e.tensor_scalar(out=xs[:, c0 + 384:c0 + 640], in0=ot[i][:, c0 + 384:c0 + 640], scalar1=st[:, 3:4], scalar2=None, op0=ALU.mult),
                     reads=[ot_b[i], stb], writes=[xs16_b[j]])
                S.op("act", lambda e, c0=c0, xs=xs, i=i: e.copy(out=xs[:, c0 + 640:c0 + 1024], in_=ot[i][:, c0 + 640:c0 + 1024]),
                     reads=[ot_b[i]], writes=[xs16_b[j]])

        def frontT(t):
            i = t % 2
            norm_T(None, None, None, None, gvT, oT[i], oT_b[i], (4, 5), use=jsel[t])

        def backMM(t):
            i = t % 2
            for blk in range(4):
                for kc in range(16):
                    S.op("pe", lambda e, blk=blk, kc=kc, i=i: e.matmul(ps[blk].t[:, :], lhsT=oT[i][:, kc, :], rhs=w_out[:, kc, blk * 512:(blk + 1) * 512],
                                                                      start=(kc == 0), stop=(kc == 15)), reads=[oT_b[i], w_out_b], writes=[ps[blk].b])

        def backA(t):
            i = t % 2
            rows = slice(t * 128, (t + 1) * 128)
            st, stb = stt[i], stt_b[i]
            resid_update(xt[i], xt_b[i], (0, 1, 2, 3), st, stb, ytmp=ot[i], ytmp_b=ot_b[i])
            S.op("sp", lambda e, i=i, rows=rows: e.dma_start(out=x1s[rows, :], in_=xt[i][:]), reads=[xt_b[i]], writes=[x1s_b], dma=True)
            rstd_from_sbuf(xt[i], xt_b[i], st, stb, 7)
            norm_T(xt[i], xt_b[i], st[:, 7:8], stb, gT[:, 0, :], hT[:, :, t * 128:(t + 1) * 128], hT_b, (6, 7))
        frontA(0)
        frontT(0)
        if NTR > 1:
            frontA(1)
        for t in range(NTR):
            backMM(t)
            if t + 1 < NTR:
                frontT(t + 1)
            backA(t)
            if t + 2 < NTR:
                frontA(t + 2)
        S.barrier()
        A.release(mA2)
        if cfg.get("stop") == "A":
            A.release(mR)
            return

        S.op("sp", lambda e: e.dma_start(out=gF[:], in_=io["gF"][1].partition_broadcast(128)), writes=[gF_b], dma=True)
        wq = sb("wq_sb", [128, 16, 512], BF16); wkv = sb("wkv_sb", [128, 16, 1024], BF16); wo = sb("wo_sb", [128, 4, D], BF16)
        wB_b = Buf("wB")
        S.op("pool", lambda e: e.dma_start(out=wq[:], in_=io["wq"].rearrange("(kc p) n -> p kc n", p=128)), writes=[wB_b], dma=True)
        for kc in range(0, 16, 8):
            S.op("pool", lambda e, kc=kc: e.dma_start(out=wkv[:, kc:kc + 8, :], in_=io["wkv"].rearrange("(kc p) n -> p kc n", p=128)[:, kc:kc + 8, :]), writes=[wB_b], dma=True)
        S.op("pool", lambda e: e.dma_start(out=wo[:], in_=io["wo"].rearrange("(kc p) n -> p kc n", p=128)), writes=[wB_b], dma=True)
        memT = sb("memT", [128, 16, 256], BF16); memT_b = Buf("memT")
        x1t = [sb(f"x1t{i}", [128, D], F32) for i in range(2)]; x1t_b = [Buf(f"x1t{i}") for i in range(2)]
        mt, mt_b = x1t, x1t_b
        for mb in range(2):
            S.op("sp", lambda e, mb=mb: e.dma_start(out=mt[mb][:], in_=io["mem"][mb * 128:(mb + 1) * 128, :]), writes=[mt_b[mb]], dma=True)
            rstd_from_sbuf(mt[mb], mt_b[mb], stt[mb], stt_b[mb], 7)
            norm_T(mt[mb], mt_b[mb], stt[mb][:, 7:8], stt_b[mb], msgT, memT[:, :, mb * 128:(mb + 1) * 128], memT_b, (4, 5))
        kT = sb("kT", [128, 4, 256], BF16); kT_b = Buf("kT")
        Vm = sb("Vm", [128, 2, 512], BF16); Vm_b = Buf("Vm")
        for h in range(4):
            bk = h // 2
            for kc in range(16):
                S.op("pe", lambda e, h=h, kc=kc, bk=bk: e.matmul(ps[bk].t[:, (h % 2) * 256:(h % 2 + 1) * 256], lhsT=wkv[:, kc, h * 128:(h + 1) * 128], rhs=memT[:, kc, :],
                                                                 start=(kc == 0), stop=(kc == 15)), reads=[wB_b, memT_b], writes=[ps[bk].b])
        for bk in range(2):
            S.op("act", lambda e, bk=bk: e.copy(out=kT[:, 2 * bk:2 * bk + 2, :].rearrange("p a m -> p (a m)"), in_=ps[bk].t[:, :]), reads=[ps[bk].b], writes=[kT_b])
        for mb in range(2):
            for kc in range(16):
                S.op("pe", lambda e, mb=mb, kc=kc: e.matmul(ps[2 + mb].t[:, :], lhsT=memT[:, kc, mb * 128:(mb + 1) * 128], rhs=wkv[:, kc, 512:1024],
                                                            start=(kc == 0), stop=(kc == 15)), reads=[wB_b, memT_b], writes=[ps[2 + mb].b])
            S.op("dve", lambda e, mb=mb: e.tensor_copy(out=Vm[:, mb, :], in_=ps[2 + mb].t[:, :]), reads=[ps[2 + mb].b], writes=[Vm_b])
        qT = sb("qT", [128, 4, 512], BF16); qT_b = Buf("qT")
        PT = [sb(f"PT{i}", [128, 2, 512], BF16) for i in range(2)]; PT_b = [Buf(f"PT{i}") for i in range(2)]
        rec = sb("rec", [128, 512], F32); rec_b = Buf("rec")
        oT2 = sb("oT2", [128, 4, 512], BF16); oT2_b = Buf("oT2")
        hk = 0
        SCL = 128 ** -0.5
        for (g0, n) in TGROUPS:
            for h in range(4):
                for kc in range(16):
                    S.op("pe", lambda e, h=h, kc=kc, g0=g0, n=n: e.matmul(ps[h].t[:, 0:n], lhsT=wq[:, kc, h * 128:(h + 1) * 128], rhs=hT[:, kc, g0:g0 + n],
                                                                          start=(kc == 0), stop=(kc == 15)), reads=[wB_b, hT_b], writes=[ps[h].b])
                if h % 2 == 0:
                    S.op("act", lambda e, h=h, n=n: e.copy(out=qT[:, h, 0:n], in_=ps[h].t[:, 0:n]), reads=[ps[h].b], writes=[qT_b])
                else:
                    S.op("dve", lambda e, h=h, n=n: e.tensor_copy(out=qT[:, h, 0:n], in_=ps[h].t[:, 0:n]), reads=[ps[h].b], writes=[qT_b])
            for h in range(4):
                pi = hk % 2
                hk += 1
                for mb in range(2):
                    S.op("pe", lambda e, h=h, mb=mb, n=n: e.matmul(ps[4 + mb].t[:, 0:n], lhsT=kT[:, h, mb * 128:(mb + 1) * 128], rhs=qT[:, h, 0:n], start=True, stop=True),
                         reads=[kT_b, qT_b], writes=[ps[4 + mb].b])
                    S.op("act", lambda e, mb=mb, n=n, pi=pi: e.activation(out=PT[pi][:, mb, 0:n], in_=ps[4 + mb].t[:, 0:n], func=AF.Exp, scale=SCL),
                         reads=[ps[4 + mb].b], writes=[PT_b[pi]])
                for mb in range(2):
                    S.op("pe", lambda e, h=h, mb=mb, n=n, pi=pi: e.matmul(ps[6].t[:, 0:n], lhsT=Vm[:, mb, h * 128:(h + 1) * 128], rhs=PT[pi][:, mb, 0:n],
                                                                          start=(mb == 0), stop=(mb == 1)), reads=[Vm_b, PT_b[pi]], writes=[ps[6].b])
                for mb in range(2):
                    S.op("pe", lambda e, mb=mb, n=n, pi=pi: e.matmul(ps[7].t[:, 0:n], lhsT=ones16[:], rhs=PT[pi][:, mb, 0:n],
                                                                     start=(mb == 0), stop=(mb == 1)), reads=[ident_b, PT_b[pi]], writes=[ps[7].b])
                S.op("dve", lambda e, n=n: e.reciprocal(out=rec[:, 0:n], in_=ps[7].t[:, 0:n]), reads=[ps[7].b], writes=[rec_b])
                S.op("dve", lambda e, h=h, n=n: e.tensor_tensor(out=oT2[:, h, 0:n], in0=ps[6].t[:, 0:n], in1=rec[:, 0:n], op=ALU.mult),
                     reads=[ps[6].b, rec_b], writes=[oT2_b])
            for tt in range(n // 128):
                t = g0 // 128 + tt
                i = t % 2
                rows = slice(t * 128, (t + 1) * 128)
                S.op("sp", lambda e, i=i, rows=rows: e.dma_start(out=x1t[i][:], in_=x1s[rows, :]), reads=[x1s_b], writes=[x1t_b[i]], dma=True)
                for blk in range(4):
                    for h in range(4):
                        S.op("pe", lambda e, blk=blk, h=h, tt=tt: e.matmul(ps[blk].t[:, :], lhsT=oT2[:, h, tt * 128:(tt + 1) * 128], rhs=wo[:, h, blk * 512:(blk + 1) * 512],
                                                                         start=(h == 0), stop=(h == 3)), reads=[oT2_b, wB_b], writes=[ps[blk].b])
                st, stb = stt[i], stt_b[i]
                resid_update(x1t[i], x1t_b[i], (0, 1, 2, 3), st, stb)
                S.op("sp", lambda e, i=i, rows=rows: e.dma_start(out=x2s[rows, :], in_=x1t[i][:]), reads=[x1t_b[i]], writes=[x2s_b], dma=True)
                rstd_from_sbuf(x1t[i], x1t_b[i], st, stb, 7)
                if t == 0:
                    S.op("dve", lambda e, st=st: e.tensor_tensor(out=st[:, 7:8], in0=st[:, 7:8], in1=flag, op=ALU.mult), reads=[stb, flag_b], writes=[stb])
                norm_T(x1t[i], x1t_b[i], st[:, 7:8], stb, gT[:, 1, :], hT[:, :, t * 128:(t + 1) * 128], hT_b, (4, 5))
        S.barrier()
        A.release(mA2)
        if cfg.get("stop") == "B":
            A.release(mR)
            return

        cw = sb("cw", [128, 2 * NJ, 3], F32); cb = sb("cb", [128, 2 * NJ], F32); cwb_b = Buf("cwb")
        S.op("sp", lambda e: e.dma_start(out=cw[:], in_=io["cw"]), writes=[cwb_b], dma=True)
        S.op("sp", lambda e: e.dma_start(out=cb[:], in_=io["cb"]), writes=[cwb_b], dma=True)
        wu = [sb(f"wu{i}", [128, 16, 256], BF16) for i in range(3)]; wu_b = [Buf(f"wu{i}") for i in range(3)]
        UG = [sb(f"UG{i}", [128, 2, TR + 2], F32) for i in range(2)]; UG_b = [Buf(f"UG{i}") for i in range(2)]
        CV = [sb(f"CV{i}", [128, 2, TR], F32) for i in range(2)]; CV_b = [Buf(f"CV{i}") for i in range(2)]
        AC = [sb(f"AC{i}", [128, TR], BF16) for i in range(2)]; AC_b = [Buf(f"AC{i}") for i in range(2)]
        for i in range(2):
            S.op("pool", lambda e, i=i: e.memset(UG[i][:, :, 0:2], 0.0), writes=[UG_b[i]])
        w_up_v = io["w_up"].rearrange("(kc p) n -> p kc n", p=128)
        pk = 0
        def load_wu(j):
            w = j % 3
            S.op("pool", lambda e, w=w, j=j: e.dma_start(out=wu[w][:, :, 0:128], in_=w_up_v[:, :, j * 128:(j + 1) * 128]), writes=[wu_b[w]], dma=True)
            S.op("pool", lambda e, w=w, j=j: e.dma_start(out=wu[w][:, :, 128:256], in_=w_up_v[:, :, DFF + j * 128:DFF + (j + 1) * 128]), writes=[wu_b[w]], dma=True)
        load_wu(0)
        load_wu(1)
        for j in range(NJ):
            i = j % 2
            w = j % 3
            if j + 2 < NJ:
                load_wu(j + 2)
            for (g0, n) in TGROUPS:
                for gv in range(2):
                    pp = ps[pk % 8]
                    pk += 1
                    for kc in range(16):
                        S.op("pe", lambda e, pp=pp, kc=kc, gv=gv, w=w, g0=g0, n=n: e.matmul(pp.t[:, 0:n], lhsT=wu[w][:, kc, gv * 128:(gv + 1) * 128], rhs=hT[:, kc, g0:g0 + n],
                                                                                         start=(kc == 0), stop=(kc == 15)), reads=[wu_b[w], hT_b], writes=[pp.b])
                    if gv == 0:
                        S.op("act", lambda e, pp=pp, gv=gv, i=i, g0=g0, n=n: e.copy(out=UG[i][:, gv, 2 + g0:2 + g0 + n], in_=pp.t[:, 0:n]), reads=[pp.b], writes=[UG_b[i]])
                    else:
                        S.op("dve", lambda e, pp=pp, gv=gv, i=i, g0=g0, n=n: e.tensor_copy(out=UG[i][:, gv, 2 + g0:2 + g0 + n], in_=pp.t[:, 0:n]), reads=[pp.b], writes=[UG_b[i]])
            for gv in range(2):
                cidx = j if gv == 0 else NJ + j
                S.op("act", lambda e, gv=gv, i=i, cidx=cidx: e.activation(out=CV[i][:, gv, :], in_=UG[i][:, gv, 2:TR + 2], func=AF.Identity,
                                                                        scale=cw[:, cidx, 2:3], bias=cb[:, cidx:cidx + 1]), reads=[UG_b[i], cwb_b], writes=[CV_b[i]])
                S.op("dve", lambda e, gv=gv, i=i, cidx=cidx: e.scalar_tensor_tensor(out=CV[i][:, gv, :], in0=UG[i][:, gv, 1:TR + 1], scalar=cw[:, cidx, 1:2], in1=CV[i][:, gv, :],
                                                                                   op0=ALU.mult, op1=ALU.add), reads=[UG_b[i], cwb_b, CV_b[i]], writes=[CV_b[i]])
                S.op("dve", lambda e, gv=gv, i=i, cidx=cidx: e.scalar_tensor_tensor(out=CV[i][:, gv, :], in0=UG[i][:, gv, 0:TR], scalar=cw[:, cidx, 0:1], in1=CV[i][:, gv, :],
                                                                                   op0=ALU.mult, op1=ALU.add), reads=[UG_b[i], cwb_b, CV_b[i]], writes=[CV_b[i]])
            S.op("act", lambda e, i=i: e.activation(out=CV[i][:, 0, :], in_=CV[i][:, 0, :], func=AF.Gelu_apprx_tanh), reads=[CV_b[i]], writes=[CV_b[i]])
            S.op("dve", lambda e, i=i: e.tensor_tensor(out=AC[i][:], in0=CV[i][:, 0, :], in1=CV[i][:, 1, :], op=ALU.mult), reads=[CV_b[i]], writes=[AC_b[i]])
            S.op("sp", lambda e, i=i, j=j: e.dma_start(out=acts.rearrange("(tt p j) t -> p tt j t", p=128, j=NJ)[:, :, j, :],
                                                       in_=AC[i][:].rearrange("p (tt t) -> p tt t", t=128)), reads=[AC_b[i]], writes=[acts_b], dma=True)
        S.barrier()
        A.release(mA)
        if cfg.get("stop") == "C":
            A.release(mR)
            return

        S.op("sp", lambda e: e.dma_start(out=gF[:], in_=io["gF"][2].partition_broadcast(128)), writes=[gF_b], dma=True)
        wd = [sb(f"wd{i}", [128, NJ, 512], BF16) for i in range(2)]; wd_b = [Buf(f"wd{i}") for i in range(2)]
        at = [sb(f"at{i}", [128, NJ, 128], BF16) for i in range(3)]; at_b = [Buf(f"at{i}") for i in range(3)]
        yst = [sb(f"yst{i}", [128, 512], F32) for i in range(3)]; yst_b = [Buf(f"yst{i}") for i in range(3)]
        ssq3 = sb("ssq3", [128, 16, 4], F32); ssq3_b = Buf("ssq3")
        acts_v = acts.rearrange("(tt p j) t -> tt p j t", p=128, j=NJ)
        w_down_v = io["w_down"].rearrange("(j p) n -> p j n", p=128)
        seq = [(blk, t) for blk in range(4) for t in range(1, NTR)]

        def load_wd(blk):
            i = blk % 2
            for j0 in range(0, NJ, 11):
                S.op("pool", lambda e, i=i, blk=blk, j0=j0: e.dma_start(out=wd[i][:, j0:j0 + 11, :], in_=w_down_v[:, j0:j0 + 11, blk * 512:(blk + 1) * 512]), writes=[wd_b[i]], dma=True)

        def load_at(k):
            a = k % 3
            t = seq[k][1]
            S.op("sp", lambda e, a=a, t=t: e.dma_start(out=at[a][:], in_=acts_v[t]), reads=[acts_b], writes=[at_b[a]], dma=True)
        load_wd(0)
        load_wd(1)
        load_at(0)
        load_at(1)
        for k, (blk, t) in enumerate(seq):
            i = blk % 2
            a = k % 3
            pp = ps[k % 8]
            if t == 1 and blk >= 1 and blk + 1 < 4:
                load_wd(blk + 1)
            if k + 2 < len(seq):
                load_at(k + 2)
            for j in range(NJ):
                S.op("pe", lambda e, pp=pp, j=j, a=a, i=i: e.matmul(pp.t[:, :], lhsT=at[a][:, j, :], rhs=wd[i][:, j, :], start=(j == 0), stop=(j == NJ - 1)),
                     reads=[at_b[a], wd_b[i]], writes=[pp.b])
            S.op("act", lambda e, pp=pp, a=a: e.copy(out=yst[a][:], in_=pp.t[:, :]), reads=[pp.b], writes=[yst_b[a]])
            S.op("dve", lambda e, a=a, t=t, blk=blk: e.scalar_tensor_tensor(out=junk[:, 0:512], in0=yst[a][:], scalar=1.0, in1=yst[a][:], op0=ALU.mult, op1=ALU.mult,
                                                                       accum_out=ssq3[:, t - 1, blk:blk + 1]),
                 reads=[yst_b[a]], writes=[junk_b, ssq3_b])
            S.op("sp", lambda e, a=a, t=t, blk=blk: e.dma_start(out=y3s[(t - 1) * 128:t * 128, blk * 512:(blk + 1) * 512], in_=yst[a][:]),
                 reads=[yst_b[a]], writes=[y3s_b], dma=True)
        rs = sb("rs", [128, 16, 4], F32); rs_b = Buf("rs")
        S.op("dve", lambda e: e.tensor_reduce(out=rs[:, :, 0], in_=ssq3[:], axis=AX.X, op=ALU.add), reads=[ssq3_b], writes=[rs_b])
        S.op("act", lambda e: e.activation(out=rs[:, :, 1], in_=rs[:, :, 0], func=AF.Sqrt, scale=1.0 / D, bias=EPS), reads=[rs_b], writes=[rs_b])
        S.op("dve", lambda e: e.reciprocal(out=rs[:, :, 2], in_=rs[:, :, 1]), reads=[rs_b], writes=[rs_b])
        yt = [sb(f"yt{i}", [128, D], F32) for i in range(2)]; yt_b = [Buf(f"yt{i}") for i in range(2)]
        x2t = [sb(f"x2t{i}", [128, D], F32) for i in range(2)]; x2t_b = [Buf(f"x2t{i}") for i in range(2)]
        def final_loads(t):
            i = t % 2
            S.op("sp", lambda e, i=i, t=t: e.dma_start(out=yt[i][:], in_=y3s[(t - 1) * 128:t * 128, :]), reads=[y3s_b], writes=[yt_b[i]], dma=True)
            S.op("sp", lambda e, i=i, t=t: e.dma_start(out=x2t[i][:], in_=x2s[t * 128:(t + 1) * 128, :]), reads=[x2s_b], writes=[x2t_b[i]], dma=True)
        final_loads(1)
        for t in range(1, NTR):
            i = t % 2
            if t + 1 < NTR:
                final_loads(t + 1)
            S.op("dve", lambda e, i=i, t=t: e.scalar_tensor_tensor(out=yt[i][:], in0=yt[i][:], scalar=rs[:, t - 1, 2:3], in1=gF[:], op0=ALU.mult, op1=ALU.mult),
                 reads=[yt_b[i], rs_b, gF_b], writes=[yt_b[i]])
            S.op("dve", lambda e, i=i: e.tensor_tensor(out=x2t[i][:], in0=x2t[i][:], in1=yt[i][:], op=ALU.add), reads=[x2t_b[i], yt_b[i]], writes=[x2t_b[i]])
            S.op("sp", lambda e, i=i, t=t: e.dma_start(out=xo[(t - 1) * 128:t * 128, :], in_=x2t[i][:]), reads=[x2t_b[i]], writes=[xo_b], dma=True)
        S.barrier()
        A.release(mR)

import contextlib

PAIRS = [[0, 1], [2, 3], [4, 5], [6, 7]]


def build_fused(stages=4, debug=False, r1_stop=None):
    nc = bass.Bass("TRN2", target_bir_lowering=False)
    T, D = 4096, 2048
    ioM = [declare_io_M(nc, "_l0", with_x=True), declare_io_M(nc, "_l1", with_x=False)]
    ioR = [declare_io_R(nc, "_l0", with_xh=True), declare_io_R(nc, "_l1", with_xh=False)]
    flag_in = nc.dram_tensor("flag2", [128, 2], F32, kind="ExternalInput").ap()
    xo_ext = nc.dram_tensor("xo", [2048, D], F32, kind="ExternalOutput").ap()

    def dint(name, shape, dt=F32):
        return nc.dram_tensor(name, list(shape), dt, kind="Internal").ap()
    scr = {"zTa": dint("zTa", [NF_A, T], BF16), "zTc": dint("zTc", [NF_C, T + 1]), "ztm": dint("ztm", [T + 1, NTM])}
    for k in ("zTa", "zTc", "ztm"):
        scr[k + "_b"] = Buf(k)
    scr["oa_scr"] = [dint(f"oa_scr{p}", [T, 65]) for p in range(3)]
    scr["oa_scr_b"] = [Buf(f"oa_scr{p}") for p in range(3)]
    o_scr = dint("o_scr", [T, 1024]); o_b = Buf("o_scr")
    ssq_scr = dint("ssq_scr", [2, 128, 32]); ssq_b = Buf("ssq_scr")
    G_o = dint("G_o", [2 * T, 1024]); G_o_b = Buf("G_o")
    G_ssq = dint("G_ssq", [4, 128, 32]); G_ssq_b = Buf("G_ssq")
    xo_scr = dint("xo_scr", [2048, D]); xo_scr_b = Buf("xo_scr")
    G_x = dint("G_x", [T, D]); G_x_b = Buf("G_x")
    xo_scr2 = dint("xo_scr2", [2048, D]); xo_scr2_b = Buf("xo_scr2")
    rs = dict(x1s=dint("x1s", [TR, D]), x1s_b=Buf("x1s"), x2s=dint("x2s", [TR, D]), x2s_b=Buf("x2s"),
              acts=dint("acts", [NTR * 128 * NJ, 128], BF16), acts_b=Buf("acts"), y3s=dint("y3s", [2048, D]), y3s_b=Buf("y3s"))
    xo_ext_b = Buf("xo_ext")
    with contextlib.ExitStack() as es:
        sems = [es.enter_context(nc.semaphore(f"s{i}")) for i in range(98)]
        S = Sched(nc, sems)
        S.arena = Arena(nc)
        A = S.arena
        psbig = es.enter_context(nc.psum_tensor("psbig", [128, 4096], F32)).ap()
        ident = A.alloc("ident", [128, 128], BF16); identf = A.alloc("identf", [128, 128], F32); ones16 = A.alloc("ones16", [128, 128], BF16)
        ident_b = Buf("ident")
        S.op("pool", lambda e: e.memset(identf[:], 0.0), writes=[ident_b])
        S.op("pool", lambda e: e.affine_select(out=identf[:], in_=identf[:], pattern=[[-1, 128]], base=0,
                                               channel_multiplier=1, compare_op=ALU.not_equal, fill=1.0), reads=[ident_b], writes=[ident_b])
        S.op("pool", lambda e: e.tensor_copy(out=ident[:], in_=identf[:]), reads=[ident_b], writes=[ident_b])
        S.op("pool", lambda e: e.memset(ones16[:], 1.0), writes=[ident_b])
        S.ident, S.identf, S.ident_b, S.ones16 = ident, identf, ident_b, ones16
        flag2 = A.alloc("flag2", [128, 2], F32); flag_b = Buf("flag2")
        S.op("sp", lambda e: e.dma_start(out=flag2[:], in_=flag_in), writes=[flag_b], dma=True)
        outM = {"o": o_scr, "o_b": o_b, "ssq": ssq_scr, "ssq_b": ssq_b,
                "ssqA": A.alloc("ssqA", [128, 32], F32), "ssqA_b": Buf("ssqA"), "ssqB": A.alloc("ssqB", [128, 32], F32), "ssqB_b": Buf("ssqB")}
        S.barrier()
        for l in range(2):
            io = ioM[l]
            if l == 1:
                io["x_rows"] = lambda t: G_x[((t % 16) // 2) * 512 + (t // 16) * 256 + (t % 2) * 128:((t % 16) // 2) * 512 + (t // 16) * 256 + (t % 2) * 128 + 128, :]
                io["x_b"] = G_x_b
            if 2 * l + 1 > stages:
                break
            emit_M(S, nc, io, scr, outM, psbig)
            S.recycle_dma_sems()
            for k in range(8):
                S.op("pool", lambda e, k=k: e.collective_compute("AllGather", ALU.bypass, replica_groups=PAIRS,
                                                                 ins=[o_scr[k * 512:(k + 1) * 512, :].opt()], outs=[G_o[k * 1024:(k + 1) * 1024, :].opt()]),
                     reads=[o_b], writes=[G_o_b], dma=True, dinc=1)
            S.op("pool", lambda e: e.collective_compute("AllGather", ALU.bypass, replica_groups=PAIRS,
                                                        ins=[ssq_scr.rearrange("g j t -> (g j) t").opt()], outs=[G_ssq.rearrange("k j t -> (k j) t").opt()]),
                 reads=[ssq_b], writes=[G_ssq_b], dma=True, dinc=1)
            if debug and 2 * l + 1 == stages:
                dbg = nc.dram_tensor("dbg_Go", [2 * T, 1024], F32, kind="ExternalOutput").ap()
                dbg2 = nc.dram_tensor("dbg_Gssq", [4, 128, 32], F32, kind="ExternalOutput").ap()
                S.op("sp", lambda e: e.dma_start(out=dbg[:, :], in_=G_o[:, :]), reads=[G_o_b], writes=[xo_ext_b], dma=True)
                S.op("sp", lambda e: e.dma_start(out=dbg2.rearrange("k j t -> (k j) t"), in_=G_ssq.rearrange("k j t -> (k j) t")), reads=[G_ssq_b], writes=[xo_ext_b], dma=True)
            if 2 * l + 2 > stages:
                break
            if l == 0:
                def x_rows(tt, io=ioR[0]):
                    return io["xh"][tt * 128:(tt + 1) * 128, :], Buf("xh_in")
                xo, xo_b = xo_scr, xo_scr_b
            else:
                def x_rows(tt):
                    if tt == 0:
                        return G_x[3712:3840, :], G_x_b
                    return xo_scr[(tt - 1) * 128:tt * 128, :], xo_scr_b
                xo, xo_b = xo_scr2, xo_scr2_b
            cfg = dict(rs, stop=(r1_stop if l == 1 else None), G_o=G_o, G_o_b=G_o_b, G_ssq=G_ssq, G_ssq_b=G_ssq_b, x_rows=x_rows, xo=xo, xo_b=xo_b, flag2=flag2, flag_b=flag_b)
            emit_R(S, nc, ioR[l], psbig, cfg)
            S.recycle_dma_sems()
            if debug and 2 * l + 2 == stages and l == 0:
                dbg3 = nc.dram_tensor("dbg_xo", [2048, D], F32, kind="ExternalOutput").ap()
                S.op("sp", lambda e: e.dma_start(out=dbg3[:, :], in_=xo_scr[:, :]), reads=[xo_scr_b], writes=[xo_ext_b], dma=True)
            if l == 0:
                for k in range(8):
                    S.op("pool", lambda e, k=k: e.collective_compute("AllGather", ALU.bypass, replica_groups=PAIRS,
                                                                     ins=[xo_scr[k * 256:(k + 1) * 256, :].opt()], outs=[G_x[k * 512:(k + 1) * 512, :].opt()]),
                         reads=[xo_scr_b], writes=[G_x_b], dma=True, dinc=1)
        if stages >= 4:
            for q in range(4):
                S.op("sp", lambda e, q=q: e.dma_start(out=xo_ext[q * 512:(q + 1) * 512, :], in_=xo_scr2[q * 512:(q + 1) * 512, :]),
                     reads=[xo_scr2_b], writes=[xo_ext_b], dma=True)
        S.final_wait("sp", [xo_ext_b])
        S.emit()
        print("fused ops", S.nops, "arena peak", A.peak, "sems left", len(S.sem_pool))
    return nc

import numpy as np
def t5_bucket(dist):
    max_exact = 16
    d = np.maximum(dist, 0)
    scaled = np.log(np.maximum(d, 1) / max_exact) / np.log(2048 / max_exact)
    large = np.minimum(max_exact + (scaled * (32 - max_exact)).astype(np.int32), 31)
    return np.where(d < max_exact, d, large).astype(np.int32)

def make_biasT(table, heads):
    qi = np.arange(128)[:, None]
    kj = np.arange(256)[None, :]
    delta = qi + 128 - kj
    valid = (delta >= 0) & (delta <= 128)
    out = np.zeros((128, len(heads), 3, 2, 128), np.float32)
    for p, d in enumerate((1, 4, 16)):
        bucket = t5_bucket(np.clip(delta, 0, 128) * d)
        for hi, h in enumerate(heads):
            b = np.where(valid, table[bucket, h], np.float32(-30000.0)).astype(np.float32)
            bt = b.T.reshape(2, 128, 128)
            out[:, hi, p, 0, :] = bt[0]
            out[:, hi, p, 1, :] = bt[1]
    return np.ascontiguousarray(out.reshape(128, len(heads) * 6, 128))

def cols_M(half):
    hA = np.arange(6) + 6 * half
    def hc(base, heads): return np.concatenate([base + h * 64 + np.arange(64) for h in heads])
    q = hc(0, hA); k = hc(768, hA); v = hc(1536, hA)
    gB = np.arange(4) + 4 * half
    u = hc(2304, gB); g = 2304 + 512 + np.concatenate([np.arange(256) + 256 * half, np.arange(256) + 256 * (1 - half)])
    c0 = 3328
    hC = np.arange(6) + 6 * half
    r = hc(c0, hC); kc = hc(c0 + 768, hC); vc = hc(c0 + 1536, hC)
    lo = c0 + 2304 + np.arange(384)
    fm = np.concatenate([q, k, r, kc, lo]); tm = np.concatenate([v, u, g, vc])
    return fm, tm


def inputs_M(d, l, b, half, x):
    fm, tm = cols_M(half)
    w_in = d["w_in"][l]
    g0 = d["sandwich_gains"][l][0]
    gB = np.arange(4) + 4 * half
    gperm = np.concatenate([np.arange(256) + 256 * half, np.arange(256) + 256 * (1 - half)])
    inm = {"x": np.ascontiguousarray(x), "g0T": np.ascontiguousarray(g0.reshape(16, 128).T),
           "w_fm": np.ascontiguousarray(w_in[:, fm]), "w_tm": np.ascontiguousarray(w_in[:, tm]),
           "biasT": make_biasT(d["rel_bias_table"], list(range(6 * half, 6 * half + 6))),
           "sgu_wT": np.ascontiguousarray(d["sgu_w"][l][gB].transpose(2, 0, 1)),
           "sgu_bT": np.ascontiguousarray(d["sgu_b"][l][gB].T),
           "sgu_ng": np.ascontiguousarray(d["sgu_norm_gain"][l][gperm]),
           }
    hC = np.arange(6) + 6 * half
    def hcols(base): return np.concatenate([base + hh * 64 + np.arange(64) for hh in hC])
    mu = d["rwkv_mu"][l]
    cp = np.zeros((128, 25), np.float32)
    def pairs(v384): return np.ascontiguousarray(v384.reshape(3, 128).T)
    cp[:, 0:3] = pairs(mu[hcols(0)]); cp[:, 3:6] = pairs(mu[hcols(768)])
    cp[:, 6:9] = pairs(d["rwkv_w0"][l][hcols(0)]); cp[:, 9:12] = pairs(d["rwkv_a0"][l][hcols(0)])
    cp[:, 12:15] = pairs(d["rwkv_k_k"][l][hcols(0)]); cp[:, 15:18] = pairs(d["rwkv_k_a"][l][hcols(0)])
    cp[:, 18:21] = pairs(d["rwkv_r_k"][l].reshape(-1)[hcols(0)])
    cp[0:64, 21] = mu[2304:2368]; cp[0:64, 22] = mu[2368:2432]
    cp[:, 23] = mu[2432:2560]; cp[:, 24] = mu[2560:2688]
    inm["cp"] = cp
    inm["cf"] = np.ascontiguousarray(np.stack([mu[hcols(1536)], d["rwkv_ln_gain"][l][hcols(0)], d["rwkv_ln_bias"][l][hcols(0)]]))
    inm["w_up"] = np.ascontiguousarray(d["rwkv_w_up"][l][:, hcols(0)])
    inm["a_up"] = np.ascontiguousarray(d["rwkv_a_up"][l][:, hcols(0)])
    inm["g_up"] = np.ascontiguousarray(d["rwkv_g_up"][l][:, hcols(0)])
    return inm


def wout_perm():
    A0 = np.arange(0, 384); A1 = np.arange(384, 768)
    B0 = 768 + np.arange(0, 256); B1 = 768 + np.arange(256, 512)
    C0 = 1280 + np.arange(0, 384); C1 = 1280 + np.arange(384, 768)
    return np.concatenate([A0, B0, C0, A1, B1, C1])


def fmT(v, n=16):
    return np.ascontiguousarray(v.reshape(n, 128).T)


def inputs_R(d, l, b, half, x_b, o0, o1, ssq0, ssq1):
    TR = 17 * 128
    lo = half * 2048 - 128
    def halo_rows(a):
        out = np.zeros((TR,) + a.shape[1:], a.dtype)
        if lo < 0:
            out[128:] = a[0:2048]
        else:
            out[:] = a[lo:lo + TR]
        return out
    xh = halo_rows(x_b)
    oc = np.concatenate([halo_rows(o0), halo_rows(o1)], axis=1)
    ssq = np.zeros((128, 17, 4), np.float32)
    for tt in range(17):
        tb = half * 16 - 1 + tt
        if tb < 0:
            continue
        ssq[:, tt, 0] = ssq0[0][:, tb]; ssq[:, tt, 1] = ssq0[1][:, tb]
        ssq[:, tt, 2] = ssq1[0][:, tb]; ssq[:, tt, 3] = ssq1[1][:, tb]
    g = d["sandwich_gains"][l]
    ag = d["attn_out_gain"][l]; sg = d["sgu_out_gain"][l]
    one = np.ones(384, np.float32)
    gv = np.concatenate([ag[0:384], sg[0:256], one, ag[384:768], sg[256:512], one])
    cwv = d["ffn_conv_w"][l]
    inm = dict(
        xh=xh, oc=np.ascontiguousarray(oc), ssq=ssq,
        w_out=np.ascontiguousarray(d["w_out"][l][wout_perm(), :]), gvT=fmT(gv),
        gT=np.ascontiguousarray(np.stack([fmT(g[2]), fmT(g[4])], axis=1)), gF=np.ascontiguousarray(np.stack([g[1], g[3], g[5]])),
        mem=np.ascontiguousarray(d["mem"][b]), msgT=fmT(d["mem_src_gain"][l]),
        wq=np.ascontiguousarray(d["mem_wq"][l]), wkv=np.ascontiguousarray(d["mem_wkv"][l]), wo=np.ascontiguousarray(d["mem_wo"][l]),
        w_up=np.ascontiguousarray(d["ffn_w_up"][l]),
        cw=np.ascontiguousarray(cwv.reshape(3, 88, 128).transpose(2, 1, 0)), cb=np.ascontiguousarray(d["ffn_conv_b"][l].reshape(88, 128).T),
        w_down=np.ascontiguousarray(d["ffn_w_down"][l]),
        flag=np.full((128, 1), float(half), np.float32),
    )
    return inm


def m_out_from_ref(oa_pre, ob_pre, oc, half):
    o = np.concatenate([oa_pre[:, 384 * half:384 * half + 384], ob_pre[:, 256 * half:256 * half + 256], oc[:, 384 * half:384 * half + 384]], axis=1)
    sA = (oa_pre[:, 384 * half:384 * half + 384] ** 2).sum(-1); sB = (ob_pre[:, 256 * half:256 * half + 256] ** 2).sum(-1)
    ssq = np.stack([sA.reshape(32, 128).T, sB.reshape(32, 128).T])
    return np.ascontiguousarray(o.astype(np.float32)), np.ascontiguousarray(ssq.astype(np.float32))


def inputs_R_fused(d, l, b, half, x_b=None):
    g = d["sandwich_gains"][l]
    ag = d["attn_out_gain"][l]; sg = d["sgu_out_gain"][l]
    one = np.ones(384, np.float32)
    gv = np.concatenate([ag[0:384], sg[0:256], one, ag[384:768], sg[256:512], one])
    cwv = d["ffn_conv_w"][l]
    inm = dict(
        w_out=np.ascontiguousarray(d["w_out"][l][wout_perm(), :]), gvT=fmT(gv),
        gT=np.ascontiguousarray(np.stack([fmT(g[2]), fmT(g[4])], axis=1)), gF=np.ascontiguousarray(np.stack([g[1], g[3], g[5]])),
        mem=np.ascontiguousarray(d["mem"][b]), msgT=fmT(d["mem_src_gain"][l]),
        wq=np.ascontiguousarray(d["mem_wq"][l]), wkv=np.ascontiguousarray(d["mem_wkv"][l]), wo=np.ascontiguousarray(d["mem_wo"][l]),
        w_up=np.ascontiguousarray(d["ffn_w_up"][l]),
        cw=np.ascontiguousarray(cwv.reshape(3, 88, 128).transpose(2, 1, 0)), cb=np.ascontiguousarray(d["ffn_conv_b"][l].reshape(88, 128).T),
        w_down=np.ascontiguousarray(d["ffn_w_down"][l]),
    )
    if x_b is not None:
        TR = 17 * 128
        lo = half * 2048 - 128
        xh = np.zeros((TR, x_b.shape[1]), np.float32)
        if lo < 0:
            xh[128:] = x_b[0:2048]
        else:
            xh[:] = x_b[lo:lo + TR]
        inm["xh"] = xh
    return inm


def inputs_fused(d, c):
    b, half = c // 2, c % 2
    x = np.asarray(d["x"], dtype=np.float32)
    out = {}
    for l in range(2):
        im = inputs_M(d, l, b, half, x[b])
        if l == 1:
            im.pop("x")
        for k, v in im.items():
            out[f"{k}_l{l}"] = v
        ir = inputs_R_fused(d, l, b, half, x[b] if l == 0 else None)
        for k, v in ir.items():
            out[f"R_{k}_l{l}"] = v
    fl = np.zeros((128, 2), np.float32)
    fl[:, 0] = float(half)
    fl[:, 1] = 1.0 - float(half)
    out["flag2"] = fl
    return out


_NC_CACHE = {}


def kernel(**inputs):
    d = {k: np.asarray(v) for k, v in inputs.items()}
    if "F" not in _NC_CACHE:
        _NC_CACHE["F"] = build_fused()
    nc = _NC_CACHE["F"]
    cores = list(range(8))
    in_maps = [inputs_fused(d, c) for c in cores]
    res = run_bass_kernel_spmd(nc, in_maps, core_ids=cores)
    x = np.stack([np.concatenate([np.asarray(res.results[2 * b]["xo"]), np.asarray(res.results[2 * b + 1]["xo"])], axis=0) for b in range(4)])
    return np.ascontiguousarray(x.astype(np.float32))
```
